# Optimizing a Trainium2 kernel written in Bass

```python
import math
import jax, jax.numpy as jnp
from jax import lax
import numpy as np

D_MODEL = 2048
BATCH = 8
SEQ = 2048
DEPTH = 2
DEC_BATCH = 32
DEC_SEQ = 1
PAST_LEN = 8192
PAGE_SIZE = 128

N_EVEN = (DEPTH + 1) // 2
N_ODD = DEPTH // 2

A_HEAD_DIM = 128
A_HEADS = D_MODEL // (2 * A_HEAD_DIM)
A_WIDTH = A_HEADS * A_HEAD_DIM
DILATION_PATTERNS = ((128, 1), (512, 4), (2048, 16))
WINDOW_MAX = 2048
QBLK = 128

REL_BUCKETS = 32
REL_MAX_DIST = 2048

CHUNK = 128
B_GROUP_CH = 128
B_WIDTH = D_MODEL // 2
B_GROUPS = B_WIDTH // B_GROUP_CH

AB_COLS = 4 * A_WIDTH + 3 * B_WIDTH
AB_SPLITS = (A_WIDTH, 2 * A_WIDTH, 3 * A_WIDTH, 4 * A_WIDTH, 4 * A_WIDTH + B_WIDTH, 4 * A_WIDTH + 2 * B_WIDTH)

C_HEAD_DIM = 64
C_HEADS = D_MODEL // C_HEAD_DIM
C_DECAY_LORA = 96
C_ICLR_LORA = 96

PLE_DIM = 256
RMS_EPS = 1e-6
LN_EPS = 1e-5
GN_EPS = 64e-5
NEG_INF = -1e30

kernel_name = 'hybrid_dilated_gmlp_rwkv7_step'


def rms_norm(x, g):
    xf = x.astype(jnp.float32)
    y = xf * lax.rsqrt(jnp.mean(xf * xf, axis=-1, keepdims=True) + RMS_EPS)
    return (y * g.astype(jnp.float32)).astype(x.dtype)


def layer_norm(x, g, b):
    xf = x.astype(jnp.float32)
    mu = jnp.mean(xf, axis=-1, keepdims=True)
    var = jnp.mean(jnp.square(xf - mu), axis=-1, keepdims=True)
    return ((xf - mu) * lax.rsqrt(var + LN_EPS) * g.astype(jnp.float32) + b.astype(jnp.float32)).astype(x.dtype)


def t5_bucket(dist):
    n_exact = REL_BUCKETS // 2
    d = jnp.maximum(dist, 1).astype(jnp.float32)
    log_b = n_exact + (jnp.log(d / n_exact) / math.log(REL_MAX_DIST / n_exact) * (REL_BUCKETS - n_exact)).astype(jnp.int32)
    return jnp.where(dist < n_exact, dist, jnp.minimum(log_b, REL_BUCKETS - 1))


def dilated_attn_prompt(q, k, v, rel_bias, window, dil):
    Bn, S, H, E = q.shape
    n_steps = window // dil
    span = dil * QBLK
    Sp = -(-S // span) * span
    nb = Sp // span

    def blocks(t):
        t = jnp.pad(t, ((0, 0), (0, Sp - S), (0, 0), (0, 0)))
        return t.reshape(Bn, nb, QBLK, dil, H, E)

    def with_prev(t):
        prev = jnp.pad(t, ((0, 0), (1, 0), (0, 0), (0, 0), (0, 0), (0, 0)))[:, :-1]
        return jnp.concatenate([prev, t], axis=2)

    qb = blocks(q)
    kc = with_prev(blocks(k))
    vc = with_prev(blocks(v))
    s = jnp.einsum('bnirhe,bnjrhe->bnrhij', qb, kc, preferred_element_type=jnp.float32) * (E ** -0.5)
    i = jnp.arange(QBLK)[:, None]
    j = jnp.arange(2 * QBLK)[None, :]
    steps = QBLK + i - j
    band = (steps >= 0) & (steps <= n_steps)
    has_prev = (jnp.arange(nb)[:, None, None] > 0) | (j >= QBLK)[None]
    valid = band[None] & has_prev
    bias = jnp.moveaxis(rel_bias[t5_bucket(jnp.clip(steps, 0) * dil)], -1, 0).astype(jnp.float32)
    s = jnp.where(valid[None, :, None, None], s + bias, NEG_INF)
    lse = jax.nn.logsumexp(s, axis=-1)
    p = jnp.exp(s - lse[..., None]).astype(v.dtype)
    o = jnp.einsum('bnrhij,bnjrhe->bnirhe', p, vc, preferred_element_type=jnp.float32)
    o = o.reshape(Bn, Sp, H, E)[:, :S]
    lse = jnp.transpose(lse, (0, 1, 4, 2, 3)).reshape(Bn, Sp, H)[:, :S]
    return o, lse


def dilated_attn_sample(q, k_all, v_all, rel_bias, window, dil):
    T, E = q.shape[1], q.shape[-1]
    L = k_all.shape[1] - T
    back = jnp.arange(window // dil + 1)
    idx = L + jnp.arange(T)[:, None] - back[None, :] * dil
    valid = idx >= 0
    idx = jnp.maximum(idx, 0)
    kg = k_all[:, idx]
    vg = v_all[:, idx]
    s = jnp.einsum('bthe,btshe->bths', q, kg, preferred_element_type=jnp.float32) * (E ** -0.5)
    bias = rel_bias[t5_bucket(back * dil)].T.astype(jnp.float32)
    s = jnp.where(valid[None, :, None, :], s + bias, NEG_INF)
    lse = jax.nn.logsumexp(s, axis=-1)
    p = jnp.exp(s - lse[..., None]).astype(vg.dtype)
    o = jnp.einsum('bths,btshe->bthe', p, vg, preferred_element_type=jnp.float32)
    return o, lse


def mix_by_denominator(results):
    outs, lses = zip(*results)
    wts = jax.nn.softmax(jnp.stack(lses), axis=0)
    return jnp.einsum('pbth,pbthe->bthe', wts, jnp.stack(outs))


def chunk_spatial_mix(v, w_s, b_s):
    Bn, T, G, C = v.shape
    Tp = -(-T // CHUNK) * CHUNK
    vp = jnp.pad(v, ((0, 0), (0, Tp - T), (0, 0), (0, 0))).reshape(Bn, Tp // CHUNK, CHUNK, G, C)
    w = w_s * jnp.tril(jnp.ones((CHUNK, CHUNK), w_s.dtype))
    out = jnp.einsum('gij,bnjgc->bnigc', w, vp) + b_s.T[None, None, :, :, None]
    return out.reshape(Bn, Tp, G, C)[:, :T]


def ab_layer(xn, attend_a, w_in, w_out, w_s, b_s, ln_g, ln_b):
    Bn, T, _ = xn.shape
    z = xn @ w_in
    q, k, v, g_a, u_b, v_b, g_b = jnp.split(z, AB_SPLITS, axis=-1)
    heads = lambda t: t.reshape(Bn, T, A_HEADS, A_HEAD_DIM)
    k_h, v_h = heads(k), heads(v)
    o_a = attend_a(heads(q), k_h, v_h).reshape(Bn, T, A_WIDTH).astype(xn.dtype)
    vn = layer_norm(jax.nn.gelu(v_b), ln_g, ln_b)
    s_b = chunk_spatial_mix(vn.reshape(Bn, T, B_GROUPS, B_GROUP_CH), w_s, b_s).reshape(Bn, T, B_WIDTH)
    o_b = jax.nn.gelu(u_b) * s_b.astype(xn.dtype)
    y = jnp.concatenate([o_a * jax.nn.silu(g_a), o_b * jax.nn.silu(g_b)], axis=-1) @ w_out
    return y, k_h, v_h, vn


def rwkv7_time_mix(xn, shift_prev, wkv_prev, mu, w_r, w_k, w_v, w_g, w_o, w0, w1, w2, a0, a1, a2, k_k, k_a, r_k, gn_g, gn_b):
    Bn, T, D = xn.shape
    H, N = C_HEADS, C_HEAD_DIM
    f32 = jnp.float32
    x_prev = jnp.concatenate([shift_prev[:, None, :].astype(xn.dtype), xn[:, :-1]], axis=1)
    xx = x_prev - xn
    xr, xw, xk, xv, xa, xg = (xn + xx * mu[m] for m in range(6))
    r = (xr @ w_r).astype(f32)
    k = (xk @ w_k).astype(f32)
    v = (xv @ w_v).astype(f32)
    g = jax.nn.silu(xg @ w_g)
    w_log = -jax.nn.softplus(-(w0 + jnp.tanh(xw @ w1) @ w2).astype(f32)) - 0.5
    decay = jnp.exp(-jnp.exp(w_log))
    a = jax.nn.sigmoid((a0 + (xa @ a1) @ a2).astype(f32))
    heads = lambda t: t.reshape(Bn, T, H, N)
    kk = heads(k * k_k.astype(f32))
    kk = kk / jnp.maximum(jnp.sqrt(jnp.sum(kk * kk, axis=-1, keepdims=True)), 1e-12)
    k = k * (1.0 + (a - 1.0) * k_a.astype(f32))
    r_h, w_h, k_h, v_h, a_h = heads(r), heads(decay), heads(k), heads(v), heads(a)

    def step(S, inp):
        r_t, w_t, k_t, v_t, kk_t, a_t = inp
        sa = jnp.einsum('bhij,bhj->bhi', S, -kk_t)
        S = S * w_t[:, :, None, :] + sa[..., :, None] * (kk_t * a_t)[..., None, :] + v_t[..., :, None] * k_t[..., None, :]
        return S, jnp.einsum('bhij,bhj->bhi', S, r_t)

    tm = lambda t: jnp.moveaxis(t, 1, 0)
    S_final, y = lax.scan(step, wkv_prev.astype(f32), (tm(r_h), tm(w_h), tm(k_h), tm(v_h), tm(kk), tm(a_h)))
    y = jnp.moveaxis(y, 0, 1)
    mean = jnp.mean(y, axis=-1, keepdims=True)
    var = jnp.mean(jnp.square(y - mean), axis=-1, keepdims=True)
    y = (y - mean) * lax.rsqrt(var + GN_EPS) * gn_g.reshape(H, N).astype(f32) + gn_b.reshape(H, N).astype(f32)
    y = y + jnp.sum(r_h * k_h * r_k.astype(f32), axis=-1, keepdims=True) * v_h
    out = (y.reshape(Bn, T, D).astype(xn.dtype) * g) @ w_o
    return out, S_final, xn[:, -1]


def per_layer_input(h, p, w_proj, w_gate):
    return jax.nn.sigmoid(h @ w_gate) * (p @ w_proj)


def setup_inputs(seed: int = 0) -> dict:
    key = jax.random.key(seed)
    ks = iter(jax.random.split(key, 48))
    nrm = lambda shape, scale: jax.random.normal(next(ks), shape, jnp.float32) * scale
    D = D_MODEL
    win_s = min(WINDOW_MAX, PAST_LEN)
    return {
        'x_prompt': nrm((BATCH, SEQ, D), 1.0),
        'x_sample': nrm((DEC_BATCH, DEC_SEQ, D), 1.0),
        'cache_a_k': nrm((N_EVEN, DEC_BATCH, win_s, A_HEADS, A_HEAD_DIM), 1.0),
        'cache_a_v': nrm((N_EVEN, DEC_BATCH, win_s, A_HEADS, A_HEAD_DIM), 1.0),
        'state_c_wkv': nrm((N_ODD, DEC_BATCH, C_HEADS, C_HEAD_DIM, C_HEAD_DIM), 0.1),
        'state_c_shift': nrm((N_ODD, DEC_BATCH, D), 1.0),
        'p_prompt': nrm((DEPTH, BATCH, SEQ, PLE_DIM), 1.0),
        'p_sample': nrm((DEPTH, DEC_BATCH, DEC_SEQ, PLE_DIM), 1.0),
        'norm_g': 1.0 + nrm((DEPTH, D), 0.02),
        'final_norm_g': 1.0 + nrm((D,), 0.02),
        'rel_bias': nrm((REL_BUCKETS, A_HEADS), 0.5),
        'ab_w_in': nrm((N_EVEN, D, AB_COLS), D ** -0.5),
        'ab_w_out': nrm((N_EVEN, A_WIDTH + B_WIDTH, D), (A_WIDTH + B_WIDTH) ** -0.5),
        'b_w_s': nrm((N_EVEN, B_GROUPS, CHUNK, CHUNK), CHUNK ** -0.5),
        'b_b_s': 1.0 + nrm((N_EVEN, B_GROUPS, CHUNK), 0.1),
        'b_ln_g': 1.0 + nrm((N_EVEN, B_WIDTH), 0.02),
        'b_ln_b': nrm((N_EVEN, B_WIDTH), 0.02),
        'c_mu': jax.random.uniform(next(ks), (N_ODD, 6, D), jnp.float32),
        'c_w_r': nrm((N_ODD, D, D), D ** -0.5),
        'c_w_k': nrm((N_ODD, D, D), D ** -0.5),
        'c_w_v': nrm((N_ODD, D, D), D ** -0.5),
        'c_w_g': nrm((N_ODD, D, D), D ** -0.5),
        'c_w_o': nrm((N_ODD, D, D), D ** -0.5),
        'c_w0': -2.0 + nrm((N_ODD, D), 0.5),
        'c_w1': nrm((N_ODD, D, C_DECAY_LORA), D ** -0.5),
        'c_w2': nrm((N_ODD, C_DECAY_LORA, D), 0.5 * C_DECAY_LORA ** -0.5),
        'c_a0': nrm((N_ODD, D), 0.1),
        'c_a1': nrm((N_ODD, D, C_ICLR_LORA), D ** -0.5),
        'c_a2': nrm((N_ODD, C_ICLR_LORA, D), 0.5 * C_ICLR_LORA ** -0.5),
        'c_k_k': 0.85 + nrm((N_ODD, D), 0.05),
        'c_k_a': 1.0 + nrm((N_ODD, D), 0.05),
        'c_r_k': nrm((N_ODD, C_HEADS, C_HEAD_DIM), 0.1),
        'c_gn_g': 1.0 + nrm((N_ODD, D), 0.02),
        'c_gn_b': nrm((N_ODD, D), 0.02),
        'ple_w_proj': nrm((DEPTH, PLE_DIM, D), 0.5 * PLE_DIM ** -0.5),
        'ple_w_gate': nrm((DEPTH, D, D), D ** -0.5),
    }


def reference(x_prompt, x_sample, cache_a_k, cache_a_v, state_c_wkv, state_c_shift, p_prompt, p_sample,
              norm_g, final_norm_g, rel_bias, ab_w_in, ab_w_out, b_w_s, b_b_s, b_ln_g, b_ln_b,
              c_mu, c_w_r, c_w_k, c_w_v, c_w_g, c_w_o, c_w0, c_w1, c_w2, c_a0, c_a1, c_a2,
              c_k_k, c_k_a, c_r_k, c_gn_g, c_gn_b, ple_w_proj, ple_w_gate):
    hp, hs = x_prompt, x_sample
    Bp, Sp_len = x_prompt.shape[0], x_prompt.shape[1]
    win_p = min(WINDOW_MAX, Sp_len)
    a_k_p, a_v_p, a_k_s, a_v_s, b_v_s = [], [], [], [], []
    c_S_p, c_x_p, c_S_s, c_x_s = [], [], [], []

    def attend_prompt(q, k, v):
        return mix_by_denominator([dilated_attn_prompt(q, k, v, rel_bias, w, d) for w, d in DILATION_PATTERNS])

    for i in range(DEPTH):
        j = i // 2
        xp = rms_norm(hp, norm_g[i])
        xs = rms_norm(hs, norm_g[i])
        if i % 2 == 0:
            ab_w = (ab_w_in[j], ab_w_out[j], b_w_s[j], b_b_s[j], b_ln_g[j], b_ln_b[j])
            mp, k_p, v_p, _ = ab_layer(xp, attend_prompt, *ab_w)
            ck, cv = cache_a_k[j], cache_a_v[j]

            def attend_sample(q, k, v, ck=ck, cv=cv):
                k_all = jnp.concatenate([ck.astype(k.dtype), k], axis=1)
                v_all = jnp.concatenate([cv.astype(v.dtype), v], axis=1)
                return mix_by_denominator([dilated_attn_sample(q, k_all, v_all, rel_bias, w, d) for w, d in DILATION_PATTERNS])

            ms, k_s, v_s, vn_s = ab_layer(xs, attend_sample, *ab_w)
            a_k_p.append(k_p[:, -win_p:])
            a_v_p.append(v_p[:, -win_p:])
            a_k_s.append(k_s)
            a_v_s.append(v_s)
            b_v_s.append(vn_s)
        else:
            c_w = (c_mu[j], c_w_r[j], c_w_k[j], c_w_v[j], c_w_g[j], c_w_o[j], c_w0[j], c_w1[j], c_w2[j],
                   c_a0[j], c_a1[j], c_a2[j], c_k_k[j], c_k_a[j], c_r_k[j], c_gn_g[j], c_gn_b[j])
            mp, S_p, sh_p = rwkv7_time_mix(xp, jnp.zeros((Bp, D_MODEL), xp.dtype),
                                           jnp.zeros((Bp, C_HEADS, C_HEAD_DIM, C_HEAD_DIM), jnp.float32), *c_w)
            ms, S_s, sh_s = rwkv7_time_mix(xs, state_c_shift[j], state_c_wkv[j], *c_w)
            c_S_p.append(S_p)
            c_x_p.append(sh_p)
            c_S_s.append(S_s)
            c_x_s.append(sh_s)
        hp = hp + mp
        hs = hs + ms
        hp = hp + per_layer_input(hp, p_prompt[i], ple_w_proj[i], ple_w_gate[i])
        hs = hs + per_layer_input(hs, p_sample[i], ple_w_proj[i], ple_w_gate[i])

    y_prompt = rms_norm(hp, final_norm_g)
    y_sample = rms_norm(hs, final_norm_g)
    return (y_prompt, y_sample, jnp.stack(a_k_p), jnp.stack(a_v_p), jnp.stack(a_k_s), jnp.stack(a_v_s),
            jnp.stack(b_v_s), jnp.stack(c_S_p), jnp.stack(c_x_p), jnp.stack(c_S_s), jnp.stack(c_x_s))
```

```python
import contextlib
import numpy as np
import concourse.bass as bass
import concourse.mybir as mybir
from concourse.bass_utils import run_bass_kernel_spmd

F32 = mybir.dt.float32
BF16 = mybir.dt.bfloat16
AF = mybir.ActivationFunctionType
ALU = mybir.AluOpType
AX = mybir.AxisListType

NCORES = 8
D = 2048
S = 2048
NT = S // 128
KC = D // 128
NS = 4
ABC = 7168
NEGB = -30000.0
import os
NORAW = bool(int(os.environ.get("NORAW", "0")))


class Buf:
    __slots__ = ("name", "lw", "rd")

    def __init__(self, name):
        self.name = name
        self.lw = None
        self.rd = []


class Op:
    __slots__ = ("eng", "fn", "deps", "is_dma", "key", "marked", "val", "waits", "raw")

    def __init__(self, eng, fn, is_dma=False, key=None):
        self.eng = eng
        self.fn = fn
        self.deps = []
        self.is_dma = is_dma
        self.key = key
        self.marked = False
        self.val = 0
        self.waits = []


class Prog:
    def __init__(self, nc, stack):
        self.nc = nc
        self.stack = stack
        self.ops = []
        self.E = {"pe": nc.tensor, "act": nc.scalar, "dve": nc.vector, "pool": nc.gpsimd, "sp": nc.sync}
        self.bufs = {}

    def buf(self, name):
        b = self.bufs.get(name)
        if b is None:
            b = Buf(name)
            self.bufs[name] = b
        return b

    def _mk(self, op, reads, writes):
        deps = []
        raw = set()
        for b in reads:
            if isinstance(b, str):
                b = self.buf(b)
            if b.lw is not None:
                deps.append(b.lw)
                raw.add(id(b.lw))
        for b in writes:
            if isinstance(b, str):
                b = self.buf(b)
            if b.lw is not None:
                deps.append(b.lw)
            deps.extend(b.rd)
        op.raw = raw
        for b in reads:
            if isinstance(b, str):
                b = self.buf(b)
            b.rd.append(op)
        for b in writes:
            if isinstance(b, str):
                b = self.buf(b)
            b.lw = op
            b.rd = []
        seen = set()
        for d in deps:
            if id(d) not in seen and d is not op:
                seen.add(id(d))
                op.deps.append(d)
        self.ops.append(op)
        return op

    def add(self, eng, fn, reads=(), writes=()):
        return self._mk(Op(eng, fn), reads, writes)

    def dma(self, eng, out, in_, reads, writes, key, **kw):
        e = self.E[eng]
        return self._mk(Op(eng, lambda: e.dma_start(out=out, in_=in_, **kw), True, key), reads, writes)

    def mm(self, out, lhsT, rhs, start, stop, reads, writes):
        pe = self.nc.tensor
        return self.add("pe", lambda: pe.matmul(out, lhsT=lhsT, rhs=rhs, start=start, stop=stop), reads, writes)

    def tr(self, out, in_, ident, reads, writes):
        pe = self.nc.tensor
        return self.add("pe", lambda: pe.transpose(out, in_, ident), reads, writes)

    def act(self, out, in_, func, reads, writes, eng="act", **kw):
        e = self.nc.scalar
        return self.add("act", lambda: e.activation(out=out, in_=in_, func=func, **kw), reads, writes)

    def copy(self, eng, out, in_, reads, writes):
        e = self.E[eng]
        if eng == "act":
            return self.add("act", lambda: e.copy(out=out, in_=in_), reads, writes)
        return self.add(eng, lambda: e.tensor_copy(out=out, in_=in_), reads, writes)

    def tt(self, eng, out, in0, in1, op, reads, writes):
        e = self.E[eng]
        return self.add(eng, lambda: e.tensor_tensor(out=out, in0=in0, in1=in1, op=op), reads, writes)

    def ts(self, eng, out, in0, s1, s2, op0, op1, reads, writes, **kw):
        e = self.E[eng]
        if op1 is None:
            return self.add(eng, lambda: e.tensor_scalar(out=out, in0=in0, scalar1=s1, scalar2=None, op0=op0, **kw),
                            reads, writes)
        return self.add(eng, lambda: e.tensor_scalar(out=out, in0=in0, scalar1=s1, scalar2=s2, op0=op0, op1=op1, **kw),
                        reads, writes)

    def stt(self, eng, out, in0, scalar, in1, op0, op1, reads, writes):
        e = self.E[eng]
        return self.add(eng, lambda: e.scalar_tensor_tensor(out=out, in0=in0, scalar=scalar, in1=in1, op0=op0, op1=op1),
                        reads, writes)

    def finish(self):
        nc = self.nc
        engs = ["pe", "act", "dve", "pool", "sp"]
        esem = {e: self.stack.enter_context(nc.semaphore("s_" + e)) for e in engs}
        dcount = {}
        keyeng = {}
        known = {e: {} for e in engs}
        dsem = {}
        pend = []
        for op in self.ops:
            w = []
            for a in op.deps:
                if a.is_dma:
                    w.append(("d", a.key, dcount[a.key]))
                else:
                    if a.eng == op.eng and not op.is_dma and (a.eng == "pe" or NORAW):
                        continue
                    a.marked = True
                    w.append(("c", a, 0))
            pend.append(w)
            if op.is_dma:
                assert keyeng.setdefault(op.key, op.eng) == op.eng, ("DMA sem shared across queues", op.key)
                dcount[op.key] = dcount.get(op.key, 0) + 16
                op.val = dcount[op.key]
        cnt = {e: 0 for e in engs}
        for op in self.ops:
            if not op.is_dma and op.marked:
                cnt[op.eng] += 1
                op.val = cnt[op.eng]
        for k in dcount:
            dsem[k] = self.stack.enter_context(nc.semaphore("d_" + k))
        nwait = 0
        for op, w in zip(self.ops, pend):
            E = self.E[op.eng]
            kn = known[op.eng]
            need = {}
            for kind, ref, val in w:
                if kind == "d":
                    sem = dsem[ref]
                    v = val
                else:
                    sem = esem[ref.eng]
                    v = ref.val
                sid = id(sem)
                if kn.get(sid, 0) >= v:
                    continue
                if sid not in need or need[sid][1] < v:
                    need[sid] = (sem, v)
            for sid, (sem, v) in need.items():
                E.wait_ge(sem, v)
                kn[sid] = v
                nwait += 1
            ins = op.fn()
            if op.is_dma:
                ins.then_inc(dsem[op.key], 16)
            elif op.marked:
                ins.then_inc(esem[op.eng], 1)
        for k, c in dcount.items():
            nc.sync.wait_ge(dsem[k], c)
        for e in engs:
            if e != "sp" and cnt[e] > 0:
                nc.sync.wait_ge(esem[e], cnt[e])
        self.stats = (len(self.ops), nwait, len(dcount))


def t5_bucket_np(dist):
    dist = np.asarray(dist, dtype=np.int64)
    n_exact = 16
    d = np.maximum(dist, 1).astype(np.float32)
    log_b = n_exact + (np.log(d / n_exact) / np.float32(np.log(2048 / n_exact)) * (32 - n_exact)).astype(np.int32)
    return np.where(dist < n_exact, dist, np.minimum(log_b, 31))


def host_consts():
    c = {}
    c["ident"] = np.eye(128, dtype=np.float32)
    oh = np.zeros((64, 3, 512), np.float32)
    for p, dil in enumerate((1, 4, 16)):
        s = np.arange(129)
        b = t5_bucket_np(s * dil)
        oh[b, p, s] = 1.0
        oh[32, p, 129:] = 1.0
    c["onehot"] = oh.reshape(64, 3 * 512)
    j = np.arange(128)[:, None]
    i = np.arange(128)[None, :]
    c["trimask"] = (i >= j).astype(np.float32)
    s_ = np.arange(64)[:, None]
    t_ = np.arange(64)[None, :]
    c["maskg"] = np.concatenate([(s_ < t_), (s_ <= t_)], axis=1).astype(np.float32)
    c["maskn"] = (t_ < s_).astype(np.float32)
    ss = np.arange(128)[:, None]
    tt = np.arange(128)[None, :]
    c["uneg"] = (-np.exp(-0.5) * ((ss <= tt) & (ss // 64 == tt // 64))).astype(np.float32)
    selb = np.zeros((4, 4, 128), np.float32)
    for b in range(4):
        selb[b, b, :] = 1.0
    c["selb"] = selb.reshape(4, 512)
    onesel = np.zeros((128, 4, 4), np.float32)
    for b in range(4):
        onesel[:, b, b] = 1.0
    c["onesel"] = onesel.reshape(128, 16)
    return c


def build(stage=99, dbg=None):
    nc = bass.Bass("TRN2", target_bir_lowering=False)
    st = contextlib.ExitStack()

    def din(name, shape):
        return nc.dram_tensor(name, list(shape), F32, kind="ExternalInput").ap()

    def dout(name, shape):
        return nc.dram_tensor(name, list(shape), F32, kind="ExternalOutput").ap()

    def dscr(name, shape, dt=F32):
        return nc.dram_tensor(name, list(shape), dt, kind="Internal").ap()

    def sb(name, shape, dt=F32):
        return st.enter_context(nc.sbuf_tensor(name, list(shape), dt))

    def ps(name, shape=(128, 512), dt=F32):
        return st.enter_context(nc.psum_tensor(name, list(shape), dt))

    xp = din("xp", (S, D))
    pp = din("pp", (2, S, 256))
    norm_g = din("norm_g", (2, D))
    rel_bias = din("rel_bias", (32, 8))
    w_in = din("ab_w_in", (D, ABC))
    w_out = din("ab_w_out", (D, D))
    b_w_s = din("b_w_s", (8, 128, 128))
    b_b_s = din("b_b_s", (1, 1024))
    b_ln_g = din("b_ln_g", (1, 1024))
    b_ln_b = din("b_ln_b", (1, 1024))
    ple_wp = din("ple_w_proj", (2, 256, D))
    ple_wg = din("ple_w_gate", (2, D, D))
    c_ident = din("c_ident", (128, 128))
    c_onehot = din("c_onehot", (64, 1536))
    c_trimask = din("c_trimask", (128, 128))
    c_selb = din("c_selb", (4, 512))
    c_onesel = din("c_onesel", (128, 16))
    xs_in = din("xs", (NS, D))
    cache_k = din("cache_k", (NS, 2048, 1024))
    cache_v = din("cache_v", (NS, 2048, 1024))
    wkv0 = din("wkv0", (NS, 32, 64, 64))
    shift0 = din("shift0", (NS, D))
    ps_in = din("ps", (2, NS, 256))
    ys_out = dout("ys", (NS, D))
    aks = dout("aks", (NS, 1024))
    avs = dout("avs", (NS, 1024))
    bvs = dout("bvs", (NS, 1024))
    cSs = dout("cSs", (NS, 32, 64, 64))
    cxs = dout("cxs", (NS, D))
    TS = 256
    rs_scr = dscr("rs_scr", (D, TS))
    ks_scr = dscr("ks_scr", (D, TS))
    vs_scr = dscr("vs_scr", (D, TS))
    as_scr = dscr("as_scr", (D, TS))
    es_scr = dscr("es_scr", (TS, D))
    gs_scr = dscr("gs_scr", (D, TS), BF16)
    yTs_scr = dscr("yTs_scr", (KC, 128, TS), BF16)
    c_maskg = din("c_maskg", (64, 128))
    c_maskn = din("c_maskn", (64, 64))
    c_uneg = din("c_uneg", (128, 128))
    c_mu = din("c_mu", (6, D))
    c_wr = din("c_w_r", (D, D))
    c_wk = din("c_w_k", (D, D))
    c_wv = din("c_w_v", (D, D))
    c_wg = din("c_w_g", (D, D))
    c_wo = din("c_w_o", (D, D))
    c_w0 = din("c_w0", (1, D))
    c_w1 = din("c_w1", (D, 96))
    c_w2 = din("c_w2", (96, D))
    c_a0 = din("c_a0", (1, D))
    c_a1 = din("c_a1", (D, 96))
    c_a2 = din("c_a2", (96, D))
    c_vecs = {n: din(n, (1, D)) for n in ("c_k_k", "c_k_a", "c_r_k", "c_gn_g", "c_gn_b")}
    final_g = din("final_norm_g", (1, D))
    r_scr = dscr("r_scr", (D, S))
    k_scr = dscr("k_scr", (D, S))
    v_scr = dscr("v_scr", (D, S))
    a_scr = dscr("a_scr", (D, S))
    e_scr = dscr("e_scr", (S, D))
    g_scr = dscr("g_scr", (D, S), BF16)
    yp_out = dout("yp", (S, D))
    cSp = dout("cSp", (32, 64, 64))
    cxp = dout("cxp", (1, D))
    akp = dout("akp", (S, 1024))
    avp = dout("avp", (S, 1024))
    dbg_out = dout("dbg", (S, D)) if dbg else None
    bias_scr = dscr("bias_scr", (3, 8, 128 * 512))
    yT_scr = dscr("yT_scr", (KC, 128, S), BF16)
    import os
    h_scr = [dscr(f"h_scr{i}", (S, D)) for i in range(int(os.environ.get("NSCR", "4")))]

    P = Prog(nc, st)

    ident_f = sb("ident_f", (128, 128))
    ident_b = sb("ident_b", (128, 128), BF16)
    ones_b = sb("ones_b", (128, 128), BF16)
    trimask = sb("trimask", (128, 128))
    xnT = sb("xnT", (128, KC, S), BF16)
    Fb = [sb(f"F{i}", (128, D)) for i in range(3)]
    Bb = [sb(f"B{i}", (128, D), BF16) for i in range(4)]
    stat = sb("stat", (128, 16))
    wst = [sb(f"wst{i}", (128, 4, 512)) for i in range(2)]
    wbf = [sb(f"wbf{i}", (128, KC, 512), BF16) for i in range(2)]
    U = sb("U", (128, 16384), BF16)
    WmT = sb("WmT", (128, 8, 128), BF16)
    bsb = sb("bsb", (128, 8, 128))
    lng = sb("lng", (128, 1024))
    lnb = sb("lnb", (128, 1024))
    wsf = sb("wsf", (128, 128))
    raug = sb("raug", (64, 8))
    pss = [ps(f"ps{i}") for i in range(8)]
    XNT = [f"xnT{t}" for t in range(NT)]

    bankctr = {}

    def bank(lo=0, hi=8):
        k = (lo, hi)
        i = lo + bankctr.get(k, 0) % (hi - lo)
        bankctr[k] = bankctr.get(k, 0) + 1
        return pss[i], f"ps{i}"

    Vp = [U[:, p * 2048:(p + 1) * 2048].rearrange("p (g e) -> p g e", e=128) for p in range(3)]
    kvst = [U[:, 6144 + i * 1024: 6144 + (i + 1) * 1024].bitcast(F32) for i in range(2)]
    BT = U[:, 8192:8192 + 1536].bitcast(F32).rearrange("p (a c) -> p a c", a=3)
    PT = [U[:, 9728 + i * 512: 9728 + (i + 1) * 512] for i in range(2)]
    vn_all = U[:, :].rearrange("p (t c) -> p t c", c=1024)
    UALL = ["Vp0", "Vp1", "Vp2", "kvst0", "kvst1", "BT", "PT0", "PT1"]

    P.dma("sp", ident_f[:], c_ident[:, :], [], ["ident_f"], "c_id")
    P.dma("sp", trimask[:], c_trimask[:, :], [], ["trimask"], "c_tri")
    P.copy("dve", ident_b[:], ident_f[:], ["ident_f"], ["ident_b"])
    P.add("dve", lambda: nc.vector.memset(ones_b[:], 1.0), [], ["ones_b"])

    oh = Fb[0][0:64, 0:1536]
    gvec = Fb[1][0:8, 0:1536].rearrange("p (a c) -> p a c", a=3)
    import os
    SK = os.environ.get("SKIP", "")
    if "a" not in SK:
        P.add("pool", lambda: nc.gpsimd.memset(raug[32:64, :], NEGB), [], ["raug"])
    if "b" not in SK:
        P.dma("sp", raug[0:32, :], rel_bias[:, :], [], ["raug"], "c_ra")
    if "c" not in SK:
        P.dma("sp", oh, c_onehot[:, :], [], ["F0"], "F0")
    BIS = int(os.environ.get("BIS", "9"))
    for p in range(3 if BIS >= 1 else 0):
        pb, pbn = bank()
        P.mm(pb[0:8, :], raug[:, :], oh[:, p * 512:(p + 1) * 512], True, True, ["raug", "F0"], [pbn])
        P.copy("act", gvec[:, p, :], pb[0:8, :], [pbn], ["F1"])
    for p in range(3 if BIS >= 2 else 0):
        dst = bias_scr[p].rearrange("h (r u) -> h r u", u=512)
        src = gvec[:, p, :].unsqueeze(1).to_broadcast([8, 128, 512])
        P.dma("sp", dst, src, ["F1"], ["bias_scr"], "gv")

    def phase_T(src, src_tok, layer, do_norm):
        if do_norm:
            g_row = norm_g[layer:layer + 1, :]
            P.dma("sp", Fb[2][:], g_row.partition_broadcast(128), [], ["F2"], "F2")
        for t in range(NT):
            s = t % 2
            xt_, xb_ = Fb[s], Bb[s]
            P.dma("sp", xt_[:], src[t * 128:(t + 1) * 128, :], [src_tok], [f"F{s}"], f"F{s}")
            if do_norm:
                P.act(Bb[2][:], xt_[:], AF.Square, [f"F{s}"], ["B2", "ssq"], accum_out=stat[:, 0:1])
                P.ts("dve", stat[:, 1:2], stat[:, 0:1], 1.0 / D, 1e-6, ALU.mult, ALU.add, ["ssq"], ["rstd0"])
                P.act(stat[:, 3:4], stat[:, 1:2], AF.Sqrt, ["rstd0"], ["rstd1"])
                P.add("dve", lambda: nc.vector.reciprocal(out=stat[:, 2:3], in_=stat[:, 3:4]), ["rstd1"], ["rstd"])
                P.stt("dve", xb_[:], xt_[:], stat[:, 2:3], Fb[2][:], ALU.mult, ALU.mult,
                      [f"F{s}", "rstd", "F2"], [f"B{s}"])
                if layer == 1 and t == NT - 1:
                    xnf = U[:, 0:4096].bitcast(F32)
                    P.stt("dve", xnf, xt_[:], stat[:, 2:3], Fb[2][:], ALU.mult, ALU.mult,
                          [f"F{s}", "rstd", "F2"], UALL + ["vn", "pT", "xnf"])
                    P.dma("pool", cxp[0:1, :], xnf[127:128, :], ["xnf"], [], "xnf")
            else:
                P.copy("dve", xb_[:], xt_[:], [f"F{s}"], [f"B{s}"])
            for q4 in range(4):
                pb, pbn = bank()
                pbv = pb[:].bitcast(BF16)
                for u in range(4):
                    kc = q4 * 4 + u
                    P.tr(pbv[:, u * 128:(u + 1) * 128], xb_[:, kc * 128:(kc + 1) * 128], ident_b[:],
                         [f"B{s}", "ident_b"], [pbn])
                eng = "act" if q4 % 2 == 0 else "dve"
                P.copy(eng, xnT[:, q4 * 4:(q4 + 1) * 4, t * 128:(t + 1) * 128],
                       pbv[:, 0:512].rearrange("p (u c) -> p u c", u=4), [pbn], [XNT[t]])

    phase_T(xp, "xp", 0, True)

    wctr = [0]

    def load_wblock(wsrc, col_groups, nk=KC):
        i = wctr[0]
        wctr[0] += 1
        s = i % 2
        name = f"wbf{s}"
        nq = (nk + 3) // 4
        for q in range(nq):
            ss = (i * 4 + q) % 2
            k4 = min(4, nk - q * 4)
            off = 0
            for (c0, ncol) in col_groups:
                src = wsrc[q * 512:q * 512 + k4 * 128, c0:c0 + ncol].rearrange("(k p) c -> p k c", p=128)
                P.dma("sp", wst[ss][:, 0:k4, off:off + ncol], src, [], [f"wst{ss}"], f"wst{ss}")
                off += ncol
            eng = "pool" if q % 2 == 0 else "dve"
            P.copy(eng, wbf[s][:, q * 4:q * 4 + k4, 0:off], wst[ss][:, 0:k4, 0:off], [f"wst{ss}"], [name])
        return wbf[s], name

    SCALE = 128 ** -0.5

    def tokset(dil, g):
        n, r = g // dil, g % dil
        start = n * 128 * dil + r
        return slice(start, start + 127 * dil + 1, dil)

    def accview(acc, dil, q4):
        if dil == 1:
            return acc[:, q4 * 512:(q4 + 1) * 512].rearrange("p (u i) -> p u i", u=4)
        if dil == 4:
            return acc[:, q4 * 512:(q4 + 1) * 512].rearrange("p (i r) -> p r i", r=4)
        return acc[:, :].rearrange("p (i r) -> p r i", r=16)[:, q4 * 4:(q4 + 1) * 4, :]

    qT, kT, gaT, yaT = Bb[0], Bb[1], Bb[2], Bb[3]
    num_acc, den_acc, tmpF = Fb[0], Fb[1], Fb[2]
    nheads = int(os.environ.get("NHEADS", "8")) if stage >= 1 else 0
    for h in range(nheads):
        W, wn = load_wblock(w_in, [(h * 128, 128), (1024 + h * 128, 128), (2048 + h * 128, 128), (3072 + h * 128, 128)])
        for p in range(3 if BIS >= 3 else 0):
            src = bass.AP(bias_scr.tensor, bias_scr[p, h].offset, [[511, 128], [1, 256]])
            P.dma("sp", BT[:, p, :], src, ["bias_scr"], ["BT"], "BT")
        for (dst, dn, c0, kind) in ((qT, "B0", 0, "q"), (kT, "B1", 128, "k"), (gaT, "B2", 384, "g")):
            if kind in os.environ.get("NOQKG", ""):
                continue
            for tb in range(4):
                pb, pbn = bank()
                for kc in range(KC):
                    P.mm(pb[:, :], W[:, kc, c0:c0 + 128], xnT[:, kc, tb * 512:(tb + 1) * 512], kc == 0, kc == KC - 1,
                         XNT[tb * 4:(tb + 1) * 4] + [wn], [pbn])
                o = dst[:, tb * 512:(tb + 1) * 512]
                if kind == "q":
                    P.act(o, pb[:, :], AF.Copy, [pbn], [dn], scale=SCALE)
                elif kind == "k":
                    P.copy("act", o, pb[:, :], [pbn], [dn])
                else:
                    P.act(o, pb[:, :], AF.Silu, [pbn], [dn])
        for t2 in range(NT // 2):
            pb, pbn = bank()
            for half in range(2):
                t = t2 * 2 + half
                for kc in range(KC):
                    P.mm(pb[:, half * 256:(half + 1) * 256], xnT[:, kc, t * 128:(t + 1) * 128], W[:, kc, 128:384],
                         kc == 0, kc == KC - 1, [XNT[t], wn], [pbn])
            ks = t2 % 2
            P.copy("act", kvst[ks], pb[:, :], [pbn], [f"kvst{ks}"])
            src = kvst[ks].rearrange("p (t two c) -> p t two c", t=2, two=2)
            P.copy("pool", Vp[0][:, t2 * 2:t2 * 2 + 2, :], src[:, :, 1, :], [f"kvst{ks}"], ["Vp0"])
            t0 = t2 * 2
            dstk = akp[t0 * 128:(t0 + 2) * 128, h * 128:(h + 1) * 128].rearrange("(t p) c -> p t c", p=128)
            dstv = avp[t0 * 128:(t0 + 2) * 128, h * 128:(h + 1) * 128].rearrange("(t p) c -> p t c", p=128)
            P.dma("pool", dstk, src[:, :, 0, :], [f"kvst{ks}"], [], f"kvst{ks}")
            P.dma("pool", dstv, src[:, :, 1, :], [f"kvst{ks}"], [], f"kvst{ks}")
        if stage < 2:
            continue
        for p, dil in ((1, 4), (2, 16)):
            for g4 in range(4):
                pb, pbn = bank()
                for u in range(4):
                    ts_ = tokset(dil, g4 * 4 + u)
                    for kc in range(KC):
                        P.mm(pb[:, u * 128:(u + 1) * 128], xnT[:, kc, ts_], W[:, kc, 256:384], kc == 0, kc == KC - 1,
                             XNT + [wn], [pbn])
                P.copy("act", Vp[p][:, g4 * 4:(g4 + 1) * 4, :], pb[:, :].rearrange("p (u c) -> p u c", u=4),
                       [pbn], [f"Vp{p}"])
        slot = 0
        for p, dil in enumerate((1, 4, 16)):
            for q4 in range(4):
                ob, obn = bank()
                db, dbn = bank()
                for half in range(2):
                    sbk, sbn = bank()
                    gs = [q4 * 4 + half * 2 + u2 for u2 in range(2)]
                    prevs = [(g // dil) > 0 for g in gs]
                    for u2, g in enumerate(gs):
                        ts_ = tokset(dil, g)
                        P.mm(sbk[:, u2 * 256:u2 * 256 + 128], kT[:, ts_], qT[:, ts_], True, True, ["B0", "B1"], [sbn])
                        if prevs[u2]:
                            tp_ = tokset(dil, g - dil)
                            P.mm(sbk[:, u2 * 256 + 128:u2 * 256 + 256], kT[:, tp_], qT[:, ts_], True, True,
                                 ["B0", "B1"], [sbn])
                    sl = slot % 2
                    slot += 1
                    tS = tmpF[:, sl * 512:(sl + 1) * 512]
                    tSn = f"tS{sl}"
                    pt = PT[sl]
                    ptn = f"PT{sl}"
                    if all(prevs):
                        P.tt("dve", tS.rearrange("p (u c) -> p u c", u=2), sbk[:, :].rearrange("p (u c) -> p u c", u=2),
                             BT[:, p, :].unsqueeze(1).to_broadcast([128, 2, 256]), ALU.add, [sbn, "BT"], [tSn, "F2"])
                        P.act(pt[:, :], tS, AF.Exp, [tSn], [ptn])
                    elif not any(prevs):
                        P.tt("dve", tS.rearrange("p (u c) -> p u c", u=2)[:, :, 0:128],
                             sbk[:, :].rearrange("p (u c) -> p u c", u=2)[:, :, 0:128],
                             BT[:, p, 0:128].unsqueeze(1).to_broadcast([128, 2, 128]), ALU.add, [sbn, "BT"], [tSn, "F2"])
                        P.act(pt[:, :].rearrange("p (u c) -> p u c", u=2)[:, :, 0:128],
                              tS.rearrange("p (u c) -> p u c", u=2)[:, :, 0:128], AF.Exp, [tSn], [ptn])
                    else:
                        for u2 in range(2):
                            w_ = 256 if prevs[u2] else 128
                            P.tt("dve", tS[:, u2 * 256:u2 * 256 + w_], sbk[:, u2 * 256:u2 * 256 + w_], BT[:, p, 0:w_],
                                 ALU.add, [sbn, "BT"], [tSn, "F2"])
                            P.act(pt[:, u2 * 256:u2 * 256 + w_], tS[:, u2 * 256:u2 * 256 + w_], AF.Exp, [tSn], [ptn])
                    for u2, g in enumerate(gs):
                        u = half * 2 + u2
                        for (ob_, obn_, lown, lprev, rd) in ((ob, obn, Vp[p][:, g, :], None, [f"Vp{p}"]),
                                                             (db, dbn, ones_b[:, :], ones_b[:, :], ["ones_b"])):
                            if lprev is None and prevs[u2]:
                                lprev = Vp[p][:, g - dil, :]
                            P.mm(ob_[:, u * 128:(u + 1) * 128], lown, pt[:, u2 * 256:u2 * 256 + 128], True, not prevs[u2],
                                 rd + [ptn], [obn_])
                            if prevs[u2]:
                                P.mm(ob_[:, u * 128:(u + 1) * 128], lprev, pt[:, u2 * 256 + 128:u2 * 256 + 256], False, True,
                                     rd + [ptn], [obn_])
                obv = ob[:, :].rearrange("p (u i) -> p u i", u=4)
                dbv = db[:, :].rearrange("p (u i) -> p u i", u=4)
                if p == 0:
                    P.copy("act", accview(num_acc, dil, q4), obv, [obn], ["F0"])
                    P.copy("act", accview(den_acc, dil, q4), dbv, [dbn], ["F1"])
                else:
                    P.tt("dve", accview(num_acc, dil, q4), obv, accview(num_acc, dil, q4), ALU.add, [obn, "F0"], ["F0"])
                    P.tt("dve", accview(den_acc, dil, q4), dbv, accview(den_acc, dil, q4), ALU.add, [dbn, "F1"], ["F1"])
        P.add("dve", lambda: nc.vector.reciprocal(out=den_acc[:], in_=den_acc[:]), ["F1"], ["F1"])
        P.tt("pool", num_acc[:], num_acc[:], den_acc[:], ALU.mult, ["F0", "F1"], ["F0"])
        P.tt("pool", yaT[:], num_acc[:], gaT[:], ALU.mult, ["F0", "B2"], ["B3"])
        P.dma("pool", yT_scr[h], yaT[:], ["B3"], [f"yT_scr{h}"], "B3")

    if stage >= 3:
        P.dma("sp", bsb[:].rearrange("p g c -> p (g c)"), b_b_s[0:1, :].partition_broadcast(128), [], ["bsb"], "c1")
        P.dma("sp", lng[:], b_ln_g[0:1, :].partition_broadcast(128), [], ["lng"], "c1")
        P.dma("sp", lnb[:], b_ln_b[0:1, :].partition_broadcast(128), [], ["lnb"], "c1")
        for g in range(8):
            P.dma("sp", wsf[:], b_w_s[g], [], ["wsf"], "wsf")
            pb, pbn = bank()
            P.tr(pb[:, 0:128], wsf[:], ident_f[:], ["wsf", "ident_f"], [pbn])
            P.tt("dve", WmT[:, g, :], pb[:, 0:128], trimask[:], ALU.mult, [pbn, "trimask"], ["WmT"])
        WA, nA = load_wblock(w_in, [(5120, 512)])
        WB, nB = load_wblock(w_in, [(5632, 512)])
        gv = Fb[0][:, 0:1024]
        for t in range(NT):
            pa, pan = bank()
            pb, pbn = bank()
            for (pq, pqn, Wq, nq) in ((pa, pan, WA, nA), (pb, pbn, WB, nB)):
                for kc in range(KC):
                    P.mm(pq[:, :], xnT[:, kc, t * 128:(t + 1) * 128], Wq[:, kc, :], kc == 0, kc == KC - 1,
                         [XNT[t], nq], [pqn])
            P.act(gv[:, 0:512], pa[:, :], AF.Gelu, [pan], ["F0", "lnsA"], accum_out=stat[:, 4:5])
            P.act(gv[:, 512:1024], pb[:, :], AF.Gelu, [pbn], ["F0", "lnsB"], accum_out=stat[:, 5:6])
            P.act(Fb[1][:, 0:1024], gv, AF.Square, ["F0"], ["F1", "lnsq"], accum_out=stat[:, 6:7])
            P.tt("dve", stat[:, 7:8], stat[:, 4:5], stat[:, 5:6], ALU.add, ["lnsA", "lnsB"], ["lnm0"])
            P.ts("dve", stat[:, 7:8], stat[:, 7:8], 1.0 / 1024, None, ALU.mult, None, ["lnm0"], ["lnm"])
            P.tt("dve", stat[:, 8:9], stat[:, 7:8], stat[:, 7:8], ALU.mult, ["lnm"], ["lnm2"])
            P.stt("dve", stat[:, 9:10], stat[:, 6:7], 1.0 / 1024, stat[:, 8:9], ALU.mult, ALU.subtract,
                  ["lnsq", "lnm2"], ["lnvar"])
            P.ts("dve", stat[:, 9:10], stat[:, 9:10], 1e-5, None, ALU.add, None, ["lnvar"], ["lnvar2"])
            P.act(stat[:, 10:11], stat[:, 9:10], AF.Sqrt, ["lnvar2"], ["lnsd"])
            P.add("dve", lambda: nc.vector.reciprocal(out=stat[:, 11:12], in_=stat[:, 10:11]), ["lnsd"], ["lnrs"])
            P.ts("dve", gv, gv, stat[:, 7:8], stat[:, 11:12], ALU.subtract, ALU.mult, ["F0", "lnm", "lnrs"], ["F0"])
            P.tt("dve", gv, gv, lng[:], ALU.mult, ["F0", "lng"], ["F0"])
            P.tt("dve", vn_all[:, t, :], gv, lnb[:], ALU.add, ["F0", "lnb"], UALL + ["vn"])
        ubT, gbT, ybT = Bb[0], Bb[1], Bb[2]
        for g2 in range(4):
            ga_, gb_ = 2 * g2, 2 * g2 + 1
            W, wn = load_wblock(w_in, [(4096 + ga_ * 128, 128), (6144 + ga_ * 128, 128),
                                       (4096 + gb_ * 128, 128), (6144 + gb_ * 128, 128)])
            for gg in range(2):
                g = 2 * g2 + gg
                for tb in range(4):
                    for (c0, dst, dn, fn) in ((gg * 256, ubT, "B0", AF.Gelu), (gg * 256 + 128, gbT, "B1", AF.Silu)):
                        pb, pbn = bank()
                        for kc in range(KC):
                            P.mm(pb[:, :], W[:, kc, c0:c0 + 128], xnT[:, kc, tb * 512:(tb + 1) * 512], kc == 0, kc == KC - 1,
                                 XNT[tb * 4:(tb + 1) * 4] + [wn], [pbn])
                        P.act(dst[:, tb * 512:(tb + 1) * 512], pb[:, :], fn, [pbn], [dn])
                    psb, psbn = bank()
                    for c4 in range(4):
                        n = tb * 4 + c4
                        P.mm(psb[:, c4 * 128:(c4 + 1) * 128], vn_all[:, n, g * 128:(g + 1) * 128], WmT[:, g, :], True, True,
                             ["vn", "WmT"], [psbn])
                    t1 = tmpF[:, (tb % 2) * 512:(tb % 2 + 1) * 512]
                    t1n = f"tS{tb % 2}"
                    P.tt("dve", t1.rearrange("p (u c) -> p u c", u=4), psb[:, :].rearrange("p (u c) -> p u c", u=4),
                         bsb[:, g, :].unsqueeze(1).to_broadcast([128, 4, 128]), ALU.add, [psbn, "bsb"], [t1n, "F2"])
                    P.tt("pool", t1, t1, ubT[:, tb * 512:(tb + 1) * 512], ALU.mult, [t1n, "B0"], [t1n])
                    P.tt("pool", ybT[:, tb * 512:(tb + 1) * 512], t1, gbT[:, tb * 512:(tb + 1) * 512], ALU.mult,
                         [t1n, "B1"], ["B2"])
                P.dma("pool", yT_scr[8 + g], ybT[:], ["B2"], [f"yT_scr{8 + g}"], "B2")

    def phase_proj_res(wsrc, src_scr, src_tok, dst_scr, dst_tok, yT_tokens):
        xres = [Fb[0][:, 0:512], Fb[0][:, 512:1024]]
        hout = [Fb[1][:, 0:512], Fb[1][:, 512:1024]]
        it = 0
        for cb in range(4):
            W, wn = load_wblock(wsrc, [(cb * 512, 512)])
            for t in range(NT):
                s = it % 2
                it += 1
                pb, pbn = bank()
                P.dma("sp", xres[s], src_scr[t * 128:(t + 1) * 128, cb * 512:(cb + 1) * 512], [src_tok], [f"xres{s}", "F0"],
                      f"xres{s}")
                for kc in range(KC):
                    P.mm(pb[:, :], xnT[:, kc, t * 128:(t + 1) * 128], W[:, kc, :], kc == 0, kc == KC - 1,
                         [XNT[t], wn], [pbn])
                P.tt("dve", hout[s], pb[:, :], xres[s], ALU.add, [pbn, f"xres{s}"], [f"hout{s}", "F1"])
                P.dma("pool", dst_scr[t * 128:(t + 1) * 128, cb * 512:(cb + 1) * 512], hout[s], [f"hout{s}"], [dst_tok],
                      f"hout{s}")

    if stage >= 4:
        for t in range(NT):
            P.dma("sp", xnT[:, :, t * 128:(t + 1) * 128],
                  yT_scr[:, :, t * 128:(t + 1) * 128].rearrange("k p c -> p k c"),
                  [f"yT_scr{k}" for k in range(KC)], [XNT[t]], f"yTl{t % 4}")
        phase_proj_res(w_out, xp, "xp", h_scr[0], "h_scr0", None)

    def phase_ple(layer, src_scr, src_tok, dst_scr, dst_tok):
        phase_T(src_scr, src_tok, layer, False)
        pT = U[:, 0:4096].rearrange("p (k c) -> p k c", k=2)
        pst = Fb[2][:, 0:256]
        pbf = Bb[3][:, 0:256]
        for t in range(NT):
            P.dma("sp", pst, pp[layer, t * 128:(t + 1) * 128, :], [], ["F2"], "F2")
            P.copy("dve", pbf, pst, ["F2"], ["B3"])
            pb, pbn = bank()
            pbv = pb[:].bitcast(BF16)
            for u in range(2):
                P.tr(pbv[:, u * 128:(u + 1) * 128], pbf[:, u * 128:(u + 1) * 128], ident_b[:], ["B3", "ident_b"], [pbn])
            P.copy("act", pT[:, :, t * 128:(t + 1) * 128], pbv[:, 0:256].rearrange("p (u c) -> p u c", u=2), [pbn],
                   UALL + ["vn", "pT"])
        xres = [Fb[0][:, 0:512], Fb[0][:, 512:1024]]
        hout = [Fb[1][:, 0:512], Fb[1][:, 512:1024]]
        sig = [Fb[0][:, 1024:1536], Fb[0][:, 1536:2048]]
        wpb = Bb[2][:, 0:1024].rearrange("p (k c) -> p k c", k=2)
        it = 0
        for cb in range(4):
            W, wn = load_wblock(ple_wg[layer], [(cb * 512, 512)])
            wps = Fb[2][:, 0:1024].rearrange("p (k c) -> p k c", k=2)
            P.dma("sp", wps, ple_wp[layer, :, cb * 512:(cb + 1) * 512].rearrange("(k p) c -> p k c", p=128), [], ["F2"], "F2")
            P.copy("dve", wpb, wps, ["F2"], ["B2"])
            for t in range(NT):
                s = it % 2
                it += 1
                pg, pgn = bank()
                pq, pqn = bank()
                P.dma("sp", xres[s], src_scr[t * 128:(t + 1) * 128, cb * 512:(cb + 1) * 512], [src_tok], [f"xres{s}", "F0"],
                      f"xres{s}")
                for kc in range(KC):
                    P.mm(pg[:, :], xnT[:, kc, t * 128:(t + 1) * 128], W[:, kc, :], kc == 0, kc == KC - 1, [XNT[t], wn], [pgn])
                for k2 in range(2):
                    P.mm(pq[:, :], pT[:, k2, t * 128:(t + 1) * 128], wpb[:, k2, :], k2 == 0, k2 == 1, ["pT", "B2"], [pqn])
                P.act(sig[s], pg[:, :], AF.Sigmoid, [pgn], [f"sig{s}", "F0"])
                P.tt("dve", sig[s], pq[:, :], sig[s], ALU.mult, [pqn, f"sig{s}"], [f"sig{s}"])
                P.tt("dve", hout[s], sig[s], xres[s], ALU.add, [f"sig{s}", f"xres{s}"], [f"hout{s}", "F1"])
                P.dma("pool", dst_scr[t * 128:(t + 1) * 128, cb * 512:(cb + 1) * 512], hout[s], [f"hout{s}"], [dst_tok],
                      f"hout{s}")

    if stage >= 5:
        phase_ple(0, h_scr[0], "h_scr0", h_scr[1], "h_scr1")


    def barrier(extra=()):
        names = list(P.bufs.keys()) + list(extra)
        P.add("pool", lambda: nc.gpsimd.memset(stat[:, 15:16], 0.0), [], names)

    rsm = sb("rsm", (128, 1088))
    mu_t = sb("mu_t", (128, 6, KC))
    R_MASKG, R_MASKN, R_ONES, R_UNEG, R_PRM, R_PC, R_XS, R_US, R_S0, R_S1, R_SP, R_ONESD = (
        0, 128, 192, 256, 384, 608, 640, 704, 768, 832, 896, 960)
    PRM_NAMES = ("c_k_k", "c_k_a", "c_r_k", "c_gn_g", "c_gn_b", "c_a0")

    def prm(name, h):
        o = R_PRM + PRM_NAMES.index(name) * 32 + h
        return rsm[0:64, o:o + 1]

    def rwkv_consts():
        P.dma("sp", rsm[0:64, R_MASKG:R_MASKG + 128], c_maskg[:, :], [], ["rsm_c"], "rc0")
        P.dma("sp", rsm[0:64, R_MASKN:R_MASKN + 64], c_maskn[:, :], [], ["rsm_c"], "rc1")
        P.dma("sp", rsm[:, R_UNEG:R_UNEG + 128], c_uneg[:, :], [], ["rsm_c"], "rc2")
        P.add("dve", lambda: nc.vector.memset(rsm[0:64, R_ONES:R_ONES + 64], 1.0), [], ["rsm_c"])
        P.add("dve", lambda: nc.vector.memset(rsm[0:64, R_ONESD:R_ONESD + 64], 1.0 / 64), [], ["rsm_c"])
        for i, nme in enumerate(PRM_NAMES):
            src_t = c_a0 if nme == "c_a0" else c_vecs[nme]
            src = bass.AP(src_t.tensor, 0, [[1, 64], [64, 32]])
            P.dma("sp", rsm[0:64, R_PRM + i * 32:R_PRM + (i + 1) * 32], src, [], ["prm"], f"rp{i}",
                  allow_slow_non_contiguous=True)
        P.dma("sp", mu_t[:], bass.AP(c_mu.tensor, 0, [[1, 128], [D, 6], [128, KC]]), [], ["mu_t"], "mu_t",
              allow_slow_non_contiguous=True)

    def rwkv_proj():
        barrier()
        xm = ([U[:, u * 2048:(u + 1) * 2048] for u in range(8)] + [Bb[i][:, :] for i in range(4)]
              + [Fb[i][:, :].bitcast(BF16)[:, a * 2048:(a + 1) * 2048] for i in range(2) for a in range(2)])
        xmn = [f"xm{k}" for k in range(KC)]
        dx = Fb[2][:, :].bitcast(BF16)[:, 0:2048]
        stg = [Fb[2][:, 1024:1536], Fb[2][:, 1536:2048]]
        hT = lng[:, :].bitcast(BF16)
        w0bc = [lnb[:, :], bsb[:, :, :].rearrange("p g c -> p (g c)")]

        def build_xm(m):
            for kc in range(KC):
                P.tt("pool", dx[:, 1:S], xnT[:, kc, 0:S - 1], xnT[:, kc, 1:S], ALU.subtract, XNT, ["dx"])
                P.ts("pool", dx[:, 0:1], xnT[:, kc, 0:1], -1.0, None, ALU.mult, None, XNT, ["dx"])
                P.stt("dve", xm[kc], dx, mu_t[:, m, kc:kc + 1], xnT[:, kc, :], ALU.mult, ALU.add,
                      ["dx", "mu_t"] + XNT, [xmn[kc]])

        itc = [0]

        def evac_store(pb, pbn, dst, dst_tok, func=None, bf=False, npart=128):
            s_ = itc[0] % 2
            itc[0] += 1
            o = stg[s_].bitcast(BF16)[0:npart, 0:512] if bf else stg[s_][0:npart, :]
            if func is not None:
                P.act(o, pb[0:npart, :], func, [pbn], [f"stg{s_}"])
            elif s_ == 0:
                P.copy("act", o, pb[0:npart, :], [pbn], [f"stg{s_}"])
            else:
                P.copy("dve", o, pb[0:npart, :], [pbn], [f"stg{s_}"])
            P.dma("pool", dst, o, [f"stg{s_}"], [dst_tok], f"stg{s_}")

        def proj_fm(wsrc, dst_scr, dst_tok, func=None, bf=False):
            for cb in range(4):
                W, wn = load_wblock(wsrc, [(cb * 512, 512)])
                for fb in range(4):
                    f0 = cb * 512 + fb * 128
                    for tb in range(4):
                        pb, pbn = bank()
                        for kc in range(KC):
                            P.mm(pb[:, :], W[:, kc, fb * 128:(fb + 1) * 128], xm[kc][:, tb * 512:(tb + 1) * 512],
                                 kc == 0, kc == KC - 1, [xmn[kc], wn], [pbn])
                        evac_store(pb, pbn, dst_scr[f0:f0 + 128, tb * 512:(tb + 1) * 512], dst_tok, func, bf)

        def load_small(wsrc):
            i = wctr[0]
            wctr[0] += 1
            s_ = i % 2
            ss = (i * 4) % 2
            stv = wst[ss][:, :, :].rearrange("p a b -> p (a b)")
            P.dma("sp", stv[0:96, :], wsrc[:, :], [], [f"wst{ss}"], f"wst{ss}")
            wv = wbf[s_][:, 0:4, :].rearrange("p a b -> p (a b)")
            P.copy("dve", wv[0:96, :], stv[0:96, :], [f"wst{ss}"], [f"wbf{s_}"])
            return wv, f"wbf{s_}"

        def lora_hidden(w1src, func):
            W, wn = load_wblock(w1src, [(0, 96)])
            for tb in range(4):
                pb, pbn = bank()
                for kc in range(KC):
                    P.mm(pb[0:96, :], W[:, kc, 0:96], xm[kc][:, tb * 512:(tb + 1) * 512], kc == 0, kc == KC - 1,
                         [xmn[kc], wn], [pbn])
                if func is None:
                    P.copy("act", hT[0:96, tb * 512:(tb + 1) * 512], pb[0:96, :], [pbn], ["hT"])
                else:
                    P.act(hT[0:96, tb * 512:(tb + 1) * 512], pb[0:96, :], func, [pbn], ["hT"])

        build_xm(0)
        proj_fm(c_wr, r_scr, "r_scr")
        build_xm(1)
        lora_hidden(c_w1, AF.Tanh)
        w2b, w2n = load_small(c_w2)
        for hf in range(2):
            P.dma("sp", w0bc[hf], c_w0[0:1, hf * 1024:(hf + 1) * 1024].partition_broadcast(128), [], [f"w0bc{hf}"],
                  f"w0bc{hf}")
        for t in range(NT):
            for cb in range(4):
                pb, pbn = bank()
                P.mm(pb[:, :], hT[0:96, t * 128:(t + 1) * 128], w2b[0:96, cb * 512:(cb + 1) * 512], True, True,
                     ["hT", w2n], [pbn])
                s_ = itc[0] % 2
                itc[0] += 1
                P.tt("dve", stg[s_], pb[:, :], w0bc[cb // 2][:, (cb % 2) * 512:(cb % 2 + 1) * 512], ALU.add,
                     [pbn, f"w0bc{cb // 2}"], [f"stg{s_}"])
                P.act(stg[s_], stg[s_], AF.Sigmoid, [f"stg{s_}"], [f"stg{s_}"])
                P.dma("pool", e_scr[t * 128:(t + 1) * 128, cb * 512:(cb + 1) * 512], stg[s_], [f"stg{s_}"], ["e_scr"],
                      f"stg{s_}")
        build_xm(2)
        proj_fm(c_wk, k_scr, "k_scr")
        build_xm(3)
        proj_fm(c_wv, v_scr, "v_scr")
        build_xm(4)
        lora_hidden(c_a1, None)
        a2b, a2n = load_small(c_a2)
        for fb in range(KC):
            for tb in range(4):
                pb, pbn = bank()
                P.mm(pb[:, :], a2b[0:96, fb * 128:(fb + 1) * 128], hT[0:96, tb * 512:(tb + 1) * 512], True, True,
                     ["hT", a2n], [pbn])
                evac_store(pb, pbn, a_scr[fb * 128:(fb + 1) * 128, tb * 512:(tb + 1) * 512], "a_scr")
        build_xm(5)
        proj_fm(c_wg, g_scr, "g_scr", AF.Silu, True)

    def rwkv_scan(heads, NCH=32, SRC=None, sample=False):
        barrier()
        T = NCH * 64
        TB = min(512, T)
        NTB = T // TB
        NTL = T // 128
        G8 = min(8, NCH)
        NG8 = NCH // G8
        NG4 = NCH // 4
        if SRC is None:
            SRC = dict(r=(r_scr, "r_scr"), k=(k_scr, "k_scr"), v=(v_scr, "v_scr"), a=(a_scr, "a_scr"),
                       e=(e_scr, "e_scr"), g=(g_scr, "g_scr"), yT=(yT_scr, "yT_scr"))
        xflat = xnT[:, :, :].rearrange("p k t -> p (k t)")
        X8 = [xflat[:, i * 4096:(i + 1) * 4096].bitcast(F32) for i in range(8)]
        U8 = [U[:, i * 4096:(i + 1) * 4096].bitcast(F32) for i in range(4)]
        W16 = [wbf[i][:, :, :].rearrange("p k c -> p (k c)").bitcast(F32) for i in range(2)]
        rF, kF, aF, vF, t1, t2, kmF, bF = [x[0:64, 0:T] for x in X8]
        rsb = U8[0][0:64, 0:T]
        Pin, Pinv = U8[0][0:64, 0:T], U8[1][0:64, 0:T]
        AR = U[:, 8192:16384].bitcast(F32)[0:64, 0:2 * T]
        BK = W16[0][0:64, 0:2 * T]
        Gbm = W16[1][0:64, 0:2 * T]
        Gkm = xflat[:, 0:8192].bitcast(F32)[0:64, 0:2 * T]
        Vt, Btok, Ktok = X8[2][0:64, 0:T], X8[4][0:64, 0:T], X8[5][0:64, 0:T]
        Nb = [X8[6][0:64, 0:T], X8[7][0:64, 0:T]]
        Tb = [U8[0][0:64, 0:T], U8[1][0:64, 0:T]]
        Q, bs, yF = Fb[0][0:64, 0:T], Fb[1][0:64, 0:T], Fb[2][0:64, 0:T]
        e2tok = Bb[0][:, :].bitcast(F32)[:, 0:NTL * 64].rearrange("p (t j) -> p t j", j=64)
        gT, ygT = Bb[1][0:64, 0:T], Bb[2][0:64, 0:T]
        AR4 = AR.rearrange("p (c a t) -> p c a t", c=NCH, a=2)
        BK4 = BK.rearrange("p (c a t) -> p c a t", c=NCH, a=2)
        c3 = lambda x: x.rearrange("p (c t) -> p c t", c=NCH)
        G3b, G3k = Gbm.rearrange("p (c t) -> p c t", c=NCH), Gkm.rearrange("p (c t) -> p c t", c=NCH)
        Sin = Bb[3][:, :].bitcast(F32)[0:64, 0:256].rearrange("p (b j) -> p b j", b=4)
        SinT = Bb[3][:, :].bitcast(F32)[0:64, 256:512].rearrange("p (b i) -> p b i", b=4)
        maskg = rsm[0:64, R_MASKG:R_MASKG + 128]
        maskn = rsm[0:64, R_MASKN:R_MASKN + 64]
        ones64 = rsm[0:64, R_ONES:R_ONES + 64]
        onesd = rsm[0:64, R_ONESD:R_ONESD + 64]
        uneg = rsm[:, R_UNEG:R_UNEG + 128]
        PCt = rsm[0:64, R_PC:R_PC + NCH]
        Xs = rsm[0:64, R_XS:R_XS + 64]
        Us = rsm[0:64, R_US:R_US + 64]
        Sb = [rsm[0:64, R_S0:R_S0 + 64], rsm[0:64, R_S1:R_S1 + 64]]
        SP = rsm[0:64, R_SP:R_SP + 64]
        idf = ident_f[0:64, 0:64]
        GKM_T = ["X8_0", "X8_1"]
        AR_T = ["U8_2", "U8_3"]
        BK_T = ["W16_0"]
        GBM_T = ["W16_1"]

        for h in heads:
            hs = slice(h * 64, (h + 1) * 64)
            P.dma("sp", rF, SRC["r"][0][hs, :], [SRC["r"][1]], ["X8_0"], "X8_0")
            P.dma("sp", kF, SRC["k"][0][hs, :], [SRC["k"][1]], ["X8_1"], "X8_1")
            P.dma("sp", aF, SRC["a"][0][hs, :], [SRC["a"][1]], ["X8_2"], "X8_2")
            P.dma("sp", vF, SRC["v"][0][hs, :], [SRC["v"][1]], ["X8_3"], "X8_3")
            P.dma("sp", e2tok, SRC["e"][0][:, hs].rearrange("(t p) j -> p t j", p=128), [SRC["e"][1]], ["B0"], "B0")
            P.dma("sp", gT, SRC["g"][0][hs, :], [SRC["g"][1]], ["B1"], "B1")
            if sample:
                P.dma("sp", Sin, wkv0[:, h].rearrange("b i j -> i b j"), [], ["B3"], "Sin")
                pb, pbn = bank()
                for b_ in range(4):
                    P.tr(pb[0:64, b_ * 64:(b_ + 1) * 64], Sin[:, b_, :], idf, ["B3", "ident_f"], [pbn])
                P.copy("act", SinT, pb[0:64, 0:256].rearrange("p (b i) -> p b i", b=4), [pbn], ["SinT"])
            P.act(aF, aF, AF.Sigmoid, ["X8_2", "prm"], ["X8_2"], bias=prm("c_a0", h))
            P.ts("dve", t1, kF, prm("c_k_k", h), None, ALU.mult, None, ["X8_1", "prm"], ["X8_4"])
            P.act(t2, t1, AF.Square, ["X8_4"], ["X8_5"])
            for tb in range(NTB):
                pb, pbn = bank()
                P.mm(pb[0:64, 0:TB], ones64, t2[:, tb * TB:(tb + 1) * TB], True, True, ["X8_5", "rsm_c"], [pbn])
                P.act(rsb[:, tb * TB:(tb + 1) * TB], pb[0:64, 0:TB], AF.Sqrt, [pbn], ["U8_0"])
            P.ts("dve", rsb, rsb, 1e-12, None, ALU.max, None, ["U8_0"], ["U8_0"])
            P.add("dve", lambda: nc.vector.reciprocal(out=rsb, in_=rsb), ["U8_0"], ["U8_0"])
            P.tt("dve", t1, t1, rsb, ALU.mult, ["X8_4", "U8_0"], ["X8_4"])
            P.ts("pool", kmF, aF, 1.0, prm("c_k_a", h), ALU.subtract, ALU.mult, ["X8_2", "prm"], ["X8_6"])
            P.stt("dve", kmF, kmF, 1.0, kF, ALU.add, ALU.mult, ["X8_6", "X8_1"], ["X8_6"])
            P.tt("pool", bF, t1, aF, ALU.mult, ["X8_4", "X8_2"], ["X8_7"])
            P.stt("dve", t2, rF, prm("c_r_k", h), kmF, ALU.mult, ALU.mult, ["X8_0", "prm", "X8_6"], ["X8_5"])
            for tb in range(NTB):
                pb, pbn = bank()
                P.mm(pb[0:64, 0:TB], ones64, t2[:, tb * TB:(tb + 1) * TB], True, True, ["X8_5", "rsm_c"], [pbn])
                P.copy("act", bs[:, tb * TB:(tb + 1) * TB], pb[0:64, 0:TB], [pbn], ["F1"])
            for t4 in range((NTL + 3) // 4):
                pb, pbn = bank()
                nu = min(4, NTL - t4 * 4)
                for u in range(nu):
                    t = t4 * 4 + u
                    P.mm(pb[0:64, u * 128:(u + 1) * 128], e2tok[:, t, :], uneg, True, True, ["B0", "rsm_c"], [pbn])
                P.act(Pin[:, t4 * 512:t4 * 512 + nu * 128], pb[0:64, 0:nu * 128], AF.Exp, [pbn], ["U8_0"])
                P.act(Pinv[:, t4 * 512:t4 * 512 + nu * 128], pb[0:64, 0:nu * 128], AF.Exp, [pbn], ["U8_1"], scale=-1.0)
            P.tt("dve", AR4[:, :, 1, :], c3(rF), c3(Pin), ALU.mult, ["X8_0", "U8_0"], AR_T)
            P.stt("dve", AR4[:, :, 0, 1:64], c3(t1)[:, :, 1:64], -1.0, c3(Pin)[:, :, 0:63], ALU.mult, ALU.mult,
                  ["X8_4", "U8_0"], AR_T)
            P.ts("dve", AR4[:, :, 0, 0:1], c3(t1)[:, :, 0:1], -1.0, None, ALU.mult, None, ["X8_4"], AR_T)
            P.tt("pool", BK4[:, :, 0, :], c3(bF), c3(Pinv), ALU.mult, ["X8_7", "U8_1"], BK_T)
            P.tt("pool", BK4[:, :, 1, :], c3(kmF), c3(Pinv), ALU.mult, ["X8_6", "U8_1"], BK_T)
            P.copy("act", PCt, c3(Pin)[:, :, 63], ["U8_0"], ["PCt"])
            for g8 in range(NG8):
                pv, pvn = bank()
                pbt, pbtn = bank()
                pkt, pktn = bank()
                for u in range(G8):
                    c = g8 * G8 + u
                    P.tr(pv[0:64, u * 64:(u + 1) * 64], vF[:, c * 64:(c + 1) * 64], idf, ["X8_3", "ident_f"], [pvn])
                    P.tr(pbt[0:64, u * 64:(u + 1) * 64], BK4[:, c, 0, :], idf, BK_T + ["ident_f"], [pbtn])
                    P.tr(pkt[0:64, u * 64:(u + 1) * 64], BK4[:, c, 1, :], idf, BK_T + ["ident_f"], [pktn])
                sl = slice(g8 * G8 * 64, (g8 + 1) * G8 * 64)
                P.copy("act", Vt[:, sl], pv[0:64, 0:G8 * 64], [pvn], ["X8_2"])
                P.copy("dve", Btok[:, sl], pbt[0:64, 0:G8 * 64], [pbtn], ["X8_4"])
                P.copy("act", Ktok[:, sl], pkt[0:64, 0:G8 * 64], [pktn], ["X8_5"])
            for g4 in range(NG4):
                pgb, pgbn = bank()
                pgk, pgkn = bank()
                for u in range(4):
                    c = g4 * 4 + u
                    arc = AR4[:, c, :, :].rearrange("p a t -> p (a t)")
                    P.mm(pgb[0:64, u * 128:(u + 1) * 128], BK4[:, c, 0, :], arc, True, True, BK_T + AR_T, [pgbn])
                    P.mm(pgk[0:64, u * 128:(u + 1) * 128], BK4[:, c, 1, :], arc, True, True, BK_T + AR_T, [pgkn])
                mg = maskg.unsqueeze(1).to_broadcast([64, 4, 128])
                P.tt("dve", G3b[:, g4 * 4:(g4 + 1) * 4, :], pgb[0:64, :].rearrange("p (u t) -> p u t", u=4), mg, ALU.mult,
                     [pgbn, "rsm_c"], GBM_T)
                P.tt("dve", G3k[:, g4 * 4:(g4 + 1) * 4, :], pgk[0:64, :].rearrange("p (u t) -> p u t", u=4), mg, ALU.mult,
                     [pgkn, "rsm_c"], GKM_T)
            if sample:
                P.copy("dve", c3(Q), idf.unsqueeze(1).to_broadcast([64, NCH, 64]), ["ident_f"], ["F0"])
            else:
                for g8 in range(NG8):
                    pn, pnn = bank()
                    for u in range(G8):
                        c = g8 * G8 + u
                        P.mm(pn[0:64, u * 64:(u + 1) * 64], AR4[:, c, 0, :], BK4[:, c, 0, :], True, True, BK_T + AR_T, [pnn])
                    P.tt("dve", c3(Nb[0])[:, g8 * G8:(g8 + 1) * G8, :], pn[0:64, 0:G8 * 64].rearrange("p (u t) -> p u t", u=G8),
                         maskn.unsqueeze(1).to_broadcast([64, G8, 64]), ALU.mult, [pnn, "rsm_c"], ["X8_6"])
                P.copy("act", c3(Tb[0]), G3b[:, :, 0:64], GBM_T, ["U8_0"])
                P.tt("pool", c3(Q), c3(Tb[0]), idf.unsqueeze(1).to_broadcast([64, NCH, 64]), ALU.add, ["U8_0", "ident_f"], ["F0"])
                NT_ = ["X8_6", "X8_7"]
                TT_ = ["U8_0", "U8_1"]
                for k in range(5):
                    a_, b_ = k % 2, (k + 1) % 2
                    for g8 in range(NG8):
                        sl3 = slice(g8 * G8, (g8 + 1) * G8)
                        if k < 4:
                            pT_, pTn = bank()
                            for u in range(G8):
                                c = g8 * G8 + u
                                P.mm(pT_[0:64, u * 64:(u + 1) * 64], c3(Nb[a_])[:, c, :], c3(Tb[a_])[:, c, :], True, True,
                                     [NT_[a_], TT_[a_]], [pTn])
                            P.copy("act", c3(Tb[b_])[:, sl3, :], pT_[0:64, 0:G8 * 64].rearrange("p (u t) -> p u t", u=G8), [pTn], [TT_[b_]])
                        pN_, pNn = bank()
                        for u in range(G8):
                            c = g8 * G8 + u
                            P.mm(pN_[0:64, u * 64:(u + 1) * 64], c3(Tb[a_])[:, c, :], c3(Nb[a_])[:, c, :], True, True,
                                 [NT_[a_], TT_[a_]], [pNn])
                        P.copy("dve", c3(Nb[b_])[:, sl3, :], pN_[0:64, 0:G8 * 64].rearrange("p (u t) -> p u t", u=G8), [pNn], [NT_[b_]])
                        pQ_, pQn = bank()
                        for u in range(G8):
                            c = g8 * G8 + u
                            P.mm(pQ_[0:64, u * 64:(u + 1) * 64], c3(Nb[b_])[:, c, :], c3(Q)[:, c, :], True, True,
                                 [NT_[b_], "F0"], [pQn])
                        P.tt("dve", c3(Q)[:, sl3, :], pQ_[0:64, 0:G8 * 64].rearrange("p (u t) -> p u t", u=G8), c3(Q)[:, sl3, :], ALU.add,
                             [pQn, "F0"], ["F0"])
            At, Wc, X2, Uv, R2 = X8[6][0:64, 0:T], X8[7][0:64, 0:T], U8[0][0:64, 0:T], U8[1][0:64, 0:T], X8[6][0:64, 0:T]
            ATm, BPC = BK[:, 0:T], BK[:, T:2 * T]
            g8v = lambda pb_: pb_[0:64, 0:G8 * 64].rearrange("p (u t) -> p u t", u=G8)
            for g8 in range(NG8):
                sl3 = slice(g8 * G8, (g8 + 1) * G8)
                pa, pan = bank()
                for u in range(G8):
                    c = g8 * G8 + u
                    P.tr(pa[0:64, u * 64:(u + 1) * 64], AR4[:, c, 0, :], idf, AR_T + ["ident_f"], [pan])
                P.copy("act", c3(At)[:, sl3, :], g8v(pa), [pan], ["X8_6"])
                pw, pwn = bank()
                for u in range(G8):
                    c = g8 * G8 + u
                    P.mm(pw[0:64, u * 64:(u + 1) * 64], c3(Q)[:, c, :], c3(At)[:, c, :], True, True, ["F0", "X8_6"], [pwn])
                P.copy("dve", c3(Wc)[:, sl3, :], g8v(pw), [pwn], ["X8_7"])
                px, pxn = bank()
                for u in range(G8):
                    c = g8 * G8 + u
                    P.mm(px[0:64, u * 64:(u + 1) * 64], G3k[:, c, 0:64], c3(Vt)[:, c, :], True, True, GKM_T + ["X8_2"], [pxn])
                P.copy("act", c3(X2)[:, sl3, :], g8v(px), [pxn], ["U8_0"])
                pu, pun = bank()
                for u in range(G8):
                    c = g8 * G8 + u
                    P.mm(pu[0:64, u * 64:(u + 1) * 64], c3(Q)[:, c, :], c3(X2)[:, c, :], True, True, ["F0", "U8_0"], [pun])
                P.copy("dve", c3(Uv)[:, sl3, :], g8v(pu), [pun], ["U8_1"])
            for g8 in range(NG8):
                sl3 = slice(g8 * G8, (g8 + 1) * G8)
                pa, pan = bank()
                pbb, pbbn = bank()
                pr, prn = bank()
                for u in range(G8):
                    c = g8 * G8 + u
                    P.mm(pa[0:64, u * 64:(u + 1) * 64], c3(Wc)[:, c, :], c3(Btok)[:, c, :], True, True, ["X8_7", "X8_4"], [pan])
                    P.mm(pbb[0:64, u * 64:(u + 1) * 64], c3(Btok)[:, c, :], c3(Uv)[:, c, :], True, False, ["X8_4", "U8_1"], [pbbn])
                    P.mm(pbb[0:64, u * 64:(u + 1) * 64], c3(Ktok)[:, c, :], c3(Vt)[:, c, :], False, True, ["X8_5", "X8_2"], [pbbn])
                    P.mm(pr[0:64, u * 64:(u + 1) * 64], c3(Wc)[:, c, :], G3b[:, c, 64:128], True, True, ["X8_7"] + GBM_T, [prn])
                P.tt("dve", c3(ATm)[:, sl3, :], g8v(pa), idf.unsqueeze(1).to_broadcast([64, G8, 64]), ALU.add,
                     [pan, "ident_f"], ["W16_0"])
                P.tt("dve", c3(BPC)[:, sl3, :], g8v(pbb), PCt[:, sl3].unsqueeze(2).to_broadcast([64, G8, 64]), ALU.mult,
                     [pbbn, "PCt"], ["W16_0"])
                P.tt("dve", c3(R2)[:, sl3, :], g8v(pr), AR4[:, sl3, 1, :], ALU.add, [prn] + AR_T, ["X8_6"])
            P.add("dve", lambda: nc.vector.memset(Sb[0], 0.0), [], ["S0"])
            Sn = ["S0", "S1"]
            pY = None
            for c in range(NCH):
                cur, nxt = Sb[c % 2], Sb[(c + 1) % 2]
                cn, nn = Sn[c % 2], Sn[(c + 1) % 2]
                if sample:
                    cur, cn = SinT[:, c, :], "SinT"
                pS, pSn = bank(0, 6)
                P.mm(pS[0:64, 0:64], c3(ATm)[:, c, :], cur, True, True, ["W16_0", cn], [pSn])
                P.stt("dve", nxt, pS[0:64, 0:64], PCt[:, c:c + 1], c3(BPC)[:, c, :], ALU.mult, ALU.add,
                      [pSn, "PCt", "W16_0"], [nn])
                if c % 8 == 0:
                    pY, pYn = bank(6, 8)
                u = c % 8
                yo = pY[0:64, u * 64:(u + 1) * 64]
                P.mm(yo, c3(Uv)[:, c, :], G3b[:, c, 64:128], True, False, GBM_T + ["U8_1"], [pYn])
                P.mm(yo, c3(Vt)[:, c, :], G3k[:, c, 64:128], False, False, GKM_T + ["X8_2"], [pYn])
                P.mm(yo, cur, c3(R2)[:, c, :], False, True, ["X8_6", cn], [pYn])
                if sample:
                    pbs, pbsn = bank(0, 6)
                    P.tr(pbs[0:64, 0:64], nxt, idf, [nn, "ident_f"], [pbsn])
                    P.copy("act", Xs, pbs[0:64, 0:64], [pbsn], ["Xs"])
                    P.dma("pool", cSs[c, h], Xs, ["Xs"], [], "Xs")
                if u == 7 or c == NCH - 1:
                    P.copy("act", yF[:, (c - u) * 64:(c + 1) * 64], pY[0:64, 0:(u + 1) * 64], [pYn], ["F2"])
            if not sample:
                pb, pbn = bank(0, 6)
                P.tr(pb[0:64, 0:64], Sb[NCH % 2], idf, [Sn[NCH % 2], "ident_f"], [pbn])
                P.copy("act", Xs, pb[0:64, 0:64], [pbn], ["Xs"])
                P.dma("pool", cSp[h], Xs, ["Xs"], [], "Xs")
            ysq, mean, e2m, tmp = X8[4][0:64, 0:T], X8[5][0:64, 0:T], X8[6][0:64, 0:T], X8[7][0:64, 0:T]
            P.act(ysq, yF, AF.Square, ["F2"], ["X8_4"])
            for tb in range(NTB):
                sl = slice(tb * TB, (tb + 1) * TB)
                pm, pmn = bank(0, 6)
                pe, pen = bank(0, 6)
                P.mm(pm[0:64, 0:TB], onesd, yF[:, sl], True, True, ["F2", "rsm_c"], [pmn])
                P.mm(pe[0:64, 0:TB], onesd, ysq[:, sl], True, True, ["X8_4", "rsm_c"], [pen])
                P.copy("act", mean[:, sl], pm[0:64, 0:TB], [pmn], ["X8_5"])
                P.copy("dve", e2m[:, sl], pe[0:64, 0:TB], [pen], ["X8_6"])
            P.tt("pool", tmp, mean, mean, ALU.mult, ["X8_5"], ["X8_7"])
            P.tt("pool", e2m, e2m, tmp, ALU.subtract, ["X8_6", "X8_7"], ["X8_6"])
            P.ts("dve", e2m, e2m, 64e-5, None, ALU.add, None, ["X8_6"], ["X8_6"])
            P.act(e2m, e2m, AF.Sqrt, ["X8_6"], ["X8_6"])
            P.add("dve", lambda: nc.vector.reciprocal(out=e2m, in_=e2m), ["X8_6"], ["X8_6"])
            P.tt("dve", yF, yF, mean, ALU.subtract, ["F2", "X8_5"], ["F2"])
            P.tt("dve", yF, yF, e2m, ALU.mult, ["F2", "X8_6"], ["F2"])
            P.ts("dve", yF, yF, prm("c_gn_g", h), prm("c_gn_b", h), ALU.mult, ALU.add, ["F2", "prm"], ["F2"])
            P.tt("pool", tmp, bs, vF, ALU.mult, ["F1", "X8_3"], ["X8_7"])
            P.tt("dve", yF, yF, tmp, ALU.add, ["F2", "X8_7"], ["F2"])
            P.tt("dve", ygT, yF, gT, ALU.mult, ["F2", "B1"], ["B2"])
            P.dma("pool", SRC["yT"][0][h // 2, (h % 2) * 64:(h % 2) * 64 + 64, :], ygT, ["B2"],
                  [f"{SRC['yT'][1]}{h // 2}"], "B2")

    def load_yT():
        for t in range(NT):
            P.dma("sp", xnT[:, :, t * 128:(t + 1) * 128],
                  yT_scr[:, :, t * 128:(t + 1) * 128].rearrange("k p c -> p k c"),
                  [f"yT_scr{k}" for k in range(KC)], [XNT[t]], f"yTl{t % 4}")

    def final_norm(src_scr, src_tok):
        P.dma("sp", Fb[2][:], final_g[0:1, :].partition_broadcast(128), [], ["F2"], "F2")
        for t in range(NT):
            s_ = t % 2
            P.dma("sp", Fb[s_][:], src_scr[t * 128:(t + 1) * 128, :], [src_tok], [f"F{s_}"], f"F{s_}")
            P.act(Bb[2][:], Fb[s_][:], AF.Square, [f"F{s_}"], ["B2", "ssq"], accum_out=stat[:, 0:1])
            P.ts("dve", stat[:, 1:2], stat[:, 0:1], 1.0 / D, 1e-6, ALU.mult, ALU.add, ["ssq"], ["rstd0"])
            P.act(stat[:, 3:4], stat[:, 1:2], AF.Sqrt, ["rstd0"], ["rstd1"])
            P.add("dve", lambda: nc.vector.reciprocal(out=stat[:, 2:3], in_=stat[:, 3:4]), ["rstd1"], ["rstd"])
            P.stt("dve", Fb[s_][:], Fb[s_][:], stat[:, 2:3], Fb[2][:], ALU.mult, ALU.mult, [f"F{s_}", "rstd", "F2"], [f"F{s_}"])
            P.dma("pool", yp_out[t * 128:(t + 1) * 128, :], Fb[s_][:], [f"F{s_}"], [], f"fo{s_}")


    def sample_path():
        barrier()
        xflat = xnT[:, :, :].rearrange("p k t -> p (k t)")

        def SV(off, n):
            return xflat[:, 2 * off:2 * (off + n)].bitcast(F32)[0:NS, :]

        zs = SV(0, ABC)
        ysl = SV(7168, D)
        hA = SV(9216, D)
        hB = SV(11264, D)
        tmpv = SV(13312, D)
        miscA = Fb[0][0:NS, :]
        miscB = Fb[1][0:NS, :]
        gainb = Fb[2][0:NS, :]
        xsb = Bb[0][0:NS, :]
        xT6 = [Bb[1][:, m * 64:(m + 1) * 64].rearrange("p (k b) -> p k b", b=NS) for m in range(6)]
        hTs = Bb[2][0:96, 0:NS]
        pTs = Bb[2][:, 64:72].rearrange("p (k b) -> p k b", b=NS)
        UF = U[:, :].bitcast(F32)
        Kg, Vg, prod, Wt = UF[:, 0:1024], UF[:, 1024:2048], UF[:, 2048:3072], UF[:, 3072:4096]
        sc = UF[:, 4096:4104]
        biasm = UF[:, 4104:4128].rearrange("p (a h) -> p a h", a=3)
        Pm = UF[:, 4128:4136]
        onesel = UF[:, 4136:4152].rearrange("p (b m) -> p b m", b=4)
        selb = UF[0:NS, 4160:4672].rearrange("p (b m) -> p b m", b=4)
        small = UF[0:NS, 4672:4800]
        zeroT = UF[:, 4800:8192]

        def s_norm(src, gain_row, dst, eps=1e-6):
            P.dma("sp", gainb, gain_row.partition_broadcast(NS), [], ["F2"], "F2")
            P.act(tmpv, src, AF.Square, ["sx"], ["stmp", "sssq"], accum_out=stat[0:NS, 0:1])
            P.ts("dve", stat[0:NS, 1:2], stat[0:NS, 0:1], 1.0 / D, eps, ALU.mult, ALU.add, ["sssq"], ["srs0"])
            P.act(stat[0:NS, 3:4], stat[0:NS, 1:2], AF.Sqrt, ["srs0"], ["srs1"])
            P.add("dve", lambda: nc.vector.reciprocal(out=stat[0:NS, 2:3], in_=stat[0:NS, 3:4]), ["srs1"], ["srs"])
            P.stt("dve", dst, src, stat[0:NS, 2:3], gainb, ALU.mult, ALU.mult, ["sx", "srs", "F2"], ["sx"])

        def s_T(src, dstT, nk=KC, tok="sx"):
            P.copy("dve", xsb[:, 0:nk * 128], src, [tok], ["B0"])
            pb, pbn = bank()
            pbv = pb[:].bitcast(BF16)
            for kc in range(nk):
                P.tr(pbv[:, kc * NS:(kc + 1) * NS], xsb[:, kc * 128:(kc + 1) * 128], ident_b[0:NS, 0:NS],
                     ["B0", "ident_b"], [pbn])
            P.copy("act", dstT, pbv[:, 0:nk * NS].rearrange("p (k b) -> p k b", b=NS), [pbn], ["sT"])

        def s_proj(wsrc, nblocks, xT, dst, nk=KC, col0=0):
            for cb in range(nblocks):
                W, wn = load_wblock(wsrc, [(col0 + cb * 512, 512)], nk)
                pb, pbn = bank()
                for kc in range(nk):
                    P.mm(pb[0:NS, :], xT[:, kc, :], W[:, kc, :], kc == 0, kc == nk - 1, ["sT", wn], [pbn])
                P.copy("act", dst[:, cb * 512:(cb + 1) * 512], pb[0:NS, :], [pbn], ["sx"])

        P.dma("sp", onesel, c_onesel[:, :].rearrange("p (b m) -> p b m", b=4), [], ["sconst"], "sc0")
        P.dma("sp", selb, c_selb[:, :].rearrange("p (b m) -> p b m", b=4), [], ["sconst"], "sc1")
        P.add("dve", lambda: nc.vector.memset(zeroT, 0.0), [], ["zeroT"])
        for p in range(3):
            src = bass.AP(bias_scr.tensor, bias_scr[p, 0].offset + 128, [[511, 128], [65536, 8], [1, 1]])
            P.dma("sp", biasm[:, p, :].unsqueeze(2), src, ["bias_scr"], ["sconst"], f"sb{p}", allow_slow_non_contiguous=True)

        xs_t = hA
        P.dma("sp", xs_t, xs_in[:, :], [], ["sx"], "sx0")
        s_norm(xs_t, norm_g[0:1, :], miscA)
        s_T(miscA, xT6[0])
        s_proj(w_in, 14, xT6[0], zs)
        P.dma("pool", aks[:, :], zs[:, 1024:2048], ["sx"], [], "so0")
        P.dma("pool", avs[:, :], zs[:, 2048:3072], ["sx"], [], "so1")
        pnumA, pnumAn = bank(5, 6)
        pnumB, pnumBn = bank(6, 7)
        pden, pdenn = bank(7, 8)
        first = True
        for b in range(NS):
            pq0, pq0n = bank(0, 5)
            pq1, pq1n = bank(0, 5)
            P.mm(pq0[:, :], selb[:, b, :], zs[:, 0:512], True, True, ["sx", "sconst"], [pq0n])
            P.mm(pq1[:, :], selb[:, b, :], zs[:, 512:1024], True, True, ["sx", "sconst"], [pq1n])
            for p, dil in enumerate((1, 4, 16)):
                r0 = 2048 - 128 * dil
                P.dma("sp", Kg, cache_k[b, r0:2048:dil, :], [], ["Kg"], "Kg")
                P.dma("sp", Vg, cache_v[b, r0:2048:dil, :], [], ["Vg"], "Vg")
                P.tt("dve", prod[:, 0:512], Kg[:, 0:512], pq0[:, :], ALU.mult, ["Kg", pq0n], ["prod"])
                P.tt("dve", prod[:, 512:1024], Kg[:, 512:1024], pq1[:, :], ALU.mult, ["Kg", pq1n], ["prod"])
                P.add("dve", lambda: nc.vector.tensor_reduce(out=sc, in_=prod.rearrange("p (h e) -> p h e", h=8),
                                                             axis=AX.X, op=ALU.add), ["prod"], ["sc"])
                P.stt("dve", sc, sc, SCALE, biasm[:, p, :], ALU.mult, ALU.add, ["sc", "sconst"], ["sc"])
                P.act(Pm, sc, AF.Exp, ["sc"], ["Pm"])
                P.tt("dve", Wt.rearrange("p (h e) -> p h e", h=8), Vg.rearrange("p (h e) -> p h e", h=8),
                     Pm.unsqueeze(2).to_broadcast([128, 8, 128]), ALU.mult, ["Vg", "Pm"], ["Wt"])
                last = (b == NS - 1 and p == 2)
                P.mm(pnumA[0:NS, :], onesel[:, b, :], Wt[:, 0:512], first, last, ["Wt", "sconst"], [pnumAn])
                P.mm(pnumB[0:NS, :], onesel[:, b, :], Wt[:, 512:1024], first, last, ["Wt", "sconst"], [pnumBn])
                P.mm(pden[0:NS, 0:8], onesel[:, b, :], Pm, first, last, ["Pm", "sconst"], [pdenn])
                first = False
        num = miscA[:, 0:1024]
        den = small[:, 0:8]
        e0 = small[:, 8:16]
        rb0 = small[:, 16:24]
        P.copy("act", num[:, 0:512], pnumA[0:NS, :], [pnumAn], ["sx"])
        P.copy("act", num[:, 512:1024], pnumB[0:NS, :], [pnumBn], ["sx"])
        P.copy("act", den, pden[0:NS, 0:8], [pdenn], ["ssm"])
        P.dma("sp", rb0, rel_bias[0:1, :].partition_broadcast(NS), [], ["ssm"], "sx1")
        qk = miscB[:, 0:1024]
        P.tt("dve", qk, zs[:, 0:1024], zs[:, 1024:2048], ALU.mult, ["sx"], ["sx"])
        P.add("dve", lambda: nc.vector.tensor_reduce(out=e0, in_=qk.rearrange("p (h e) -> p h e", h=8), axis=AX.X,
                                                     op=ALU.add), ["sx", "ssm"], ["ssm"])
        P.stt("dve", e0, e0, SCALE, rb0, ALU.mult, ALU.add, ["ssm"], ["ssm"])
        P.act(e0, e0, AF.Exp, ["ssm"], ["ssm"])
        P.ts("dve", e0, e0, 3.0, None, ALU.mult, None, ["ssm"], ["ssm"])
        P.tt("dve", den, den, e0, ALU.add, ["ssm"], ["ssm"])
        P.add("dve", lambda: nc.vector.reciprocal(out=den, in_=den), ["ssm"], ["ssm"])
        v3 = lambda x: x.rearrange("p (h e) -> p h e", h=8)
        P.tt("dve", v3(qk), v3(zs[:, 2048:3072]), e0.unsqueeze(2).to_broadcast([NS, 8, 128]), ALU.mult, ["sx", "ssm"], ["sx"])
        P.tt("dve", num, num, qk, ALU.add, ["sx"], ["sx"])
        P.tt("dve", v3(num), v3(num), den.unsqueeze(2).to_broadcast([NS, 8, 128]), ALU.mult, ["sx", "ssm"], ["sx"])
        P.act(qk, zs[:, 3072:4096], AF.Silu, ["sx"], ["sx"])
        P.tt("dve", ysl[:, 0:1024], num, qk, ALU.mult, ["sx"], ["sx"])
        gvs = miscA[:, 0:1024]
        lgs, lbs = miscB[:, 0:1024], miscB[:, 1024:2048]
        w00, b00 = small[:, 24:32], small[:, 32:40]
        P.dma("sp", lgs, b_ln_g[0:1, :].partition_broadcast(NS), [], ["sx"], "sx2")
        P.dma("sp", lbs, b_ln_b[0:1, :].partition_broadcast(NS), [], ["sx"], "sx3")
        P.dma("sp", w00, bass.AP(b_w_s.tensor, 0, [[0, NS], [16384, 8]]), [], ["ssm"], "sx4", allow_slow_non_contiguous=True)
        P.dma("sp", b00, bass.AP(b_b_s.tensor, 0, [[0, NS], [128, 8]]), [], ["ssm"], "sx5", allow_slow_non_contiguous=True)
        P.act(gvs, zs[:, 5120:6144], AF.Gelu, ["sx"], ["sx", "slA"], accum_out=stat[0:NS, 4:5])
        P.act(tmpv[:, 0:1024], gvs, AF.Square, ["sx"], ["stmp", "slq"], accum_out=stat[0:NS, 6:7])
        P.ts("dve", stat[0:NS, 7:8], stat[0:NS, 4:5], 1.0 / 1024, None, ALU.mult, None, ["slA"], ["slm"])
        P.tt("dve", stat[0:NS, 8:9], stat[0:NS, 7:8], stat[0:NS, 7:8], ALU.mult, ["slm"], ["slm2"])
        P.stt("dve", stat[0:NS, 9:10], stat[0:NS, 6:7], 1.0 / 1024, stat[0:NS, 8:9], ALU.mult, ALU.subtract,
              ["slq", "slm2"], ["slv"])
        P.ts("dve", stat[0:NS, 9:10], stat[0:NS, 9:10], 1e-5, None, ALU.add, None, ["slv"], ["slv2"])
        P.act(stat[0:NS, 10:11], stat[0:NS, 9:10], AF.Sqrt, ["slv2"], ["slsd"])
        P.add("dve", lambda: nc.vector.reciprocal(out=stat[0:NS, 11:12], in_=stat[0:NS, 10:11]), ["slsd"], ["slrs"])
        P.ts("dve", gvs, gvs, stat[0:NS, 7:8], stat[0:NS, 11:12], ALU.subtract, ALU.mult, ["sx", "slm", "slrs"], ["sx"])
        P.tt("dve", gvs, gvs, lgs, ALU.mult, ["sx"], ["sx"])
        P.tt("dve", gvs, gvs, lbs, ALU.add, ["sx"], ["sx"])
        P.dma("pool", bvs[:, :], gvs, ["sx"], [], "so2")
        P.tt("dve", v3(gvs), v3(gvs), w00.unsqueeze(2).to_broadcast([NS, 8, 128]), ALU.mult, ["sx", "ssm"], ["sx"])
        P.tt("dve", v3(gvs), v3(gvs), b00.unsqueeze(2).to_broadcast([NS, 8, 128]), ALU.add, ["sx", "ssm"], ["sx"])
        P.act(lgs, zs[:, 4096:5120], AF.Gelu, ["sx"], ["sx"])
        P.act(lbs, zs[:, 6144:7168], AF.Silu, ["sx"], ["sx"])
        P.tt("dve", gvs, gvs, lgs, ALU.mult, ["sx"], ["sx"])
        P.tt("dve", ysl[:, 1024:2048], gvs, lbs, ALU.mult, ["sx"], ["sx"])

        def s_res_proj(wsrc, y, hin, hout):
            s_T(y, xT6[0])
            s_proj(wsrc, 4, xT6[0], tmpv)
            P.tt("dve", hout, hin, tmpv, ALU.add, ["sx"], ["sx"])

        def s_ple(layer, hin, hout):
            s_T(hin, xT6[0])
            s_proj(ple_wg[layer], 4, xT6[0], tmpv)
            P.act(tmpv, tmpv, AF.Sigmoid, ["sx"], ["sx"])
            P.dma("sp", miscB[:, 0:256], ps_in[layer], [], ["sx"], "sx6")
            s_T(miscB[:, 0:256], pTs, 2)
            s_proj(ple_wp[layer], 4, pTs, miscA, 2)
            P.tt("dve", tmpv, tmpv, miscA, ALU.mult, ["sx"], ["sx"])
            P.tt("dve", hout, hin, tmpv, ALU.add, ["sx"], ["sx"])

        s_res_proj(w_out, ysl, hA, hB)
        s_ple(0, hB, hA)
        s_norm(hA, norm_g[1:2, :], ysl)
        P.dma("pool", cxs[:, :], ysl, ["sx"], [], "so3")
        P.dma("sp", miscB, shift0[:, :], [], ["sx"], "sx7")
        P.tt("dve", miscB, miscB, ysl, ALU.subtract, ["sx"], ["sx"])
        for m in range(6):
            P.dma("sp", gainb, c_mu[m:m + 1, :].partition_broadcast(NS), [], ["F2"], "F2")
            P.tt("dve", miscA, miscB, gainb, ALU.mult, ["sx", "F2"], ["sx"])
            P.tt("dve", miscA, miscA, ysl, ALU.add, ["sx"], ["sx"])
            s_T(miscA, xT6[m], tok="sx")
        for (dst, tok) in ((rs_scr, "rs_scr"), (ks_scr, "ks_scr"), (vs_scr, "vs_scr"), (as_scr, "as_scr")):
            d3 = dst.rearrange("(k p) t -> p k t", p=128)
            for hf in range(2):
                P.dma("sp", d3[:, hf * 8:(hf + 1) * 8, :], zeroT[:, 0:2048].rearrange("p (k t) -> p k t", k=8), ["zeroT"], [tok],
                      "sz_" + tok)
        P.dma("sp", es_scr.rearrange("(k p) f -> p k f", p=128), zeroT[:, 0:2048].unsqueeze(1).to_broadcast([128, 2, 2048]),
              ["zeroT"], ["es_scr"], "sz_es")
        P.dma("sp", gs_scr.rearrange("(k p) t -> p k t", p=128),
              zeroT[:, 0:2048].bitcast(BF16).rearrange("p (k t) -> p k t", k=16), ["zeroT"], ["gs_scr"], "sz_gs")

        def s_proj_fm(wsrc, xT, dst_scr, tok, func=None, bf=False):
            for cb in range(4):
                W, wn = load_wblock(wsrc, [(cb * 512, 512)])
                pb, pbn = bank()
                for fb in range(4):
                    for kc in range(KC):
                        P.mm(pb[:, fb * NS:(fb + 1) * NS], W[:, kc, fb * 128:(fb + 1) * 128], xT[:, kc, :], kc == 0, kc == KC - 1,
                             ["sT", wn], [pbn])
                stg_ = (Fb[2][:, 1024:1040].bitcast(BF16)[:, 0:16] if bf else Fb[2][:, 1024:1040])
                if func is None:
                    P.copy("act", stg_, pb[:, 0:16], [pbn], ["sstg"])
                else:
                    P.act(stg_, pb[:, 0:16], func, [pbn], ["sstg"])
                for fb in range(4):
                    f0 = cb * 512 + fb * 128
                    P.dma("pool", dst_scr[f0:f0 + 128, 0:TS:64], stg_[:, fb * NS:(fb + 1) * NS], ["sstg"], [tok], "sstg",
                          allow_slow_non_contiguous=True)

        def s_lora_hidden(w1src, xT, func):
            W, wn = load_wblock(w1src, [(0, 96)])
            pb, pbn = bank()
            for kc in range(KC):
                P.mm(pb[0:96, 0:NS], W[:, kc, 0:96], xT[:, kc, :], kc == 0, kc == KC - 1, ["sT", wn], [pbn])
            if func is None:
                P.copy("act", hTs, pb[0:96, 0:NS], [pbn], ["shT"])
            else:
                P.act(hTs, pb[0:96, 0:NS], func, [pbn], ["shT"])

        def s_load_small(wsrc):
            i = wctr[0]
            wctr[0] += 1
            s_ = i % 2
            ss = (i * 4) % 2
            stv = wst[ss][:, :, :].rearrange("p a b -> p (a b)")
            P.dma("sp", stv[0:96, :], wsrc[:, :], [], [f"wst{ss}"], f"wst{ss}")
            wv = wbf[s_][:, 0:4, :].rearrange("p a b -> p (a b)")
            P.copy("dve", wv[0:96, :], stv[0:96, :], [f"wst{ss}"], [f"wbf{s_}"])
            return wv, f"wbf{s_}"

        s_proj_fm(c_wr, xT6[0], rs_scr, "rs_scr")
        s_proj_fm(c_wk, xT6[2], ks_scr, "ks_scr")
        s_proj_fm(c_wv, xT6[3], vs_scr, "vs_scr")
        s_proj_fm(c_wg, xT6[5], gs_scr, "gs_scr", AF.Silu, True)
        s_lora_hidden(c_a1, xT6[4], None)
        a2b, a2n = s_load_small(c_a2)
        pb, pbn = bank()
        for fb in range(KC):
            P.mm(pb[:, fb * NS:(fb + 1) * NS], a2b[0:96, fb * 128:(fb + 1) * 128], hTs, True, True, ["shT", a2n], [pbn])
        stg_a = Fb[2][:, 1040:1104]
        P.copy("act", stg_a, pb[:, 0:64], [pbn], ["sstg2"])
        for fb in range(KC):
            P.dma("pool", as_scr[fb * 128:(fb + 1) * 128, 0:TS:64], stg_a[:, fb * NS:(fb + 1) * NS], ["sstg2"], ["as_scr"],
                  "sstg2", allow_slow_non_contiguous=True)
        s_lora_hidden(c_w1, xT6[1], AF.Tanh)
        w2b, w2n = s_load_small(c_w2)
        P.dma("sp", miscB, c_w0[0:1, :].partition_broadcast(NS), [], ["sx"], "sx8")
        for cb in range(4):
            pb, pbn = bank()
            P.mm(pb[0:NS, :], hTs, w2b[0:96, cb * 512:(cb + 1) * 512], True, True, ["shT", w2n], [pbn])
            P.tt("dve", miscA[:, cb * 512:(cb + 1) * 512], pb[0:NS, :], miscB[:, cb * 512:(cb + 1) * 512], ALU.add, [pbn, "sx"],
                 ["sx"])
        P.act(miscA, miscA, AF.Sigmoid, ["sx"], ["sx"])
        P.dma("pool", es_scr[0:TS:64, :], miscA, ["sx"], ["es_scr"], "so4")
        P.copy("dve", lng[0:NS, :], hA[:, 0:1024], ["sx"], ["hsave"])
        P.copy("dve", lnb[0:NS, :], hA[:, 1024:2048], ["sx"], ["hsave"])
        rwkv_scan(range(32), 4, dict(r=(rs_scr, "rs_scr"), k=(ks_scr, "ks_scr"), v=(vs_scr, "vs_scr"), a=(as_scr, "as_scr"),
                                     e=(es_scr, "es_scr"), g=(gs_scr, "gs_scr"), yT=(yTs_scr, "yTs_scr")), True)
        barrier()
        ysT = xT6[0]
        for kc in range(KC):
            P.dma("sp", ysT[:, kc, :], yTs_scr[kc, :, 0:TS:64], [f"yTs_scr{k}" for k in range(KC)], ["sT"],
                  "sx9", allow_slow_non_contiguous=True)
        P.copy("dve", hA[:, 0:1024], lng[0:NS, :], ["hsave"], ["sx"])
        P.copy("dve", hA[:, 1024:2048], lnb[0:NS, :], ["hsave"], ["sx"])
        s_proj(c_wo, 4, ysT, tmpv)
        P.tt("dve", hB, hA, tmpv, ALU.add, ["sx"], ["sx"])
        s_ple(1, hB, hA)
        s_norm(hA, final_g[0:1, :], ysl)
        P.dma("pool", ys_out[:, :], ysl, ["sx"], [], "so5")

    if stage >= 6:
        rwkv_consts()
        phase_T(h_scr[1], "h_scr1", 1, True)
        rwkv_proj()
    if stage >= 7:
        nh1 = int(os.environ.get("NH1", "32"))
        rwkv_scan(range(nh1))
    if stage >= 8:
        barrier()
        load_yT()
        phase_proj_res(c_wo, h_scr[1], "h_scr1", h_scr[2], "h_scr2", None)
    if stage >= 9:
        phase_ple(1, h_scr[2], "h_scr2", h_scr[3], "h_scr3")
        final_norm(h_scr[3], "h_scr3")
    if stage == -1:
        rwkv_consts()
    if stage >= 10 or stage == -1:
        sample_path()

    if dbg:
        src, tok = {"h1": (h_scr[0], "h_scr0"), "h2": (h_scr[1], "h_scr1"), "h3": (h_scr[2], "h_scr2"),
                    "h4": (h_scr[3], "h_scr3"), "r": (r_scr, "r_scr"), "k": (k_scr, "k_scr"), "v": (v_scr, "v_scr"),
                    "a": (a_scr, "a_scr"), "e": (e_scr, "e_scr")}[dbg]
        for t in range(NT):
            s = t % 2
            P.dma("sp", Fb[s][:], src[t * 128:(t + 1) * 128, :], [tok], [f"F{s}"], f"F{s}")
            P.dma("sp", dbg_out[t * 128:(t + 1) * 128, :], Fb[s][:], [f"F{s}"], [], f"F{s}")

    P.sbuf_left = nc.sbuf_bytes_remaining
    P.finish()
    st.close()
    return nc, P


_CACHE = {}
OUT_NAMES = ["y_prompt","y_sample","a_k_prompt","a_v_prompt","a_k_sample","a_v_sample","b_v_sample","c_wkv_prompt","c_shift_prompt","c_wkv_sample","c_shift_sample"]


def make_in_maps(inputs):
    consts = host_consts()
    f = lambda a: np.ascontiguousarray(a, dtype=np.float32)
    x_prompt = f(inputs["x_prompt"])
    p_prompt = f(inputs["p_prompt"])
    shared = {
        "norm_g": f(inputs["norm_g"]),
        "rel_bias": f(inputs["rel_bias"]),
        "ab_w_in": f(inputs["ab_w_in"][0]),
        "ab_w_out": f(inputs["ab_w_out"][0]),
        "b_w_s": f(inputs["b_w_s"][0]),
        "b_b_s": f(inputs["b_b_s"][0]).reshape(1, 1024),
        "b_ln_g": f(inputs["b_ln_g"]).reshape(1, 1024),
        "b_ln_b": f(inputs["b_ln_b"]).reshape(1, 1024),
        "ple_w_proj": f(inputs["ple_w_proj"]),
        "ple_w_gate": f(inputs["ple_w_gate"]),
        "c_maskg": consts["maskg"], "c_maskn": consts["maskn"], "c_uneg": consts["uneg"],
        "c_selb": consts["selb"], "c_onesel": consts["onesel"],
        "c_mu": f(inputs["c_mu"][0]),
        "c_w_r": f(inputs["c_w_r"][0]), "c_w_k": f(inputs["c_w_k"][0]), "c_w_v": f(inputs["c_w_v"][0]),
        "c_w_g": f(inputs["c_w_g"][0]), "c_w_o": f(inputs["c_w_o"][0]),
        "c_w0": f(inputs["c_w0"]).reshape(1, D), "c_w1": f(inputs["c_w1"][0]), "c_w2": f(inputs["c_w2"][0]),
        "c_a0": f(inputs["c_a0"]).reshape(1, D), "c_a1": f(inputs["c_a1"][0]), "c_a2": f(inputs["c_a2"][0]),
        "c_k_k": f(inputs["c_k_k"]).reshape(1, D), "c_k_a": f(inputs["c_k_a"]).reshape(1, D),
        "c_r_k": f(inputs["c_r_k"]).reshape(1, D), "c_gn_g": f(inputs["c_gn_g"]).reshape(1, D),
        "c_gn_b": f(inputs["c_gn_b"]).reshape(1, D), "final_norm_g": f(inputs["final_norm_g"]).reshape(1, D),
        "c_ident": consts["ident"],
        "c_onehot": consts["onehot"],
        "c_trimask": consts["trimask"],
    }
    in_maps = []
    for c in range(NCORES):
        m = dict(shared)
        m["xp"] = x_prompt[c]
        sl = slice(c * NS, (c + 1) * NS)
        m["xs"] = f(inputs["x_sample"][sl, 0])
        m["cache_k"] = f(inputs["cache_a_k"][0, sl]).reshape(NS, 2048, 1024)
        m["cache_v"] = f(inputs["cache_a_v"][0, sl]).reshape(NS, 2048, 1024)
        m["wkv0"] = f(inputs["state_c_wkv"][0, sl])
        m["shift0"] = f(inputs["state_c_shift"][0, sl])
        m["ps"] = f(inputs["p_sample"][:, sl, 0])
        m["pp"] = np.ascontiguousarray(p_prompt[:, c])
        in_maps.append(m)
    return in_maps


def run_raw(inputs, stage=99, dbg=None):
    key = (stage, dbg)
    if key not in _CACHE:
        _CACHE[key] = build(stage, dbg)
    nc, P = _CACHE[key]
    res = run_bass_kernel_spmd(nc, make_in_maps(inputs), core_ids=list(range(NCORES)))
    return res.results


def kernel(**inputs):
    r = run_raw(inputs)
    st = lambda k, shp: np.stack([r[c][k].reshape(shp) for c in range(NCORES)])
    cat = lambda k, shp: np.concatenate([r[c][k].reshape(shp) for c in range(NCORES)], axis=0)
    y_prompt = st("yp", (S, D))
    y_sample = cat("ys", (NS, 1, D))
    akp = st("akp", (S, 8, 128))[None]
    avp = st("avp", (S, 8, 128))[None]
    aks = cat("aks", (NS, 1, 8, 128))[None]
    avs = cat("avs", (NS, 1, 8, 128))[None]
    bvs = cat("bvs", (NS, 1, 1024))[None]
    cSp = st("cSp", (32, 64, 64))[None]
    cxp = st("cxp", (D,))[None]
    cSs = cat("cSs", (NS, 32, 64, 64))[None]
    cxs = cat("cxs", (NS, D))[None]
    return (y_prompt, y_sample, akp, avp, aks, avs, bvs, cSp, cxp, cSs, cxs)
```

```python
import contextlib
import numpy as np
import concourse.bass as bass
import concourse.mybir as mybir
from concourse.bass_utils import run_bass_kernel_spmd

F32 = mybir.dt.float32
BF16 = mybir.dt.bfloat16
AF = mybir.ActivationFunctionType
ALU = mybir.AluOpType
AX = mybir.AxisListType

NCORES = 8
D = 2048
S = 2048
NT = S // 128
KC = D // 128
NS = 4
ABC = 7168
NEGB = -30000.0
import os
NORAW = bool(int(os.environ.get("NORAW", "0")))


class Buf:
    __slots__ = ("name", "lw", "rd")

    def __init__(self, name):
        self.name = name
        self.lw = None
        self.rd = []


class Op:
    __slots__ = ("eng", "fn", "deps", "is_dma", "key", "marked", "val", "waits", "raw")

    def __init__(self, eng, fn, is_dma=False, key=None):
        self.eng = eng
        self.fn = fn
        self.deps = []
        self.is_dma = is_dma
        self.key = key
        self.marked = False
        self.val = 0
        self.waits = []


class Prog:
    def __init__(self, nc, stack):
        self.nc = nc
        self.stack = stack
        self.ops = []
        self.E = {"pe": nc.tensor, "act": nc.scalar, "dve": nc.vector, "pool": nc.gpsimd, "sp": nc.sync}
        self.bufs = {}

    def buf(self, name):
        b = self.bufs.get(name)
        if b is None:
            b = Buf(name)
            self.bufs[name] = b
        return b

    def _mk(self, op, reads, writes):
        deps = []
        raw = set()
        for b in reads:
            if isinstance(b, str):
                b = self.buf(b)
            if b.lw is not None:
                deps.append(b.lw)
                raw.add(id(b.lw))
        for b in writes:
            if isinstance(b, str):
                b = self.buf(b)
            if b.lw is not None:
                deps.append(b.lw)
            deps.extend(b.rd)
        op.raw = raw
        for b in reads:
            if isinstance(b, str):
                b = self.buf(b)
            b.rd.append(op)
        for b in writes:
            if isinstance(b, str):
                b = self.buf(b)
            b.lw = op
            b.rd = []
        seen = set()
        for d in deps:
            if id(d) not in seen and d is not op:
                seen.add(id(d))
                op.deps.append(d)
        self.ops.append(op)
        return op

    def add(self, eng, fn, reads=(), writes=()):
        return self._mk(Op(eng, fn), reads, writes)

    def dma(self, eng, out, in_, reads, writes, key, **kw):
        e = self.E[eng]
        return self._mk(Op(eng, lambda: e.dma_start(out=out, in_=in_, **kw), True, key), reads, writes)

    def mm(self, out, lhsT, rhs, start, stop, reads, writes):
        pe = self.nc.tensor
        return self.add("pe", lambda: pe.matmul(out, lhsT=lhsT, rhs=rhs, start=start, stop=stop), reads, writes)

    def tr(self, out, in_, ident, reads, writes):
        pe = self.nc.tensor
        return self.add("pe", lambda: pe.transpose(out, in_, ident), reads, writes)

    def act(self, out, in_, func, reads, writes, eng="act", **kw):
        e = self.nc.scalar
        return self.add("act", lambda: e.activation(out=out, in_=in_, func=func, **kw), reads, writes)

    def copy(self, eng, out, in_, reads, writes):
        e = self.E[eng]
        if eng == "act":
            return self.add("act", lambda: e.copy(out=out, in_=in_), reads, writes)
        return self.add(eng, lambda: e.tensor_copy(out=out, in_=in_), reads, writes)

    def tt(self, eng, out, in0, in1, op, reads, writes):
        e = self.E[eng]
        return self.add(eng, lambda: e.tensor_tensor(out=out, in0=in0, in1=in1, op=op), reads, writes)

    def ts(self, eng, out, in0, s1, s2, op0, op1, reads, writes, **kw):
        e = self.E[eng]
        if op1 is None:
            return self.add(eng, lambda: e.tensor_scalar(out=out, in0=in0, scalar1=s1, scalar2=None, op0=op0, **kw),
                            reads, writes)
        return self.add(eng, lambda: e.tensor_scalar(out=out, in0=in0, scalar1=s1, scalar2=s2, op0=op0, op1=op1, **kw),
                        reads, writes)

    def stt(self, eng, out, in0, scalar, in1, op0, op1, reads, writes):
        e = self.E[eng]
        return self.add(eng, lambda: e.scalar_tensor_tensor(out=out, in0=in0, scalar=scalar, in1=in1, op0=op0, op1=op1),
                        reads, writes)

    def finish(self):
        nc = self.nc
        engs = ["pe", "act", "dve", "pool", "sp"]
        esem = {e: self.stack.enter_context(nc.semaphore("s_" + e)) for e in engs}
        dcount = {}
        keyeng = {}
        known = {e: {} for e in engs}
        dsem = {}
        pend = []
        for op in self.ops:
            w = []
            for a in op.deps:
                if a.is_dma:
                    w.append(("d", a.key, dcount[a.key]))
                else:
                    if a.eng == op.eng and not op.is_dma and (a.eng == "pe" or NORAW):
                        continue
                    a.marked = True
                    w.append(("c", a, 0))
            pend.append(w)
            if op.is_dma:
                assert keyeng.setdefault(op.key, op.eng) == op.eng, ("DMA sem shared across queues", op.key)
                dcount[op.key] = dcount.get(op.key, 0) + 16
                op.val = dcount[op.key]
        cnt = {e: 0 for e in engs}
        for op in self.ops:
            if not op.is_dma and op.marked:
                cnt[op.eng] += 1
                op.val = cnt[op.eng]
        for k in dcount:
            dsem[k] = self.stack.enter_context(nc.semaphore("d_" + k))
        nwait = 0
        for op, w in zip(self.ops, pend):
            E = self.E[op.eng]
            kn = known[op.eng]
            need = {}
            for kind, ref, val in w:
                if kind == "d":
                    sem = dsem[ref]
                    v = val
                else:
                    sem = esem[ref.eng]
                    v = ref.val
                sid = id(sem)
                if kn.get(sid, 0) >= v:
                    continue
                if sid not in need or need[sid][1] < v:
                    need[sid] = (sem, v)
            for sid, (sem, v) in need.items():
                E.wait_ge(sem, v)
                kn[sid] = v
                nwait += 1
            ins = op.fn()
            if op.is_dma:
                ins.then_inc(dsem[op.key], 16)
            elif op.marked:
                ins.then_inc(esem[op.eng], 1)
        for k, c in dcount.items():
            nc.sync.wait_ge(dsem[k], c)
        for e in engs:
            if e != "sp" and cnt[e] > 0:
                nc.sync.wait_ge(esem[e], cnt[e])
        self.stats = (len(self.ops), nwait, len(dcount))


def t5_bucket_np(dist):
    dist = np.asarray(dist, dtype=np.int64)
    n_exact = 16
    d = np.maximum(dist, 1).astype(np.float32)
    log_b = n_exact + (np.log(d / n_exact) / np.float32(np.log(2048 / n_exact)) * (32 - n_exact)).astype(np.int32)
    return np.where(dist < n_exact, dist, np.minimum(log_b, 31))


def host_consts():
    c = {}
    c["ident"] = np.eye(128, dtype=np.float32)
    oh = np.zeros((64, 3, 512), np.float32)
    for p, dil in enumerate((1, 4, 16)):
        s = np.arange(129)
        b = t5_bucket_np(s * dil)
        oh[b, p, s] = 1.0
        oh[32, p, 129:] = 1.0
    c["onehot"] = oh.reshape(64, 3 * 512)
    j = np.arange(128)[:, None]
    i = np.arange(128)[None, :]
    c["trimask"] = (i >= j).astype(np.float32)
    s_ = np.arange(64)[:, None]
    t_ = np.arange(64)[None, :]
    c["maskg"] = np.concatenate([(s_ < t_), (s_ <= t_)], axis=1).astype(np.float32)
    c["maskn"] = (t_ < s_).astype(np.float32)
    ss = np.arange(128)[:, None]
    tt = np.arange(128)[None, :]
    c["uneg"] = (-np.exp(-0.5) * ((ss <= tt) & (ss // 64 == tt // 64))).astype(np.float32)
    selb = np.zeros((4, 4, 128), np.float32)
    for b in range(4):
        selb[b, b, :] = 1.0
    c["selb"] = selb.reshape(4, 512)
    onesel = np.zeros((128, 4, 4), np.float32)
    for b in range(4):
        onesel[:, b, b] = 1.0
    c["onesel"] = onesel.reshape(128, 16)
    return c


def build(stage=99, dbg=None):
    nc = bass.Bass("TRN2", target_bir_lowering=False)
    st = contextlib.ExitStack()

    def din(name, shape):
        return nc.dram_tensor(name, list(shape), F32, kind="ExternalInput").ap()

    def dout(name, shape):
        return nc.dram_tensor(name, list(shape), F32, kind="ExternalOutput").ap()

    def dscr(name, shape, dt=F32):
        return nc.dram_tensor(name, list(shape), dt, kind="Internal").ap()

    def sb(name, shape, dt=F32):
        return st.enter_context(nc.sbuf_tensor(name, list(shape), dt))

    def ps(name, shape=(128, 512), dt=F32):
        return st.enter_context(nc.psum_tensor(name, list(shape), dt))

    xp = din("xp", (S, D))
    pp = din("pp", (2, S, 256))
    norm_g = din("norm_g", (2, D))
    rel_bias = din("rel_bias", (32, 8))
    w_in = din("ab_w_in", (D, ABC))
    w_out = din("ab_w_out", (D, D))
    b_w_s = din("b_w_s", (8, 128, 128))
    b_b_s = din("b_b_s", (1, 1024))
    b_ln_g = din("b_ln_g", (1, 1024))
    b_ln_b = din("b_ln_b", (1, 1024))
    ple_wp = din("ple_w_proj", (2, 256, D))
    ple_wg = din("ple_w_gate", (2, D, D))
    c_ident = din("c_ident", (128, 128))
    c_onehot = din("c_onehot", (64, 1536))
    c_trimask = din("c_trimask", (128, 128))
    c_selb = din("c_selb", (4, 512))
    c_onesel = din("c_onesel", (128, 16))
    xs_in = din("xs", (NS, D))
    cache_k = din("cache_k", (NS, 2048, 1024))
    cache_v = din("cache_v", (NS, 2048, 1024))
    wkv0 = din("wkv0", (NS, 32, 64, 64))
    shift0 = din("shift0", (NS, D))
    ps_in = din("ps", (2, NS, 256))
    ys_out = dout("ys", (NS, D))
    aks = dout("aks", (NS, 1024))
    avs = dout("avs", (NS, 1024))
    bvs = dout("bvs", (NS, 1024))
    cSs = dout("cSs", (NS, 32, 64, 64))
    cxs = dout("cxs", (NS, D))
    TS = 256
    rs_scr = dscr("rs_scr", (D, TS))
    ks_scr = dscr("ks_scr", (D, TS))
    vs_scr = dscr("vs_scr", (D, TS))
    as_scr = dscr("as_scr", (D, TS))
    es_scr = dscr("es_scr", (TS, D))
    gs_scr = dscr("gs_scr", (D, TS), BF16)
    yTs_scr = dscr("yTs_scr", (KC, 128, TS), BF16)
    c_maskg = din("c_maskg", (64, 128))
    c_maskn = din("c_maskn", (64, 64))
    c_uneg = din("c_uneg", (128, 128))
    c_mu = din("c_mu", (6, D))
    c_wr = din("c_w_r", (D, D))
    c_wk = din("c_w_k", (D, D))
    c_wv = din("c_w_v", (D, D))
    c_wg = din("c_w_g", (D, D))
    c_wo = din("c_w_o", (D, D))
    c_w0 = din("c_w0", (1, D))
    c_w1 = din("c_w1", (D, 96))
    c_w2 = din("c_w2", (96, D))
    c_a0 = din("c_a0", (1, D))
    c_a1 = din("c_a1", (D, 96))
    c_a2 = din("c_a2", (96, D))
    c_vecs = {n: din(n, (1, D)) for n in ("c_k_k", "c_k_a", "c_r_k", "c_gn_g", "c_gn_b")}
    final_g = din("final_norm_g", (1, D))
    r_scr = dscr("r_scr", (D, S))
    k_scr = dscr("k_scr", (D, S))
    v_scr = dscr("v_scr", (D, S))
    a_scr = dscr("a_scr", (D, S))
    e_scr = dscr("e_scr", (S, D))
    g_scr = dscr("g_scr", (D, S), BF16)
    yp_out = dout("yp", (S, D))
    cSp = dout("cSp", (32, 64, 64))
    cxp = dout("cxp", (1, D))
    akp = dout("akp", (S, 1024))
    avp = dout("avp", (S, 1024))
    dbg_out = dout("dbg", (S, D)) if dbg else None
    bias_scr = dscr("bias_scr", (3, 8, 128 * 512))
    yT_scr = dscr("yT_scr", (KC, 128, S), BF16)
    import os
    h_scr = [dscr(f"h_scr{i}", (S, D)) for i in range(int(os.environ.get("NSCR", "4")))]

    P = Prog(nc, st)

    ident_f = sb("ident_f", (128, 128))
    ident_b = sb("ident_b", (128, 128), BF16)
    ones_b = sb("ones_b", (128, 128), BF16)
    trimask = sb("trimask", (128, 128))
    xnT = sb("xnT", (128, KC, S), BF16)
    Fb = [sb(f"F{i}", (128, D)) for i in range(3)]
    Bb = [sb(f"B{i}", (128, D), BF16) for i in range(4)]
    stat = sb("stat", (128, 16))
    wst = [sb(f"wst{i}", (128, 4, 512)) for i in range(2)]
    wbf = [sb(f"wbf{i}", (128, KC, 512), BF16) for i in range(2)]
    U = sb("U", (128, 16384), BF16)
    WmT = sb("WmT", (128, 8, 128), BF16)
    bsb = sb("bsb", (128, 8, 128))
    lng = sb("lng", (128, 1024))
    lnb = sb("lnb", (128, 1024))
    wsf = sb("wsf", (128, 128))
    raug = sb("raug", (64, 8))
    pss = [ps(f"ps{i}") for i in range(8)]
    XNT = [f"xnT{t}" for t in range(NT)]

    bankctr = {}

    def bank(lo=0, hi=8):
        k = (lo, hi)
        i = lo + bankctr.get(k, 0) % (hi - lo)
        bankctr[k] = bankctr.get(k, 0) + 1
        return pss[i], f"ps{i}"

    Vp = [U[:, p * 2048:(p + 1) * 2048].rearrange("p (g e) -> p g e", e=128) for p in range(3)]
    kvst = [U[:, 6144 + i * 1024: 6144 + (i + 1) * 1024].bitcast(F32) for i in range(2)]
    BT = U[:, 8192:8192 + 1536].bitcast(F32).rearrange("p (a c) -> p a c", a=3)
    PT = [U[:, 9728 + i * 512: 9728 + (i + 1) * 512] for i in range(2)]
    vn_all = U[:, :].rearrange("p (t c) -> p t c", c=1024)
    UALL = ["Vp0", "Vp1", "Vp2", "kvst0", "kvst1", "BT", "PT0", "PT1"]

    P.dma("sp", ident_f[:], c_ident[:, :], [], ["ident_f"], "c_id")
    P.dma("sp", trimask[:], c_trimask[:, :], [], ["trimask"], "c_tri")
    P.copy("dve", ident_b[:], ident_f[:], ["ident_f"], ["ident_b"])
    P.add("dve", lambda: nc.vector.memset(ones_b[:], 1.0), [], ["ones_b"])

    oh = Fb[0][0:64, 0:1536]
    gvec = Fb[1][0:8, 0:1536].rearrange("p (a c) -> p a c", a=3)
    import os
    SK = os.environ.get("SKIP", "")
    if "a" not in SK:
        P.add("pool", lambda: nc.gpsimd.memset(raug[32:64, :], NEGB), [], ["raug"])
    if "b" not in SK:
        P.dma("sp", raug[0:32, :], rel_bias[:, :], [], ["raug"], "c_ra")
    if "c" not in SK:
        P.dma("sp", oh, c_onehot[:, :], [], ["F0"], "F0")
    BIS = int(os.environ.get("BIS", "9"))
    for p in range(3 if BIS >= 1 else 0):
        pb, pbn = bank()
        P.mm(pb[0:8, :], raug[:, :], oh[:, p * 512:(p + 1) * 512], True, True, ["raug", "F0"], [pbn])
        P.copy("act", gvec[:, p, :], pb[0:8, :], [pbn], ["F1"])
    for p in range(3 if BIS >= 2 else 0):
        dst = bias_scr[p].rearrange("h (r u) -> h r u", u=512)
        src = gvec[:, p, :].unsqueeze(1).to_broadcast([8, 128, 512])
        P.dma("sp", dst, src, ["F1"], ["bias_scr"], "gv")

    def phase_T(src, src_tok, layer, do_norm):
        if do_norm:
            g_row = norm_g[layer:layer + 1, :]
            P.dma("sp", Fb[2][:], g_row.partition_broadcast(128), [], ["F2"], "F2")
        for t in range(NT):
            s = t % 2
            xt_, xb_ = Fb[s], Bb[s]
            P.dma("sp", xt_[:], src[t * 128:(t + 1) * 128, :], [src_tok], [f"F{s}"], f"F{s}")
            if do_norm:
                P.act(Bb[2][:], xt_[:], AF.Square, [f"F{s}"], ["B2", "ssq"], accum_out=stat[:, 0:1])
                P.ts("dve", stat[:, 1:2], stat[:, 0:1], 1.0 / D, 1e-6, ALU.mult, ALU.add, ["ssq"], ["rstd0"])
                P.act(stat[:, 3:4], stat[:, 1:2], AF.Sqrt, ["rstd0"], ["rstd1"])
                P.add("dve", lambda: nc.vector.reciprocal(out=stat[:, 2:3], in_=stat[:, 3:4]), ["rstd1"], ["rstd"])
                P.stt("dve", xb_[:], xt_[:], stat[:, 2:3], Fb[2][:], ALU.mult, ALU.mult,
                      [f"F{s}", "rstd", "F2"], [f"B{s}"])
                if layer == 1 and t == NT - 1:
                    xnf = U[:, 0:4096].bitcast(F32)
                    P.stt("dve", xnf, xt_[:], stat[:, 2:3], Fb[2][:], ALU.mult, ALU.mult,
                          [f"F{s}", "rstd", "F2"], UALL + ["vn", "pT", "xnf"])
                    P.dma("pool", cxp[0:1, :], xnf[127:128, :], ["xnf"], [], "xnf")
            else:
                P.copy("dve", xb_[:], xt_[:], [f"F{s}"], [f"B{s}"])
            for q4 in range(4):
                pb, pbn = bank()
                pbv = pb[:].bitcast(BF16)
                for u in range(4):
                    kc = q4 * 4 + u
                    P.tr(pbv[:, u * 128:(u + 1) * 128], xb_[:, kc * 128:(kc + 1) * 128], ident_b[:],
                         [f"B{s}", "ident_b"], [pbn])
                eng = "act" if q4 % 2 == 0 else "dve"
                P.copy(eng, xnT[:, q4 * 4:(q4 + 1) * 4, t * 128:(t + 1) * 128],
                       pbv[:, 0:512].rearrange("p (u c) -> p u c", u=4), [pbn], [XNT[t]])

    phase_T(xp, "xp", 0, True)

    wctr = [0]

    def load_wblock(wsrc, col_groups, nk=KC):
        i = wctr[0]
        wctr[0] += 1
        s = i % 2
        name = f"wbf{s}"
        nq = (nk + 3) // 4
        for q in range(nq):
            ss = (i * 4 + q) % 2
            k4 = min(4, nk - q * 4)
            off = 0
            for (c0, ncol) in col_groups:
                src = wsrc[q * 512:q * 512 + k4 * 128, c0:c0 + ncol].rearrange("(k p) c -> p k c", p=128)
                P.dma("sp", wst[ss][:, 0:k4, off:off + ncol], src, [], [f"wst{ss}"], f"wst{ss}")
                off += ncol
            eng = "pool" if q % 2 == 0 else "dve"
            P.copy(eng, wbf[s][:, q * 4:q * 4 + k4, 0:off], wst[ss][:, 0:k4, 0:off], [f"wst{ss}"], [name])
        return wbf[s], name

    SCALE = 128 ** -0.5

    def tokset(dil, g):
        n, r = g // dil, g % dil
        start = n * 128 * dil + r
        return slice(start, start + 127 * dil + 1, dil)

    def accview(acc, dil, q4):
        if dil == 1:
            return acc[:, q4 * 512:(q4 + 1) * 512].rearrange("p (u i) -> p u i", u=4)
        if dil == 4:
            return acc[:, q4 * 512:(q4 + 1) * 512].rearrange("p (i r) -> p r i", r=4)
        return acc[:, :].rearrange("p (i r) -> p r i", r=16)[:, q4 * 4:(q4 + 1) * 4, :]

    qT, kT, gaT, yaT = Bb[0], Bb[1], Bb[2], Bb[3]
    num_acc, den_acc, tmpF = Fb[0], Fb[1], Fb[2]
    nheads = int(os.environ.get("NHEADS", "8")) if stage >= 1 else 0
    for h in range(nheads):
        W, wn = load_wblock(w_in, [(h * 128, 128), (1024 + h * 128, 128), (2048 + h * 128, 128), (3072 + h * 128, 128)])
        for p in range(3 if BIS >= 3 else 0):
            src = bass.AP(bias_scr.tensor, bias_scr[p, h].offset, [[511, 128], [1, 256]])
            P.dma("sp", BT[:, p, :], src, ["bias_scr"], ["BT"], "BT")
        for (dst, dn, c0, kind) in ((qT, "B0", 0, "q"), (kT, "B1", 128, "k"), (gaT, "B2", 384, "g")):
            if kind in os.environ.get("NOQKG", ""):
                continue
            for tb in range(4):
                pb, pbn = bank()
                for kc in range(KC):
                    P.mm(pb[:, :], W[:, kc, c0:c0 + 128], xnT[:, kc, tb * 512:(tb + 1) * 512], kc == 0, kc == KC - 1,
                         XNT[tb * 4:(tb + 1) * 4] + [wn], [pbn])
                o = dst[:, tb * 512:(tb + 1) * 512]
                if kind == "q":
                    P.act(o, pb[:, :], AF.Copy, [pbn], [dn], scale=SCALE)
                elif kind == "k":
                    P.copy("act", o, pb[:, :], [pbn], [dn])
                else:
                    P.act(o, pb[:, :], AF.Silu, [pbn], [dn])
        for t2 in range(NT // 2):
            pb, pbn = bank()
            for half in range(2):
                t = t2 * 2 + half
                for kc in range(KC):
                    P.mm(pb[:, half * 256:(half + 1) * 256], xnT[:, kc, t * 128:(t + 1) * 128], W[:, kc, 128:384],
                         kc == 0, kc == KC - 1, [XNT[t], wn], [pbn])
            ks = t2 % 2
            P.copy("act", kvst[ks], pb[:, :], [pbn], [f"kvst{ks}"])
            src = kvst[ks].rearrange("p (t two c) -> p t two c", t=2, two=2)
            P.copy("pool", Vp[0][:, t2 * 2:t2 * 2 + 2, :], src[:, :, 1, :], [f"kvst{ks}"], ["Vp0"])
            t0 = t2 * 2
            dstk = akp[t0 * 128:(t0 + 2) * 128, h * 128:(h + 1) * 128].rearrange("(t p) c -> p t c", p=128)
            dstv = avp[t0 * 128:(t0 + 2) * 128, h * 128:(h + 1) * 128].rearrange("(t p) c -> p t c", p=128)
            P.dma("pool", dstk, src[:, :, 0, :], [f"kvst{ks}"], [], f"kvst{ks}")
            P.dma("pool", dstv, src[:, :, 1, :], [f"kvst{ks}"], [], f"kvst{ks}")
        if stage < 2:
            continue
        for p, dil in ((1, 4), (2, 16)):
            for g4 in range(4):
                pb, pbn = bank()
                for u in range(4):
                    ts_ = tokset(dil, g4 * 4 + u)
                    for kc in range(KC):
                        P.mm(pb[:, u * 128:(u + 1) * 128], xnT[:, kc, ts_], W[:, kc, 256:384], kc == 0, kc == KC - 1,
                             XNT + [wn], [pbn])
                P.copy("act", Vp[p][:, g4 * 4:(g4 + 1) * 4, :], pb[:, :].rearrange("p (u c) -> p u c", u=4),
                       [pbn], [f"Vp{p}"])
        slot = 0
        for p, dil in enumerate((1, 4, 16)):
            for q4 in range(4):
                ob, obn = bank()
                db, dbn = bank()
                for half in range(2):
                    sbk, sbn = bank()
                    gs = [q4 * 4 + half * 2 + u2 for u2 in range(2)]
                    prevs = [(g // dil) > 0 for g in gs]
                    for u2, g in enumerate(gs):
                        ts_ = tokset(dil, g)
                        P.mm(sbk[:, u2 * 256:u2 * 256 + 128], kT[:, ts_], qT[:, ts_], True, True, ["B0", "B1"], [sbn])
                        if prevs[u2]:
                            tp_ = tokset(dil, g - dil)
                            P.mm(sbk[:, u2 * 256 + 128:u2 * 256 + 256], kT[:, tp_], qT[:, ts_], True, True,
                                 ["B0", "B1"], [sbn])
                    sl = slot % 2
                    slot += 1
                    tS = tmpF[:, sl * 512:(sl + 1) * 512]
                    tSn = f"tS{sl}"
                    pt = PT[sl]
                    ptn = f"PT{sl}"
                    if all(prevs):
                        P.tt("dve", tS.rearrange("p (u c) -> p u c", u=2), sbk[:, :].rearrange("p (u c) -> p u c", u=2),
                             BT[:, p, :].unsqueeze(1).to_broadcast([128, 2, 256]), ALU.add, [sbn, "BT"], [tSn, "F2"])
                        P.act(pt[:, :], tS, AF.Exp, [tSn], [ptn])
                    elif not any(prevs):
                        P.tt("dve", tS.rearrange("p (u c) -> p u c", u=2)[:, :, 0:128],
                             sbk[:, :].rearrange("p (u c) -> p u c", u=2)[:, :, 0:128],
                             BT[:, p, 0:128].unsqueeze(1).to_broadcast([128, 2, 128]), ALU.add, [sbn, "BT"], [tSn, "F2"])
                        P.act(pt[:, :].rearrange("p (u c) -> p u c", u=2)[:, :, 0:128],
                              tS.rearrange("p (u c) -> p u c", u=2)[:, :, 0:128], AF.Exp, [tSn], [ptn])
                    else:
                        for u2 in range(2):
                            w_ = 256 if prevs[u2] else 128
                            P.tt("dve", tS[:, u2 * 256:u2 * 256 + w_], sbk[:, u2 * 256:u2 * 256 + w_], BT[:, p, 0:w_],
                                 ALU.add, [sbn, "BT"], [tSn, "F2"])
                            P.act(pt[:, u2 * 256:u2 * 256 + w_], tS[:, u2 * 256:u2 * 256 + w_], AF.Exp, [tSn], [ptn])
                    for u2, g in enumerate(gs):
                        u = half * 2 + u2
                        for (ob_, obn_, lown, lprev, rd) in ((ob, obn, Vp[p][:, g, :], None, [f"Vp{p}"]),
                                                             (db, dbn, ones_b[:, :], ones_b[:, :], ["ones_b"])):
                            if lprev is None and prevs[u2]:
                                lprev = Vp[p][:, g - dil, :]
                            P.mm(ob_[:, u * 128:(u + 1) * 128], lown, pt[:, u2 * 256:u2 * 256 + 128], True, not prevs[u2],
                                 rd + [ptn], [obn_])
                            if prevs[u2]:
                                P.mm(ob_[:, u * 128:(u + 1) * 128], lprev, pt[:, u2 * 256 + 128:u2 * 256 + 256], False, True,
                                     rd + [ptn], [obn_])
                obv = ob[:, :].rearrange("p (u i) -> p u i", u=4)
                dbv = db[:, :].rearrange("p (u i) -> p u i", u=4)
                if p == 0:
                    P.copy("act", accview(num_acc, dil, q4), obv, [obn], ["F0"])
                    P.copy("act", accview(den_acc, dil, q4), dbv, [dbn], ["F1"])
                else:
                    P.tt("dve", accview(num_acc, dil, q4), obv, accview(num_acc, dil, q4), ALU.add, [obn, "F0"], ["F0"])
                    P.tt("dve", accview(den_acc, dil, q4), dbv, accview(den_acc, dil, q4), ALU.add, [dbn, "F1"], ["F1"])
        P.add("dve", lambda: nc.vector.reciprocal(out=den_acc[:], in_=den_acc[:]), ["F1"], ["F1"])
        P.tt("pool", num_acc[:], num_acc[:], den_acc[:], ALU.mult, ["F0", "F1"], ["F0"])
        P.tt("pool", yaT[:], num_acc[:], gaT[:], ALU.mult, ["F0", "B2"], ["B3"])
        P.dma("pool", yT_scr[h], yaT[:], ["B3"], [f"yT_scr{h}"], "B3")

    if stage >= 3:
        P.dma("sp", bsb[:].rearrange("p g c -> p (g c)"), b_b_s[0:1, :].partition_broadcast(128), [], ["bsb"], "c1")
        P.dma("sp", lng[:], b_ln_g[0:1, :].partition_broadcast(128), [], ["lng"], "c1")
        P.dma("sp", lnb[:], b_ln_b[0:1, :].partition_broadcast(128), [], ["lnb"], "c1")
        for g in range(8):
            P.dma("sp", wsf[:], b_w_s[g], [], ["wsf"], "wsf")
            pb, pbn = bank()
            P.tr(pb[:, 0:128], wsf[:], ident_f[:], ["wsf", "ident_f"], [pbn])
            P.tt("dve", WmT[:, g, :], pb[:, 0:128], trimask[:], ALU.mult, [pbn, "trimask"], ["WmT"])
        WA, nA = load_wblock(w_in, [(5120, 512)])
        WB, nB = load_wblock(w_in, [(5632, 512)])
        gv = Fb[0][:, 0:1024]
        for t in range(NT):
            pa, pan = bank()
            pb, pbn = bank()
            for (pq, pqn, Wq, nq) in ((pa, pan, WA, nA), (pb, pbn, WB, nB)):
                for kc in range(KC):
                    P.mm(pq[:, :], xnT[:, kc, t * 128:(t + 1) * 128], Wq[:, kc, :], kc == 0, kc == KC - 1,
                         [XNT[t], nq], [pqn])
            P.act(gv[:, 0:512], pa[:, :], AF.Gelu, [pan], ["F0", "lnsA"], accum_out=stat[:, 4:5])
            P.act(gv[:, 512:1024], pb[:, :], AF.Gelu, [pbn], ["F0", "lnsB"], accum_out=stat[:, 5:6])
            P.act(Fb[1][:, 0:1024], gv, AF.Square, ["F0"], ["F1", "lnsq"], accum_out=stat[:, 6:7])
            P.tt("dve", stat[:, 7:8], stat[:, 4:5], stat[:, 5:6], ALU.add, ["lnsA", "lnsB"], ["lnm0"])
            P.ts("dve", stat[:, 7:8], stat[:, 7:8], 1.0 / 1024, None, ALU.mult, None, ["lnm0"], ["lnm"])
            P.tt("dve", stat[:, 8:9], stat[:, 7:8], stat[:, 7:8], ALU.mult, ["lnm"], ["lnm2"])
            P.stt("dve", stat[:, 9:10], stat[:, 6:7], 1.0 / 1024, stat[:, 8:9], ALU.mult, ALU.subtract,
                  ["lnsq", "lnm2"], ["lnvar"])
            P.ts("dve", stat[:, 9:10], stat[:, 9:10], 1e-5, None, ALU.add, None, ["lnvar"], ["lnvar2"])
            P.act(stat[:, 10:11], stat[:, 9:10], AF.Sqrt, ["lnvar2"], ["lnsd"])
            P.add("dve", lambda: nc.vector.reciprocal(out=stat[:, 11:12], in_=stat[:, 10:11]), ["lnsd"], ["lnrs"])
            P.ts("dve", gv, gv, stat[:, 7:8], stat[:, 11:12], ALU.subtract, ALU.mult, ["F0", "lnm", "lnrs"], ["F0"])
            P.tt("dve", gv, gv, lng[:], ALU.mult, ["F0", "lng"], ["F0"])
            P.tt("dve", vn_all[:, t, :], gv, lnb[:], ALU.add, ["F0", "lnb"], UALL + ["vn"])
        ubT, gbT, ybT = Bb[0], Bb[1], Bb[2]
        for g2 in range(4):
            ga_, gb_ = 2 * g2, 2 * g2 + 1
            W, wn = load_wblock(w_in, [(4096 + ga_ * 128, 128), (6144 + ga_ * 128, 128),
                                       (4096 + gb_ * 128, 128), (6144 + gb_ * 128, 128)])
            for gg in range(2):
                g = 2 * g2 + gg
                for tb in range(4):
                    for (c0, dst, dn, fn) in ((gg * 256, ubT, "B0", AF.Gelu), (gg * 256 + 128, gbT, "B1", AF.Silu)):
                        pb, pbn = bank()
                        for kc in range(KC):
                            P.mm(pb[:, :], W[:, kc, c0:c0 + 128], xnT[:, kc, tb * 512:(tb + 1) * 512], kc == 0, kc == KC - 1,
                                 XNT[tb * 4:(tb + 1) * 4] + [wn], [pbn])
                        P.act(dst[:, tb * 512:(tb + 1) * 512], pb[:, :], fn, [pbn], [dn])
                    psb, psbn = bank()
                    for c4 in range(4):
                        n = tb * 4 + c4
                        P.mm(psb[:, c4 * 128:(c4 + 1) * 128], vn_all[:, n, g * 128:(g + 1) * 128], WmT[:, g, :], True, True,
                             ["vn", "WmT"], [psbn])
                    t1 = tmpF[:, (tb % 2) * 512:(tb % 2 + 1) * 512]
                    t1n = f"tS{tb % 2}"
                    P.tt("dve", t1.rearrange("p (u c) -> p u c", u=4), psb[:, :].rearrange("p (u c) -> p u c", u=4),
                         bsb[:, g, :].unsqueeze(1).to_broadcast([128, 4, 128]), ALU.add, [psbn, "bsb"], [t1n, "F2"])
                    P.tt("pool", t1, t1, ubT[:, tb * 512:(tb + 1) * 512], ALU.mult, [t1n, "B0"], [t1n])
                    P.tt("pool", ybT[:, tb * 512:(tb + 1) * 512], t1, gbT[:, tb * 512:(tb + 1) * 512], ALU.mult,
                         [t1n, "B1"], ["B2"])
                P.dma("pool", yT_scr[8 + g], ybT[:], ["B2"], [f"yT_scr{8 + g}"], "B2")

    def phase_proj_res(wsrc, src_scr, src_tok, dst_scr, dst_tok, yT_tokens):
        xres = [Fb[0][:, 0:512], Fb[0][:, 512:1024]]
        hout = [Fb[1][:, 0:512], Fb[1][:, 512:1024]]
        it = 0
        for cb in range(4):
            W, wn = load_wblock(wsrc, [(cb * 512, 512)])
            for t in range(NT):
                s = it % 2
                it += 1
                pb, pbn = bank()
                P.dma("sp", xres[s], src_scr[t * 128:(t + 1) * 128, cb * 512:(cb + 1) * 512], [src_tok], [f"xres{s}", "F0"],
                      f"xres{s}")
                for kc in range(KC):
                    P.mm(pb[:, :], xnT[:, kc, t * 128:(t + 1) * 128], W[:, kc, :], kc == 0, kc == KC - 1,
                         [XNT[t], wn], [pbn])
                P.tt("dve", hout[s], pb[:, :], xres[s], ALU.add, [pbn, f"xres{s}"], [f"hout{s}", "F1"])
                P.dma("pool", dst_scr[t * 128:(t + 1) * 128, cb * 512:(cb + 1) * 512], hout[s], [f"hout{s}"], [dst_tok],
                      f"hout{s}")

    if stage >= 4:
        for t in range(NT):
            P.dma("sp", xnT[:, :, t * 128:(t + 1) * 128],
                  yT_scr[:, :, t * 128:(t + 1) * 128].rearrange("k p c -> p k c"),
                  [f"yT_scr{k}" for k in range(KC)], [XNT[t]], f"yTl{t % 4}")
        phase_proj_res(w_out, xp, "xp", h_scr[0], "h_scr0", None)

    def phase_ple(layer, src_scr, src_tok, dst_scr, dst_tok):
        phase_T(src_scr, src_tok, layer, False)
        pT = U[:, 0:4096].rearrange("p (k c) -> p k c", k=2)
        pst = Fb[2][:, 0:256]
        pbf = Bb[3][:, 0:256]
        for t in range(NT):
            P.dma("sp", pst, pp[layer, t * 128:(t + 1) * 128, :], [], ["F2"], "F2")
            P.copy("dve", pbf, pst, ["F2"], ["B3"])
            pb, pbn = bank()
            pbv = pb[:].bitcast(BF16)
            for u in range(2):
                P.tr(pbv[:, u * 128:(u + 1) * 128], pbf[:, u * 128:(u + 1) * 128], ident_b[:], ["B3", "ident_b"], [pbn])
            P.copy("act", pT[:, :, t * 128:(t + 1) * 128], pbv[:, 0:256].rearrange("p (u c) -> p u c", u=2), [pbn],
                   UALL + ["vn", "pT"])
        xres = [Fb[0][:, 0:512], Fb[0][:, 512:1024]]
        hout = [Fb[1][:, 0:512], Fb[1][:, 512:1024]]
        sig = [Fb[0][:, 1024:1536], Fb[0][:, 1536:2048]]
        wpb = Bb[2][:, 0:1024].rearrange("p (k c) -> p k c", k=2)
        it = 0
        for cb in range(4):
            W, wn = load_wblock(ple_wg[layer], [(cb * 512, 512)])
            wps = Fb[2][:, 0:1024].rearrange("p (k c) -> p k c", k=2)
            P.dma("sp", wps, ple_wp[layer, :, cb * 512:(cb + 1) * 512].rearrange("(k p) c -> p k c", p=128), [], ["F2"], "F2")
            P.copy("dve", wpb, wps, ["F2"], ["B2"])
            for t in range(NT):
                s = it % 2
                it += 1
                pg, pgn = bank()
                pq, pqn = bank()
                P.dma("sp", xres[s], src_scr[t * 128:(t + 1) * 128, cb * 512:(cb + 1) * 512], [src_tok], [f"xres{s}", "F0"],
                      f"xres{s}")
                for kc in range(KC):
                    P.mm(pg[:, :], xnT[:, kc, t * 128:(t + 1) * 128], W[:, kc, :], kc == 0, kc == KC - 1, [XNT[t], wn], [pgn])
                for k2 in range(2):
                    P.mm(pq[:, :], pT[:, k2, t * 128:(t + 1) * 128], wpb[:, k2, :], k2 == 0, k2 == 1, ["pT", "B2"], [pqn])
                P.act(sig[s], pg[:, :], AF.Sigmoid, [pgn], [f"sig{s}", "F0"])
                P.tt("dve", sig[s], pq[:, :], sig[s], ALU.mult, [pqn, f"sig{s}"], [f"sig{s}"])
                P.tt("dve", hout[s], sig[s], xres[s], ALU.add, [f"sig{s}", f"xres{s}"], [f"hout{s}", "F1"])
                P.dma("pool", dst_scr[t * 128:(t + 1) * 128, cb * 512:(cb + 1) * 512], hout[s], [f"hout{s}"], [dst_tok],
                      f"hout{s}")

    if stage >= 5:
        phase_ple(0, h_scr[0], "h_scr0", h_scr[1], "h_scr1")


    def barrier(extra=()):
        names = list(P.bufs.keys()) + list(extra)
        P.add("pool", lambda: nc.gpsimd.memset(stat[:, 15:16], 0.0), [], names)

    rsm = sb("rsm", (128, 1088))
    mu_t = sb("mu_t", (128, 6, KC))
    R_MASKG, R_MASKN, R_ONES, R_UNEG, R_PRM, R_PC, R_XS, R_US, R_S0, R_S1, R_SP, R_ONESD = (
        0, 128, 192, 256, 384, 608, 640, 704, 768, 832, 896, 960)
    PRM_NAMES = ("c_k_k", "c_k_a", "c_r_k", "c_gn_g", "c_gn_b", "c_a0")

    def prm(name, h):
        o = R_PRM + PRM_NAMES.index(name) * 32 + h
        return rsm[0:64, o:o + 1]

    def rwkv_consts():
        P.dma("sp", rsm[0:64, R_MASKG:R_MASKG + 128], c_maskg[:, :], [], ["rsm_c"], "rc0")
        P.dma("sp", rsm[0:64, R_MASKN:R_MASKN + 64], c_maskn[:, :], [], ["rsm_c"], "rc1")
        P.dma("sp", rsm[:, R_UNEG:R_UNEG + 128], c_uneg[:, :], [], ["rsm_c"], "rc2")
        P.add("dve", lambda: nc.vector.memset(rsm[0:64, R_ONES:R_ONES + 64], 1.0), [], ["rsm_c"])
        P.add("dve", lambda: nc.vector.memset(rsm[0:64, R_ONESD:R_ONESD + 64], 1.0 / 64), [], ["rsm_c"])
        for i, nme in enumerate(PRM_NAMES):
            src_t = c_a0 if nme == "c_a0" else c_vecs[nme]
            src = bass.AP(src_t.tensor, 0, [[1, 64], [64, 32]])
            P.dma("sp", rsm[0:64, R_PRM + i * 32:R_PRM + (i + 1) * 32], src, [], ["prm"], f"rp{i}",
                  allow_slow_non_contiguous=True)
        P.dma("sp", mu_t[:], bass.AP(c_mu.tensor, 0, [[1, 128], [D, 6], [128, KC]]), [], ["mu_t"], "mu_t",
              allow_slow_non_contiguous=True)

    def rwkv_proj():
        barrier()
        xm = ([U[:, u * 2048:(u + 1) * 2048] for u in range(8)] + [Bb[i][:, :] for i in range(4)]
              + [Fb[i][:, :].bitcast(BF16)[:, a * 2048:(a + 1) * 2048] for i in range(2) for a in range(2)])
        xmn = [f"xm{k}" for k in range(KC)]
        dx = Fb[2][:, :].bitcast(BF16)[:, 0:2048]
        stg = [Fb[2][:, 1024:1536], Fb[2][:, 1536:2048]]
        hT = lng[:, :].bitcast(BF16)
        w0bc = [lnb[:, :], bsb[:, :, :].rearrange("p g c -> p (g c)")]

        def build_xm(m):
            for kc in range(KC):
                P.tt("pool", dx[:, 1:S], xnT[:, kc, 0:S - 1], xnT[:, kc, 1:S], ALU.subtract, XNT, ["dx"])
                P.ts("pool", dx[:, 0:1], xnT[:, kc, 0:1], -1.0, None, ALU.mult, None, XNT, ["dx"])
                P.stt("dve", xm[kc], dx, mu_t[:, m, kc:kc + 1], xnT[:, kc, :], ALU.mult, ALU.add,
                      ["dx", "mu_t"] + XNT, [xmn[kc]])

        itc = [0]

        def evac_store(pb, pbn, dst, dst_tok, func=None, bf=False, npart=128):
            s_ = itc[0] % 2
            itc[0] += 1
            o = stg[s_].bitcast(BF16)[0:npart, 0:512] if bf else stg[s_][0:npart, :]
            if func is not None:
                P.act(o, pb[0:npart, :], func, [pbn], [f"stg{s_}"])
            elif s_ == 0:
                P.copy("act", o, pb[0:npart, :], [pbn], [f"stg{s_}"])
            else:
                P.copy("dve", o, pb[0:npart, :], [pbn], [f"stg{s_}"])
            P.dma("pool", dst, o, [f"stg{s_}"], [dst_tok], f"stg{s_}")

        def proj_fm(wsrc, dst_scr, dst_tok, func=None, bf=False):
            for cb in range(4):
                W, wn = load_wblock(wsrc, [(cb * 512, 512)])
                for fb in range(4):
                    f0 = cb * 512 + fb * 128
                    for tb in range(4):
                        pb, pbn = bank()
                        for kc in range(KC):
                            P.mm(pb[:, :], W[:, kc, fb * 128:(fb + 1) * 128], xm[kc][:, tb * 512:(tb + 1) * 512],
                                 kc == 0, kc == KC - 1, [xmn[kc], wn], [pbn])
                        evac_store(pb, pbn, dst_scr[f0:f0 + 128, tb * 512:(tb + 1) * 512], dst_tok, func, bf)

        def load_small(wsrc):
            i = wctr[0]
            wctr[0] += 1
            s_ = i % 2
            ss = (i * 4) % 2
            stv = wst[ss][:, :, :].rearrange("p a b -> p (a b)")
            P.dma("sp", stv[0:96, :], wsrc[:, :], [], [f"wst{ss}"], f"wst{ss}")
            wv = wbf[s_][:, 0:4, :].rearrange("p a b -> p (a b)")
            P.copy("dve", wv[0:96, :], stv[0:96, :], [f"wst{ss}"], [f"wbf{s_}"])
            return wv, f"wbf{s_}"

        def lora_hidden(w1src, func):
            W, wn = load_wblock(w1src, [(0, 96)])
            for tb in range(4):
                pb, pbn = bank()
                for kc in range(KC):
                    P.mm(pb[0:96, :], W[:, kc, 0:96], xm[kc][:, tb * 512:(tb + 1) * 512], kc == 0, kc == KC - 1,
                         [xmn[kc], wn], [pbn])
                if func is None:
                    P.copy("act", hT[0:96, tb * 512:(tb + 1) * 512], pb[0:96, :], [pbn], ["hT"])
                else:
                    P.act(hT[0:96, tb * 512:(tb + 1) * 512], pb[0:96, :], func, [pbn], ["hT"])

        build_xm(0)
        proj_fm(c_wr, r_scr, "r_scr")
        build_xm(1)
        lora_hidden(c_w1, AF.Tanh)
        w2b, w2n = load_small(c_w2)
        for hf in range(2):
            P.dma("sp", w0bc[hf], c_w0[0:1, hf * 1024:(hf + 1) * 1024].partition_broadcast(128), [], [f"w0bc{hf}"],
                  f"w0bc{hf}")
        for t in range(NT):
            for cb in range(4):
                pb, pbn = bank()
                P.mm(pb[:, :], hT[0:96, t * 128:(t + 1) * 128], w2b[0:96, cb * 512:(cb + 1) * 512], True, True,
                     ["hT", w2n], [pbn])
                s_ = itc[0] % 2
                itc[0] += 1
                P.tt("dve", stg[s_], pb[:, :], w0bc[cb // 2][:, (cb % 2) * 512:(cb % 2 + 1) * 512], ALU.add,
                     [pbn, f"w0bc{cb // 2}"], [f"stg{s_}"])
                P.act(stg[s_], stg[s_], AF.Sigmoid, [f"stg{s_}"], [f"stg{s_}"])
                P.dma("pool", e_scr[t * 128:(t + 1) * 128, cb * 512:(cb + 1) * 512], stg[s_], [f"stg{s_}"], ["e_scr"],
                      f"stg{s_}")
        build_xm(2)
        proj_fm(c_wk, k_scr, "k_scr")
        build_xm(3)
        proj_fm(c_wv, v_scr, "v_scr")
        build_xm(4)
        lora_hidden(c_a1, None)
        a2b, a2n = load_small(c_a2)
        for fb in range(KC):
            for tb in range(4):
                pb, pbn = bank()
                P.mm(pb[:, :], a2b[0:96, fb * 128:(fb + 1) * 128], hT[0:96, tb * 512:(tb + 1) * 512], True, True,
                     ["hT", a2n], [pbn])
                evac_store(pb, pbn, a_scr[fb * 128:(fb + 1) * 128, tb * 512:(tb + 1) * 512], "a_scr")
        build_xm(5)
        proj_fm(c_wg, g_scr, "g_scr", AF.Silu, True)

    def rwkv_scan(heads, NCH=32, SRC=None, sample=False):
        barrier()
        T = NCH * 64
        TB = min(512, T)
        NTB = T // TB
        NTL = T // 128
        G8 = min(8, NCH)
        NG8 = NCH // G8
        NG4 = NCH // 4
        if SRC is None:
            SRC = dict(r=(r_scr, "r_scr"), k=(k_scr, "k_scr"), v=(v_scr, "v_scr"), a=(a_scr, "a_scr"),
                       e=(e_scr, "e_scr"), g=(g_scr, "g_scr"), yT=(yT_scr, "yT_scr"))
        xflat = xnT[:, :, :].rearrange("p k t -> p (k t)")
        X8 = [xflat[:, i * 4096:(i + 1) * 4096].bitcast(F32) for i in range(8)]
        U8 = [U[:, i * 4096:(i + 1) * 4096].bitcast(F32) for i in range(4)]
        W16 = [wbf[i][:, :, :].rearrange("p k c -> p (k c)").bitcast(F32) for i in range(2)]
        rF, kF, aF, vF, t1, t2, kmF, bF = [x[0:64, 0:T] for x in X8]
        rsb = U8[0][0:64, 0:T]
        Pin, Pinv = U8[0][0:64, 0:T], U8[1][0:64, 0:T]
        AR = U[:, 8192:16384].bitcast(F32)[0:64, 0:2 * T]
        BK = W16[0][0:64, 0:2 * T]
        Gbm = W16[1][0:64, 0:2 * T]
        Gkm = xflat[:, 0:8192].bitcast(F32)[0:64, 0:2 * T]
        Vt, Btok, Ktok = X8[2][0:64, 0:T], X8[4][0:64, 0:T], X8[5][0:64, 0:T]
        Nb = [X8[6][0:64, 0:T], X8[7][0:64, 0:T]]
        Tb = [U8[0][0:64, 0:T], U8[1][0:64, 0:T]]
        Q, bs, yF = Fb[0][0:64, 0:T], Fb[1][0:64, 0:T], Fb[2][0:64, 0:T]
        e2tok = Bb[0][:, :].bitcast(F32)[:, 0:NTL * 64].rearrange("p (t j) -> p t j", j=64)
        gT, ygT = Bb[1][0:64, 0:T], Bb[2][0:64, 0:T]
        AR4 = AR.rearrange("p (c a t) -> p c a t", c=NCH, a=2)
        BK4 = BK.rearrange("p (c a t) -> p c a t", c=NCH, a=2)
        c3 = lambda x: x.rearrange("p (c t) -> p c t", c=NCH)
        G3b, G3k = Gbm.rearrange("p (c t) -> p c t", c=NCH), Gkm.rearrange("p (c t) -> p c t", c=NCH)
        Sin = Bb[3][:, :].bitcast(F32)[0:64, 0:256].rearrange("p (b j) -> p b j", b=4)
        SinT = Bb[3][:, :].bitcast(F32)[0:64, 256:512].rearrange("p (b i) -> p b i", b=4)
        maskg = rsm[0:64, R_MASKG:R_MASKG + 128]
        maskn = rsm[0:64, R_MASKN:R_MASKN + 64]
        ones64 = rsm[0:64, R_ONES:R_ONES + 64]
        onesd = rsm[0:64, R_ONESD:R_ONESD + 64]
        uneg = rsm[:, R_UNEG:R_UNEG + 128]
        PCt = rsm[0:64, R_PC:R_PC + NCH]
        Xs = rsm[0:64, R_XS:R_XS + 64]
        Us = rsm[0:64, R_US:R_US + 64]
        Sb = [rsm[0:64, R_S0:R_S0 + 64], rsm[0:64, R_S1:R_S1 + 64]]
        SP = rsm[0:64, R_SP:R_SP + 64]
        idf = ident_f[0:64, 0:64]
        GKM_T = ["X8_0", "X8_1"]
        AR_T = ["U8_2", "U8_3"]
        BK_T = ["W16_0"]
        GBM_T = ["W16_1"]

        for h in heads:
            hs = slice(h * 64, (h + 1) * 64)
            P.dma("sp", rF, SRC["r"][0][hs, :], [SRC["r"][1]], ["X8_0"], "X8_0")
            P.dma("act", kF, SRC["k"][0][hs, :], [SRC["k"][1]], ["X8_1"], "X8_1")
            P.dma("sp", aF, SRC["a"][0][hs, :], [SRC["a"][1]], ["X8_2"], "X8_2")
            P.dma("act", vF, SRC["v"][0][hs, :], [SRC["v"][1]], ["X8_3"], "X8_3")
            P.dma("sp", e2tok, SRC["e"][0][:, hs].rearrange("(t p) j -> p t j", p=128), [SRC["e"][1]], ["B0"], "B0")
            P.dma("act", gT, SRC["g"][0][hs, :], [SRC["g"][1]], ["B1"], "B1")
            if sample:
                P.dma("sp", Sin, wkv0[:, h].rearrange("b i j -> i b j"), [], ["B3"], "Sin")
                pb, pbn = bank()
                for b_ in range(4):
                    P.tr(pb[0:64, b_ * 64:(b_ + 1) * 64], Sin[:, b_, :], idf, ["B3", "ident_f"], [pbn])
                P.copy("act", SinT, pb[0:64, 0:256].rearrange("p (b i) -> p b i", b=4), [pbn], ["SinT"])
            P.act(aF, aF, AF.Sigmoid, ["X8_2", "prm"], ["X8_2"], bias=prm("c_a0", h))
            P.ts("dve", t1, kF, prm("c_k_k", h), None, ALU.mult, None, ["X8_1", "prm"], ["X8_4"])
            P.act(t2, t1, AF.Square, ["X8_4"], ["X8_5"])
            for tb in range(NTB):
                pb, pbn = bank()
                P.mm(pb[0:64, 0:TB], ones64, t2[:, tb * TB:(tb + 1) * TB], True, True, ["X8_5", "rsm_c"], [pbn])
                P.act(rsb[:, tb * TB:(tb + 1) * TB], pb[0:64, 0:TB], AF.Sqrt, [pbn], ["U8_0"])
            P.ts("dve", rsb, rsb, 1e-12, None, ALU.max, None, ["U8_0"], ["U8_0"])
            P.add("dve", lambda: nc.vector.reciprocal(out=rsb, in_=rsb), ["U8_0"], ["U8_0"])
            P.tt("dve", t1, t1, rsb, ALU.mult, ["X8_4", "U8_0"], ["X8_4"])
            P.ts("pool", kmF, aF, 1.0, prm("c_k_a", h), ALU.subtract, ALU.mult, ["X8_2", "prm"], ["X8_6"])
            P.stt("dve", kmF, kmF, 1.0, kF, ALU.add, ALU.mult, ["X8_6", "X8_1"], ["X8_6"])
            P.tt("dve", bF, t1, aF, ALU.mult, ["X8_4", "X8_2"], ["X8_7"])
            P.stt("dve", t2, rF, prm("c_r_k", h), kmF, ALU.mult, ALU.mult, ["X8_0", "prm", "X8_6"], ["X8_5"])
            for tb in range(NTB):
                pb, pbn = bank()
                P.mm(pb[0:64, 0:TB], ones64, t2[:, tb * TB:(tb + 1) * TB], True, True, ["X8_5", "rsm_c"], [pbn])
                P.copy("act", bs[:, tb * TB:(tb + 1) * TB], pb[0:64, 0:TB], [pbn], ["F1"])
            for t4 in range((NTL + 3) // 4):
                pb, pbn = bank()
                nu = min(4, NTL - t4 * 4)
                for u in range(nu):
                    t = t4 * 4 + u
                    P.mm(pb[0:64, u * 128:(u + 1) * 128], e2tok[:, t, :], uneg, True, True, ["B0", "rsm_c"], [pbn])
                P.act(Pin[:, t4 * 512:t4 * 512 + nu * 128], pb[0:64, 0:nu * 128], AF.Exp, [pbn], ["U8_0"])
                P.act(Pinv[:, t4 * 512:t4 * 512 + nu * 128], pb[0:64, 0:nu * 128], AF.Exp, [pbn], ["U8_1"], scale=-1.0)
            P.tt("pool", AR4[:, :, 1, :], c3(rF), c3(Pin), ALU.mult, ["X8_0", "U8_0"], AR_T)
            P.stt("dve", AR4[:, :, 0, 1:64], c3(t1)[:, :, 1:64], -1.0, c3(Pin)[:, :, 0:63], ALU.mult, ALU.mult,
                  ["X8_4", "U8_0"], AR_T)
            P.ts("dve", AR4[:, :, 0, 0:1], c3(t1)[:, :, 0:1], -1.0, None, ALU.mult, None, ["X8_4"], AR_T)
            P.tt("dve", BK4[:, :, 0, :], c3(bF), c3(Pinv), ALU.mult, ["X8_7", "U8_1"], BK_T)
            P.tt("pool", BK4[:, :, 1, :], c3(kmF), c3(Pinv), ALU.mult, ["X8_6", "U8_1"], BK_T)
            P.copy("act", PCt, c3(Pin)[:, :, 63], ["U8_0"], ["PCt"])
            for g8 in range(NG8):
                pv, pvn = bank()
                pbt, pbtn = bank()
                pkt, pktn = bank()
                for u in range(G8):
                    c = g8 * G8 + u
                    P.tr(pv[0:64, u * 64:(u + 1) * 64], vF[:, c * 64:(c + 1) * 64], idf, ["X8_3", "ident_f"], [pvn])
                    P.tr(pbt[0:64, u * 64:(u + 1) * 64], BK4[:, c, 0, :], idf, BK_T + ["ident_f"], [pbtn])
                    P.tr(pkt[0:64, u * 64:(u + 1) * 64], BK4[:, c, 1, :], idf, BK_T + ["ident_f"], [pktn])
                sl = slice(g8 * G8 * 64, (g8 + 1) * G8 * 64)
                P.copy("act", Vt[:, sl], pv[0:64, 0:G8 * 64], [pvn], ["X8_2"])
                P.copy("dve", Btok[:, sl], pbt[0:64, 0:G8 * 64], [pbtn], ["X8_4"])
                P.copy("act", Ktok[:, sl], pkt[0:64, 0:G8 * 64], [pktn], ["X8_5"])
            for g4 in range(NG4):
                pgb, pgbn = bank()
                pgk, pgkn = bank()
                for u in range(4):
                    c = g4 * 4 + u
                    arc = AR4[:, c, :, :].rearrange("p a t -> p (a t)")
                    P.mm(pgb[0:64, u * 128:(u + 1) * 128], BK4[:, c, 0, :], arc, True, True, BK_T + AR_T, [pgbn])
                    P.mm(pgk[0:64, u * 128:(u + 1) * 128], BK4[:, c, 1, :], arc, True, True, BK_T + AR_T, [pgkn])
                mg = maskg.unsqueeze(1).to_broadcast([64, 4, 128])
                P.tt("dve", G3b[:, g4 * 4:(g4 + 1) * 4, :], pgb[0:64, :].rearrange("p (u t) -> p u t", u=4), mg, ALU.mult,
                     [pgbn, "rsm_c"], GBM_T)
                P.tt("dve", G3k[:, g4 * 4:(g4 + 1) * 4, :], pgk[0:64, :].rearrange("p (u t) -> p u t", u=4), mg, ALU.mult,
                     [pgkn, "rsm_c"], GKM_T)
            if sample:
                P.copy("dve", c3(Q), idf.unsqueeze(1).to_broadcast([64, NCH, 64]), ["ident_f"], ["F0"])
            else:
                for g8 in range(NG8):
                    pn, pnn = bank()
                    for u in range(G8):
                        c = g8 * G8 + u
                        P.mm(pn[0:64, u * 64:(u + 1) * 64], AR4[:, c, 0, :], BK4[:, c, 0, :], True, True, BK_T + AR_T, [pnn])
                    P.tt("dve", c3(Nb[0])[:, g8 * G8:(g8 + 1) * G8, :], pn[0:64, 0:G8 * 64].rearrange("p (u t) -> p u t", u=G8),
                         maskn.unsqueeze(1).to_broadcast([64, G8, 64]), ALU.mult, [pnn, "rsm_c"], ["X8_6"])
                P.copy("act", c3(Tb[0]), G3b[:, :, 0:64], GBM_T, ["U8_0"])
                P.tt("pool", c3(Q), c3(Tb[0]), idf.unsqueeze(1).to_broadcast([64, NCH, 64]), ALU.add, ["U8_0", "ident_f"], ["F0"])
                NT_ = ["X8_6", "X8_7"]
                TT_ = ["U8_0", "U8_1"]
                for k in range(5):
                    a_, b_ = k % 2, (k + 1) % 2
                    for g8 in range(NG8):
                        sl3 = slice(g8 * G8, (g8 + 1) * G8)
                        if k < 4:
                            pT_, pTn = bank()
                            for u in range(G8):
                                c = g8 * G8 + u
                                P.mm(pT_[0:64, u * 64:(u + 1) * 64], c3(Nb[a_])[:, c, :], c3(Tb[a_])[:, c, :], True, True,
                                     [NT_[a_], TT_[a_]], [pTn])
                            P.copy("act", c3(Tb[b_])[:, sl3, :], pT_[0:64, 0:G8 * 64].rearrange("p (u t) -> p u t", u=G8), [pTn], [TT_[b_]])
                        pN_, pNn = bank()
                        for u in range(G8):
                            c = g8 * G8 + u
                            P.mm(pN_[0:64, u * 64:(u + 1) * 64], c3(Tb[a_])[:, c, :], c3(Nb[a_])[:, c, :], True, True,
                                 [NT_[a_], TT_[a_]], [pNn])
                        P.copy("dve", c3(Nb[b_])[:, sl3, :], pN_[0:64, 0:G8 * 64].rearrange("p (u t) -> p u t", u=G8), [pNn], [NT_[b_]])
                        pQ_, pQn = bank()
                        for u in range(G8):
                            c = g8 * G8 + u
                            P.mm(pQ_[0:64, u * 64:(u + 1) * 64], c3(Nb[b_])[:, c, :], c3(Q)[:, c, :], True, True,
                                 [NT_[b_], "F0"], [pQn])
                        P.tt("dve", c3(Q)[:, sl3, :], pQ_[0:64, 0:G8 * 64].rearrange("p (u t) -> p u t", u=G8), c3(Q)[:, sl3, :], ALU.add,
                             [pQn, "F0"], ["F0"])
            At, Wc, X2, Uv, R2 = X8[6][0:64, 0:T], X8[7][0:64, 0:T], U8[0][0:64, 0:T], U8[1][0:64, 0:T], X8[6][0:64, 0:T]
            ATm, BPC = BK[:, 0:T], BK[:, T:2 * T]
            g8v = lambda pb_: pb_[0:64, 0:G8 * 64].rearrange("p (u t) -> p u t", u=G8)
            for g8 in range(NG8):
                sl3 = slice(g8 * G8, (g8 + 1) * G8)
                pa, pan = bank()
                for u in range(G8):
                    c = g8 * G8 + u
                    P.tr(pa[0:64, u * 64:(u + 1) * 64], AR4[:, c, 0, :], idf, AR_T + ["ident_f"], [pan])
                P.copy("act", c3(At)[:, sl3, :], g8v(pa), [pan], ["X8_6"])
                pw, pwn = bank()
                for u in range(G8):
                    c = g8 * G8 + u
                    P.mm(pw[0:64, u * 64:(u + 1) * 64], c3(Q)[:, c, :], c3(At)[:, c, :], True, True, ["F0", "X8_6"], [pwn])
                P.copy("dve", c3(Wc)[:, sl3, :], g8v(pw), [pwn], ["X8_7"])
                px, pxn = bank()
                for u in range(G8):
                    c = g8 * G8 + u
                    P.mm(px[0:64, u * 64:(u + 1) * 64], G3k[:, c, 0:64], c3(Vt)[:, c, :], True, True, GKM_T + ["X8_2"], [pxn])
                P.copy("act", c3(X2)[:, sl3, :], g8v(px), [pxn], ["U8_0"])
                pu, pun = bank()
                for u in range(G8):
                    c = g8 * G8 + u
                    P.mm(pu[0:64, u * 64:(u + 1) * 64], c3(Q)[:, c, :], c3(X2)[:, c, :], True, True, ["F0", "U8_0"], [pun])
                P.copy("dve", c3(Uv)[:, sl3, :], g8v(pu), [pun], ["U8_1"])
            for g8 in range(NG8):
                sl3 = slice(g8 * G8, (g8 + 1) * G8)
                pa, pan = bank()
                pbb, pbbn = bank()
                pr, prn = bank()
                for u in range(G8):
                    c = g8 * G8 + u
                    P.mm(pa[0:64, u * 64:(u + 1) * 64], c3(Wc)[:, c, :], c3(Btok)[:, c, :], True, True, ["X8_7", "X8_4"], [pan])
                    P.mm(pbb[0:64, u * 64:(u + 1) * 64], c3(Btok)[:, c, :], c3(Uv)[:, c, :], True, False, ["X8_4", "U8_1"], [pbbn])
                    P.mm(pbb[0:64, u * 64:(u + 1) * 64], c3(Ktok)[:, c, :], c3(Vt)[:, c, :], False, True, ["X8_5", "X8_2"], [pbbn])
                    P.mm(pr[0:64, u * 64:(u + 1) * 64], c3(Wc)[:, c, :], G3b[:, c, 64:128], True, True, ["X8_7"] + GBM_T, [prn])
                P.tt("dve", c3(ATm)[:, sl3, :], g8v(pa), idf.unsqueeze(1).to_broadcast([64, G8, 64]), ALU.add,
                     [pan, "ident_f"], ["W16_0"])
                P.tt("dve", c3(BPC)[:, sl3, :], g8v(pbb), PCt[:, sl3].unsqueeze(2).to_broadcast([64, G8, 64]), ALU.mult,
                     [pbbn, "PCt"], ["W16_0"])
                P.tt("dve", c3(R2)[:, sl3, :], g8v(pr), AR4[:, sl3, 1, :], ALU.add, [prn] + AR_T, ["X8_6"])
            P.add("dve", lambda: nc.vector.memset(Sb[0], 0.0), [], ["S0"])
            Sn = ["S0", "S1"]
            pY = None
            for c in range(NCH):
                cur, nxt = Sb[c % 2], Sb[(c + 1) % 2]
                cn, nn = Sn[c % 2], Sn[(c + 1) % 2]
                if sample:
                    cur, cn = SinT[:, c, :], "SinT"
                pS, pSn = bank(0, 6)
                P.mm(pS[0:64, 0:64], c3(ATm)[:, c, :], cur, True, True, ["W16_0", cn], [pSn])
                P.stt("dve", nxt, pS[0:64, 0:64], PCt[:, c:c + 1], c3(BPC)[:, c, :], ALU.mult, ALU.add,
                      [pSn, "PCt", "W16_0"], [nn])
                if c % 8 == 0:
                    pY, pYn = bank(6, 8)
                u = c % 8
                yo = pY[0:64, u * 64:(u + 1) * 64]
                P.mm(yo, c3(Uv)[:, c, :], G3b[:, c, 64:128], True, False, GBM_T + ["U8_1"], [pYn])
                P.mm(yo, c3(Vt)[:, c, :], G3k[:, c, 64:128], False, False, GKM_T + ["X8_2"], [pYn])
                P.mm(yo, cur, c3(R2)[:, c, :], False, True, ["X8_6", cn], [pYn])
                if sample:
                    pbs, pbsn = bank(0, 6)
                    P.tr(pbs[0:64, 0:64], nxt, idf, [nn, "ident_f"], [pbsn])
                    P.copy("act", Xs, pbs[0:64, 0:64], [pbsn], ["Xs"])
                    P.dma("pool", cSs[c, h], Xs, ["Xs"], [], "Xs")
                if u == 7 or c == NCH - 1:
                    P.copy("act", yF[:, (c - u) * 64:(c + 1) * 64], pY[0:64, 0:(u + 1) * 64], [pYn], ["F2"])
            if not sample:
                pb, pbn = bank(0, 6)
                P.tr(pb[0:64, 0:64], Sb[NCH % 2], idf, [Sn[NCH % 2], "ident_f"], [pbn])
                P.copy("act", Xs, pb[0:64, 0:64], [pbn], ["Xs"])
                P.dma("pool", cSp[h], Xs, ["Xs"], [], "Xs")
            ysq, mean, e2m, tmp = X8[4][0:64, 0:T], X8[5][0:64, 0:T], X8[6][0:64, 0:T], X8[7][0:64, 0:T]
            P.act(ysq, yF, AF.Square, ["F2"], ["X8_4"])
            for tb in range(NTB):
                sl = slice(tb * TB, (tb + 1) * TB)
                pm, pmn = bank(0, 6)
                pe, pen = bank(0, 6)
                P.mm(pm[0:64, 0:TB], onesd, yF[:, sl], True, True, ["F2", "rsm_c"], [pmn])
                P.mm(pe[0:64, 0:TB], onesd, ysq[:, sl], True, True, ["X8_4", "rsm_c"], [pen])
                P.copy("act", mean[:, sl], pm[0:64, 0:TB], [pmn], ["X8_5"])
                P.copy("dve", e2m[:, sl], pe[0:64, 0:TB], [pen], ["X8_6"])
            P.tt("pool", tmp, mean, mean, ALU.mult, ["X8_5"], ["X8_7"])
            P.tt("pool", e2m, e2m, tmp, ALU.subtract, ["X8_6", "X8_7"], ["X8_6"])
            P.ts("dve", e2m, e2m, 64e-5, None, ALU.add, None, ["X8_6"], ["X8_6"])
            P.act(e2m, e2m, AF.Sqrt, ["X8_6"], ["X8_6"])
            P.add("dve", lambda: nc.vector.reciprocal(out=e2m, in_=e2m), ["X8_6"], ["X8_6"])
            P.tt("dve", yF, yF, mean, ALU.subtract, ["F2", "X8_5"], ["F2"])
            P.tt("dve", yF, yF, e2m, ALU.mult, ["F2", "X8_6"], ["F2"])
            P.ts("dve", yF, yF, prm("c_gn_g", h), prm("c_gn_b", h), ALU.mult, ALU.add, ["F2", "prm"], ["F2"])
            P.tt("pool", tmp, bs, vF, ALU.mult, ["F1", "X8_3"], ["X8_7"])
            P.tt("dve", yF, yF, tmp, ALU.add, ["F2", "X8_7"], ["F2"])
            P.tt("dve", ygT, yF, gT, ALU.mult, ["F2", "B1"], ["B2"])
            P.dma("pool", SRC["yT"][0][h // 2, (h % 2) * 64:(h % 2) * 64 + 64, :], ygT, ["B2"],
                  [f"{SRC['yT'][1]}{h // 2}"], "B2")

    def load_yT():
        for t in range(NT):
            P.dma("sp", xnT[:, :, t * 128:(t + 1) * 128],
                  yT_scr[:, :, t * 128:(t + 1) * 128].rearrange("k p c -> p k c"),
                  [f"yT_scr{k}" for k in range(KC)], [XNT[t]], f"yTl{t % 4}")

    def final_norm(src_scr, src_tok):
        P.dma("sp", Fb[2][:], final_g[0:1, :].partition_broadcast(128), [], ["F2"], "F2")
        for t in range(NT):
            s_ = t % 2
            P.dma("sp", Fb[s_][:], src_scr[t * 128:(t + 1) * 128, :], [src_tok], [f"F{s_}"], f"F{s_}")
            P.act(Bb[2][:], Fb[s_][:], AF.Square, [f"F{s_}"], ["B2", "ssq"], accum_out=stat[:, 0:1])
            P.ts("dve", stat[:, 1:2], stat[:, 0:1], 1.0 / D, 1e-6, ALU.mult, ALU.add, ["ssq"], ["rstd0"])
            P.act(stat[:, 3:4], stat[:, 1:2], AF.Sqrt, ["rstd0"], ["rstd1"])
            P.add("dve", lambda: nc.vector.reciprocal(out=stat[:, 2:3], in_=stat[:, 3:4]), ["rstd1"], ["rstd"])
            P.stt("dve", Fb[s_][:], Fb[s_][:], stat[:, 2:3], Fb[2][:], ALU.mult, ALU.mult, [f"F{s_}", "rstd", "F2"], [f"F{s_}"])
            P.dma("pool", yp_out[t * 128:(t + 1) * 128, :], Fb[s_][:], [f"F{s_}"], [], f"fo{s_}")


    def sample_path():
        barrier()
        xflat = xnT[:, :, :].rearrange("p k t -> p (k t)")

        def SV(off, n):
            return xflat[:, 2 * off:2 * (off + n)].bitcast(F32)[0:NS, :]

        zs = SV(0, ABC)
        ysl = SV(7168, D)
        hA = SV(9216, D)
        hB = SV(11264, D)
        tmpv = SV(13312, D)
        miscA = Fb[0][0:NS, :]
        miscB = Fb[1][0:NS, :]
        gainb = Fb[2][0:NS, :]
        xsb = Bb[0][0:NS, :]
        xT6 = [Bb[1][:, m * 64:(m + 1) * 64].rearrange("p (k b) -> p k b", b=NS) for m in range(6)]
        hTs = Bb[2][0:96, 0:NS]
        pTs = Bb[2][:, 64:72].rearrange("p (k b) -> p k b", b=NS)
        UF = U[:, :].bitcast(F32)
        Kg, Vg, prod, Wt = UF[:, 0:1024], UF[:, 1024:2048], UF[:, 2048:3072], UF[:, 3072:4096]
        sc = UF[:, 4096:4104]
        biasm = UF[:, 4104:4128].rearrange("p (a h) -> p a h", a=3)
        Pm = UF[:, 4128:4136]
        onesel = UF[:, 4136:4152].rearrange("p (b m) -> p b m", b=4)
        selb = UF[0:NS, 4160:4672].rearrange("p (b m) -> p b m", b=4)
        small = UF[0:NS, 4672:4800]
        zeroT = UF[:, 4800:8192]

        def s_norm(src, gain_row, dst, eps=1e-6):
            P.dma("sp", gainb, gain_row.partition_broadcast(NS), [], ["F2"], "F2")
            P.act(tmpv, src, AF.Square, ["sx"], ["stmp", "sssq"], accum_out=stat[0:NS, 0:1])
            P.ts("dve", stat[0:NS, 1:2], stat[0:NS, 0:1], 1.0 / D, eps, ALU.mult, ALU.add, ["sssq"], ["srs0"])
            P.act(stat[0:NS, 3:4], stat[0:NS, 1:2], AF.Sqrt, ["srs0"], ["srs1"])
            P.add("dve", lambda: nc.vector.reciprocal(out=stat[0:NS, 2:3], in_=stat[0:NS, 3:4]), ["srs1"], ["srs"])
            P.stt("dve", dst, src, stat[0:NS, 2:3], gainb, ALU.mult, ALU.mult, ["sx", "srs", "F2"], ["sx"])

        def s_T(src, dstT, nk=KC, tok="sx"):
            P.copy("dve", xsb[:, 0:nk * 128], src, [tok], ["B0"])
            pb, pbn = bank()
            pbv = pb[:].bitcast(BF16)
            for kc in range(nk):
                P.tr(pbv[:, kc * NS:(kc + 1) * NS], xsb[:, kc * 128:(kc + 1) * 128], ident_b[0:NS, 0:NS],
                     ["B0", "ident_b"], [pbn])
            P.copy("act", dstT, pbv[:, 0:nk * NS].rearrange("p (k b) -> p k b", b=NS), [pbn], ["sT"])

        def s_proj(wsrc, nblocks, xT, dst, nk=KC, col0=0):
            for cb in range(nblocks):
                W, wn = load_wblock(wsrc, [(col0 + cb * 512, 512)], nk)
                pb, pbn = bank()
                for kc in range(nk):
                    P.mm(pb[0:NS, :], xT[:, kc, :], W[:, kc, :], kc == 0, kc == nk - 1, ["sT", wn], [pbn])
                P.copy("act", dst[:, cb * 512:(cb + 1) * 512], pb[0:NS, :], [pbn], ["sx"])

        P.dma("sp", onesel, c_onesel[:, :].rearrange("p (b m) -> p b m", b=4), [], ["sconst"], "sc0")
        P.dma("sp", selb, c_selb[:, :].rearrange("p (b m) -> p b m", b=4), [], ["sconst"], "sc1")
        P.add("dve", lambda: nc.vector.memset(zeroT, 0.0), [], ["zeroT"])
        for p in range(3):
            src = bass.AP(bias_scr.tensor, bias_scr[p, 0].offset + 128, [[511, 128], [65536, 8], [1, 1]])
            P.dma("sp", biasm[:, p, :].unsqueeze(2), src, ["bias_scr"], ["sconst"], f"sb{p}", allow_slow_non_contiguous=True)

        xs_t = hA
        P.dma("sp", xs_t, xs_in[:, :], [], ["sx"], "sx0")
        s_norm(xs_t, norm_g[0:1, :], miscA)
        s_T(miscA, xT6[0])
        s_proj(w_in, 14, xT6[0], zs)
        P.dma("pool", aks[:, :], zs[:, 1024:2048], ["sx"], [], "so0")
        P.dma("pool", avs[:, :], zs[:, 2048:3072], ["sx"], [], "so1")
        pnumA, pnumAn = bank(5, 6)
        pnumB, pnumBn = bank(6, 7)
        pden, pdenn = bank(7, 8)
        first = True
        for b in range(NS):
            pq0, pq0n = bank(0, 5)
            pq1, pq1n = bank(0, 5)
            P.mm(pq0[:, :], selb[:, b, :], zs[:, 0:512], True, True, ["sx", "sconst"], [pq0n])
            P.mm(pq1[:, :], selb[:, b, :], zs[:, 512:1024], True, True, ["sx", "sconst"], [pq1n])
            for p, dil in enumerate((1, 4, 16)):
                r0 = 2048 - 128 * dil
                P.dma("sp", Kg, cache_k[b, r0:2048:dil, :], [], ["Kg"], "Kg")
                P.dma("sp", Vg, cache_v[b, r0:2048:dil, :], [], ["Vg"], "Vg")
                P.tt("dve", prod[:, 0:512], Kg[:, 0:512], pq0[:, :], ALU.mult, ["Kg", pq0n], ["prod"])
                P.tt("dve", prod[:, 512:1024], Kg[:, 512:1024], pq1[:, :], ALU.mult, ["Kg", pq1n], ["prod"])
                P.add("dve", lambda: nc.vector.tensor_reduce(out=sc, in_=prod.rearrange("p (h e) -> p h e", h=8),
                                                             axis=AX.X, op=ALU.add), ["prod"], ["sc"])
                P.stt("dve", sc, sc, SCALE, biasm[:, p, :], ALU.mult, ALU.add, ["sc", "sconst"], ["sc"])
                P.act(Pm, sc, AF.Exp, ["sc"], ["Pm"])
                P.tt("dve", Wt.rearrange("p (h e) -> p h e", h=8), Vg.rearrange("p (h e) -> p h e", h=8),
                     Pm.unsqueeze(2).to_broadcast([128, 8, 128]), ALU.mult, ["Vg", "Pm"], ["Wt"])
                last = (b == NS - 1 and p == 2)
                P.mm(pnumA[0:NS, :], onesel[:, b, :], Wt[:, 0:512], first, last, ["Wt", "sconst"], [pnumAn])
                P.mm(pnumB[0:NS, :], onesel[:, b, :], Wt[:, 512:1024], first, last, ["Wt", "sconst"], [pnumBn])
                P.mm(pden[0:NS, 0:8], onesel[:, b, :], Pm, first, last, ["Pm", "sconst"], [pdenn])
                first = False
        num = miscA[:, 0:1024]
        den = small[:, 0:8]
        e0 = small[:, 8:16]
        rb0 = small[:, 16:24]
        P.copy("act", num[:, 0:512], pnumA[0:NS, :], [pnumAn], ["sx"])
        P.copy("act", num[:, 512:1024], pnumB[0:NS, :], [pnumBn], ["sx"])
        P.copy("act", den, pden[0:NS, 0:8], [pdenn], ["ssm"])
        P.dma("sp", rb0, rel_bias[0:1, :].partition_broadcast(NS), [], ["ssm"], "sx1")
        qk = miscB[:, 0:1024]
        P.tt("dve", qk, zs[:, 0:1024], zs[:, 1024:2048], ALU.mult, ["sx"], ["sx"])
        P.add("dve", lambda: nc.vector.tensor_reduce(out=e0, in_=qk.rearrange("p (h e) -> p h e", h=8), axis=AX.X,
                                                     op=ALU.add), ["sx", "ssm"], ["ssm"])
        P.stt("dve", e0, e0, SCALE, rb0, ALU.mult, ALU.add, ["ssm"], ["ssm"])
        P.act(e0, e0, AF.Exp, ["ssm"], ["ssm"])
        P.ts("dve", e0, e0, 3.0, None, ALU.mult, None, ["ssm"], ["ssm"])
        P.tt("dve", den, den, e0, ALU.add, ["ssm"], ["ssm"])
        P.add("dve", lambda: nc.vector.reciprocal(out=den, in_=den), ["ssm"], ["ssm"])
        v3 = lambda x: x.rearrange("p (h e) -> p h e", h=8)
        P.tt("dve", v3(qk), v3(zs[:, 2048:3072]), e0.unsqueeze(2).to_broadcast([NS, 8, 128]), ALU.mult, ["sx", "ssm"], ["sx"])
        P.tt("dve", num, num, qk, ALU.add, ["sx"], ["sx"])
        P.tt("dve", v3(num), v3(num), den.unsqueeze(2).to_broadcast([NS, 8, 128]), ALU.mult, ["sx", "ssm"], ["sx"])
        P.act(qk, zs[:, 3072:4096], AF.Silu, ["sx"], ["sx"])
        P.tt("dve", ysl[:, 0:1024], num, qk, ALU.mult, ["sx"], ["sx"])
        gvs = miscA[:, 0:1024]
        lgs, lbs = miscB[:, 0:1024], miscB[:, 1024:2048]
        w00, b00 = small[:, 24:32], small[:, 32:40]
        P.dma("sp", lgs, b_ln_g[0:1, :].partition_broadcast(NS), [], ["sx"], "sx2")
        P.dma("sp", lbs, b_ln_b[0:1, :].partition_broadcast(NS), [], ["sx"], "sx3")
        P.dma("sp", w00, bass.AP(b_w_s.tensor, 0, [[0, NS], [16384, 8]]), [], ["ssm"], "sx4", allow_slow_non_contiguous=True)
        P.dma("sp", b00, bass.AP(b_b_s.tensor, 0, [[0, NS], [128, 8]]), [], ["ssm"], "sx5", allow_slow_non_contiguous=True)
        P.act(gvs, zs[:, 5120:6144], AF.Gelu, ["sx"], ["sx", "slA"], accum_out=stat[0:NS, 4:5])
        P.act(tmpv[:, 0:1024], gvs, AF.Square, ["sx"], ["stmp", "slq"], accum_out=stat[0:NS, 6:7])
        P.ts("dve", stat[0:NS, 7:8], stat[0:NS, 4:5], 1.0 / 1024, None, ALU.mult, None, ["slA"], ["slm"])
        P.tt("dve", stat[0:NS, 8:9], stat[0:NS, 7:8], stat[0:NS, 7:8], ALU.mult, ["slm"], ["slm2"])
        P.stt("dve", stat[0:NS, 9:10], stat[0:NS, 6:7], 1.0 / 1024, stat[0:NS, 8:9], ALU.mult, ALU.subtract,
              ["slq", "slm2"], ["slv"])
        P.ts("dve", stat[0:NS, 9:10], stat[0:NS, 9:10], 1e-5, None, ALU.add, None, ["slv"], ["slv2"])
        P.act(stat[0:NS, 10:11], stat[0:NS, 9:10], AF.Sqrt, ["slv2"], ["slsd"])
        P.add("dve", lambda: nc.vector.reciprocal(out=stat[0:NS, 11:12], in_=stat[0:NS, 10:11]), ["slsd"], ["slrs"])
        P.ts("dve", gvs, gvs, stat[0:NS, 7:8], stat[0:NS, 11:12], ALU.subtract, ALU.mult, ["sx", "slm", "slrs"], ["sx"])
        P.tt("dve", gvs, gvs, lgs, ALU.mult, ["sx"], ["sx"])
        P.tt("dve", gvs, gvs, lbs, ALU.add, ["sx"], ["sx"])
        P.dma("pool", bvs[:, :], gvs, ["sx"], [], "so2")
        P.tt("dve", v3(gvs), v3(gvs), w00.unsqueeze(2).to_broadcast([NS, 8, 128]), ALU.mult, ["sx", "ssm"], ["sx"])
        P.tt("dve", v3(gvs), v3(gvs), b00.unsqueeze(2).to_broadcast([NS, 8, 128]), ALU.add, ["sx", "ssm"], ["sx"])
        P.act(lgs, zs[:, 4096:5120], AF.Gelu, ["sx"], ["sx"])
        P.act(lbs, zs[:, 6144:7168], AF.Silu, ["sx"], ["sx"])
        P.tt("dve", gvs, gvs, lgs, ALU.mult, ["sx"], ["sx"])
        P.tt("dve", ysl[:, 1024:2048], gvs, lbs, ALU.mult, ["sx"], ["sx"])

        def s_res_proj(wsrc, y, hin, hout):
            s_T(y, xT6[0])
            s_proj(wsrc, 4, xT6[0], tmpv)
            P.tt("dve", hout, hin, tmpv, ALU.add, ["sx"], ["sx"])

        def s_ple(layer, hin, hout):
            s_T(hin, xT6[0])
            s_proj(ple_wg[layer], 4, xT6[0], tmpv)
            P.act(tmpv, tmpv, AF.Sigmoid, ["sx"], ["sx"])
            P.dma("sp", miscB[:, 0:256], ps_in[layer], [], ["sx"], "sx6")
            s_T(miscB[:, 0:256], pTs, 2)
            s_proj(ple_wp[layer], 4, pTs, miscA, 2)
            P.tt("dve", tmpv, tmpv, miscA, ALU.mult, ["sx"], ["sx"])
            P.tt("dve", hout, hin, tmpv, ALU.add, ["sx"], ["sx"])

        s_res_proj(w_out, ysl, hA, hB)
        s_ple(0, hB, hA)
        s_norm(hA, norm_g[1:2, :], ysl)
        P.dma("pool", cxs[:, :], ysl, ["sx"], [], "so3")
        P.dma("sp", miscB, shift0[:, :], [], ["sx"], "sx7")
        P.tt("dve", miscB, miscB, ysl, ALU.subtract, ["sx"], ["sx"])
        for m in range(6):
            P.dma("sp", gainb, c_mu[m:m + 1, :].partition_broadcast(NS), [], ["F2"], "F2")
            P.tt("dve", miscA, miscB, gainb, ALU.mult, ["sx", "F2"], ["sx"])
            P.tt("dve", miscA, miscA, ysl, ALU.add, ["sx"], ["sx"])
            s_T(miscA, xT6[m], tok="sx")
        for (dst, tok) in ((rs_scr, "rs_scr"), (ks_scr, "ks_scr"), (vs_scr, "vs_scr"), (as_scr, "as_scr")):
            d3 = dst.rearrange("(k p) t -> p k t", p=128)
            for hf in range(2):
                P.dma("sp", d3[:, hf * 8:(hf + 1) * 8, :], zeroT[:, 0:2048].rearrange("p (k t) -> p k t", k=8), ["zeroT"], [tok],
                      "sz_" + tok)
        P.dma("sp", es_scr.rearrange("(k p) f -> p k f", p=128), zeroT[:, 0:2048].unsqueeze(1).to_broadcast([128, 2, 2048]),
              ["zeroT"], ["es_scr"], "sz_es")
        P.dma("sp", gs_scr.rearrange("(k p) t -> p k t", p=128),
              zeroT[:, 0:2048].bitcast(BF16).rearrange("p (k t) -> p k t", k=16), ["zeroT"], ["gs_scr"], "sz_gs")

        def s_proj_fm(wsrc, xT, dst_scr, tok, func=None, bf=False):
            for cb in range(4):
                W, wn = load_wblock(wsrc, [(cb * 512, 512)])
                pb, pbn = bank()
                for fb in range(4):
                    for kc in range(KC):
                        P.mm(pb[:, fb * NS:(fb + 1) * NS], W[:, kc, fb * 128:(fb + 1) * 128], xT[:, kc, :], kc == 0, kc == KC - 1,
                             ["sT", wn], [pbn])
                stg_ = (Fb[2][:, 1024:1040].bitcast(BF16)[:, 0:16] if bf else Fb[2][:, 1024:1040])
                if func is None:
                    P.copy("act", stg_, pb[:, 0:16], [pbn], ["sstg"])
                else:
                    P.act(stg_, pb[:, 0:16], func, [pbn], ["sstg"])
                for fb in range(4):
                    f0 = cb * 512 + fb * 128
                    P.dma("pool", dst_scr[f0:f0 + 128, 0:TS:64], stg_[:, fb * NS:(fb + 1) * NS], ["sstg"], [tok], "sstg",
                          allow_slow_non_contiguous=True)

        def s_lora_hidden(w1src, xT, func):
            W, wn = load_wblock(w1src, [(0, 96)])
            pb, pbn = bank()
            for kc in range(KC):
                P.mm(pb[0:96, 0:NS], W[:, kc, 0:96], xT[:, kc, :], kc == 0, kc == KC - 1, ["sT", wn], [pbn])
            if func is None:
                P.copy("act", hTs, pb[0:96, 0:NS], [pbn], ["shT"])
            else:
                P.act(hTs, pb[0:96, 0:NS], func, [pbn], ["shT"])

        def s_load_small(wsrc):
            i = wctr[0]
            wctr[0] += 1
            s_ = i % 2
            ss = (i * 4) % 2
            stv = wst[ss][:, :, :].rearrange("p a b -> p (a b)")
            P.dma("sp", stv[0:96, :], wsrc[:, :], [], [f"wst{ss}"], f"wst{ss}")
            wv = wbf[s_][:, 0:4, :].rearrange("p a b -> p (a b)")
            P.copy("dve", wv[0:96, :], stv[0:96, :], [f"wst{ss}"], [f"wbf{s_}"])
            return wv, f"wbf{s_}"

        s_proj_fm(c_wr, xT6[0], rs_scr, "rs_scr")
        s_proj_fm(c_wk, xT6[2], ks_scr, "ks_scr")
        s_proj_fm(c_wv, xT6[3], vs_scr, "vs_scr")
        s_proj_fm(c_wg, xT6[5], gs_scr, "gs_scr", AF.Silu, True)
        s_lora_hidden(c_a1, xT6[4], None)
        a2b, a2n = s_load_small(c_a2)
        pb, pbn = bank()
        for fb in range(KC):
            P.mm(pb[:, fb * NS:(fb + 1) * NS], a2b[0:96, fb * 128:(fb + 1) * 128], hTs, True, True, ["shT", a2n], [pbn])
        stg_a = Fb[2][:, 1040:1104]
        P.copy("act", stg_a, pb[:, 0:64], [pbn], ["sstg2"])
        for fb in range(KC):
            P.dma("pool", as_scr[fb * 128:(fb + 1) * 128, 0:TS:64], stg_a[:, fb * NS:(fb + 1) * NS], ["sstg2"], ["as_scr"],
                  "sstg2", allow_slow_non_contiguous=True)
        s_lora_hidden(c_w1, xT6[1], AF.Tanh)
        w2b, w2n = s_load_small(c_w2)
        P.dma("sp", miscB, c_w0[0:1, :].partition_broadcast(NS), [], ["sx"], "sx8")
        for cb in range(4):
            pb, pbn = bank()
            P.mm(pb[0:NS, :], hTs, w2b[0:96, cb * 512:(cb + 1) * 512], True, True, ["shT", w2n], [pbn])
            P.tt("dve", miscA[:, cb * 512:(cb + 1) * 512], pb[0:NS, :], miscB[:, cb * 512:(cb + 1) * 512], ALU.add, [pbn, "sx"],
                 ["sx"])
        P.act(miscA, miscA, AF.Sigmoid, ["sx"], ["sx"])
        P.dma("pool", es_scr[0:TS:64, :], miscA, ["sx"], ["es_scr"], "so4")
        P.copy("dve", lng[0:NS, :], hA[:, 0:1024], ["sx"], ["hsave"])
        P.copy("dve", lnb[0:NS, :], hA[:, 1024:2048], ["sx"], ["hsave"])
        rwkv_scan(range(32), 4, dict(r=(rs_scr, "rs_scr"), k=(ks_scr, "ks_scr"), v=(vs_scr, "vs_scr"), a=(as_scr, "as_scr"),
                                     e=(es_scr, "es_scr"), g=(gs_scr, "gs_scr"), yT=(yTs_scr, "yTs_scr")), True)
        barrier()
        ysT = xT6[0]
        for kc in range(KC):
            P.dma("sp", ysT[:, kc, :], yTs_scr[kc, :, 0:TS:64], [f"yTs_scr{k}" for k in range(KC)], ["sT"],
                  "sx9", allow_slow_non_contiguous=True)
        P.copy("dve", hA[:, 0:1024], lng[0:NS, :], ["hsave"], ["sx"])
        P.copy("dve", hA[:, 1024:2048], lnb[0:NS, :], ["hsave"], ["sx"])
        s_proj(c_wo, 4, ysT, tmpv)
        P.tt("dve", hB, hA, tmpv, ALU.add, ["sx"], ["sx"])
        s_ple(1, hB, hA)
        s_norm(hA, final_g[0:1, :], ysl)
        P.dma("pool", ys_out[:, :], ysl, ["sx"], [], "so5")

    if stage >= 6:
        rwkv_consts()
        phase_T(h_scr[1], "h_scr1", 1, True)
        rwkv_proj()
    if stage >= 7:
        nh1 = int(os.environ.get("NH1", "32"))
        rwkv_scan(range(nh1))
    if stage >= 8:
        barrier()
        load_yT()
        phase_proj_res(c_wo, h_scr[1], "h_scr1", h_scr[2], "h_scr2", None)
    if stage >= 9:
        phase_ple(1, h_scr[2], "h_scr2", h_scr[3], "h_scr3")
        final_norm(h_scr[3], "h_scr3")
    if stage == -1:
        rwkv_consts()
    if stage >= 10 or stage == -1:
        sample_path()

    if dbg:
        src, tok = {"h1": (h_scr[0], "h_scr0"), "h2": (h_scr[1], "h_scr1"), "h3": (h_scr[2], "h_scr2"),
                    "h4": (h_scr[3], "h_scr3"), "r": (r_scr, "r_scr"), "k": (k_scr, "k_scr"), "v": (v_scr, "v_scr"),
                    "a": (a_scr, "a_scr"), "e": (e_scr, "e_scr")}[dbg]
        for t in range(NT):
            s = t % 2
            P.dma("sp", Fb[s][:], src[t * 128:(t + 1) * 128, :], [tok], [f"F{s}"], f"F{s}")
            P.dma("sp", dbg_out[t * 128:(t + 1) * 128, :], Fb[s][:], [f"F{s}"], [], f"F{s}")

    P.sbuf_left = nc.sbuf_bytes_remaining
    P.finish()
    st.close()
    return nc, P


_CACHE = {}
OUT_NAMES = ["y_prompt","y_sample","a_k_prompt","a_v_prompt","a_k_sample","a_v_sample","b_v_sample","c_wkv_prompt","c_shift_prompt","c_wkv_sample","c_shift_sample"]


def make_in_maps(inputs):
    consts = host_consts()
    f = lambda a: np.ascontiguousarray(a, dtype=np.float32)
    x_prompt = f(inputs["x_prompt"])
    p_prompt = f(inputs["p_prompt"])
    shared = {
        "norm_g": f(inputs["norm_g"]),
        "rel_bias": f(inputs["rel_bias"]),
        "ab_w_in": f(inputs["ab_w_in"][0]),
        "ab_w_out": f(inputs["ab_w_out"][0]),
        "b_w_s": f(inputs["b_w_s"][0]),
        "b_b_s": f(inputs["b_b_s"][0]).reshape(1, 1024),
        "b_ln_g": f(inputs["b_ln_g"]).reshape(1, 1024),
        "b_ln_b": f(inputs["b_ln_b"]).reshape(1, 1024),
        "ple_w_proj": f(inputs["ple_w_proj"]),
        "ple_w_gate": f(inputs["ple_w_gate"]),
        "c_maskg": consts["maskg"], "c_maskn": consts["maskn"], "c_uneg": consts["uneg"],
        "c_selb": consts["selb"], "c_onesel": consts["onesel"],
        "c_mu": f(inputs["c_mu"][0]),
        "c_w_r": f(inputs["c_w_r"][0]), "c_w_k": f(inputs["c_w_k"][0]), "c_w_v": f(inputs["c_w_v"][0]),
        "c_w_g": f(inputs["c_w_g"][0]), "c_w_o": f(inputs["c_w_o"][0]),
        "c_w0": f(inputs["c_w0"]).reshape(1, D), "c_w1": f(inputs["c_w1"][0]), "c_w2": f(inputs["c_w2"][0]),
        "c_a0": f(inputs["c_a0"]).reshape(1, D), "c_a1": f(inputs["c_a1"][0]), "c_a2": f(inputs["c_a2"][0]),
        "c_k_k": f(inputs["c_k_k"]).reshape(1, D), "c_k_a": f(inputs["c_k_a"]).reshape(1, D),
        "c_r_k": f(inputs["c_r_k"]).reshape(1, D), "c_gn_g": f(inputs["c_gn_g"]).reshape(1, D),
        "c_gn_b": f(inputs["c_gn_b"]).reshape(1, D), "final_norm_g": f(inputs["final_norm_g"]).reshape(1, D),
        "c_ident": consts["ident"],
        "c_onehot": consts["onehot"],
        "c_trimask": consts["trimask"],
    }
    in_maps = []
    for c in range(NCORES):
        m = dict(shared)
        m["xp"] = x_prompt[c]
        sl = slice(c * NS, (c + 1) * NS)
        m["xs"] = f(inputs["x_sample"][sl, 0])
        m["cache_k"] = f(inputs["cache_a_k"][0, sl]).reshape(NS, 2048, 1024)
        m["cache_v"] = f(inputs["cache_a_v"][0, sl]).reshape(NS, 2048, 1024)
        m["wkv0"] = f(inputs["state_c_wkv"][0, sl])
        m["shift0"] = f(inputs["state_c_shift"][0, sl])
        m["ps"] = f(inputs["p_sample"][:, sl, 0])
        m["pp"] = np.ascontiguousarray(p_prompt[:, c])
        in_maps.append(m)
    return in_maps


def run_raw(inputs, stage=99, dbg=None):
    key = (stage, dbg)
    if key not in _CACHE:
        _CACHE[key] = build(stage, dbg)
    nc, P = _CACHE[key]
    res = run_bass_kernel_spmd(nc, make_in_maps(inputs), core_ids=list(range(NCORES)))
    return res.results


def kernel(**inputs):
    r = run_raw(inputs)
    st = lambda k, shp: np.stack([r[c][k].reshape(shp) for c in range(NCORES)])
    cat = lambda k, shp: np.concatenate([r[c][k].reshape(shp) for c in range(NCORES)], axis=0)
    y_prompt = st("yp", (S, D))
    y_sample = cat("ys", (NS, 1, D))
    akp = st("akp", (S, 8, 128))[None]
    avp = st("avp", (S, 8, 128))[None]
    aks = cat("aks", (NS, 1, 8, 128))[None]
    avs = cat("avs", (NS, 1, 8, 128))[None]
    bvs = cat("bvs", (NS, 1, 1024))[None]
    cSp = st("cSp", (32, 64, 64))[None]
    cxp = st("cxp", (D,))[None]
    cSs = cat("cSs", (NS, 32, 64, 64))[None]
    cxs = cat("cxs", (NS, D))[None]
    return (y_prompt, y_sample, akp, avp, aks, avs, bvs, cSp, cxp, cSs, cxs)
```

```python
import contextlib
import numpy as np
import concourse.bass as bass
import concourse.mybir as mybir
from concourse.bass_utils import run_bass_kernel_spmd

F32 = mybir.dt.float32
BF16 = mybir.dt.bfloat16
AF = mybir.ActivationFunctionType
ALU = mybir.AluOpType
AX = mybir.AxisListType

NCORES = 8
D = 2048
S = 2048
NT = S // 128
KC = D // 128
NS = 4
ABC = 7168
NEGB = -30000.0
import os
NORAW = bool(int(os.environ.get("NORAW", "0")))


class Buf:
    __slots__ = ("name", "lw", "rd")

    def __init__(self, name):
        self.name = name
        self.lw = None
        self.rd = []


class Op:
    __slots__ = ("eng", "fn", "deps", "is_dma", "key", "marked", "val", "waits", "raw")

    def __init__(self, eng, fn, is_dma=False, key=None):
        self.eng = eng
        self.fn = fn
        self.deps = []
        self.is_dma = is_dma
        self.key = key
        self.marked = False
        self.val = 0
        self.waits = []


class Prog:
    def __init__(self, nc, stack):
        self.nc = nc
        self.stack = stack
        self.ops = []
        self.E = {"pe": nc.tensor, "act": nc.scalar, "dve": nc.vector, "pool": nc.gpsimd, "sp": nc.sync}
        self.bufs = {}

    def buf(self, name):
        b = self.bufs.get(name)
        if b is None:
            b = Buf(name)
            self.bufs[name] = b
        return b

    def _mk(self, op, reads, writes):
        deps = []
        raw = set()
        for b in reads:
            if isinstance(b, str):
                b = self.buf(b)
            if b.lw is not None:
                deps.append(b.lw)
                raw.add(id(b.lw))
        for b in writes:
            if isinstance(b, str):
                b = self.buf(b)
            if b.lw is not None:
                deps.append(b.lw)
            deps.extend(b.rd)
        op.raw = raw
        for b in reads:
            if isinstance(b, str):
                b = self.buf(b)
            b.rd.append(op)
        for b in writes:
            if isinstance(b, str):
                b = self.buf(b)
            b.lw = op
            b.rd = []
        seen = set()
        for d in deps:
            if id(d) not in seen and d is not op:
                seen.add(id(d))
                op.deps.append(d)
        self.ops.append(op)
        return op

    def add(self, eng, fn, reads=(), writes=()):
        return self._mk(Op(eng, fn), reads, writes)

    def dma(self, eng, out, in_, reads, writes, key, **kw):
        e = self.E[eng]
        return self._mk(Op(eng, lambda: e.dma_start(out=out, in_=in_, **kw), True, key), reads, writes)

    def mm(self, out, lhsT, rhs, start, stop, reads, writes):
        pe = self.nc.tensor
        return self.add("pe", lambda: pe.matmul(out, lhsT=lhsT, rhs=rhs, start=start, stop=stop), reads, writes)

    def tr(self, out, in_, ident, reads, writes):
        pe = self.nc.tensor
        return self.add("pe", lambda: pe.transpose(out, in_, ident), reads, writes)

    def act(self, out, in_, func, reads, writes, eng="act", **kw):
        e = self.nc.scalar
        return self.add("act", lambda: e.activation(out=out, in_=in_, func=func, **kw), reads, writes)

    def copy(self, eng, out, in_, reads, writes):
        e = self.E[eng]
        if eng == "act":
            return self.add("act", lambda: e.copy(out=out, in_=in_), reads, writes)
        return self.add(eng, lambda: e.tensor_copy(out=out, in_=in_), reads, writes)

    def tt(self, eng, out, in0, in1, op, reads, writes):
        e = self.E[eng]
        return self.add(eng, lambda: e.tensor_tensor(out=out, in0=in0, in1=in1, op=op), reads, writes)

    def ts(self, eng, out, in0, s1, s2, op0, op1, reads, writes, **kw):
        e = self.E[eng]
        if op1 is None:
            return self.add(eng, lambda: e.tensor_scalar(out=out, in0=in0, scalar1=s1, scalar2=None, op0=op0, **kw),
                            reads, writes)
        return self.add(eng, lambda: e.tensor_scalar(out=out, in0=in0, scalar1=s1, scalar2=s2, op0=op0, op1=op1, **kw),
                        reads, writes)

    def stt(self, eng, out, in0, scalar, in1, op0, op1, reads, writes):
        e = self.E[eng]
        return self.add(eng, lambda: e.scalar_tensor_tensor(out=out, in0=in0, scalar=scalar, in1=in1, op0=op0, op1=op1),
                        reads, writes)

    def finish(self):
        nc = self.nc
        engs = ["pe", "act", "dve", "pool", "sp"]
        esem = {e: self.stack.enter_context(nc.semaphore("s_" + e)) for e in engs}
        dcount = {}
        keyeng = {}
        known = {e: {} for e in engs}
        dsem = {}
        pend = []
        for op in self.ops:
            w = []
            for a in op.deps:
                if a.is_dma:
                    w.append(("d", a.key, dcount[a.key]))
                else:
                    if a.eng == op.eng and not op.is_dma and (a.eng == "pe" or NORAW):
                        continue
                    a.marked = True
                    w.append(("c", a, 0))
            pend.append(w)
            if op.is_dma:
                assert keyeng.setdefault(op.key, op.eng) == op.eng, ("DMA sem shared across queues", op.key)
                dcount[op.key] = dcount.get(op.key, 0) + 16
                op.val = dcount[op.key]
        cnt = {e: 0 for e in engs}
        for op in self.ops:
            if not op.is_dma and op.marked:
                cnt[op.eng] += 1
                op.val = cnt[op.eng]
        for k in dcount:
            dsem[k] = self.stack.enter_context(nc.semaphore("d_" + k))
        nwait = 0
        for op, w in zip(self.ops, pend):
            E = self.E[op.eng]
            kn = known[op.eng]
            need = {}
            for kind, ref, val in w:
                if kind == "d":
                    sem = dsem[ref]
                    v = val
                else:
                    sem = esem[ref.eng]
                    v = ref.val
                sid = id(sem)
                if kn.get(sid, 0) >= v:
                    continue
                if sid not in need or need[sid][1] < v:
                    need[sid] = (sem, v)
            for sid, (sem, v) in need.items():
                E.wait_ge(sem, v)
                kn[sid] = v
                nwait += 1
            ins = op.fn()
            if op.is_dma:
                ins.then_inc(dsem[op.key], 16)
            elif op.marked:
                ins.then_inc(esem[op.eng], 1)
        for k, c in dcount.items():
            nc.sync.wait_ge(dsem[k], c)
        for e in engs:
            if e != "sp" and cnt[e] > 0:
                nc.sync.wait_ge(esem[e], cnt[e])
        self.stats = (len(self.ops), nwait, len(dcount))


def t5_bucket_np(dist):
    dist = np.asarray(dist, dtype=np.int64)
    n_exact = 16
    d = np.maximum(dist, 1).astype(np.float32)
    log_b = n_exact + (np.log(d / n_exact) / np.float32(np.log(2048 / n_exact)) * (32 - n_exact)).astype(np.int32)
    return np.where(dist < n_exact, dist, np.minimum(log_b, 31))


def host_consts():
    c = {}
    c["ident"] = np.eye(128, dtype=np.float32)
    oh = np.zeros((64, 3, 512), np.float32)
    for p, dil in enumerate((1, 4, 16)):
        s = np.arange(129)
        b = t5_bucket_np(s * dil)
        oh[b, p, s] = 1.0
        oh[32, p, 129:] = 1.0
    c["onehot"] = oh.reshape(64, 3 * 512)
    j = np.arange(128)[:, None]
    i = np.arange(128)[None, :]
    c["trimask"] = (i >= j).astype(np.float32)
    s_ = np.arange(64)[:, None]
    t_ = np.arange(64)[None, :]
    c["maskg"] = np.concatenate([(s_ < t_), (s_ <= t_)], axis=1).astype(np.float32)
    c["maskn"] = (t_ < s_).astype(np.float32)
    ss = np.arange(128)[:, None]
    tt = np.arange(128)[None, :]
    c["uneg"] = (-np.exp(-0.5) * ((ss <= tt) & (ss // 64 == tt // 64))).astype(np.float32)
    selb = np.zeros((4, 4, 128), np.float32)
    for b in range(4):
        selb[b, b, :] = 1.0
    c["selb"] = selb.reshape(4, 512)
    onesel = np.zeros((128, 4, 4), np.float32)
    for b in range(4):
        onesel[:, b, b] = 1.0
    c["onesel"] = onesel.reshape(128, 16)
    return c


def build(stage=99, dbg=None):
    nc = bass.Bass("TRN2", target_bir_lowering=False)
    st = contextlib.ExitStack()

    def din(name, shape):
        return nc.dram_tensor(name, list(shape), F32, kind="ExternalInput").ap()

    def dout(name, shape):
        return nc.dram_tensor(name, list(shape), F32, kind="ExternalOutput").ap()

    def dscr(name, shape, dt=F32):
        return nc.dram_tensor(name, list(shape), dt, kind="Internal").ap()

    def sb(name, shape, dt=F32):
        return st.enter_context(nc.sbuf_tensor(name, list(shape), dt))

    def ps(name, shape=(128, 512), dt=F32):
        return st.enter_context(nc.psum_tensor(name, list(shape), dt))

    xp = din("xp", (S, D))
    pp = din("pp", (2, S, 256))
    norm_g = din("norm_g", (2, D))
    rel_bias = din("rel_bias", (32, 8))
    w_in = din("ab_w_in", (D, ABC))
    w_out = din("ab_w_out", (D, D))
    b_w_s = din("b_w_s", (8, 128, 128))
    b_b_s = din("b_b_s", (1, 1024))
    b_ln_g = din("b_ln_g", (1, 1024))
    b_ln_b = din("b_ln_b", (1, 1024))
    ple_wp = din("ple_w_proj", (2, 256, D))
    ple_wg = din("ple_w_gate", (2, D, D))
    c_ident = din("c_ident", (128, 128))
    c_onehot = din("c_onehot", (64, 1536))
    c_trimask = din("c_trimask", (128, 128))
    c_selb = din("c_selb", (4, 512))
    c_onesel = din("c_onesel", (128, 16))
    xs_in = din("xs", (NS, D))
    cache_k = din("cache_k", (NS, 2048, 1024))
    cache_v = din("cache_v", (NS, 2048, 1024))
    wkv0 = din("wkv0", (NS, 32, 64, 64))
    shift0 = din("shift0", (NS, D))
    ps_in = din("ps", (2, NS, 256))
    ys_out = dout("ys", (NS, D))
    aks = dout("aks", (NS, 1024))
    avs = dout("avs", (NS, 1024))
    bvs = dout("bvs", (NS, 1024))
    cSs = dout("cSs", (NS, 32, 64, 64))
    cxs = dout("cxs", (NS, D))
    TS = 256
    rs_scr = dscr("rs_scr", (D, TS))
    ks_scr = dscr("ks_scr", (D, TS))
    vs_scr = dscr("vs_scr", (D, TS))
    as_scr = dscr("as_scr", (D, TS))
    es_scr = dscr("es_scr", (TS, D))
    gs_scr = dscr("gs_scr", (D, TS), BF16)
    yTs_scr = dscr("yTs_scr", (KC, 128, TS), BF16)
    c_maskg = din("c_maskg", (64, 128))
    c_maskn = din("c_maskn", (64, 64))
    c_uneg = din("c_uneg", (128, 128))
    c_mu = din("c_mu", (6, D))
    c_wr = din("c_w_r", (D, D))
    c_wk = din("c_w_k", (D, D))
    c_wv = din("c_w_v", (D, D))
    c_wg = din("c_w_g", (D, D))
    c_wo = din("c_w_o", (D, D))
    c_w0 = din("c_w0", (1, D))
    c_w1 = din("c_w1", (D, 96))
    c_w2 = din("c_w2", (96, D))
    c_a0 = din("c_a0", (1, D))
    c_a1 = din("c_a1", (D, 96))
    c_a2 = din("c_a2", (96, D))
    c_vecs = {n: din(n, (1, D)) for n in ("c_k_k", "c_k_a", "c_r_k", "c_gn_g", "c_gn_b")}
    final_g = din("final_norm_g", (1, D))
    r_scr = dscr("r_scr", (D, S))
    k_scr = dscr("k_scr", (D, S))
    v_scr = dscr("v_scr", (D, S))
    a_scr = dscr("a_scr", (D, S))
    e_scr = dscr("e_scr", (S, D))
    g_scr = dscr("g_scr", (D, S), BF16)
    yp_out = dout("yp", (S, D))
    cSp = dout("cSp", (32, 64, 64))
    cxp = dout("cxp", (1, D))
    akp = dout("akp", (S, 1024))
    avp = dout("avp", (S, 1024))
    dbg_out = dout("dbg", (S, D)) if dbg else None
    bias_scr = dscr("bias_scr", (3, 8, 128 * 512))
    yT_scr = dscr("yT_scr", (KC, 128, S), BF16)
    import os
    h_scr = [dscr(f"h_scr{i}", (S, D)) for i in range(int(os.environ.get("NSCR", "4")))]

    P = Prog(nc, st)

    ident_f = sb("ident_f", (128, 128))
    ident_b = sb("ident_b", (128, 128), BF16)
    ones_b = sb("ones_b", (128, 128), BF16)
    trimask = sb("trimask", (128, 128))
    xnT = sb("xnT", (128, KC, S), BF16)
    Fb = [sb(f"F{i}", (128, D)) for i in range(3)]
    Bb = [sb(f"B{i}", (128, D), BF16) for i in range(4)]
    stat = sb("stat", (128, 16))
    wst = [sb(f"wst{i}", (128, 4, 512)) for i in range(2)]
    wbf = [sb(f"wbf{i}", (128, KC, 512), BF16) for i in range(2)]
    U = sb("U", (128, 16384), BF16)
    WmT = sb("WmT", (128, 8, 128), BF16)
    bsb = sb("bsb", (128, 8, 128))
    lng = sb("lng", (128, 1024))
    lnb = sb("lnb", (128, 1024))
    wsf = sb("wsf", (128, 128))
    raug = sb("raug", (64, 8))
    pss = [ps(f"ps{i}") for i in range(8)]
    XNT = [f"xnT{t}" for t in range(NT)]

    bankctr = {}

    def bank(lo=0, hi=8):
        k = (lo, hi)
        i = lo + bankctr.get(k, 0) % (hi - lo)
        bankctr[k] = bankctr.get(k, 0) + 1
        return pss[i], f"ps{i}"

    Vp = [U[:, p * 2048:(p + 1) * 2048].rearrange("p (g e) -> p g e", e=128) for p in range(3)]
    kvst = [U[:, 6144 + i * 1024: 6144 + (i + 1) * 1024].bitcast(F32) for i in range(2)]
    BT = U[:, 8192:8192 + 1536].bitcast(F32).rearrange("p (a c) -> p a c", a=3)
    PT = [U[:, 9728 + i * 512: 9728 + (i + 1) * 512] for i in range(2)]
    vn_all = U[:, :].rearrange("p (t c) -> p t c", c=1024)
    UALL = ["Vp0", "Vp1", "Vp2", "kvst0", "kvst1", "BT", "PT0", "PT1"]

    P.dma("sp", ident_f[:], c_ident[:, :], [], ["ident_f"], "c_id")
    P.dma("sp", trimask[:], c_trimask[:, :], [], ["trimask"], "c_tri")
    P.copy("dve", ident_b[:], ident_f[:], ["ident_f"], ["ident_b"])
    P.add("dve", lambda: nc.vector.memset(ones_b[:], 1.0), [], ["ones_b"])

    oh = Fb[0][0:64, 0:1536]
    gvec = Fb[1][0:8, 0:1536].rearrange("p (a c) -> p a c", a=3)
    import os
    SK = os.environ.get("SKIP", "")
    if "a" not in SK:
        P.add("pool", lambda: nc.gpsimd.memset(raug[32:64, :], NEGB), [], ["raug"])
    if "b" not in SK:
        P.dma("sp", raug[0:32, :], rel_bias[:, :], [], ["raug"], "c_ra")
    if "c" not in SK:
        P.dma("sp", oh, c_onehot[:, :], [], ["F0"], "F0")
    BIS = int(os.environ.get("BIS", "9"))
    for p in range(3 if BIS >= 1 else 0):
        pb, pbn = bank()
        P.mm(pb[0:8, :], raug[:, :], oh[:, p * 512:(p + 1) * 512], True, True, ["raug", "F0"], [pbn])
        P.copy("act", gvec[:, p, :], pb[0:8, :], [pbn], ["F1"])
    for p in range(3 if BIS >= 2 else 0):
        dst = bias_scr[p].rearrange("h (r u) -> h r u", u=512)
        src = gvec[:, p, :].unsqueeze(1).to_broadcast([8, 128, 512])
        P.dma("sp", dst, src, ["F1"], ["bias_scr"], "gv")

    def phase_T(src, src_tok, layer, do_norm):
        if do_norm:
            g_row = norm_g[layer:layer + 1, :]
            P.dma("sp", Fb[2][:], g_row.partition_broadcast(128), [], ["F2"], "F2")
        for t in range(NT):
            s = t % 2
            xt_, xb_ = Fb[s], Bb[s]
            P.dma("sp", xt_[:], src[t * 128:(t + 1) * 128, :], [src_tok], [f"F{s}"], f"F{s}")
            if do_norm:
                P.act(Bb[2][:], xt_[:], AF.Square, [f"F{s}"], ["B2", "ssq"], accum_out=stat[:, 0:1])
                P.ts("dve", stat[:, 1:2], stat[:, 0:1], 1.0 / D, 1e-6, ALU.mult, ALU.add, ["ssq"], ["rstd0"])
                P.act(stat[:, 3:4], stat[:, 1:2], AF.Sqrt, ["rstd0"], ["rstd1"])
                P.add("dve", lambda: nc.vector.reciprocal(out=stat[:, 2:3], in_=stat[:, 3:4]), ["rstd1"], ["rstd"])
                P.stt("dve", xb_[:], xt_[:], stat[:, 2:3], Fb[2][:], ALU.mult, ALU.mult,
                      [f"F{s}", "rstd", "F2"], [f"B{s}"])
                if layer == 1 and t == NT - 1:
                    xnf = U[:, 0:4096].bitcast(F32)
                    P.stt("dve", xnf, xt_[:], stat[:, 2:3], Fb[2][:], ALU.mult, ALU.mult,
                          [f"F{s}", "rstd", "F2"], UALL + ["vn", "pT", "xnf"])
                    P.dma("pool", cxp[0:1, :], xnf[127:128, :], ["xnf"], [], "xnf")
            else:
                P.copy("dve", xb_[:], xt_[:], [f"F{s}"], [f"B{s}"])
            for q4 in range(4):
                pb, pbn = bank()
                pbv = pb[:].bitcast(BF16)
                for u in range(4):
                    kc = q4 * 4 + u
                    P.tr(pbv[:, u * 128:(u + 1) * 128], xb_[:, kc * 128:(kc + 1) * 128], ident_b[:],
                         [f"B{s}", "ident_b"], [pbn])
                eng = "act" if q4 % 2 == 0 else "dve"
                P.copy(eng, xnT[:, q4 * 4:(q4 + 1) * 4, t * 128:(t + 1) * 128],
                       pbv[:, 0:512].rearrange("p (u c) -> p u c", u=4), [pbn], [XNT[t]])

    phase_T(xp, "xp", 0, True)

    wctr = [0]

    def load_wblock(wsrc, col_groups, nk=KC):
        i = wctr[0]
        wctr[0] += 1
        s = i % 2
        name = f"wbf{s}"
        nq = (nk + 3) // 4
        for q in range(nq):
            ss = (i * 4 + q) % 2
            k4 = min(4, nk - q * 4)
            off = 0
            for (c0, ncol) in col_groups:
                src = wsrc[q * 512:q * 512 + k4 * 128, c0:c0 + ncol].rearrange("(k p) c -> p k c", p=128)
                P.dma("sp", wst[ss][:, 0:k4, off:off + ncol], src, [], [f"wst{ss}"], f"wst{ss}")
                off += ncol
            eng = "pool" if q % 2 == 0 else "dve"
            P.copy(eng, wbf[s][:, q * 4:q * 4 + k4, 0:off], wst[ss][:, 0:k4, 0:off], [f"wst{ss}"], [name])
        return wbf[s], name

    SCALE = 128 ** -0.5

    def tokset(dil, g):
        n, r = g // dil, g % dil
        start = n * 128 * dil + r
        return slice(start, start + 127 * dil + 1, dil)

    def accview(acc, dil, q4):
        if dil == 1:
            return acc[:, q4 * 512:(q4 + 1) * 512].rearrange("p (u i) -> p u i", u=4)
        if dil == 4:
            return acc[:, q4 * 512:(q4 + 1) * 512].rearrange("p (i r) -> p r i", r=4)
        return acc[:, :].rearrange("p (i r) -> p r i", r=16)[:, q4 * 4:(q4 + 1) * 4, :]

    qT, kT, gaT, yaT = Bb[0], Bb[1], Bb[2], Bb[3]
    num_acc, den_acc, tmpF = Fb[0], Fb[1], Fb[2]
    nheads = int(os.environ.get("NHEADS", "8")) if stage >= 1 else 0
    for h in range(nheads):
        W, wn = load_wblock(w_in, [(h * 128, 128), (1024 + h * 128, 128), (2048 + h * 128, 128), (3072 + h * 128, 128)])
        for p in range(3 if BIS >= 3 else 0):
            src = bass.AP(bias_scr.tensor, bias_scr[p, h].offset, [[511, 128], [1, 256]])
            P.dma("sp", BT[:, p, :], src, ["bias_scr"], ["BT"], "BT")
        for (dst, dn, c0, kind) in ((qT, "B0", 0, "q"), (kT, "B1", 128, "k"), (gaT, "B2", 384, "g")):
            if kind in os.environ.get("NOQKG", ""):
                continue
            for tb in range(4):
                pb, pbn = bank()
                for kc in range(KC):
                    P.mm(pb[:, :], W[:, kc, c0:c0 + 128], xnT[:, kc, tb * 512:(tb + 1) * 512], kc == 0, kc == KC - 1,
                         XNT[tb * 4:(tb + 1) * 4] + [wn], [pbn])
                o = dst[:, tb * 512:(tb + 1) * 512]
                if kind == "q":
                    P.act(o, pb[:, :], AF.Copy, [pbn], [dn], scale=SCALE)
                elif kind == "k":
                    P.copy("act", o, pb[:, :], [pbn], [dn])
                else:
                    P.act(o, pb[:, :], AF.Silu, [pbn], [dn])
        for t2 in range(NT // 2):
            pb, pbn = bank()
            for half in range(2):
                t = t2 * 2 + half
                for kc in range(KC):
                    P.mm(pb[:, half * 256:(half + 1) * 256], xnT[:, kc, t * 128:(t + 1) * 128], W[:, kc, 128:384],
                         kc == 0, kc == KC - 1, [XNT[t], wn], [pbn])
            ks = t2 % 2
            P.copy("act", kvst[ks], pb[:, :], [pbn], [f"kvst{ks}"])
            src = kvst[ks].rearrange("p (t two c) -> p t two c", t=2, two=2)
            P.copy("pool", Vp[0][:, t2 * 2:t2 * 2 + 2, :], src[:, :, 1, :], [f"kvst{ks}"], ["Vp0"])
            t0 = t2 * 2
            dstk = akp[t0 * 128:(t0 + 2) * 128, h * 128:(h + 1) * 128].rearrange("(t p) c -> p t c", p=128)
            dstv = avp[t0 * 128:(t0 + 2) * 128, h * 128:(h + 1) * 128].rearrange("(t p) c -> p t c", p=128)
            P.dma("pool", dstk, src[:, :, 0, :], [f"kvst{ks}"], [], f"kvst{ks}")
            P.dma("pool", dstv, src[:, :, 1, :], [f"kvst{ks}"], [], f"kvst{ks}")
        if stage < 2:
            continue
        for p, dil in ((1, 4), (2, 16)):
            for g4 in range(4):
                pb, pbn = bank()
                for u in range(4):
                    ts_ = tokset(dil, g4 * 4 + u)
                    for kc in range(KC):
                        P.mm(pb[:, u * 128:(u + 1) * 128], xnT[:, kc, ts_], W[:, kc, 256:384], kc == 0, kc == KC - 1,
                             XNT + [wn], [pbn])
                P.copy("act", Vp[p][:, g4 * 4:(g4 + 1) * 4, :], pb[:, :].rearrange("p (u c) -> p u c", u=4),
                       [pbn], [f"Vp{p}"])
        slot = 0
        for p, dil in enumerate((1, 4, 16)):
            for q4 in range(4):
                ob, obn = bank()
                db, dbn = bank()
                for half in range(2):
                    sbk, sbn = bank()
                    gs = [q4 * 4 + half * 2 + u2 for u2 in range(2)]
                    prevs = [(g // dil) > 0 for g in gs]
                    for u2, g in enumerate(gs):
                        ts_ = tokset(dil, g)
                        P.mm(sbk[:, u2 * 256:u2 * 256 + 128], kT[:, ts_], qT[:, ts_], True, True, ["B0", "B1"], [sbn])
                        if prevs[u2]:
                            tp_ = tokset(dil, g - dil)
                            P.mm(sbk[:, u2 * 256 + 128:u2 * 256 + 256], kT[:, tp_], qT[:, ts_], True, True,
                                 ["B0", "B1"], [sbn])
                    sl = slot % 2
                    slot += 1
                    tS = tmpF[:, sl * 512:(sl + 1) * 512]
                    tSn = f"tS{sl}"
                    pt = PT[sl]
                    ptn = f"PT{sl}"
                    if all(prevs):
                        P.tt("dve", tS.rearrange("p (u c) -> p u c", u=2), sbk[:, :].rearrange("p (u c) -> p u c", u=2),
                             BT[:, p, :].unsqueeze(1).to_broadcast([128, 2, 256]), ALU.add, [sbn, "BT"], [tSn, "F2"])
                        P.act(pt[:, :], tS, AF.Exp, [tSn], [ptn])
                    elif not any(prevs):
                        P.tt("dve", tS.rearrange("p (u c) -> p u c", u=2)[:, :, 0:128],
                             sbk[:, :].rearrange("p (u c) -> p u c", u=2)[:, :, 0:128],
                             BT[:, p, 0:128].unsqueeze(1).to_broadcast([128, 2, 128]), ALU.add, [sbn, "BT"], [tSn, "F2"])
                        P.act(pt[:, :].rearrange("p (u c) -> p u c", u=2)[:, :, 0:128],
                              tS.rearrange("p (u c) -> p u c", u=2)[:, :, 0:128], AF.Exp, [tSn], [ptn])
                    else:
                        for u2 in range(2):
                            w_ = 256 if prevs[u2] else 128
                            P.tt("dve", tS[:, u2 * 256:u2 * 256 + w_], sbk[:, u2 * 256:u2 * 256 + w_], BT[:, p, 0:w_],
                                 ALU.add, [sbn, "BT"], [tSn, "F2"])
                            P.act(pt[:, u2 * 256:u2 * 256 + w_], tS[:, u2 * 256:u2 * 256 + w_], AF.Exp, [tSn], [ptn])
                    for u2, g in enumerate(gs):
                        u = half * 2 + u2
                        for (ob_, obn_, lown, lprev, rd) in ((ob, obn, Vp[p][:, g, :], None, [f"Vp{p}"]),
                                                             (db, dbn, ones_b[:, :], ones_b[:, :], ["ones_b"])):
                            if lprev is None and prevs[u2]:
                                lprev = Vp[p][:, g - dil, :]
                            P.mm(ob_[:, u * 128:(u + 1) * 128], lown, pt[:, u2 * 256:u2 * 256 + 128], True, not prevs[u2],
                                 rd + [ptn], [obn_])
                            if prevs[u2]:
                                P.mm(ob_[:, u * 128:(u + 1) * 128], lprev, pt[:, u2 * 256 + 128:u2 * 256 + 256], False, True,
                                     rd + [ptn], [obn_])
                obv = ob[:, :].rearrange("p (u i) -> p u i", u=4)
                dbv = db[:, :].rearrange("p (u i) -> p u i", u=4)
                if p == 0:
                    P.copy("act", accview(num_acc, dil, q4), obv, [obn], ["F0"])
                    P.copy("act", accview(den_acc, dil, q4), dbv, [dbn], ["F1"])
                else:
                    P.tt("dve", accview(num_acc, dil, q4), obv, accview(num_acc, dil, q4), ALU.add, [obn, "F0"], ["F0"])
                    P.tt("dve", accview(den_acc, dil, q4), dbv, accview(den_acc, dil, q4), ALU.add, [dbn, "F1"], ["F1"])
        P.add("dve", lambda: nc.vector.reciprocal(out=den_acc[:], in_=den_acc[:]), ["F1"], ["F1"])
        P.tt("pool", num_acc[:], num_acc[:], den_acc[:], ALU.mult, ["F0", "F1"], ["F0"])
        P.tt("pool", yaT[:], num_acc[:], gaT[:], ALU.mult, ["F0", "B2"], ["B3"])
        P.dma("pool", yT_scr[h], yaT[:], ["B3"], [f"yT_scr{h}"], "B3")

    if stage >= 3:
        P.dma("sp", bsb[:].rearrange("p g c -> p (g c)"), b_b_s[0:1, :].partition_broadcast(128), [], ["bsb"], "c1")
        P.dma("sp", lng[:], b_ln_g[0:1, :].partition_broadcast(128), [], ["lng"], "c1")
        P.dma("sp", lnb[:], b_ln_b[0:1, :].partition_broadcast(128), [], ["lnb"], "c1")
        for g in range(8):
            P.dma("sp", wsf[:], b_w_s[g], [], ["wsf"], "wsf")
            pb, pbn = bank()
            P.tr(pb[:, 0:128], wsf[:], ident_f[:], ["wsf", "ident_f"], [pbn])
            P.tt("dve", WmT[:, g, :], pb[:, 0:128], trimask[:], ALU.mult, [pbn, "trimask"], ["WmT"])
        WA, nA = load_wblock(w_in, [(5120, 512)])
        WB, nB = load_wblock(w_in, [(5632, 512)])
        gv = Fb[0][:, 0:1024]
        for t in range(NT):
            pa, pan = bank()
            pb, pbn = bank()
            for (pq, pqn, Wq, nq) in ((pa, pan, WA, nA), (pb, pbn, WB, nB)):
                for kc in range(KC):
                    P.mm(pq[:, :], xnT[:, kc, t * 128:(t + 1) * 128], Wq[:, kc, :], kc == 0, kc == KC - 1,
                         [XNT[t], nq], [pqn])
            P.act(gv[:, 0:512], pa[:, :], AF.Gelu, [pan], ["F0", "lnsA"], accum_out=stat[:, 4:5])
            P.act(gv[:, 512:1024], pb[:, :], AF.Gelu, [pbn], ["F0", "lnsB"], accum_out=stat[:, 5:6])
            P.act(Fb[1][:, 0:1024], gv, AF.Square, ["F0"], ["F1", "lnsq"], accum_out=stat[:, 6:7])
            P.tt("dve", stat[:, 7:8], stat[:, 4:5], stat[:, 5:6], ALU.add, ["lnsA", "lnsB"], ["lnm0"])
            P.ts("dve", stat[:, 7:8], stat[:, 7:8], 1.0 / 1024, None, ALU.mult, None, ["lnm0"], ["lnm"])
            P.tt("dve", stat[:, 8:9], stat[:, 7:8], stat[:, 7:8], ALU.mult, ["lnm"], ["lnm2"])
            P.stt("dve", stat[:, 9:10], stat[:, 6:7], 1.0 / 1024, stat[:, 8:9], ALU.mult, ALU.subtract,
                  ["lnsq", "lnm2"], ["lnvar"])
            P.ts("dve", stat[:, 9:10], stat[:, 9:10], 1e-5, None, ALU.add, None, ["lnvar"], ["lnvar2"])
            P.act(stat[:, 10:11], stat[:, 9:10], AF.Sqrt, ["lnvar2"], ["lnsd"])
            P.add("dve", lambda: nc.vector.reciprocal(out=stat[:, 11:12], in_=stat[:, 10:11]), ["lnsd"], ["lnrs"])
            P.ts("dve", gv, gv, stat[:, 7:8], stat[:, 11:12], ALU.subtract, ALU.mult, ["F0", "lnm", "lnrs"], ["F0"])
            P.tt("dve", gv, gv, lng[:], ALU.mult, ["F0", "lng"], ["F0"])
            P.tt("dve", vn_all[:, t, :], gv, lnb[:], ALU.add, ["F0", "lnb"], UALL + ["vn"])
        ubT, gbT, ybT = Bb[0], Bb[1], Bb[2]
        for g2 in range(4):
            ga_, gb_ = 2 * g2, 2 * g2 + 1
            W, wn = load_wblock(w_in, [(4096 + ga_ * 128, 128), (6144 + ga_ * 128, 128),
                                       (4096 + gb_ * 128, 128), (6144 + gb_ * 128, 128)])
            for gg in range(2):
                g = 2 * g2 + gg
                for tb in range(4):
                    for (c0, dst, dn, fn) in ((gg * 256, ubT, "B0", AF.Gelu), (gg * 256 + 128, gbT, "B1", AF.Silu)):
                        pb, pbn = bank()
                        for kc in range(KC):
                            P.mm(pb[:, :], W[:, kc, c0:c0 + 128], xnT[:, kc, tb * 512:(tb + 1) * 512], kc == 0, kc == KC - 1,
                                 XNT[tb * 4:(tb + 1) * 4] + [wn], [pbn])
                        P.act(dst[:, tb * 512:(tb + 1) * 512], pb[:, :], fn, [pbn], [dn])
                    psb, psbn = bank()
                    for c4 in range(4):
                        n = tb * 4 + c4
                        P.mm(psb[:, c4 * 128:(c4 + 1) * 128], vn_all[:, n, g * 128:(g + 1) * 128], WmT[:, g, :], True, True,
                             ["vn", "WmT"], [psbn])
                    t1 = tmpF[:, (tb % 2) * 512:(tb % 2 + 1) * 512]
                    t1n = f"tS{tb % 2}"
                    P.tt("dve", t1.rearrange("p (u c) -> p u c", u=4), psb[:, :].rearrange("p (u c) -> p u c", u=4),
                         bsb[:, g, :].unsqueeze(1).to_broadcast([128, 4, 128]), ALU.add, [psbn, "bsb"], [t1n, "F2"])
                    P.tt("pool", t1, t1, ubT[:, tb * 512:(tb + 1) * 512], ALU.mult, [t1n, "B0"], [t1n])
                    P.tt("pool", ybT[:, tb * 512:(tb + 1) * 512], t1, gbT[:, tb * 512:(tb + 1) * 512], ALU.mult,
                         [t1n, "B1"], ["B2"])
                P.dma("pool", yT_scr[8 + g], ybT[:], ["B2"], [f"yT_scr{8 + g}"], "B2")

    def phase_proj_res(wsrc, src_scr, src_tok, dst_scr, dst_tok, yT_tokens):
        xres = [Fb[0][:, 0:512], Fb[0][:, 512:1024]]
        hout = [Fb[1][:, 0:512], Fb[1][:, 512:1024]]
        it = 0
        for cb in range(4):
            W, wn = load_wblock(wsrc, [(cb * 512, 512)])
            for t in range(NT):
                s = it % 2
                it += 1
                pb, pbn = bank()
                P.dma("sp", xres[s], src_scr[t * 128:(t + 1) * 128, cb * 512:(cb + 1) * 512], [src_tok], [f"xres{s}", "F0"],
                      f"xres{s}")
                for kc in range(KC):
                    P.mm(pb[:, :], xnT[:, kc, t * 128:(t + 1) * 128], W[:, kc, :], kc == 0, kc == KC - 1,
                         [XNT[t], wn], [pbn])
                P.tt("dve", hout[s], pb[:, :], xres[s], ALU.add, [pbn, f"xres{s}"], [f"hout{s}", "F1"])
                P.dma("pool", dst_scr[t * 128:(t + 1) * 128, cb * 512:(cb + 1) * 512], hout[s], [f"hout{s}"], [dst_tok],
                      f"hout{s}")

    if stage >= 4:
        for t in range(NT):
            P.dma("sp", xnT[:, :, t * 128:(t + 1) * 128],
                  yT_scr[:, :, t * 128:(t + 1) * 128].rearrange("k p c -> p k c"),
                  [f"yT_scr{k}" for k in range(KC)], [XNT[t]], f"yTl{t % 4}")
        phase_proj_res(w_out, xp, "xp", h_scr[0], "h_scr0", None)

    def phase_ple(layer, src_scr, src_tok, dst_scr, dst_tok):
        phase_T(src_scr, src_tok, layer, False)
        pT = U[:, 0:4096].rearrange("p (k c) -> p k c", k=2)
        pst = Fb[2][:, 0:256]
        pbf = Bb[3][:, 0:256]
        for t in range(NT):
            P.dma("sp", pst, pp[layer, t * 128:(t + 1) * 128, :], [], ["F2"], "F2")
            P.copy("dve", pbf, pst, ["F2"], ["B3"])
            pb, pbn = bank()
            pbv = pb[:].bitcast(BF16)
            for u in range(2):
                P.tr(pbv[:, u * 128:(u + 1) * 128], pbf[:, u * 128:(u + 1) * 128], ident_b[:], ["B3", "ident_b"], [pbn])
            P.copy("act", pT[:, :, t * 128:(t + 1) * 128], pbv[:, 0:256].rearrange("p (u c) -> p u c", u=2), [pbn],
                   UALL + ["vn", "pT"])
        xres = [Fb[0][:, 0:512], Fb[0][:, 512:1024]]
        hout = [Fb[1][:, 0:512], Fb[1][:, 512:1024]]
        sig = [Fb[0][:, 1024:1536], Fb[0][:, 1536:2048]]
        wpb = Bb[2][:, 0:1024].rearrange("p (k c) -> p k c", k=2)
        it = 0
        for cb in range(4):
            W, wn = load_wblock(ple_wg[layer], [(cb * 512, 512)])
            wps = Fb[2][:, 0:1024].rearrange("p (k c) -> p k c", k=2)
            P.dma("sp", wps, ple_wp[layer, :, cb * 512:(cb + 1) * 512].rearrange("(k p) c -> p k c", p=128), [], ["F2"], "F2")
            P.copy("dve", wpb, wps, ["F2"], ["B2"])
            for t in range(NT):
                s = it % 2
                it += 1
                pg, pgn = bank()
                pq, pqn = bank()
                P.dma("sp", xres[s], src_scr[t * 128:(t + 1) * 128, cb * 512:(cb + 1) * 512], [src_tok], [f"xres{s}", "F0"],
                      f"xres{s}")
                for kc in range(KC):
                    P.mm(pg[:, :], xnT[:, kc, t * 128:(t + 1) * 128], W[:, kc, :], kc == 0, kc == KC - 1, [XNT[t], wn], [pgn])
                for k2 in range(2):
                    P.mm(pq[:, :], pT[:, k2, t * 128:(t + 1) * 128], wpb[:, k2, :], k2 == 0, k2 == 1, ["pT", "B2"], [pqn])
                P.act(sig[s], pg[:, :], AF.Sigmoid, [pgn], [f"sig{s}", "F0"])
                P.tt("dve", sig[s], pq[:, :], sig[s], ALU.mult, [pqn, f"sig{s}"], [f"sig{s}"])
                P.tt("dve", hout[s], sig[s], xres[s], ALU.add, [f"sig{s}", f"xres{s}"], [f"hout{s}", "F1"])
                P.dma("pool", dst_scr[t * 128:(t + 1) * 128, cb * 512:(cb + 1) * 512], hout[s], [f"hout{s}"], [dst_tok],
                      f"hout{s}")

    if stage >= 5:
        phase_ple(0, h_scr[0], "h_scr0", h_scr[1], "h_scr1")


    def barrier(extra=()):
        names = list(P.bufs.keys()) + list(extra)
        P.add("pool", lambda: nc.gpsimd.memset(stat[:, 15:16], 0.0), [], names)

    rsm = sb("rsm", (128, 1088))
    mu_t = sb("mu_t", (128, 6, KC))
    R_MASKG, R_MASKN, R_ONES, R_UNEG, R_PRM, R_PC, R_XS, R_US, R_S0, R_S1, R_SP, R_ONESD = (
        0, 128, 192, 256, 384, 608, 640, 704, 768, 832, 896, 960)
    PRM_NAMES = ("c_k_k", "c_k_a", "c_r_k", "c_gn_g", "c_gn_b", "c_a0")

    def prm(name, h):
        o = R_PRM + PRM_NAMES.index(name) * 32 + h
        return rsm[0:64, o:o + 1]

    def rwkv_consts():
        P.dma("sp", rsm[0:64, R_MASKG:R_MASKG + 128], c_maskg[:, :], [], ["rsm_c"], "rc0")
        P.dma("sp", rsm[0:64, R_MASKN:R_MASKN + 64], c_maskn[:, :], [], ["rsm_c"], "rc1")
        P.dma("sp", rsm[:, R_UNEG:R_UNEG + 128], c_uneg[:, :], [], ["rsm_c"], "rc2")
        P.add("dve", lambda: nc.vector.memset(rsm[0:64, R_ONES:R_ONES + 64], 1.0), [], ["rsm_c"])
        P.add("dve", lambda: nc.vector.memset(rsm[0:64, R_ONESD:R_ONESD + 64], 1.0 / 64), [], ["rsm_c"])
        for i, nme in enumerate(PRM_NAMES):
            src_t = c_a0 if nme == "c_a0" else c_vecs[nme]
            src = bass.AP(src_t.tensor, 0, [[1, 64], [64, 32]])
            P.dma("sp", rsm[0:64, R_PRM + i * 32:R_PRM + (i + 1) * 32], src, [], ["prm"], f"rp{i}",
                  allow_slow_non_contiguous=True)
        P.dma("sp", mu_t[:], bass.AP(c_mu.tensor, 0, [[1, 128], [D, 6], [128, KC]]), [], ["mu_t"], "mu_t",
              allow_slow_non_contiguous=True)

    def rwkv_proj():
        barrier()
        xm = ([U[:, u * 2048:(u + 1) * 2048] for u in range(8)] + [Bb[i][:, :] for i in range(4)]
              + [Fb[i][:, :].bitcast(BF16)[:, a * 2048:(a + 1) * 2048] for i in range(2) for a in range(2)])
        xmn = [f"xm{k}" for k in range(KC)]
        dx = Fb[2][:, :].bitcast(BF16)[:, 0:2048]
        stg = [Fb[2][:, 1024:1536], Fb[2][:, 1536:2048]]
        hT = lng[:, :].bitcast(BF16)
        w0bc = [lnb[:, :], bsb[:, :, :].rearrange("p g c -> p (g c)")]

        def build_xm(m):
            for kc in range(KC):
                P.tt("pool", dx[:, 1:S], xnT[:, kc, 0:S - 1], xnT[:, kc, 1:S], ALU.subtract, XNT, ["dx"])
                P.ts("pool", dx[:, 0:1], xnT[:, kc, 0:1], -1.0, None, ALU.mult, None, XNT, ["dx"])
                P.stt("dve", xm[kc], dx, mu_t[:, m, kc:kc + 1], xnT[:, kc, :], ALU.mult, ALU.add,
                      ["dx", "mu_t"] + XNT, [xmn[kc]])

        itc = [0]

        def evac_store(pb, pbn, dst, dst_tok, func=None, bf=False, npart=128):
            s_ = itc[0] % 2
            itc[0] += 1
            o = stg[s_].bitcast(BF16)[0:npart, 0:512] if bf else stg[s_][0:npart, :]
            if func is not None:
                P.act(o, pb[0:npart, :], func, [pbn], [f"stg{s_}"])
            elif s_ == 0:
                P.copy("act", o, pb[0:npart, :], [pbn], [f"stg{s_}"])
            else:
                P.copy("dve", o, pb[0:npart, :], [pbn], [f"stg{s_}"])
            P.dma("act" if s_ == 0 else "pool", dst, o, [f"stg{s_}"], [dst_tok], f"stg{s_}")

        def proj_fm(wsrc, dst_scr, dst_tok, func=None, bf=False):
            for cb in range(4):
                W, wn = load_wblock(wsrc, [(cb * 512, 512)])
                for fb in range(4):
                    f0 = cb * 512 + fb * 128
                    for tb in range(4):
                        pb, pbn = bank()
                        for kc in range(KC):
                            P.mm(pb[:, :], W[:, kc, fb * 128:(fb + 1) * 128], xm[kc][:, tb * 512:(tb + 1) * 512],
                                 kc == 0, kc == KC - 1, [xmn[kc], wn], [pbn])
                        evac_store(pb, pbn, dst_scr[f0:f0 + 128, tb * 512:(tb + 1) * 512], dst_tok, func, bf)

        def load_small(wsrc):
            i = wctr[0]
            wctr[0] += 1
            s_ = i % 2
            ss = (i * 4) % 2
            stv = wst[ss][:, :, :].rearrange("p a b -> p (a b)")
            P.dma("sp", stv[0:96, :], wsrc[:, :], [], [f"wst{ss}"], f"wst{ss}")
            wv = wbf[s_][:, 0:4, :].rearrange("p a b -> p (a b)")
            P.copy("dve", wv[0:96, :], stv[0:96, :], [f"wst{ss}"], [f"wbf{s_}"])
            return wv, f"wbf{s_}"

        def lora_hidden(w1src, func):
            W, wn = load_wblock(w1src, [(0, 96)])
            for tb in range(4):
                pb, pbn = bank()
                for kc in range(KC):
                    P.mm(pb[0:96, :], W[:, kc, 0:96], xm[kc][:, tb * 512:(tb + 1) * 512], kc == 0, kc == KC - 1,
                         [xmn[kc], wn], [pbn])
                if func is None:
                    P.copy("act", hT[0:96, tb * 512:(tb + 1) * 512], pb[0:96, :], [pbn], ["hT"])
                else:
                    P.act(hT[0:96, tb * 512:(tb + 1) * 512], pb[0:96, :], func, [pbn], ["hT"])

        build_xm(0)
        proj_fm(c_wr, r_scr, "r_scr")
        build_xm(1)
        lora_hidden(c_w1, AF.Tanh)
        w2b, w2n = load_small(c_w2)
        for hf in range(2):
            P.dma("sp", w0bc[hf], c_w0[0:1, hf * 1024:(hf + 1) * 1024].partition_broadcast(128), [], [f"w0bc{hf}"],
                  f"w0bc{hf}")
        for t in range(NT):
            for cb in range(4):
                pb, pbn = bank()
                P.mm(pb[:, :], hT[0:96, t * 128:(t + 1) * 128], w2b[0:96, cb * 512:(cb + 1) * 512], True, True,
                     ["hT", w2n], [pbn])
                s_ = itc[0] % 2
                itc[0] += 1
                P.tt("dve", stg[s_], pb[:, :], w0bc[cb // 2][:, (cb % 2) * 512:(cb % 2 + 1) * 512], ALU.add,
                     [pbn, f"w0bc{cb // 2}"], [f"stg{s_}"])
                P.act(stg[s_], stg[s_], AF.Sigmoid, [f"stg{s_}"], [f"stg{s_}"])
                P.dma("act" if s_ == 0 else "pool", e_scr[t * 128:(t + 1) * 128, cb * 512:(cb + 1) * 512], stg[s_],
                      [f"stg{s_}"], ["e_scr"], f"stg{s_}")
        build_xm(2)
        proj_fm(c_wk, k_scr, "k_scr")
        build_xm(3)
        proj_fm(c_wv, v_scr, "v_scr")
        build_xm(4)
        lora_hidden(c_a1, None)
        a2b, a2n = load_small(c_a2)
        for fb in range(KC):
            for tb in range(4):
                pb, pbn = bank()
                P.mm(pb[:, :], a2b[0:96, fb * 128:(fb + 1) * 128], hT[0:96, tb * 512:(tb + 1) * 512], True, True,
                     ["hT", a2n], [pbn])
                evac_store(pb, pbn, a_scr[fb * 128:(fb + 1) * 128, tb * 512:(tb + 1) * 512], "a_scr")
        build_xm(5)
        proj_fm(c_wg, g_scr, "g_scr", AF.Silu, True)

    def rwkv_scan(heads, NCH=32, SRC=None, sample=False):
        barrier()
        T = NCH * 64
        TB = min(512, T)
        NTB = T // TB
        NTL = T // 128
        G8 = min(8, NCH)
        NG8 = NCH // G8
        NG4 = NCH // 4
        if SRC is None:
            SRC = dict(r=(r_scr, "r_scr"), k=(k_scr, "k_scr"), v=(v_scr, "v_scr"), a=(a_scr, "a_scr"),
                       e=(e_scr, "e_scr"), g=(g_scr, "g_scr"), yT=(yT_scr, "yT_scr"))
        xflat = xnT[:, :, :].rearrange("p k t -> p (k t)")
        X8 = [xflat[:, i * 4096:(i + 1) * 4096].bitcast(F32) for i in range(8)]
        U8 = [U[:, i * 4096:(i + 1) * 4096].bitcast(F32) for i in range(4)]
        W16 = [wbf[i][:, :, :].rearrange("p k c -> p (k c)").bitcast(F32) for i in range(2)]
        rF, kF, aF, vF, t1, t2, kmF, bF = [x[0:64, 0:T] for x in X8]
        rsb = U8[0][0:64, 0:T]
        Pin, Pinv = U8[0][0:64, 0:T], U8[1][0:64, 0:T]
        AR = U[:, 8192:16384].bitcast(F32)[0:64, 0:2 * T]
        BK = W16[0][0:64, 0:2 * T]
        Gbm = W16[1][0:64, 0:2 * T]
        Gkm = xflat[:, 0:8192].bitcast(F32)[0:64, 0:2 * T]
        Vt, Btok, Ktok = X8[2][0:64, 0:T], X8[4][0:64, 0:T], X8[5][0:64, 0:T]
        Nb = [X8[6][0:64, 0:T], X8[7][0:64, 0:T]]
        Tb = [U8[0][0:64, 0:T], U8[1][0:64, 0:T]]
        Q, bs, yF = Fb[0][0:64, 0:T], Fb[1][0:64, 0:T], Fb[2][0:64, 0:T]
        e2tok = Bb[0][:, :].bitcast(F32)[:, 0:NTL * 64].rearrange("p (t j) -> p t j", j=64)
        gT, ygT = Bb[1][0:64, 0:T], Bb[2][0:64, 0:T]
        AR4 = AR.rearrange("p (c a t) -> p c a t", c=NCH, a=2)
        BK4 = BK.rearrange("p (c a t) -> p c a t", c=NCH, a=2)
        c3 = lambda x: x.rearrange("p (c t) -> p c t", c=NCH)
        G3b, G3k = Gbm.rearrange("p (c t) -> p c t", c=NCH), Gkm.rearrange("p (c t) -> p c t", c=NCH)
        Sin = Bb[3][:, :].bitcast(F32)[0:64, 0:256].rearrange("p (b j) -> p b j", b=4)
        SinT = Bb[3][:, :].bitcast(F32)[0:64, 256:512].rearrange("p (b i) -> p b i", b=4)
        maskg = rsm[0:64, R_MASKG:R_MASKG + 128]
        maskn = rsm[0:64, R_MASKN:R_MASKN + 64]
        ones64 = rsm[0:64, R_ONES:R_ONES + 64]
        onesd = rsm[0:64, R_ONESD:R_ONESD + 64]
        uneg = rsm[:, R_UNEG:R_UNEG + 128]
        PCt = rsm[0:64, R_PC:R_PC + NCH]
        Xs = rsm[0:64, R_XS:R_XS + 64]
        Us = rsm[0:64, R_US:R_US + 64]
        Sb = [rsm[0:64, R_S0:R_S0 + 64], rsm[0:64, R_S1:R_S1 + 64]]
        SP = rsm[0:64, R_SP:R_SP + 64]
        idf = ident_f[0:64, 0:64]
        GKM_T = ["X8_0", "X8_1"]
        AR_T = ["U8_2", "U8_3"]
        BK_T = ["W16_0"]
        GBM_T = ["W16_1"]

        for h in heads:
            hs = slice(h * 64, (h + 1) * 64)
            P.dma("sp", rF, SRC["r"][0][hs, :], [SRC["r"][1]], ["X8_0"], "X8_0")
            P.dma("sp", kF, SRC["k"][0][hs, :], [SRC["k"][1]], ["X8_1"], "X8_1")
            P.dma("sp", aF, SRC["a"][0][hs, :], [SRC["a"][1]], ["X8_2"], "X8_2")
            P.dma("sp", vF, SRC["v"][0][hs, :], [SRC["v"][1]], ["X8_3"], "X8_3")
            P.dma("sp", e2tok, SRC["e"][0][:, hs].rearrange("(t p) j -> p t j", p=128), [SRC["e"][1]], ["B0"], "B0")
            P.dma("sp", gT, SRC["g"][0][hs, :], [SRC["g"][1]], ["B1"], "B1")
            if sample:
                P.dma("sp", Sin, wkv0[:, h].rearrange("b i j -> i b j"), [], ["B3"], "Sin")
                pb, pbn = bank()
                for b_ in range(4):
                    P.tr(pb[0:64, b_ * 64:(b_ + 1) * 64], Sin[:, b_, :], idf, ["B3", "ident_f"], [pbn])
                P.copy("act", SinT, pb[0:64, 0:256].rearrange("p (b i) -> p b i", b=4), [pbn], ["SinT"])
            P.act(aF, aF, AF.Sigmoid, ["X8_2", "prm"], ["X8_2"], bias=prm("c_a0", h))
            P.ts("dve", t1, kF, prm("c_k_k", h), None, ALU.mult, None, ["X8_1", "prm"], ["X8_4"])
            P.act(t2, t1, AF.Square, ["X8_4"], ["X8_5"])
            for tb in range(NTB):
                pb, pbn = bank()
                P.mm(pb[0:64, 0:TB], ones64, t2[:, tb * TB:(tb + 1) * TB], True, True, ["X8_5", "rsm_c"], [pbn])
                P.act(rsb[:, tb * TB:(tb + 1) * TB], pb[0:64, 0:TB], AF.Sqrt, [pbn], ["U8_0"])
            P.ts("dve", rsb, rsb, 1e-12, None, ALU.max, None, ["U8_0"], ["U8_0"])
            P.add("dve", lambda: nc.vector.reciprocal(out=rsb, in_=rsb), ["U8_0"], ["U8_0"])
            P.tt("dve", t1, t1, rsb, ALU.mult, ["X8_4", "U8_0"], ["X8_4"])
            P.ts("pool", kmF, aF, 1.0, prm("c_k_a", h), ALU.subtract, ALU.mult, ["X8_2", "prm"], ["X8_6"])
            P.stt("dve", kmF, kmF, 1.0, kF, ALU.add, ALU.mult, ["X8_6", "X8_1"], ["X8_6"])
            P.tt("pool", bF, t1, aF, ALU.mult, ["X8_4", "X8_2"], ["X8_7"])
            P.stt("dve", t2, rF, prm("c_r_k", h), kmF, ALU.mult, ALU.mult, ["X8_0", "prm", "X8_6"], ["X8_5"])
            for tb in range(NTB):
                pb, pbn = bank()
                P.mm(pb[0:64, 0:TB], ones64, t2[:, tb * TB:(tb + 1) * TB], True, True, ["X8_5", "rsm_c"], [pbn])
                P.copy("act", bs[:, tb * TB:(tb + 1) * TB], pb[0:64, 0:TB], [pbn], ["F1"])
            for t4 in range((NTL + 3) // 4):
                pb, pbn = bank()
                nu = min(4, NTL - t4 * 4)
                for u in range(nu):
                    t = t4 * 4 + u
                    P.mm(pb[0:64, u * 128:(u + 1) * 128], e2tok[:, t, :], uneg, True, True, ["B0", "rsm_c"], [pbn])
                P.act(Pin[:, t4 * 512:t4 * 512 + nu * 128], pb[0:64, 0:nu * 128], AF.Exp, [pbn], ["U8_0"])
                P.act(Pinv[:, t4 * 512:t4 * 512 + nu * 128], pb[0:64, 0:nu * 128], AF.Exp, [pbn], ["U8_1"], scale=-1.0)
            P.tt("dve", AR4[:, :, 1, :], c3(rF), c3(Pin), ALU.mult, ["X8_0", "U8_0"], AR_T)
            P.stt("dve", AR4[:, :, 0, 1:64], c3(t1)[:, :, 1:64], -1.0, c3(Pin)[:, :, 0:63], ALU.mult, ALU.mult,
                  ["X8_4", "U8_0"], AR_T)
            P.ts("dve", AR4[:, :, 0, 0:1], c3(t1)[:, :, 0:1], -1.0, None, ALU.mult, None, ["X8_4"], AR_T)
            P.tt("pool", BK4[:, :, 0, :], c3(bF), c3(Pinv), ALU.mult, ["X8_7", "U8_1"], BK_T)
            P.tt("pool", BK4[:, :, 1, :], c3(kmF), c3(Pinv), ALU.mult, ["X8_6", "U8_1"], BK_T)
            P.copy("act", PCt, c3(Pin)[:, :, 63], ["U8_0"], ["PCt"])
            for g8 in range(NG8):
                pv, pvn = bank()
                pbt, pbtn = bank()
                pkt, pktn = bank()
                for u in range(G8):
                    c = g8 * G8 + u
                    P.tr(pv[0:64, u * 64:(u + 1) * 64], vF[:, c * 64:(c + 1) * 64], idf, ["X8_3", "ident_f"], [pvn])
                    P.tr(pbt[0:64, u * 64:(u + 1) * 64], BK4[:, c, 0, :], idf, BK_T + ["ident_f"], [pbtn])
                    P.tr(pkt[0:64, u * 64:(u + 1) * 64], BK4[:, c, 1, :], idf, BK_T + ["ident_f"], [pktn])
                sl = slice(g8 * G8 * 64, (g8 + 1) * G8 * 64)
                P.copy("act", Vt[:, sl], pv[0:64, 0:G8 * 64], [pvn], ["X8_2"])
                P.copy("dve", Btok[:, sl], pbt[0:64, 0:G8 * 64], [pbtn], ["X8_4"])
                P.copy("act", Ktok[:, sl], pkt[0:64, 0:G8 * 64], [pktn], ["X8_5"])
            for g4 in range(NG4):
                pgb, pgbn = bank()
                pgk, pgkn = bank()
                for u in range(4):
                    c = g4 * 4 + u
                    arc = AR4[:, c, :, :].rearrange("p a t -> p (a t)")
                    P.mm(pgb[0:64, u * 128:(u + 1) * 128], BK4[:, c, 0, :], arc, True, True, BK_T + AR_T, [pgbn])
                    P.mm(pgk[0:64, u * 128:(u + 1) * 128], BK4[:, c, 1, :], arc, True, True, BK_T + AR_T, [pgkn])
                mg = maskg.unsqueeze(1).to_broadcast([64, 4, 128])
                P.tt("dve", G3b[:, g4 * 4:(g4 + 1) * 4, :], pgb[0:64, :].rearrange("p (u t) -> p u t", u=4), mg, ALU.mult,
                     [pgbn, "rsm_c"], GBM_T)
                P.tt("dve", G3k[:, g4 * 4:(g4 + 1) * 4, :], pgk[0:64, :].rearrange("p (u t) -> p u t", u=4), mg, ALU.mult,
                     [pgkn, "rsm_c"], GKM_T)
            if sample:
                P.copy("dve", c3(Q), idf.unsqueeze(1).to_broadcast([64, NCH, 64]), ["ident_f"], ["F0"])
            else:
                for g8 in range(NG8):
                    pn, pnn = bank()
                    for u in range(G8):
                        c = g8 * G8 + u
                        P.mm(pn[0:64, u * 64:(u + 1) * 64], AR4[:, c, 0, :], BK4[:, c, 0, :], True, True, BK_T + AR_T, [pnn])
                    P.tt("dve", c3(Nb[0])[:, g8 * G8:(g8 + 1) * G8, :], pn[0:64, 0:G8 * 64].rearrange("p (u t) -> p u t", u=G8),
                         maskn.unsqueeze(1).to_broadcast([64, G8, 64]), ALU.mult, [pnn, "rsm_c"], ["X8_6"])
                P.copy("act", c3(Tb[0]), G3b[:, :, 0:64], GBM_T, ["U8_0"])
                P.tt("pool", c3(Q), c3(Tb[0]), idf.unsqueeze(1).to_broadcast([64, NCH, 64]), ALU.add, ["U8_0", "ident_f"], ["F0"])
                NT_ = ["X8_6", "X8_7"]
                TT_ = ["U8_0", "U8_1"]
                for k in range(5):
                    a_, b_ = k % 2, (k + 1) % 2
                    for g8 in range(NG8):
                        sl3 = slice(g8 * G8, (g8 + 1) * G8)
                        if k < 4:
                            pT_, pTn = bank()
                            for u in range(G8):
                                c = g8 * G8 + u
                                P.mm(pT_[0:64, u * 64:(u + 1) * 64], c3(Nb[a_])[:, c, :], c3(Tb[a_])[:, c, :], True, True,
                                     [NT_[a_], TT_[a_]], [pTn])
                            P.copy("act", c3(Tb[b_])[:, sl3, :], pT_[0:64, 0:G8 * 64].rearrange("p (u t) -> p u t", u=G8), [pTn], [TT_[b_]])
                        pN_, pNn = bank()
                        for u in range(G8):
                            c = g8 * G8 + u
                            P.mm(pN_[0:64, u * 64:(u + 1) * 64], c3(Tb[a_])[:, c, :], c3(Nb[a_])[:, c, :], True, True,
                                 [NT_[a_], TT_[a_]], [pNn])
                        P.copy("dve", c3(Nb[b_])[:, sl3, :], pN_[0:64, 0:G8 * 64].rearrange("p (u t) -> p u t", u=G8), [pNn], [NT_[b_]])
                        pQ_, pQn = bank()
                        for u in range(G8):
                            c = g8 * G8 + u
                            P.mm(pQ_[0:64, u * 64:(u + 1) * 64], c3(Nb[b_])[:, c, :], c3(Q)[:, c, :], True, True,
                                 [NT_[b_], "F0"], [pQn])
                        P.tt("dve", c3(Q)[:, sl3, :], pQ_[0:64, 0:G8 * 64].rearrange("p (u t) -> p u t", u=G8), c3(Q)[:, sl3, :], ALU.add,
                             [pQn, "F0"], ["F0"])
            At, Wc, X2, Uv, R2 = X8[6][0:64, 0:T], X8[7][0:64, 0:T], U8[0][0:64, 0:T], U8[1][0:64, 0:T], X8[6][0:64, 0:T]
            ATm, BPC = BK[:, 0:T], BK[:, T:2 * T]
            g8v = lambda pb_: pb_[0:64, 0:G8 * 64].rearrange("p (u t) -> p u t", u=G8)
            for g8 in range(NG8):
                sl3 = slice(g8 * G8, (g8 + 1) * G8)
                pa, pan = bank()
                for u in range(G8):
                    c = g8 * G8 + u
                    P.tr(pa[0:64, u * 64:(u + 1) * 64], AR4[:, c, 0, :], idf, AR_T + ["ident_f"], [pan])
                P.copy("act", c3(At)[:, sl3, :], g8v(pa), [pan], ["X8_6"])
                pw, pwn = bank()
                for u in range(G8):
                    c = g8 * G8 + u
                    P.mm(pw[0:64, u * 64:(u + 1) * 64], c3(Q)[:, c, :], c3(At)[:, c, :], True, True, ["F0", "X8_6"], [pwn])
                P.copy("dve", c3(Wc)[:, sl3, :], g8v(pw), [pwn], ["X8_7"])
                px, pxn = bank()
                for u in range(G8):
                    c = g8 * G8 + u
                    P.mm(px[0:64, u * 64:(u + 1) * 64], G3k[:, c, 0:64], c3(Vt)[:, c, :], True, True, GKM_T + ["X8_2"], [pxn])
                P.copy("act", c3(X2)[:, sl3, :], g8v(px), [pxn], ["U8_0"])
                pu, pun = bank()
                for u in range(G8):
                    c = g8 * G8 + u
                    P.mm(pu[0:64, u * 64:(u + 1) * 64], c3(Q)[:, c, :], c3(X2)[:, c, :], True, True, ["F0", "U8_0"], [pun])
                P.copy("dve", c3(Uv)[:, sl3, :], g8v(pu), [pun], ["U8_1"])
            for g8 in range(NG8):
                sl3 = slice(g8 * G8, (g8 + 1) * G8)
                pa, pan = bank()
                pbb, pbbn = bank()
                pr, prn = bank()
                for u in range(G8):
                    c = g8 * G8 + u
                    P.mm(pa[0:64, u * 64:(u + 1) * 64], c3(Wc)[:, c, :], c3(Btok)[:, c, :], True, True, ["X8_7", "X8_4"], [pan])
                    P.mm(pbb[0:64, u * 64:(u + 1) * 64], c3(Btok)[:, c, :], c3(Uv)[:, c, :], True, False, ["X8_4", "U8_1"], [pbbn])
                    P.mm(pbb[0:64, u * 64:(u + 1) * 64], c3(Ktok)[:, c, :], c3(Vt)[:, c, :], False, True, ["X8_5", "X8_2"], [pbbn])
                    P.mm(pr[0:64, u * 64:(u + 1) * 64], c3(Wc)[:, c, :], G3b[:, c, 64:128], True, True, ["X8_7"] + GBM_T, [prn])
                P.tt("dve", c3(ATm)[:, sl3, :], g8v(pa), idf.unsqueeze(1).to_broadcast([64, G8, 64]), ALU.add,
                     [pan, "ident_f"], ["W16_0"])
                P.tt("dve", c3(BPC)[:, sl3, :], g8v(pbb), PCt[:, sl3].unsqueeze(2).to_broadcast([64, G8, 64]), ALU.mult,
                     [pbbn, "PCt"], ["W16_0"])
                P.tt("dve", c3(R2)[:, sl3, :], g8v(pr), AR4[:, sl3, 1, :], ALU.add, [prn] + AR_T, ["X8_6"])
            P.add("dve", lambda: nc.vector.memset(Sb[0], 0.0), [], ["S0"])
            Sn = ["S0", "S1"]
            pY = None
            for c in range(NCH):
                cur, nxt = Sb[c % 2], Sb[(c + 1) % 2]
                cn, nn = Sn[c % 2], Sn[(c + 1) % 2]
                if sample:
                    cur, cn = SinT[:, c, :], "SinT"
                pS, pSn = bank(0, 6)
                P.mm(pS[0:64, 0:64], c3(ATm)[:, c, :], cur, True, True, ["W16_0", cn], [pSn])
                P.stt("dve", nxt, pS[0:64, 0:64], PCt[:, c:c + 1], c3(BPC)[:, c, :], ALU.mult, ALU.add,
                      [pSn, "PCt", "W16_0"], [nn])
                if c % 8 == 0:
                    pY, pYn = bank(6, 8)
                u = c % 8
                yo = pY[0:64, u * 64:(u + 1) * 64]
                P.mm(yo, c3(Uv)[:, c, :], G3b[:, c, 64:128], True, False, GBM_T + ["U8_1"], [pYn])
                P.mm(yo, c3(Vt)[:, c, :], G3k[:, c, 64:128], False, False, GKM_T + ["X8_2"], [pYn])
                P.mm(yo, cur, c3(R2)[:, c, :], False, True, ["X8_6", cn], [pYn])
                if sample:
                    pbs, pbsn = bank(0, 6)
                    P.tr(pbs[0:64, 0:64], nxt, idf, [nn, "ident_f"], [pbsn])
                    P.copy("act", Xs, pbs[0:64, 0:64], [pbsn], ["Xs"])
                    P.dma("pool", cSs[c, h], Xs, ["Xs"], [], "Xs")
                if u == 7 or c == NCH - 1:
                    P.copy("act", yF[:, (c - u) * 64:(c + 1) * 64], pY[0:64, 0:(u + 1) * 64], [pYn], ["F2"])
            if not sample:
                pb, pbn = bank(0, 6)
                P.tr(pb[0:64, 0:64], Sb[NCH % 2], idf, [Sn[NCH % 2], "ident_f"], [pbn])
                P.copy("act", Xs, pb[0:64, 0:64], [pbn], ["Xs"])
                P.dma("pool", cSp[h], Xs, ["Xs"], [], "Xs")
            ysq, mean, e2m, tmp = X8[4][0:64, 0:T], X8[5][0:64, 0:T], X8[6][0:64, 0:T], X8[7][0:64, 0:T]
            P.act(ysq, yF, AF.Square, ["F2"], ["X8_4"])
            for tb in range(NTB):
                sl = slice(tb * TB, (tb + 1) * TB)
                pm, pmn = bank(0, 6)
                pe, pen = bank(0, 6)
                P.mm(pm[0:64, 0:TB], onesd, yF[:, sl], True, True, ["F2", "rsm_c"], [pmn])
                P.mm(pe[0:64, 0:TB], onesd, ysq[:, sl], True, True, ["X8_4", "rsm_c"], [pen])
                P.copy("act", mean[:, sl], pm[0:64, 0:TB], [pmn], ["X8_5"])
                P.copy("dve", e2m[:, sl], pe[0:64, 0:TB], [pen], ["X8_6"])
            P.tt("pool", tmp, mean, mean, ALU.mult, ["X8_5"], ["X8_7"])
            P.tt("pool", e2m, e2m, tmp, ALU.subtract, ["X8_6", "X8_7"], ["X8_6"])
            P.ts("dve", e2m, e2m, 64e-5, None, ALU.add, None, ["X8_6"], ["X8_6"])
            P.act(e2m, e2m, AF.Sqrt, ["X8_6"], ["X8_6"])
            P.add("dve", lambda: nc.vector.reciprocal(out=e2m, in_=e2m), ["X8_6"], ["X8_6"])
            P.tt("dve", yF, yF, mean, ALU.subtract, ["F2", "X8_5"], ["F2"])
            P.tt("dve", yF, yF, e2m, ALU.mult, ["F2", "X8_6"], ["F2"])
            P.ts("dve", yF, yF, prm("c_gn_g", h), prm("c_gn_b", h), ALU.mult, ALU.add, ["F2", "prm"], ["F2"])
            P.tt("pool", tmp, bs, vF, ALU.mult, ["F1", "X8_3"], ["X8_7"])
            P.tt("dve", yF, yF, tmp, ALU.add, ["F2", "X8_7"], ["F2"])
            P.tt("dve", ygT, yF, gT, ALU.mult, ["F2", "B1"], ["B2"])
            P.dma("pool", SRC["yT"][0][h // 2, (h % 2) * 64:(h % 2) * 64 + 64, :], ygT, ["B2"],
                  [f"{SRC['yT'][1]}{h // 2}"], "B2")

    def load_yT():
        for t in range(NT):
            P.dma("sp", xnT[:, :, t * 128:(t + 1) * 128],
                  yT_scr[:, :, t * 128:(t + 1) * 128].rearrange("k p c -> p k c"),
                  [f"yT_scr{k}" for k in range(KC)], [XNT[t]], f"yTl{t % 4}")

    def final_norm(src_scr, src_tok):
        P.dma("sp", Fb[2][:], final_g[0:1, :].partition_broadcast(128), [], ["F2"], "F2")
        for t in range(NT):
            s_ = t % 2
            P.dma("sp", Fb[s_][:], src_scr[t * 128:(t + 1) * 128, :], [src_tok], [f"F{s_}"], f"F{s_}")
            P.act(Bb[2][:], Fb[s_][:], AF.Square, [f"F{s_}"], ["B2", "ssq"], accum_out=stat[:, 0:1])
            P.ts("dve", stat[:, 1:2], stat[:, 0:1], 1.0 / D, 1e-6, ALU.mult, ALU.add, ["ssq"], ["rstd0"])
            P.act(stat[:, 3:4], stat[:, 1:2], AF.Sqrt, ["rstd0"], ["rstd1"])
            P.add("dve", lambda: nc.vector.reciprocal(out=stat[:, 2:3], in_=stat[:, 3:4]), ["rstd1"], ["rstd"])
            P.stt("dve", Fb[s_][:], Fb[s_][:], stat[:, 2:3], Fb[2][:], ALU.mult, ALU.mult, [f"F{s_}", "rstd", "F2"], [f"F{s_}"])
            P.dma("pool", yp_out[t * 128:(t + 1) * 128, :], Fb[s_][:], [f"F{s_}"], [], f"fo{s_}")


    def sample_path():
        barrier()
        xflat = xnT[:, :, :].rearrange("p k t -> p (k t)")

        def SV(off, n):
            return xflat[:, 2 * off:2 * (off + n)].bitcast(F32)[0:NS, :]

        zs = SV(0, ABC)
        ysl = SV(7168, D)
        hA = SV(9216, D)
        hB = SV(11264, D)
        tmpv = SV(13312, D)
        miscA = Fb[0][0:NS, :]
        miscB = Fb[1][0:NS, :]
        gainb = Fb[2][0:NS, :]
        xsb = Bb[0][0:NS, :]
        xT6 = [Bb[1][:, m * 64:(m + 1) * 64].rearrange("p (k b) -> p k b", b=NS) for m in range(6)]
        hTs = Bb[2][0:96, 0:NS]
        pTs = Bb[2][:, 64:72].rearrange("p (k b) -> p k b", b=NS)
        UF = U[:, :].bitcast(F32)
        Kg, Vg, prod, Wt = UF[:, 0:1024], UF[:, 1024:2048], UF[:, 2048:3072], UF[:, 3072:4096]
        sc = UF[:, 4096:4104]
        biasm = UF[:, 4104:4128].rearrange("p (a h) -> p a h", a=3)
        Pm = UF[:, 4128:4136]
        onesel = UF[:, 4136:4152].rearrange("p (b m) -> p b m", b=4)
        selb = UF[0:NS, 4160:4672].rearrange("p (b m) -> p b m", b=4)
        small = UF[0:NS, 4672:4800]
        zeroT = UF[:, 4800:8192]

        def s_norm(src, gain_row, dst, eps=1e-6):
            P.dma("sp", gainb, gain_row.partition_broadcast(NS), [], ["F2"], "F2")
            P.act(tmpv, src, AF.Square, ["sx"], ["stmp", "sssq"], accum_out=stat[0:NS, 0:1])
            P.ts("dve", stat[0:NS, 1:2], stat[0:NS, 0:1], 1.0 / D, eps, ALU.mult, ALU.add, ["sssq"], ["srs0"])
            P.act(stat[0:NS, 3:4], stat[0:NS, 1:2], AF.Sqrt, ["srs0"], ["srs1"])
            P.add("dve", lambda: nc.vector.reciprocal(out=stat[0:NS, 2:3], in_=stat[0:NS, 3:4]), ["srs1"], ["srs"])
            P.stt("dve", dst, src, stat[0:NS, 2:3], gainb, ALU.mult, ALU.mult, ["sx", "srs", "F2"], ["sx"])

        def s_T(src, dstT, nk=KC, tok="sx"):
            P.copy("dve", xsb[:, 0:nk * 128], src, [tok], ["B0"])
            pb, pbn = bank()
            pbv = pb[:].bitcast(BF16)
            for kc in range(nk):
                P.tr(pbv[:, kc * NS:(kc + 1) * NS], xsb[:, kc * 128:(kc + 1) * 128], ident_b[0:NS, 0:NS],
                     ["B0", "ident_b"], [pbn])
            P.copy("act", dstT, pbv[:, 0:nk * NS].rearrange("p (k b) -> p k b", b=NS), [pbn], ["sT"])

        def s_proj(wsrc, nblocks, xT, dst, nk=KC, col0=0):
            for cb in range(nblocks):
                W, wn = load_wblock(wsrc, [(col0 + cb * 512, 512)], nk)
                pb, pbn = bank()
                for kc in range(nk):
                    P.mm(pb[0:NS, :], xT[:, kc, :], W[:, kc, :], kc == 0, kc == nk - 1, ["sT", wn], [pbn])
                P.copy("act", dst[:, cb * 512:(cb + 1) * 512], pb[0:NS, :], [pbn], ["sx"])

        P.dma("sp", onesel, c_onesel[:, :].rearrange("p (b m) -> p b m", b=4), [], ["sconst"], "sc0")
        P.dma("sp", selb, c_selb[:, :].rearrange("p (b m) -> p b m", b=4), [], ["sconst"], "sc1")
        P.add("dve", lambda: nc.vector.memset(zeroT, 0.0), [], ["zeroT"])
        for p in range(3):
            src = bass.AP(bias_scr.tensor, bias_scr[p, 0].offset + 128, [[511, 128], [65536, 8], [1, 1]])
            P.dma("sp", biasm[:, p, :].unsqueeze(2), src, ["bias_scr"], ["sconst"], f"sb{p}", allow_slow_non_contiguous=True)

        xs_t = hA
        P.dma("sp", xs_t, xs_in[:, :], [], ["sx"], "sx0")
        s_norm(xs_t, norm_g[0:1, :], miscA)
        s_T(miscA, xT6[0])
        s_proj(w_in, 14, xT6[0], zs)
        P.dma("pool", aks[:, :], zs[:, 1024:2048], ["sx"], [], "so0")
        P.dma("pool", avs[:, :], zs[:, 2048:3072], ["sx"], [], "so1")
        pnumA, pnumAn = bank(5, 6)
        pnumB, pnumBn = bank(6, 7)
        pden, pdenn = bank(7, 8)
        first = True
        for b in range(NS):
            pq0, pq0n = bank(0, 5)
            pq1, pq1n = bank(0, 5)
            P.mm(pq0[:, :], selb[:, b, :], zs[:, 0:512], True, True, ["sx", "sconst"], [pq0n])
            P.mm(pq1[:, :], selb[:, b, :], zs[:, 512:1024], True, True, ["sx", "sconst"], [pq1n])
            for p, dil in enumerate((1, 4, 16)):
                r0 = 2048 - 128 * dil
                P.dma("sp", Kg, cache_k[b, r0:2048:dil, :], [], ["Kg"], "Kg")
                P.dma("sp", Vg, cache_v[b, r0:2048:dil, :], [], ["Vg"], "Vg")
                P.tt("dve", prod[:, 0:512], Kg[:, 0:512], pq0[:, :], ALU.mult, ["Kg", pq0n], ["prod"])
                P.tt("dve", prod[:, 512:1024], Kg[:, 512:1024], pq1[:, :], ALU.mult, ["Kg", pq1n], ["prod"])
                P.add("dve", lambda: nc.vector.tensor_reduce(out=sc, in_=prod.rearrange("p (h e) -> p h e", h=8),
                                                             axis=AX.X, op=ALU.add), ["prod"], ["sc"])
                P.stt("dve", sc, sc, SCALE, biasm[:, p, :], ALU.mult, ALU.add, ["sc", "sconst"], ["sc"])
                P.act(Pm, sc, AF.Exp, ["sc"], ["Pm"])
                P.tt("dve", Wt.rearrange("p (h e) -> p h e", h=8), Vg.rearrange("p (h e) -> p h e", h=8),
                     Pm.unsqueeze(2).to_broadcast([128, 8, 128]), ALU.mult, ["Vg", "Pm"], ["Wt"])
                last = (b == NS - 1 and p == 2)
                P.mm(pnumA[0:NS, :], onesel[:, b, :], Wt[:, 0:512], first, last, ["Wt", "sconst"], [pnumAn])
                P.mm(pnumB[0:NS, :], onesel[:, b, :], Wt[:, 512:1024], first, last, ["Wt", "sconst"], [pnumBn])
                P.mm(pden[0:NS, 0:8], onesel[:, b, :], Pm, first, last, ["Pm", "sconst"], [pdenn])
                first = False
        num = miscA[:, 0:1024]
        den = small[:, 0:8]
        e0 = small[:, 8:16]
        rb0 = small[:, 16:24]
        P.copy("act", num[:, 0:512], pnumA[0:NS, :], [pnumAn], ["sx"])
        P.copy("act", num[:, 512:1024], pnumB[0:NS, :], [pnumBn], ["sx"])
        P.copy("act", den, pden[0:NS, 0:8], [pdenn], ["ssm"])
        P.dma("sp", rb0, rel_bias[0:1, :].partition_broadcast(NS), [], ["ssm"], "sx1")
        qk = miscB[:, 0:1024]
        P.tt("dve", qk, zs[:, 0:1024], zs[:, 1024:2048], ALU.mult, ["sx"], ["sx"])
        P.add("dve", lambda: nc.vector.tensor_reduce(out=e0, in_=qk.rearrange("p (h e) -> p h e", h=8), axis=AX.X,
                                                     op=ALU.add), ["sx", "ssm"], ["ssm"])
        P.stt("dve", e0, e0, SCALE, rb0, ALU.mult, ALU.add, ["ssm"], ["ssm"])
        P.act(e0, e0, AF.Exp, ["ssm"], ["ssm"])
        P.ts("dve", e0, e0, 3.0, None, ALU.mult, None, ["ssm"], ["ssm"])
        P.tt("dve", den, den, e0, ALU.add, ["ssm"], ["ssm"])
        P.add("dve", lambda: nc.vector.reciprocal(out=den, in_=den), ["ssm"], ["ssm"])
        v3 = lambda x: x.rearrange("p (h e) -> p h e", h=8)
        P.tt("dve", v3(qk), v3(zs[:, 2048:3072]), e0.unsqueeze(2).to_broadcast([NS, 8, 128]), ALU.mult, ["sx", "ssm"], ["sx"])
        P.tt("dve", num, num, qk, ALU.add, ["sx"], ["sx"])
        P.tt("dve", v3(num), v3(num), den.unsqueeze(2).to_broadcast([NS, 8, 128]), ALU.mult, ["sx", "ssm"], ["sx"])
        P.act(qk, zs[:, 3072:4096], AF.Silu, ["sx"], ["sx"])
        P.tt("dve", ysl[:, 0:1024], num, qk, ALU.mult, ["sx"], ["sx"])
        gvs = miscA[:, 0:1024]
        lgs, lbs = miscB[:, 0:1024], miscB[:, 1024:2048]
        w00, b00 = small[:, 24:32], small[:, 32:40]
        P.dma("sp", lgs, b_ln_g[0:1, :].partition_broadcast(NS), [], ["sx"], "sx2")
        P.dma("sp", lbs, b_ln_b[0:1, :].partition_broadcast(NS), [], ["sx"], "sx3")
        P.dma("sp", w00, bass.AP(b_w_s.tensor, 0, [[0, NS], [16384, 8]]), [], ["ssm"], "sx4", allow_slow_non_contiguous=True)
        P.dma("sp", b00, bass.AP(b_b_s.tensor, 0, [[0, NS], [128, 8]]), [], ["ssm"], "sx5", allow_slow_non_contiguous=True)
        P.act(gvs, zs[:, 5120:6144], AF.Gelu, ["sx"], ["sx", "slA"], accum_out=stat[0:NS, 4:5])
        P.act(tmpv[:, 0:1024], gvs, AF.Square, ["sx"], ["stmp", "slq"], accum_out=stat[0:NS, 6:7])
        P.ts("dve", stat[0:NS, 7:8], stat[0:NS, 4:5], 1.0 / 1024, None, ALU.mult, None, ["slA"], ["slm"])
        P.tt("dve", stat[0:NS, 8:9], stat[0:NS, 7:8], stat[0:NS, 7:8], ALU.mult, ["slm"], ["slm2"])
        P.stt("dve", stat[0:NS, 9:10], stat[0:NS, 6:7], 1.0 / 1024, stat[0:NS, 8:9], ALU.mult, ALU.subtract,
              ["slq", "slm2"], ["slv"])
        P.ts("dve", stat[0:NS, 9:10], stat[0:NS, 9:10], 1e-5, None, ALU.add, None, ["slv"], ["slv2"])
        P.act(stat[0:NS, 10:11], stat[0:NS, 9:10], AF.Sqrt, ["slv2"], ["slsd"])
        P.add("dve", lambda: nc.vector.reciprocal(out=stat[0:NS, 11:12], in_=stat[0:NS, 10:11]), ["slsd"], ["slrs"])
        P.ts("dve", gvs, gvs, stat[0:NS, 7:8], stat[0:NS, 11:12], ALU.subtract, ALU.mult, ["sx", "slm", "slrs"], ["sx"])
        P.tt("dve", gvs, gvs, lgs, ALU.mult, ["sx"], ["sx"])
        P.tt("dve", gvs, gvs, lbs, ALU.add, ["sx"], ["sx"])
        P.dma("pool", bvs[:, :], gvs, ["sx"], [], "so2")
        P.tt("dve", v3(gvs), v3(gvs), w00.unsqueeze(2).to_broadcast([NS, 8, 128]), ALU.mult, ["sx", "ssm"], ["sx"])
        P.tt("dve", v3(gvs), v3(gvs), b00.unsqueeze(2).to_broadcast([NS, 8, 128]), ALU.add, ["sx", "ssm"], ["sx"])
        P.act(lgs, zs[:, 4096:5120], AF.Gelu, ["sx"], ["sx"])
        P.act(lbs, zs[:, 6144:7168], AF.Silu, ["sx"], ["sx"])
        P.tt("dve", gvs, gvs, lgs, ALU.mult, ["sx"], ["sx"])
        P.tt("dve", ysl[:, 1024:2048], gvs, lbs, ALU.mult, ["sx"], ["sx"])

        def s_res_proj(wsrc, y, hin, hout):
            s_T(y, xT6[0])
            s_proj(wsrc, 4, xT6[0], tmpv)
            P.tt("dve", hout, hin, tmpv, ALU.add, ["sx"], ["sx"])

        def s_ple(layer, hin, hout):
            s_T(hin, xT6[0])
            s_proj(ple_wg[layer], 4, xT6[0], tmpv)
            P.act(tmpv, tmpv, AF.Sigmoid, ["sx"], ["sx"])
            P.dma("sp", miscB[:, 0:256], ps_in[layer], [], ["sx"], "sx6")
            s_T(miscB[:, 0:256], pTs, 2)
            s_proj(ple_wp[layer], 4, pTs, miscA, 2)
            P.tt("dve", tmpv, tmpv, miscA, ALU.mult, ["sx"], ["sx"])
            P.tt("dve", hout, hin, tmpv, ALU.add, ["sx"], ["sx"])

        s_res_proj(w_out, ysl, hA, hB)
        s_ple(0, hB, hA)
        s_norm(hA, norm_g[1:2, :], ysl)
        P.dma("pool", cxs[:, :], ysl, ["sx"], [], "so3")
        P.dma("sp", miscB, shift0[:, :], [], ["sx"], "sx7")
        P.tt("dve", miscB, miscB, ysl, ALU.subtract, ["sx"], ["sx"])
        for m in range(6):
            P.dma("sp", gainb, c_mu[m:m + 1, :].partition_broadcast(NS), [], ["F2"], "F2")
            P.tt("dve", miscA, miscB, gainb, ALU.mult, ["sx", "F2"], ["sx"])
            P.tt("dve", miscA, miscA, ysl, ALU.add, ["sx"], ["sx"])
            s_T(miscA, xT6[m], tok="sx")
        for (dst, tok) in ((rs_scr, "rs_scr"), (ks_scr, "ks_scr"), (vs_scr, "vs_scr"), (as_scr, "as_scr")):
            d3 = dst.rearrange("(k p) t -> p k t", p=128)
            for hf in range(2):
                P.dma("sp", d3[:, hf * 8:(hf + 1) * 8, :], zeroT[:, 0:2048].rearrange("p (k t) -> p k t", k=8), ["zeroT"], [tok],
                      "sz_" + tok)
        P.dma("sp", es_scr.rearrange("(k p) f -> p k f", p=128), zeroT[:, 0:2048].unsqueeze(1).to_broadcast([128, 2, 2048]),
              ["zeroT"], ["es_scr"], "sz_es")
        P.dma("sp", gs_scr.rearrange("(k p) t -> p k t", p=128),
              zeroT[:, 0:2048].bitcast(BF16).rearrange("p (k t) -> p k t", k=16), ["zeroT"], ["gs_scr"], "sz_gs")

        def s_proj_fm(wsrc, xT, dst_scr, tok, func=None, bf=False):
            for cb in range(4):
                W, wn = load_wblock(wsrc, [(cb * 512, 512)])
                pb, pbn = bank()
                for fb in range(4):
                    for kc in range(KC):
                        P.mm(pb[:, fb * NS:(fb + 1) * NS], W[:, kc, fb * 128:(fb + 1) * 128], xT[:, kc, :], kc == 0, kc == KC - 1,
                             ["sT", wn], [pbn])
                stg_ = (Fb[2][:, 1024:1040].bitcast(BF16)[:, 0:16] if bf else Fb[2][:, 1024:1040])
                if func is None:
                    P.copy("act", stg_, pb[:, 0:16], [pbn], ["sstg"])
                else:
                    P.act(stg_, pb[:, 0:16], func, [pbn], ["sstg"])
                for fb in range(4):
                    f0 = cb * 512 + fb * 128
                    P.dma("pool", dst_scr[f0:f0 + 128, 0:TS:64], stg_[:, fb * NS:(fb + 1) * NS], ["sstg"], [tok], "sstg",
                          allow_slow_non_contiguous=True)

        def s_lora_hidden(w1src, xT, func):
            W, wn = load_wblock(w1src, [(0, 96)])
            pb, pbn = bank()
            for kc in range(KC):
                P.mm(pb[0:96, 0:NS], W[:, kc, 0:96], xT[:, kc, :], kc == 0, kc == KC - 1, ["sT", wn], [pbn])
            if func is None:
                P.copy("act", hTs, pb[0:96, 0:NS], [pbn], ["shT"])
            else:
                P.act(hTs, pb[0:96, 0:NS], func, [pbn], ["shT"])

        def s_load_small(wsrc):
            i = wctr[0]
            wctr[0] += 1
            s_ = i % 2
            ss = (i * 4) % 2
            stv = wst[ss][:, :, :].rearrange("p a b -> p (a b)")
            P.dma("sp", stv[0:96, :], wsrc[:, :], [], [f"wst{ss}"], f"wst{ss}")
            wv = wbf[s_][:, 0:4, :].rearrange("p a b -> p (a b)")
            P.copy("dve", wv[0:96, :], stv[0:96, :], [f"wst{ss}"], [f"wbf{s_}"])
            return wv, f"wbf{s_}"

        s_proj_fm(c_wr, xT6[0], rs_scr, "rs_scr")
        s_proj_fm(c_wk, xT6[2], ks_scr, "ks_scr")
        s_proj_fm(c_wv, xT6[3], vs_scr, "vs_scr")
        s_proj_fm(c_wg, xT6[5], gs_scr, "gs_scr", AF.Silu, True)
        s_lora_hidden(c_a1, xT6[4], None)
        a2b, a2n = s_load_small(c_a2)
        pb, pbn = bank()
        for fb in range(KC):
            P.mm(pb[:, fb * NS:(fb + 1) * NS], a2b[0:96, fb * 128:(fb + 1) * 128], hTs, True, True, ["shT", a2n], [pbn])
        stg_a = Fb[2][:, 1040:1104]
        P.copy("act", stg_a, pb[:, 0:64], [pbn], ["sstg2"])
        for fb in range(KC):
            P.dma("pool", as_scr[fb * 128:(fb + 1) * 128, 0:TS:64], stg_a[:, fb * NS:(fb + 1) * NS], ["sstg2"], ["as_scr"],
                  "sstg2", allow_slow_non_contiguous=True)
        s_lora_hidden(c_w1, xT6[1], AF.Tanh)
        w2b, w2n = s_load_small(c_w2)
        P.dma("sp", miscB, c_w0[0:1, :].partition_broadcast(NS), [], ["sx"], "sx8")
        for cb in range(4):
            pb, pbn = bank()
            P.mm(pb[0:NS, :], hTs, w2b[0:96, cb * 512:(cb + 1) * 512], True, True, ["shT", w2n], [pbn])
            P.tt("dve", miscA[:, cb * 512:(cb + 1) * 512], pb[0:NS, :], miscB[:, cb * 512:(cb + 1) * 512], ALU.add, [pbn, "sx"],
                 ["sx"])
        P.act(miscA, miscA, AF.Sigmoid, ["sx"], ["sx"])
        P.dma("pool", es_scr[0:TS:64, :], miscA, ["sx"], ["es_scr"], "so4")
        P.copy("dve", lng[0:NS, :], hA[:, 0:1024], ["sx"], ["hsave"])
        P.copy("dve", lnb[0:NS, :], hA[:, 1024:2048], ["sx"], ["hsave"])
        rwkv_scan(range(32), 4, dict(r=(rs_scr, "rs_scr"), k=(ks_scr, "ks_scr"), v=(vs_scr, "vs_scr"), a=(as_scr, "as_scr"),
                                     e=(es_scr, "es_scr"), g=(gs_scr, "gs_scr"), yT=(yTs_scr, "yTs_scr")), True)
        barrier()
        ysT = xT6[0]
        for kc in range(KC):
            P.dma("sp", ysT[:, kc, :], yTs_scr[kc, :, 0:TS:64], [f"yTs_scr{k}" for k in range(KC)], ["sT"],
                  "sx9", allow_slow_non_contiguous=True)
        P.copy("dve", hA[:, 0:1024], lng[0:NS, :], ["hsave"], ["sx"])
        P.copy("dve", hA[:, 1024:2048], lnb[0:NS, :], ["hsave"], ["sx"])
        s_proj(c_wo, 4, ysT, tmpv)
        P.tt("dve", hB, hA, tmpv, ALU.add, ["sx"], ["sx"])
        s_ple(1, hB, hA)
        s_norm(hA, final_g[0:1, :], ysl)
        P.dma("pool", ys_out[:, :], ysl, ["sx"], [], "so5")

    if stage >= 6:
        rwkv_consts()
        phase_T(h_scr[1], "h_scr1", 1, True)
        rwkv_proj()
    if stage >= 7:
        nh1 = int(os.environ.get("NH1", "32"))
        rwkv_scan(range(nh1))
    if stage >= 8:
        barrier()
        load_yT()
        phase_proj_res(c_wo, h_scr[1], "h_scr1", h_scr[2], "h_scr2", None)
    if stage >= 9:
        phase_ple(1, h_scr[2], "h_scr2", h_scr[3], "h_scr3")
        final_norm(h_scr[3], "h_scr3")
    if stage == -1:
        rwkv_consts()
    if stage >= 10 or stage == -1:
        sample_path()

    if dbg:
        src, tok = {"h1": (h_scr[0], "h_scr0"), "h2": (h_scr[1], "h_scr1"), "h3": (h_scr[2], "h_scr2"),
                    "h4": (h_scr[3], "h_scr3"), "r": (r_scr, "r_scr"), "k": (k_scr, "k_scr"), "v": (v_scr, "v_scr"),
                    "a": (a_scr, "a_scr"), "e": (e_scr, "e_scr")}[dbg]
        for t in range(NT):
            s = t % 2
            P.dma("sp", Fb[s][:], src[t * 128:(t + 1) * 128, :], [tok], [f"F{s}"], f"F{s}")
            P.dma("sp", dbg_out[t * 128:(t + 1) * 128, :], Fb[s][:], [f"F{s}"], [], f"F{s}")

    P.sbuf_left = nc.sbuf_bytes_remaining
    P.finish()
    st.close()
    return nc, P


_CACHE = {}
OUT_NAMES = ["y_prompt","y_sample","a_k_prompt","a_v_prompt","a_k_sample","a_v_sample","b_v_sample","c_wkv_prompt","c_shift_prompt","c_wkv_sample","c_shift_sample"]


def make_in_maps(inputs):
    consts = host_consts()
    f = lambda a: np.ascontiguousarray(a, dtype=np.float32)
    x_prompt = f(inputs["x_prompt"])
    p_prompt = f(inputs["p_prompt"])
    shared = {
        "norm_g": f(inputs["norm_g"]),
        "rel_bias": f(inputs["rel_bias"]),
        "ab_w_in": f(inputs["ab_w_in"][0]),
        "ab_w_out": f(inputs["ab_w_out"][0]),
        "b_w_s": f(inputs["b_w_s"][0]),
        "b_b_s": f(inputs["b_b_s"][0]).reshape(1, 1024),
        "b_ln_g": f(inputs["b_ln_g"]).reshape(1, 1024),
        "b_ln_b": f(inputs["b_ln_b"]).reshape(1, 1024),
        "ple_w_proj": f(inputs["ple_w_proj"]),
        "ple_w_gate": f(inputs["ple_w_gate"]),
        "c_maskg": consts["maskg"], "c_maskn": consts["maskn"], "c_uneg": consts["uneg"],
        "c_selb": consts["selb"], "c_onesel": consts["onesel"],
        "c_mu": f(inputs["c_mu"][0]),
        "c_w_r": f(inputs["c_w_r"][0]), "c_w_k": f(inputs["c_w_k"][0]), "c_w_v": f(inputs["c_w_v"][0]),
        "c_w_g": f(inputs["c_w_g"][0]), "c_w_o": f(inputs["c_w_o"][0]),
        "c_w0": f(inputs["c_w0"]).reshape(1, D), "c_w1": f(inputs["c_w1"][0]), "c_w2": f(inputs["c_w2"][0]),
        "c_a0": f(inputs["c_a0"]).reshape(1, D), "c_a1": f(inputs["c_a1"][0]), "c_a2": f(inputs["c_a2"][0]),
        "c_k_k": f(inputs["c_k_k"]).reshape(1, D), "c_k_a": f(inputs["c_k_a"]).reshape(1, D),
        "c_r_k": f(inputs["c_r_k"]).reshape(1, D), "c_gn_g": f(inputs["c_gn_g"]).reshape(1, D),
        "c_gn_b": f(inputs["c_gn_b"]).reshape(1, D), "final_norm_g": f(inputs["final_norm_g"]).reshape(1, D),
        "c_ident": consts["ident"],
        "c_onehot": consts["onehot"],
        "c_trimask": consts["trimask"],
    }
    in_maps = []
    for c in range(NCORES):
        m = dict(shared)
        m["xp"] = x_prompt[c]
        sl = slice(c * NS, (c + 1) * NS)
        m["xs"] = f(inputs["x_sample"][sl, 0])
        m["cache_k"] = f(inputs["cache_a_k"][0, sl]).reshape(NS, 2048, 1024)
        m["cache_v"] = f(inputs["cache_a_v"][0, sl]).reshape(NS, 2048, 1024)
        m["wkv0"] = f(inputs["state_c_wkv"][0, sl])
        m["shift0"] = f(inputs["state_c_shift"][0, sl])
        m["ps"] = f(inputs["p_sample"][:, sl, 0])
        m["pp"] = np.ascontiguousarray(p_prompt[:, c])
        in_maps.append(m)
    return in_maps


def run_raw(inputs, stage=99, dbg=None):
    key = (stage, dbg)
    if key not in _CACHE:
        _CACHE[key] = build(stage, dbg)
    nc, P = _CACHE[key]
    res = run_bass_kernel_spmd(nc, make_in_maps(inputs), core_ids=list(range(NCORES)))
    return res.results


def kernel(**inputs):
    r = run_raw(inputs)
    st = lambda k, shp: np.stack([r[c][k].reshape(shp) for c in range(NCORES)])
    cat = lambda k, shp: np.concatenate([r[c][k].reshape(shp) for c in range(NCORES)], axis=0)
    y_prompt = st("yp", (S, D))
    y_sample = cat("ys", (NS, 1, D))
    akp = st("akp", (S, 8, 128))[None]
    avp = st("avp", (S, 8, 128))[None]
    aks = cat("aks", (NS, 1, 8, 128))[None]
    avs = cat("avs", (NS, 1, 8, 128))[None]
    bvs = cat("bvs", (NS, 1, 1024))[None]
    cSp = st("cSp", (32, 64, 64))[None]
    cxp = st("cxp", (D,))[None]
    cSs = cat("cSs", (NS, 32, 64, 64))[None]
    cxs = cat("cxs", (NS, D))[None]
    return (y_prompt, y_sample, akp, avp, aks, avs, bvs, cSp, cxp, cSs, cxs)
```

```python
import contextlib
import numpy as np
import concourse.bass as bass
import concourse.mybir as mybir
from concourse.bass_utils import run_bass_kernel_spmd

F32 = mybir.dt.float32
BF16 = mybir.dt.bfloat16
AF = mybir.ActivationFunctionType
ALU = mybir.AluOpType
AX = mybir.AxisListType

NCORES = 8
D = 2048
S = 2048
NT = S // 128
KC = D // 128
NS = 4
ABC = 7168
NEGB = -30000.0
import os
NORAW = bool(int(os.environ.get("NORAW", "0")))


class Buf:
    __slots__ = ("name", "lw", "rd")

    def __init__(self, name):
        self.name = name
        self.lw = None
        self.rd = []


class Op:
    __slots__ = ("eng", "fn", "deps", "is_dma", "key", "marked", "val", "waits", "raw")

    def __init__(self, eng, fn, is_dma=False, key=None):
        self.eng = eng
        self.fn = fn
        self.deps = []
        self.is_dma = is_dma
        self.key = key
        self.marked = False
        self.val = 0
        self.waits = []


class Prog:
    def __init__(self, nc, stack):
        self.nc = nc
        self.stack = stack
        self.ops = []
        self.E = {"pe": nc.tensor, "act": nc.scalar, "dve": nc.vector, "pool": nc.gpsimd, "sp": nc.sync}
        self.bufs = {}

    def buf(self, name):
        b = self.bufs.get(name)
        if b is None:
            b = Buf(name)
            self.bufs[name] = b
        return b

    def _mk(self, op, reads, writes):
        deps = []
        raw = set()
        for b in reads:
            if isinstance(b, str):
                b = self.buf(b)
            if b.lw is not None:
                deps.append(b.lw)
                raw.add(id(b.lw))
        for b in writes:
            if isinstance(b, str):
                b = self.buf(b)
            if b.lw is not None:
                deps.append(b.lw)
            deps.extend(b.rd)
        op.raw = raw
        for b in reads:
            if isinstance(b, str):
                b = self.buf(b)
            b.rd.append(op)
        for b in writes:
            if isinstance(b, str):
                b = self.buf(b)
            b.lw = op
            b.rd = []
        seen = set()
        for d in deps:
            if id(d) not in seen and d is not op:
                seen.add(id(d))
                op.deps.append(d)
        self.ops.append(op)
        return op

    def add(self, eng, fn, reads=(), writes=()):
        return self._mk(Op(eng, fn), reads, writes)

    def dma(self, eng, out, in_, reads, writes, key, **kw):
        e = self.E[eng]
        return self._mk(Op(eng, lambda: e.dma_start(out=out, in_=in_, **kw), True, key), reads, writes)

    def mm(self, out, lhsT, rhs, start, stop, reads, writes):
        pe = self.nc.tensor
        return self.add("pe", lambda: pe.matmul(out, lhsT=lhsT, rhs=rhs, start=start, stop=stop), reads, writes)

    def tr(self, out, in_, ident, reads, writes):
        pe = self.nc.tensor
        return self.add("pe", lambda: pe.transpose(out, in_, ident), reads, writes)

    def act(self, out, in_, func, reads, writes, eng="act", **kw):
        e = self.nc.scalar
        return self.add("act", lambda: e.activation(out=out, in_=in_, func=func, **kw), reads, writes)

    def copy(self, eng, out, in_, reads, writes):
        e = self.E[eng]
        if eng == "act":
            return self.add("act", lambda: e.copy(out=out, in_=in_), reads, writes)
        return self.add(eng, lambda: e.tensor_copy(out=out, in_=in_), reads, writes)

    def tt(self, eng, out, in0, in1, op, reads, writes):
        e = self.E[eng]
        return self.add(eng, lambda: e.tensor_tensor(out=out, in0=in0, in1=in1, op=op), reads, writes)

    def ts(self, eng, out, in0, s1, s2, op0, op1, reads, writes, **kw):
        e = self.E[eng]
        if op1 is None:
            return self.add(eng, lambda: e.tensor_scalar(out=out, in0=in0, scalar1=s1, scalar2=None, op0=op0, **kw),
                            reads, writes)
        return self.add(eng, lambda: e.tensor_scalar(out=out, in0=in0, scalar1=s1, scalar2=s2, op0=op0, op1=op1, **kw),
                        reads, writes)

    def stt(self, eng, out, in0, scalar, in1, op0, op1, reads, writes):
        e = self.E[eng]
        return self.add(eng, lambda: e.scalar_tensor_tensor(out=out, in0=in0, scalar=scalar, in1=in1, op0=op0, op1=op1),
                        reads, writes)

    def finish(self):
        nc = self.nc
        engs = ["pe", "act", "dve", "pool", "sp"]
        esem = {e: self.stack.enter_context(nc.semaphore("s_" + e)) for e in engs}
        dcount = {}
        keyeng = {}
        known = {e: {} for e in engs}
        dsem = {}
        pend = []
        for op in self.ops:
            w = []
            for a in op.deps:
                if a.is_dma:
                    w.append(("d", a.key, dcount[a.key]))
                else:
                    if a.eng == op.eng and not op.is_dma and (a.eng == "pe" or NORAW):
                        continue
                    a.marked = True
                    w.append(("c", a, 0))
            pend.append(w)
            if op.is_dma:
                assert keyeng.setdefault(op.key, op.eng) == op.eng, ("DMA sem shared across queues", op.key)
                dcount[op.key] = dcount.get(op.key, 0) + 16
                op.val = dcount[op.key]
        cnt = {e: 0 for e in engs}
        for op in self.ops:
            if not op.is_dma and op.marked:
                cnt[op.eng] += 1
                op.val = cnt[op.eng]
        for k in dcount:
            dsem[k] = self.stack.enter_context(nc.semaphore("d_" + k))
        nwait = 0
        for op, w in zip(self.ops, pend):
            E = self.E[op.eng]
            kn = known[op.eng]
            need = {}
            for kind, ref, val in w:
                if kind == "d":
                    sem = dsem[ref]
                    v = val
                else:
                    sem = esem[ref.eng]
                    v = ref.val
                sid = id(sem)
                if kn.get(sid, 0) >= v:
                    continue
                if sid not in need or need[sid][1] < v:
                    need[sid] = (sem, v)
            for sid, (sem, v) in need.items():
                E.wait_ge(sem, v)
                kn[sid] = v
                nwait += 1
            ins = op.fn()
            if op.is_dma:
                ins.then_inc(dsem[op.key], 16)
            elif op.marked:
                ins.then_inc(esem[op.eng], 1)
        for k, c in dcount.items():
            nc.sync.wait_ge(dsem[k], c)
        for e in engs:
            if e != "sp" and cnt[e] > 0:
                nc.sync.wait_ge(esem[e], cnt[e])
        self.stats = (len(self.ops), nwait, len(dcount))


def t5_bucket_np(dist):
    dist = np.asarray(dist, dtype=np.int64)
    n_exact = 16
    d = np.maximum(dist, 1).astype(np.float32)
    log_b = n_exact + (np.log(d / n_exact) / np.float32(np.log(2048 / n_exact)) * (32 - n_exact)).astype(np.int32)
    return np.where(dist < n_exact, dist, np.minimum(log_b, 31))


def host_consts():
    c = {}
    c["ident"] = np.eye(128, dtype=np.float32)
    oh = np.zeros((64, 3, 512), np.float32)
    for p, dil in enumerate((1, 4, 16)):
        s = np.arange(129)
        b = t5_bucket_np(s * dil)
        oh[b, p, s] = 1.0
        oh[32, p, 129:] = 1.0
    c["onehot"] = oh.reshape(64, 3 * 512)
    j = np.arange(128)[:, None]
    i = np.arange(128)[None, :]
    c["trimask"] = (i >= j).astype(np.float32)
    s_ = np.arange(64)[:, None]
    t_ = np.arange(64)[None, :]
    c["maskg"] = np.concatenate([(s_ < t_), (s_ <= t_)], axis=1).astype(np.float32)
    c["maskn"] = (t_ < s_).astype(np.float32)
    ss = np.arange(128)[:, None]
    tt = np.arange(128)[None, :]
    c["uneg"] = (-np.exp(-0.5) * ((ss <= tt) & (ss // 64 == tt // 64))).astype(np.float32)
    selb = np.zeros((4, 4, 128), np.float32)
    for b in range(4):
        selb[b, b, :] = 1.0
    c["selb"] = selb.reshape(4, 512)
    onesel = np.zeros((128, 4, 4), np.float32)
    for b in range(4):
        onesel[:, b, b] = 1.0
    c["onesel"] = onesel.reshape(128, 16)
    return c


def build(stage=99, dbg=None):
    nc = bass.Bass("TRN2", target_bir_lowering=False)
    st = contextlib.ExitStack()

    def din(name, shape):
        return nc.dram_tensor(name, list(shape), F32, kind="ExternalInput").ap()

    def dout(name, shape):
        return nc.dram_tensor(name, list(shape), F32, kind="ExternalOutput").ap()

    def dscr(name, shape, dt=F32):
        return nc.dram_tensor(name, list(shape), dt, kind="Internal").ap()

    def sb(name, shape, dt=F32):
        return st.enter_context(nc.sbuf_tensor(name, list(shape), dt))

    def ps(name, shape=(128, 512), dt=F32):
        return st.enter_context(nc.psum_tensor(name, list(shape), dt))

    xp = din("xp", (S, D))
    pp = din("pp", (2, S, 256))
    norm_g = din("norm_g", (2, D))
    rel_bias = din("rel_bias", (32, 8))
    w_in = din("ab_w_in", (D, ABC))
    w_out = din("ab_w_out", (D, D))
    b_w_s = din("b_w_s", (8, 128, 128))
    b_b_s = din("b_b_s", (1, 1024))
    b_ln_g = din("b_ln_g", (1, 1024))
    b_ln_b = din("b_ln_b", (1, 1024))
    ple_wp = din("ple_w_proj", (2, 256, D))
    ple_wg = din("ple_w_gate", (2, D, D))
    c_ident = din("c_ident", (128, 128))
    c_onehot = din("c_onehot", (64, 1536))
    c_trimask = din("c_trimask", (128, 128))
    c_selb = din("c_selb", (4, 512))
    c_onesel = din("c_onesel", (128, 16))
    xs_in = din("xs", (NS, D))
    cache_k = din("cache_k", (NS, 2048, 1024))
    cache_v = din("cache_v", (NS, 2048, 1024))
    wkv0 = din("wkv0", (NS, 32, 64, 64))
    shift0 = din("shift0", (NS, D))
    ps_in = din("ps", (2, NS, 256))
    ys_out = dout("ys", (NS, D))
    aks = dout("aks", (NS, 1024))
    avs = dout("avs", (NS, 1024))
    bvs = dout("bvs", (NS, 1024))
    cSs = dout("cSs", (NS, 32, 64, 64))
    cxs = dout("cxs", (NS, D))
    TS = 256
    rs_scr = dscr("rs_scr", (D, TS))
    ks_scr = dscr("ks_scr", (D, TS))
    vs_scr = dscr("vs_scr", (D, TS))
    as_scr = dscr("as_scr", (D, TS))
    es_scr = dscr("es_scr", (TS, D))
    gs_scr = dscr("gs_scr", (D, TS), BF16)
    yTs_scr = dscr("yTs_scr", (KC, 128, TS), BF16)
    c_maskg = din("c_maskg", (64, 128))
    c_maskn = din("c_maskn", (64, 64))
    c_uneg = din("c_uneg", (128, 128))
    c_mu = din("c_mu", (6, D))
    c_wr = din("c_w_r", (D, D))
    c_wk = din("c_w_k", (D, D))
    c_wv = din("c_w_v", (D, D))
    c_wg = din("c_w_g", (D, D))
    c_wo = din("c_w_o", (D, D))
    c_w0 = din("c_w0", (1, D))
    c_w1 = din("c_w1", (D, 96))
    c_w2 = din("c_w2", (96, D))
    c_a0 = din("c_a0", (1, D))
    c_a1 = din("c_a1", (D, 96))
    c_a2 = din("c_a2", (96, D))
    c_vecs = {n: din(n, (1, D)) for n in ("c_k_k", "c_k_a", "c_r_k", "c_gn_g", "c_gn_b")}
    final_g = din("final_norm_g", (1, D))
    r_scr = dscr("r_scr", (D, S))
    k_scr = dscr("k_scr", (D, S))
    v_scr = dscr("v_scr", (D, S))
    a_scr = dscr("a_scr", (D, S))
    e_scr = dscr("e_scr", (S, D))
    g_scr = dscr("g_scr", (D, S), BF16)
    yp_out = dout("yp", (S, D))
    cSp = dout("cSp", (32, 64, 64))
    cxp = dout("cxp", (1, D))
    akp = dout("akp", (S, 1024))
    avp = dout("avp", (S, 1024))
    dbg_out = dout("dbg", (S, D)) if dbg else None
    bias_scr = dscr("bias_scr", (3, 8, 128 * 512))
    yT_scr = dscr("yT_scr", (KC, 128, S), BF16)
    import os
    h_scr = [dscr(f"h_scr{i}", (S, D)) for i in range(int(os.environ.get("NSCR", "4")))]

    P = Prog(nc, st)

    ident_f = sb("ident_f", (128, 128))
    ident_b = sb("ident_b", (128, 128), BF16)
    ones_b = sb("ones_b", (128, 128), BF16)
    trimask = sb("trimask", (128, 128))
    xnT = sb("xnT", (128, KC, S), BF16)
    Fb = [sb(f"F{i}", (128, D)) for i in range(3)]
    Bb = [sb(f"B{i}", (128, D), BF16) for i in range(4)]
    stat = sb("stat", (128, 16))
    wst = [sb(f"wst{i}", (128, 4, 512)) for i in range(2)]
    wbf = [sb(f"wbf{i}", (128, KC, 512), BF16) for i in range(2)]
    U = sb("U", (128, 16384), BF16)
    WmT = sb("WmT", (128, 8, 128), BF16)
    bsb = sb("bsb", (128, 8, 128))
    lng = sb("lng", (128, 1024))
    lnb = sb("lnb", (128, 1024))
    wsf = sb("wsf", (128, 128))
    raug = sb("raug", (64, 8))
    pss = [ps(f"ps{i}") for i in range(8)]
    XNT = [f"xnT{t}" for t in range(NT)]

    bankctr = {}

    def bank(lo=0, hi=8):
        k = (lo, hi)
        i = lo + bankctr.get(k, 0) % (hi - lo)
        bankctr[k] = bankctr.get(k, 0) + 1
        return pss[i], f"ps{i}"

    Vp = [U[:, p * 2048:(p + 1) * 2048].rearrange("p (g e) -> p g e", e=128) for p in range(3)]
    kvst = [U[:, 6144 + i * 1024: 6144 + (i + 1) * 1024].bitcast(F32) for i in range(2)]
    BT = U[:, 8192:8192 + 1536].bitcast(F32).rearrange("p (a c) -> p a c", a=3)
    PT = [U[:, 9728 + i * 512: 9728 + (i + 1) * 512] for i in range(2)]
    vn_all = U[:, :].rearrange("p (t c) -> p t c", c=1024)
    UALL = ["Vp0", "Vp1", "Vp2", "kvst0", "kvst1", "BT", "PT0", "PT1"]

    P.dma("sp", ident_f[:], c_ident[:, :], [], ["ident_f"], "c_id")
    P.dma("sp", trimask[:], c_trimask[:, :], [], ["trimask"], "c_tri")
    P.copy("dve", ident_b[:], ident_f[:], ["ident_f"], ["ident_b"])
    P.add("dve", lambda: nc.vector.memset(ones_b[:], 1.0), [], ["ones_b"])

    oh = Fb[0][0:64, 0:1536]
    gvec = Fb[1][0:8, 0:1536].rearrange("p (a c) -> p a c", a=3)
    import os
    SK = os.environ.get("SKIP", "")
    if "a" not in SK:
        P.add("pool", lambda: nc.gpsimd.memset(raug[32:64, :], NEGB), [], ["raug"])
    if "b" not in SK:
        P.dma("sp", raug[0:32, :], rel_bias[:, :], [], ["raug"], "c_ra")
    if "c" not in SK:
        P.dma("sp", oh, c_onehot[:, :], [], ["F0"], "F0")
    BIS = int(os.environ.get("BIS", "9"))
    for p in range(3 if BIS >= 1 else 0):
        pb, pbn = bank()
        P.mm(pb[0:8, :], raug[:, :], oh[:, p * 512:(p + 1) * 512], True, True, ["raug", "F0"], [pbn])
        P.copy("act", gvec[:, p, :], pb[0:8, :], [pbn], ["F1"])
    for p in range(3 if BIS >= 2 else 0):
        dst = bias_scr[p].rearrange("h (r u) -> h r u", u=512)
        src = gvec[:, p, :].unsqueeze(1).to_broadcast([8, 128, 512])
        P.dma("sp", dst, src, ["F1"], ["bias_scr"], "gv")

    def phase_T(src, src_tok, layer, do_norm):
        if do_norm:
            g_row = norm_g[layer:layer + 1, :]
            P.dma("sp", Fb[2][:], g_row.partition_broadcast(128), [], ["F2"], "F2")
        for t in range(NT):
            s = t % 2
            xt_, xb_ = Fb[s], Bb[s]
            P.dma("sp", xt_[:], src[t * 128:(t + 1) * 128, :], [src_tok], [f"F{s}"], f"F{s}")
            if do_norm:
                P.act(Bb[2][:], xt_[:], AF.Square, [f"F{s}"], ["B2", "ssq"], accum_out=stat[:, 0:1])
                P.ts("dve", stat[:, 1:2], stat[:, 0:1], 1.0 / D, 1e-6, ALU.mult, ALU.add, ["ssq"], ["rstd0"])
                P.act(stat[:, 3:4], stat[:, 1:2], AF.Sqrt, ["rstd0"], ["rstd1"])
                P.add("dve", lambda: nc.vector.reciprocal(out=stat[:, 2:3], in_=stat[:, 3:4]), ["rstd1"], ["rstd"])
                P.stt("dve", xb_[:], xt_[:], stat[:, 2:3], Fb[2][:], ALU.mult, ALU.mult,
                      [f"F{s}", "rstd", "F2"], [f"B{s}"])
                if layer == 1 and t == NT - 1:
                    xnf = U[:, 0:4096].bitcast(F32)
                    P.stt("dve", xnf, xt_[:], stat[:, 2:3], Fb[2][:], ALU.mult, ALU.mult,
                          [f"F{s}", "rstd", "F2"], UALL + ["vn", "pT", "xnf"])
                    P.dma("pool", cxp[0:1, :], xnf[127:128, :], ["xnf"], [], "xnf")
            else:
                P.copy("dve", xb_[:], xt_[:], [f"F{s}"], [f"B{s}"])
            for q4 in range(4):
                pb, pbn = bank()
                pbv = pb[:].bitcast(BF16)
                for u in range(4):
                    kc = q4 * 4 + u
                    P.tr(pbv[:, u * 128:(u + 1) * 128], xb_[:, kc * 128:(kc + 1) * 128], ident_b[:],
                         [f"B{s}", "ident_b"], [pbn])
                eng = "act" if q4 % 2 == 0 else "dve"
                P.copy(eng, xnT[:, q4 * 4:(q4 + 1) * 4, t * 128:(t + 1) * 128],
                       pbv[:, 0:512].rearrange("p (u c) -> p u c", u=4), [pbn], [XNT[t]])

    phase_T(xp, "xp", 0, True)

    wctr = [0]

    def load_wblock(wsrc, col_groups, nk=KC):
        i = wctr[0]
        wctr[0] += 1
        s = i % 2
        name = f"wbf{s}"
        nq = (nk + 3) // 4
        for q in range(nq):
            ss = (i * 4 + q) % 2
            k4 = min(4, nk - q * 4)
            off = 0
            for (c0, ncol) in col_groups:
                src = wsrc[q * 512:q * 512 + k4 * 128, c0:c0 + ncol].rearrange("(k p) c -> p k c", p=128)
                P.dma("sp", wst[ss][:, 0:k4, off:off + ncol], src, [], [f"wst{ss}"], f"wst{ss}")
                off += ncol
            eng = "pool" if q % 2 == 0 else "dve"
            P.copy(eng, wbf[s][:, q * 4:q * 4 + k4, 0:off], wst[ss][:, 0:k4, 0:off], [f"wst{ss}"], [name])
        return wbf[s], name

    SCALE = 128 ** -0.5

    def tokset(dil, g):
        n, r = g // dil, g % dil
        start = n * 128 * dil + r
        return slice(start, start + 127 * dil + 1, dil)

    def accview(acc, dil, q4):
        if dil == 1:
            return acc[:, q4 * 512:(q4 + 1) * 512].rearrange("p (u i) -> p u i", u=4)
        if dil == 4:
            return acc[:, q4 * 512:(q4 + 1) * 512].rearrange("p (i r) -> p r i", r=4)
        return acc[:, :].rearrange("p (i r) -> p r i", r=16)[:, q4 * 4:(q4 + 1) * 4, :]

    qT, kT, gaT, yaT = Bb[0], Bb[1], Bb[2], Bb[3]
    num_acc, den_acc, tmpF = Fb[0], Fb[1], Fb[2]
    nheads = int(os.environ.get("NHEADS", "8")) if stage >= 1 else 0
    for h in range(nheads):
        W, wn = load_wblock(w_in, [(h * 128, 128), (1024 + h * 128, 128), (2048 + h * 128, 128), (3072 + h * 128, 128)])
        for p in range(3 if BIS >= 3 else 0):
            src = bass.AP(bias_scr.tensor, bias_scr[p, h].offset, [[511, 128], [1, 256]])
            P.dma("sp", BT[:, p, :], src, ["bias_scr"], ["BT"], "BT")
        for (dst, dn, c0, kind) in ((qT, "B0", 0, "q"), (kT, "B1", 128, "k"), (gaT, "B2", 384, "g")):
            if kind in os.environ.get("NOQKG", ""):
                continue
            for tb in range(4):
                pb, pbn = bank()
                for kc in range(KC):
                    P.mm(pb[:, :], W[:, kc, c0:c0 + 128], xnT[:, kc, tb * 512:(tb + 1) * 512], kc == 0, kc == KC - 1,
                         XNT[tb * 4:(tb + 1) * 4] + [wn], [pbn])
                o = dst[:, tb * 512:(tb + 1) * 512]
                if kind == "q":
                    P.act(o, pb[:, :], AF.Copy, [pbn], [dn], scale=SCALE)
                elif kind == "k":
                    P.copy("act", o, pb[:, :], [pbn], [dn])
                else:
                    P.act(o, pb[:, :], AF.Silu, [pbn], [dn])
        for t2 in range(NT // 2):
            pb, pbn = bank()
            for half in range(2):
                t = t2 * 2 + half
                for kc in range(KC):
                    P.mm(pb[:, half * 256:(half + 1) * 256], xnT[:, kc, t * 128:(t + 1) * 128], W[:, kc, 128:384],
                         kc == 0, kc == KC - 1, [XNT[t], wn], [pbn])
            ks = t2 % 2
            P.copy("act", kvst[ks], pb[:, :], [pbn], [f"kvst{ks}"])
            src = kvst[ks].rearrange("p (t two c) -> p t two c", t=2, two=2)
            P.copy("pool", Vp[0][:, t2 * 2:t2 * 2 + 2, :], src[:, :, 1, :], [f"kvst{ks}"], ["Vp0"])
            t0 = t2 * 2
            dstk = akp[t0 * 128:(t0 + 2) * 128, h * 128:(h + 1) * 128].rearrange("(t p) c -> p t c", p=128)
            dstv = avp[t0 * 128:(t0 + 2) * 128, h * 128:(h + 1) * 128].rearrange("(t p) c -> p t c", p=128)
            P.dma("pool", dstk, src[:, :, 0, :], [f"kvst{ks}"], [], f"kvst{ks}")
            P.dma("pool", dstv, src[:, :, 1, :], [f"kvst{ks}"], [], f"kvst{ks}")
        if stage < 2:
            continue
        for p, dil in ((1, 4), (2, 16)):
            for g4 in range(4):
                pb, pbn = bank()
                for u in range(4):
                    ts_ = tokset(dil, g4 * 4 + u)
                    for kc in range(KC):
                        P.mm(pb[:, u * 128:(u + 1) * 128], xnT[:, kc, ts_], W[:, kc, 256:384], kc == 0, kc == KC - 1,
                             XNT + [wn], [pbn])
                P.copy("act", Vp[p][:, g4 * 4:(g4 + 1) * 4, :], pb[:, :].rearrange("p (u c) -> p u c", u=4),
                       [pbn], [f"Vp{p}"])
        slot = 0
        for p, dil in enumerate((1, 4, 16)):
            for q4 in range(4):
                ob, obn = bank()
                db, dbn = bank()
                for half in range(2):
                    sbk, sbn = bank()
                    gs = [q4 * 4 + half * 2 + u2 for u2 in range(2)]
                    prevs = [(g // dil) > 0 for g in gs]
                    for u2, g in enumerate(gs):
                        ts_ = tokset(dil, g)
                        P.mm(sbk[:, u2 * 256:u2 * 256 + 128], kT[:, ts_], qT[:, ts_], True, True, ["B0", "B1"], [sbn])
                        if prevs[u2]:
                            tp_ = tokset(dil, g - dil)
                            P.mm(sbk[:, u2 * 256 + 128:u2 * 256 + 256], kT[:, tp_], qT[:, ts_], True, True,
                                 ["B0", "B1"], [sbn])
                    sl = slot % 2
                    slot += 1
                    tS = tmpF[:, sl * 512:(sl + 1) * 512]
                    tSn = f"tS{sl}"
                    pt = PT[sl]
                    ptn = f"PT{sl}"
                    if all(prevs):
                        P.tt("dve", tS.rearrange("p (u c) -> p u c", u=2), sbk[:, :].rearrange("p (u c) -> p u c", u=2),
                             BT[:, p, :].unsqueeze(1).to_broadcast([128, 2, 256]), ALU.add, [sbn, "BT"], [tSn, "F2"])
                        P.act(pt[:, :], tS, AF.Exp, [tSn], [ptn])
                    elif not any(prevs):
                        P.tt("dve", tS.rearrange("p (u c) -> p u c", u=2)[:, :, 0:128],
                             sbk[:, :].rearrange("p (u c) -> p u c", u=2)[:, :, 0:128],
                             BT[:, p, 0:128].unsqueeze(1).to_broadcast([128, 2, 128]), ALU.add, [sbn, "BT"], [tSn, "F2"])
                        P.act(pt[:, :].rearrange("p (u c) -> p u c", u=2)[:, :, 0:128],
                              tS.rearrange("p (u c) -> p u c", u=2)[:, :, 0:128], AF.Exp, [tSn], [ptn])
                    else:
                        for u2 in range(2):
                            w_ = 256 if prevs[u2] else 128
                            P.tt("dve", tS[:, u2 * 256:u2 * 256 + w_], sbk[:, u2 * 256:u2 * 256 + w_], BT[:, p, 0:w_],
                                 ALU.add, [sbn, "BT"], [tSn, "F2"])
                            P.act(pt[:, u2 * 256:u2 * 256 + w_], tS[:, u2 * 256:u2 * 256 + w_], AF.Exp, [tSn], [ptn])
                    for u2, g in enumerate(gs):
                        u = half * 2 + u2
                        for (ob_, obn_, lown, lprev, rd) in ((ob, obn, Vp[p][:, g, :], None, [f"Vp{p}"]),
                                                             (db, dbn, ones_b[:, :], ones_b[:, :], ["ones_b"])):
                            if lprev is None and prevs[u2]:
                                lprev = Vp[p][:, g - dil, :]
                            P.mm(ob_[:, u * 128:(u + 1) * 128], lown, pt[:, u2 * 256:u2 * 256 + 128], True, not prevs[u2],
                                 rd + [ptn], [obn_])
                            if prevs[u2]:
                                P.mm(ob_[:, u * 128:(u + 1) * 128], lprev, pt[:, u2 * 256 + 128:u2 * 256 + 256], False, True,
                                     rd + [ptn], [obn_])
                obv = ob[:, :].rearrange("p (u i) -> p u i", u=4)
                dbv = db[:, :].rearrange("p (u i) -> p u i", u=4)
                if p == 0:
                    P.copy("act", accview(num_acc, dil, q4), obv, [obn], ["F0"])
                    P.copy("act", accview(den_acc, dil, q4), dbv, [dbn], ["F1"])
                else:
                    P.tt("dve", accview(num_acc, dil, q4), obv, accview(num_acc, dil, q4), ALU.add, [obn, "F0"], ["F0"])
                    P.tt("dve", accview(den_acc, dil, q4), dbv, accview(den_acc, dil, q4), ALU.add, [dbn, "F1"], ["F1"])
        P.add("dve", lambda: nc.vector.reciprocal(out=den_acc[:], in_=den_acc[:]), ["F1"], ["F1"])
        P.tt("pool", num_acc[:], num_acc[:], den_acc[:], ALU.mult, ["F0", "F1"], ["F0"])
        P.tt("pool", yaT[:], num_acc[:], gaT[:], ALU.mult, ["F0", "B2"], ["B3"])
        P.dma("pool", yT_scr[h], yaT[:], ["B3"], [f"yT_scr{h}"], "B3")

    if stage >= 3:
        P.dma("sp", bsb[:].rearrange("p g c -> p (g c)"), b_b_s[0:1, :].partition_broadcast(128), [], ["bsb"], "c1")
        P.dma("sp", lng[:], b_ln_g[0:1, :].partition_broadcast(128), [], ["lng"], "c1")
        P.dma("sp", lnb[:], b_ln_b[0:1, :].partition_broadcast(128), [], ["lnb"], "c1")
        for g in range(8):
            P.dma("sp", wsf[:], b_w_s[g], [], ["wsf"], "wsf")
            pb, pbn = bank()
            P.tr(pb[:, 0:128], wsf[:], ident_f[:], ["wsf", "ident_f"], [pbn])
            P.tt("dve", WmT[:, g, :], pb[:, 0:128], trimask[:], ALU.mult, [pbn, "trimask"], ["WmT"])
        WA, nA = load_wblock(w_in, [(5120, 512)])
        WB, nB = load_wblock(w_in, [(5632, 512)])
        gv = Fb[0][:, 0:1024]
        for t in range(NT):
            pa, pan = bank()
            pb, pbn = bank()
            for (pq, pqn, Wq, nq) in ((pa, pan, WA, nA), (pb, pbn, WB, nB)):
                for kc in range(KC):
                    P.mm(pq[:, :], xnT[:, kc, t * 128:(t + 1) * 128], Wq[:, kc, :], kc == 0, kc == KC - 1,
                         [XNT[t], nq], [pqn])
            P.act(gv[:, 0:512], pa[:, :], AF.Gelu, [pan], ["F0", "lnsA"], accum_out=stat[:, 4:5])
            P.act(gv[:, 512:1024], pb[:, :], AF.Gelu, [pbn], ["F0", "lnsB"], accum_out=stat[:, 5:6])
            P.act(Fb[1][:, 0:1024], gv, AF.Square, ["F0"], ["F1", "lnsq"], accum_out=stat[:, 6:7])
            P.tt("dve", stat[:, 7:8], stat[:, 4:5], stat[:, 5:6], ALU.add, ["lnsA", "lnsB"], ["lnm0"])
            P.ts("dve", stat[:, 7:8], stat[:, 7:8], 1.0 / 1024, None, ALU.mult, None, ["lnm0"], ["lnm"])
            P.tt("dve", stat[:, 8:9], stat[:, 7:8], stat[:, 7:8], ALU.mult, ["lnm"], ["lnm2"])
            P.stt("dve", stat[:, 9:10], stat[:, 6:7], 1.0 / 1024, stat[:, 8:9], ALU.mult, ALU.subtract,
                  ["lnsq", "lnm2"], ["lnvar"])
            P.ts("dve", stat[:, 9:10], stat[:, 9:10], 1e-5, None, ALU.add, None, ["lnvar"], ["lnvar2"])
            P.act(stat[:, 10:11], stat[:, 9:10], AF.Sqrt, ["lnvar2"], ["lnsd"])
            P.add("dve", lambda: nc.vector.reciprocal(out=stat[:, 11:12], in_=stat[:, 10:11]), ["lnsd"], ["lnrs"])
            P.ts("dve", gv, gv, stat[:, 7:8], stat[:, 11:12], ALU.subtract, ALU.mult, ["F0", "lnm", "lnrs"], ["F0"])
            P.tt("dve", gv, gv, lng[:], ALU.mult, ["F0", "lng"], ["F0"])
            P.tt("dve", vn_all[:, t, :], gv, lnb[:], ALU.add, ["F0", "lnb"], UALL + ["vn"])
        ubT, gbT, ybT = Bb[0], Bb[1], Bb[2]
        for g2 in range(4):
            ga_, gb_ = 2 * g2, 2 * g2 + 1
            W, wn = load_wblock(w_in, [(4096 + ga_ * 128, 128), (6144 + ga_ * 128, 128),
                                       (4096 + gb_ * 128, 128), (6144 + gb_ * 128, 128)])
            for gg in range(2):
                g = 2 * g2 + gg
                for tb in range(4):
                    for (c0, dst, dn, fn) in ((gg * 256, ubT, "B0", AF.Gelu), (gg * 256 + 128, gbT, "B1", AF.Silu)):
                        pb, pbn = bank()
                        for kc in range(KC):
                            P.mm(pb[:, :], W[:, kc, c0:c0 + 128], xnT[:, kc, tb * 512:(tb + 1) * 512], kc == 0, kc == KC - 1,
                                 XNT[tb * 4:(tb + 1) * 4] + [wn], [pbn])
                        P.act(dst[:, tb * 512:(tb + 1) * 512], pb[:, :], fn, [pbn], [dn])
                    psb, psbn = bank()
                    for c4 in range(4):
                        n = tb * 4 + c4
                        P.mm(psb[:, c4 * 128:(c4 + 1) * 128], vn_all[:, n, g * 128:(g + 1) * 128], WmT[:, g, :], True, True,
                             ["vn", "WmT"], [psbn])
                    t1 = tmpF[:, (tb % 2) * 512:(tb % 2 + 1) * 512]
                    t1n = f"tS{tb % 2}"
                    P.tt("dve", t1.rearrange("p (u c) -> p u c", u=4), psb[:, :].rearrange("p (u c) -> p u c", u=4),
                         bsb[:, g, :].unsqueeze(1).to_broadcast([128, 4, 128]), ALU.add, [psbn, "bsb"], [t1n, "F2"])
                    P.tt("pool", t1, t1, ubT[:, tb * 512:(tb + 1) * 512], ALU.mult, [t1n, "B0"], [t1n])
                    P.tt("pool", ybT[:, tb * 512:(tb + 1) * 512], t1, gbT[:, tb * 512:(tb + 1) * 512], ALU.mult,
                         [t1n, "B1"], ["B2"])
                P.dma("pool", yT_scr[8 + g], ybT[:], ["B2"], [f"yT_scr{8 + g}"], "B2")

    def phase_proj_res(wsrc, src_scr, src_tok, dst_scr, dst_tok, yT_tokens):
        xres = [Fb[0][:, 0:512], Fb[0][:, 512:1024]]
        hout = [Fb[1][:, 0:512], Fb[1][:, 512:1024]]
        it = 0
        for cb in range(4):
            W, wn = load_wblock(wsrc, [(cb * 512, 512)])
            for t in range(NT):
                s = it % 2
                it += 1
                pb, pbn = bank()
                P.dma("sp", xres[s], src_scr[t * 128:(t + 1) * 128, cb * 512:(cb + 1) * 512], [src_tok], [f"xres{s}", "F0"],
                      f"xres{s}")
                for kc in range(KC):
                    P.mm(pb[:, :], xnT[:, kc, t * 128:(t + 1) * 128], W[:, kc, :], kc == 0, kc == KC - 1,
                         [XNT[t], wn], [pbn])
                P.tt("dve", hout[s], pb[:, :], xres[s], ALU.add, [pbn, f"xres{s}"], [f"hout{s}", "F1"])
                P.dma("act" if s == 0 else "pool", dst_scr[t * 128:(t + 1) * 128, cb * 512:(cb + 1) * 512], hout[s],
                      [f"hout{s}"], [dst_tok], f"houtp{s}")

    if stage >= 4:
        for t in range(NT):
            P.dma("sp", xnT[:, :, t * 128:(t + 1) * 128],
                  yT_scr[:, :, t * 128:(t + 1) * 128].rearrange("k p c -> p k c"),
                  [f"yT_scr{k}" for k in range(KC)], [XNT[t]], f"yTl{t % 4}")
        phase_proj_res(w_out, xp, "xp", h_scr[0], "h_scr0", None)

    def phase_ple(layer, src_scr, src_tok, dst_scr, dst_tok):
        phase_T(src_scr, src_tok, layer, False)
        pT = U[:, 0:4096].rearrange("p (k c) -> p k c", k=2)
        pst = Fb[2][:, 0:256]
        pbf = Bb[3][:, 0:256]
        for t in range(NT):
            P.dma("sp", pst, pp[layer, t * 128:(t + 1) * 128, :], [], ["F2"], "F2")
            P.copy("dve", pbf, pst, ["F2"], ["B3"])
            pb, pbn = bank()
            pbv = pb[:].bitcast(BF16)
            for u in range(2):
                P.tr(pbv[:, u * 128:(u + 1) * 128], pbf[:, u * 128:(u + 1) * 128], ident_b[:], ["B3", "ident_b"], [pbn])
            P.copy("act", pT[:, :, t * 128:(t + 1) * 128], pbv[:, 0:256].rearrange("p (u c) -> p u c", u=2), [pbn],
                   UALL + ["vn", "pT"])
        xres = [Fb[0][:, 0:512], Fb[0][:, 512:1024]]
        hout = [Fb[1][:, 0:512], Fb[1][:, 512:1024]]
        sig = [Fb[0][:, 1024:1536], Fb[0][:, 1536:2048]]
        wpb = Bb[2][:, 0:1024].rearrange("p (k c) -> p k c", k=2)
        it = 0
        for cb in range(4):
            W, wn = load_wblock(ple_wg[layer], [(cb * 512, 512)])
            wps = Fb[2][:, 0:1024].rearrange("p (k c) -> p k c", k=2)
            P.dma("sp", wps, ple_wp[layer, :, cb * 512:(cb + 1) * 512].rearrange("(k p) c -> p k c", p=128), [], ["F2"], "F2")
            P.copy("dve", wpb, wps, ["F2"], ["B2"])
            for t in range(NT):
                s = it % 2
                it += 1
                pg, pgn = bank()
                pq, pqn = bank()
                P.dma("sp", xres[s], src_scr[t * 128:(t + 1) * 128, cb * 512:(cb + 1) * 512], [src_tok], [f"xres{s}", "F0"],
                      f"xres{s}")
                for kc in range(KC):
                    P.mm(pg[:, :], xnT[:, kc, t * 128:(t + 1) * 128], W[:, kc, :], kc == 0, kc == KC - 1, [XNT[t], wn], [pgn])
                for k2 in range(2):
                    P.mm(pq[:, :], pT[:, k2, t * 128:(t + 1) * 128], wpb[:, k2, :], k2 == 0, k2 == 1, ["pT", "B2"], [pqn])
                P.act(sig[s], pg[:, :], AF.Sigmoid, [pgn], [f"sig{s}", "F0"])
                P.tt("dve", sig[s], pq[:, :], sig[s], ALU.mult, [pqn, f"sig{s}"], [f"sig{s}"])
                P.tt("dve", hout[s], sig[s], xres[s], ALU.add, [f"sig{s}", f"xres{s}"], [f"hout{s}", "F1"])
                P.dma("pool", dst_scr[t * 128:(t + 1) * 128, cb * 512:(cb + 1) * 512], hout[s], [f"hout{s}"], [dst_tok],
                      f"hout{s}")

    if stage >= 5:
        phase_ple(0, h_scr[0], "h_scr0", h_scr[1], "h_scr1")


    def barrier(extra=()):
        names = list(P.bufs.keys()) + list(extra)
        P.add("pool", lambda: nc.gpsimd.memset(stat[:, 15:16], 0.0), [], names)

    rsm = sb("rsm", (128, 1088))
    mu_t = sb("mu_t", (128, 6, KC))
    R_MASKG, R_MASKN, R_ONES, R_UNEG, R_PRM, R_PC, R_XS, R_US, R_S0, R_S1, R_SP, R_ONESD = (
        0, 128, 192, 256, 384, 608, 640, 704, 768, 832, 896, 960)
    PRM_NAMES = ("c_k_k", "c_k_a", "c_r_k", "c_gn_g", "c_gn_b", "c_a0")

    def prm(name, h):
        o = R_PRM + PRM_NAMES.index(name) * 32 + h
        return rsm[0:64, o:o + 1]

    def rwkv_consts():
        P.dma("sp", rsm[0:64, R_MASKG:R_MASKG + 128], c_maskg[:, :], [], ["rsm_c"], "rc0")
        P.dma("sp", rsm[0:64, R_MASKN:R_MASKN + 64], c_maskn[:, :], [], ["rsm_c"], "rc1")
        P.dma("sp", rsm[:, R_UNEG:R_UNEG + 128], c_uneg[:, :], [], ["rsm_c"], "rc2")
        P.add("dve", lambda: nc.vector.memset(rsm[0:64, R_ONES:R_ONES + 64], 1.0), [], ["rsm_c"])
        P.add("dve", lambda: nc.vector.memset(rsm[0:64, R_ONESD:R_ONESD + 64], 1.0 / 64), [], ["rsm_c"])
        for i, nme in enumerate(PRM_NAMES):
            src_t = c_a0 if nme == "c_a0" else c_vecs[nme]
            src = bass.AP(src_t.tensor, 0, [[1, 64], [64, 32]])
            P.dma("sp", rsm[0:64, R_PRM + i * 32:R_PRM + (i + 1) * 32], src, [], ["prm"], f"rp{i}",
                  allow_slow_non_contiguous=True)
        P.dma("sp", mu_t[:], bass.AP(c_mu.tensor, 0, [[1, 128], [D, 6], [128, KC]]), [], ["mu_t"], "mu_t",
              allow_slow_non_contiguous=True)

    def rwkv_proj():
        barrier()
        xm = ([U[:, u * 2048:(u + 1) * 2048] for u in range(8)] + [Bb[i][:, :] for i in range(4)]
              + [Fb[i][:, :].bitcast(BF16)[:, a * 2048:(a + 1) * 2048] for i in range(2) for a in range(2)])
        xmn = [f"xm{k}" for k in range(KC)]
        dx = Fb[2][:, :].bitcast(BF16)[:, 0:2048]
        stg = [Fb[2][:, 1024:1536], Fb[2][:, 1536:2048]]
        hT = lng[:, :].bitcast(BF16)
        w0bc = [lnb[:, :], bsb[:, :, :].rearrange("p g c -> p (g c)")]

        def build_xm(m):
            for kc in range(KC):
                P.tt("pool", dx[:, 1:S], xnT[:, kc, 0:S - 1], xnT[:, kc, 1:S], ALU.subtract, XNT, ["dx"])
                P.ts("pool", dx[:, 0:1], xnT[:, kc, 0:1], -1.0, None, ALU.mult, None, XNT, ["dx"])
                P.stt("dve", xm[kc], dx, mu_t[:, m, kc:kc + 1], xnT[:, kc, :], ALU.mult, ALU.add,
                      ["dx", "mu_t"] + XNT, [xmn[kc]])

        itc = [0]

        def evac_store(pb, pbn, dst, dst_tok, func=None, bf=False, npart=128):
            s_ = itc[0] % 2
            itc[0] += 1
            o = stg[s_].bitcast(BF16)[0:npart, 0:512] if bf else stg[s_][0:npart, :]
            if func is not None:
                P.act(o, pb[0:npart, :], func, [pbn], [f"stg{s_}"])
            elif s_ == 0:
                P.copy("act", o, pb[0:npart, :], [pbn], [f"stg{s_}"])
            else:
                P.copy("dve", o, pb[0:npart, :], [pbn], [f"stg{s_}"])
            P.dma("act" if s_ == 0 else "pool", dst, o, [f"stg{s_}"], [dst_tok], f"stg{s_}")

        def proj_fm(wsrc, dst_scr, dst_tok, func=None, bf=False):
            for cb in range(4):
                W, wn = load_wblock(wsrc, [(cb * 512, 512)])
                for fb in range(4):
                    f0 = cb * 512 + fb * 128
                    for tb in range(4):
                        pb, pbn = bank()
                        for kc in range(KC):
                            P.mm(pb[:, :], W[:, kc, fb * 128:(fb + 1) * 128], xm[kc][:, tb * 512:(tb + 1) * 512],
                                 kc == 0, kc == KC - 1, [xmn[kc], wn], [pbn])
                        evac_store(pb, pbn, dst_scr[f0:f0 + 128, tb * 512:(tb + 1) * 512], dst_tok, func, bf)

        def load_small(wsrc):
            i = wctr[0]
            wctr[0] += 1
            s_ = i % 2
            ss = (i * 4) % 2
            stv = wst[ss][:, :, :].rearrange("p a b -> p (a b)")
            P.dma("sp", stv[0:96, :], wsrc[:, :], [], [f"wst{ss}"], f"wst{ss}")
            wv = wbf[s_][:, 0:4, :].rearrange("p a b -> p (a b)")
            P.copy("dve", wv[0:96, :], stv[0:96, :], [f"wst{ss}"], [f"wbf{s_}"])
            return wv, f"wbf{s_}"

        def lora_hidden(w1src, func):
            W, wn = load_wblock(w1src, [(0, 96)])
            for tb in range(4):
                pb, pbn = bank()
                for kc in range(KC):
                    P.mm(pb[0:96, :], W[:, kc, 0:96], xm[kc][:, tb * 512:(tb + 1) * 512], kc == 0, kc == KC - 1,
                         [xmn[kc], wn], [pbn])
                if func is None:
                    P.copy("act", hT[0:96, tb * 512:(tb + 1) * 512], pb[0:96, :], [pbn], ["hT"])
                else:
                    P.act(hT[0:96, tb * 512:(tb + 1) * 512], pb[0:96, :], func, [pbn], ["hT"])

        build_xm(0)
        proj_fm(c_wr, r_scr, "r_scr")
        build_xm(1)
        lora_hidden(c_w1, AF.Tanh)
        w2b, w2n = load_small(c_w2)
        for hf in range(2):
            P.dma("sp", w0bc[hf], c_w0[0:1, hf * 1024:(hf + 1) * 1024].partition_broadcast(128), [], [f"w0bc{hf}"],
                  f"w0bc{hf}")
        for t in range(NT):
            for cb in range(4):
                pb, pbn = bank()
                P.mm(pb[:, :], hT[0:96, t * 128:(t + 1) * 128], w2b[0:96, cb * 512:(cb + 1) * 512], True, True,
                     ["hT", w2n], [pbn])
                s_ = itc[0] % 2
                itc[0] += 1
                P.tt("dve", stg[s_], pb[:, :], w0bc[cb // 2][:, (cb % 2) * 512:(cb % 2 + 1) * 512], ALU.add,
                     [pbn, f"w0bc{cb // 2}"], [f"stg{s_}"])
                P.act(stg[s_], stg[s_], AF.Sigmoid, [f"stg{s_}"], [f"stg{s_}"])
                P.dma("act" if s_ == 0 else "pool", e_scr[t * 128:(t + 1) * 128, cb * 512:(cb + 1) * 512], stg[s_],
                      [f"stg{s_}"], ["e_scr"], f"stg{s_}")
        build_xm(2)
        proj_fm(c_wk, k_scr, "k_scr")
        build_xm(3)
        proj_fm(c_wv, v_scr, "v_scr")
        build_xm(4)
        lora_hidden(c_a1, None)
        a2b, a2n = load_small(c_a2)
        for fb in range(KC):
            for tb in range(4):
                pb, pbn = bank()
                P.mm(pb[:, :], a2b[0:96, fb * 128:(fb + 1) * 128], hT[0:96, tb * 512:(tb + 1) * 512], True, True,
                     ["hT", a2n], [pbn])
                evac_store(pb, pbn, a_scr[fb * 128:(fb + 1) * 128, tb * 512:(tb + 1) * 512], "a_scr")
        build_xm(5)
        proj_fm(c_wg, g_scr, "g_scr", AF.Silu, True)

    def rwkv_scan(heads, NCH=32, SRC=None, sample=False):
        barrier()
        T = NCH * 64
        TB = min(512, T)
        NTB = T // TB
        NTL = T // 128
        G8 = min(8, NCH)
        NG8 = NCH // G8
        NG4 = NCH // 4
        if SRC is None:
            SRC = dict(r=(r_scr, "r_scr"), k=(k_scr, "k_scr"), v=(v_scr, "v_scr"), a=(a_scr, "a_scr"),
                       e=(e_scr, "e_scr"), g=(g_scr, "g_scr"), yT=(yT_scr, "yT_scr"))
        xflat = xnT[:, :, :].rearrange("p k t -> p (k t)")
        X8 = [xflat[:, i * 4096:(i + 1) * 4096].bitcast(F32) for i in range(8)]
        U8 = [U[:, i * 4096:(i + 1) * 4096].bitcast(F32) for i in range(4)]
        W16 = [wbf[i][:, :, :].rearrange("p k c -> p (k c)").bitcast(F32) for i in range(2)]
        rF, kF, aF, vF, t1, t2, kmF, bF = [x[0:64, 0:T] for x in X8]
        rsb = U8[0][0:64, 0:T]
        Pin, Pinv = U8[0][0:64, 0:T], U8[1][0:64, 0:T]
        AR = U[:, 8192:16384].bitcast(F32)[0:64, 0:2 * T]
        BK = W16[0][0:64, 0:2 * T]
        Gbm = W16[1][0:64, 0:2 * T]
        Gkm = xflat[:, 0:8192].bitcast(F32)[0:64, 0:2 * T]
        Vt, Btok, Ktok = X8[2][0:64, 0:T], X8[4][0:64, 0:T], X8[5][0:64, 0:T]
        Nb = [X8[6][0:64, 0:T], X8[7][0:64, 0:T]]
        Tb = [U8[0][0:64, 0:T], U8[1][0:64, 0:T]]
        Q, bs, yF = Fb[0][0:64, 0:T], Fb[1][0:64, 0:T], Fb[2][0:64, 0:T]
        e2tok = Bb[0][:, :].bitcast(F32)[:, 0:NTL * 64].rearrange("p (t j) -> p t j", j=64)
        gT, ygT = Bb[1][0:64, 0:T], Bb[2][0:64, 0:T]
        AR4 = AR.rearrange("p (c a t) -> p c a t", c=NCH, a=2)
        BK4 = BK.rearrange("p (c a t) -> p c a t", c=NCH, a=2)
        c3 = lambda x: x.rearrange("p (c t) -> p c t", c=NCH)
        G3b, G3k = Gbm.rearrange("p (c t) -> p c t", c=NCH), Gkm.rearrange("p (c t) -> p c t", c=NCH)
        Sin = Bb[3][:, :].bitcast(F32)[0:64, 0:256].rearrange("p (b j) -> p b j", b=4)
        SinT = Bb[3][:, :].bitcast(F32)[0:64, 256:512].rearrange("p (b i) -> p b i", b=4)
        maskg = rsm[0:64, R_MASKG:R_MASKG + 128]
        maskn = rsm[0:64, R_MASKN:R_MASKN + 64]
        ones64 = rsm[0:64, R_ONES:R_ONES + 64]
        onesd = rsm[0:64, R_ONESD:R_ONESD + 64]
        uneg = rsm[:, R_UNEG:R_UNEG + 128]
        PCt = rsm[0:64, R_PC:R_PC + NCH]
        Xs = rsm[0:64, R_XS:R_XS + 64]
        Us = rsm[0:64, R_US:R_US + 64]
        Sb = [rsm[0:64, R_S0:R_S0 + 64], rsm[0:64, R_S1:R_S1 + 64]]
        SP = rsm[0:64, R_SP:R_SP + 64]
        idf = ident_f[0:64, 0:64]
        GKM_T = ["X8_0", "X8_1"]
        AR_T = ["U8_2", "U8_3"]
        BK_T = ["W16_0"]
        GBM_T = ["W16_1"]

        for h in heads:
            hs = slice(h * 64, (h + 1) * 64)
            P.dma("sp", rF, SRC["r"][0][hs, :], [SRC["r"][1]], ["X8_0"], "X8_0")
            P.dma("sp", kF, SRC["k"][0][hs, :], [SRC["k"][1]], ["X8_1"], "X8_1")
            P.dma("sp", aF, SRC["a"][0][hs, :], [SRC["a"][1]], ["X8_2"], "X8_2")
            P.dma("sp", vF, SRC["v"][0][hs, :], [SRC["v"][1]], ["X8_3"], "X8_3")
            P.dma("sp", e2tok, SRC["e"][0][:, hs].rearrange("(t p) j -> p t j", p=128), [SRC["e"][1]], ["B0"], "B0")
            P.dma("sp", gT, SRC["g"][0][hs, :], [SRC["g"][1]], ["B1"], "B1")
            if sample:
                P.dma("sp", Sin, wkv0[:, h].rearrange("b i j -> i b j"), [], ["B3"], "Sin")
                pb, pbn = bank()
                for b_ in range(4):
                    P.tr(pb[0:64, b_ * 64:(b_ + 1) * 64], Sin[:, b_, :], idf, ["B3", "ident_f"], [pbn])
                P.copy("act", SinT, pb[0:64, 0:256].rearrange("p (b i) -> p b i", b=4), [pbn], ["SinT"])
            P.act(aF, aF, AF.Sigmoid, ["X8_2", "prm"], ["X8_2"], bias=prm("c_a0", h))
            P.ts("dve", t1, kF, prm("c_k_k", h), None, ALU.mult, None, ["X8_1", "prm"], ["X8_4"])
            P.act(t2, t1, AF.Square, ["X8_4"], ["X8_5"])
            for tb in range(NTB):
                pb, pbn = bank()
                P.mm(pb[0:64, 0:TB], ones64, t2[:, tb * TB:(tb + 1) * TB], True, True, ["X8_5", "rsm_c"], [pbn])
                P.act(rsb[:, tb * TB:(tb + 1) * TB], pb[0:64, 0:TB], AF.Sqrt, [pbn], ["U8_0"])
            P.ts("dve", rsb, rsb, 1e-12, None, ALU.max, None, ["U8_0"], ["U8_0"])
            P.add("dve", lambda: nc.vector.reciprocal(out=rsb, in_=rsb), ["U8_0"], ["U8_0"])
            P.tt("dve", t1, t1, rsb, ALU.mult, ["X8_4", "U8_0"], ["X8_4"])
            P.ts("pool", kmF, aF, 1.0, prm("c_k_a", h), ALU.subtract, ALU.mult, ["X8_2", "prm"], ["X8_6"])
            P.stt("dve", kmF, kmF, 1.0, kF, ALU.add, ALU.mult, ["X8_6", "X8_1"], ["X8_6"])
            P.tt("pool", bF, t1, aF, ALU.mult, ["X8_4", "X8_2"], ["X8_7"])
            P.stt("dve", t2, rF, prm("c_r_k", h), kmF, ALU.mult, ALU.mult, ["X8_0", "prm", "X8_6"], ["X8_5"])
            for tb in range(NTB):
                pb, pbn = bank()
                P.mm(pb[0:64, 0:TB], ones64, t2[:, tb * TB:(tb + 1) * TB], True, True, ["X8_5", "rsm_c"], [pbn])
                P.copy("act", bs[:, tb * TB:(tb + 1) * TB], pb[0:64, 0:TB], [pbn], ["F1"])
            for t4 in range((NTL + 3) // 4):
                pb, pbn = bank()
                nu = min(4, NTL - t4 * 4)
                for u in range(nu):
                    t = t4 * 4 + u
                    P.mm(pb[0:64, u * 128:(u + 1) * 128], e2tok[:, t, :], uneg, True, True, ["B0", "rsm_c"], [pbn])
                P.act(Pin[:, t4 * 512:t4 * 512 + nu * 128], pb[0:64, 0:nu * 128], AF.Exp, [pbn], ["U8_0"])
                P.act(Pinv[:, t4 * 512:t4 * 512 + nu * 128], pb[0:64, 0:nu * 128], AF.Exp, [pbn], ["U8_1"], scale=-1.0)
            P.tt("dve", AR4[:, :, 1, :], c3(rF), c3(Pin), ALU.mult, ["X8_0", "U8_0"], AR_T)
            P.stt("dve", AR4[:, :, 0, 1:64], c3(t1)[:, :, 1:64], -1.0, c3(Pin)[:, :, 0:63], ALU.mult, ALU.mult,
                  ["X8_4", "U8_0"], AR_T)
            P.ts("dve", AR4[:, :, 0, 0:1], c3(t1)[:, :, 0:1], -1.0, None, ALU.mult, None, ["X8_4"], AR_T)
            P.tt("pool", BK4[:, :, 0, :], c3(bF), c3(Pinv), ALU.mult, ["X8_7", "U8_1"], BK_T)
            P.tt("pool", BK4[:, :, 1, :], c3(kmF), c3(Pinv), ALU.mult, ["X8_6", "U8_1"], BK_T)
            P.copy("act", PCt, c3(Pin)[:, :, 63], ["U8_0"], ["PCt"])
            for g8 in range(NG8):
                pv, pvn = bank()
                pbt, pbtn = bank()
                pkt, pktn = bank()
                for u in range(G8):
                    c = g8 * G8 + u
                    P.tr(pv[0:64, u * 64:(u + 1) * 64], vF[:, c * 64:(c + 1) * 64], idf, ["X8_3", "ident_f"], [pvn])
                    P.tr(pbt[0:64, u * 64:(u + 1) * 64], BK4[:, c, 0, :], idf, BK_T + ["ident_f"], [pbtn])
                    P.tr(pkt[0:64, u * 64:(u + 1) * 64], BK4[:, c, 1, :], idf, BK_T + ["ident_f"], [pktn])
                sl = slice(g8 * G8 * 64, (g8 + 1) * G8 * 64)
                P.copy("act", Vt[:, sl], pv[0:64, 0:G8 * 64], [pvn], ["X8_2"])
                P.copy("dve", Btok[:, sl], pbt[0:64, 0:G8 * 64], [pbtn], ["X8_4"])
                P.copy("act", Ktok[:, sl], pkt[0:64, 0:G8 * 64], [pktn], ["X8_5"])
            for g4 in range(NG4):
                pgb, pgbn = bank()
                pgk, pgkn = bank()
                for u in range(4):
                    c = g4 * 4 + u
                    arc = AR4[:, c, :, :].rearrange("p a t -> p (a t)")
                    P.mm(pgb[0:64, u * 128:(u + 1) * 128], BK4[:, c, 0, :], arc, True, True, BK_T + AR_T, [pgbn])
                    P.mm(pgk[0:64, u * 128:(u + 1) * 128], BK4[:, c, 1, :], arc, True, True, BK_T + AR_T, [pgkn])
                mg = maskg.unsqueeze(1).to_broadcast([64, 4, 128])
                P.tt("dve", G3b[:, g4 * 4:(g4 + 1) * 4, :], pgb[0:64, :].rearrange("p (u t) -> p u t", u=4), mg, ALU.mult,
                     [pgbn, "rsm_c"], GBM_T)
                P.tt("dve", G3k[:, g4 * 4:(g4 + 1) * 4, :], pgk[0:64, :].rearrange("p (u t) -> p u t", u=4), mg, ALU.mult,
                     [pgkn, "rsm_c"], GKM_T)
            if sample:
                P.copy("dve", c3(Q), idf.unsqueeze(1).to_broadcast([64, NCH, 64]), ["ident_f"], ["F0"])
            else:
                for g8 in range(NG8):
                    pn, pnn = bank()
                    for u in range(G8):
                        c = g8 * G8 + u
                        P.mm(pn[0:64, u * 64:(u + 1) * 64], AR4[:, c, 0, :], BK4[:, c, 0, :], True, True, BK_T + AR_T, [pnn])
                    P.tt("dve", c3(Nb[0])[:, g8 * G8:(g8 + 1) * G8, :], pn[0:64, 0:G8 * 64].rearrange("p (u t) -> p u t", u=G8),
                         maskn.unsqueeze(1).to_broadcast([64, G8, 64]), ALU.mult, [pnn, "rsm_c"], ["X8_6"])
                P.copy("act", c3(Tb[0]), G3b[:, :, 0:64], GBM_T, ["U8_0"])
                P.tt("pool", c3(Q), c3(Tb[0]), idf.unsqueeze(1).to_broadcast([64, NCH, 64]), ALU.add, ["U8_0", "ident_f"], ["F0"])
                NT_ = ["X8_6", "X8_7"]
                TT_ = ["U8_0", "U8_1"]
                for k in range(5):
                    a_, b_ = k % 2, (k + 1) % 2
                    for g8 in range(NG8):
                        sl3 = slice(g8 * G8, (g8 + 1) * G8)
                        if k < 4:
                            pT_, pTn = bank()
                            for u in range(G8):
                                c = g8 * G8 + u
                                P.mm(pT_[0:64, u * 64:(u + 1) * 64], c3(Nb[a_])[:, c, :], c3(Tb[a_])[:, c, :], True, True,
                                     [NT_[a_], TT_[a_]], [pTn])
                            P.copy("act", c3(Tb[b_])[:, sl3, :], pT_[0:64, 0:G8 * 64].rearrange("p (u t) -> p u t", u=G8), [pTn], [TT_[b_]])
                        pN_, pNn = bank()
                        for u in range(G8):
                            c = g8 * G8 + u
                            P.mm(pN_[0:64, u * 64:(u + 1) * 64], c3(Tb[a_])[:, c, :], c3(Nb[a_])[:, c, :], True, True,
                                 [NT_[a_], TT_[a_]], [pNn])
                        P.copy("dve", c3(Nb[b_])[:, sl3, :], pN_[0:64, 0:G8 * 64].rearrange("p (u t) -> p u t", u=G8), [pNn], [NT_[b_]])
                        pQ_, pQn = bank()
                        for u in range(G8):
                            c = g8 * G8 + u
                            P.mm(pQ_[0:64, u * 64:(u + 1) * 64], c3(Nb[b_])[:, c, :], c3(Q)[:, c, :], True, True,
                                 [NT_[b_], "F0"], [pQn])
                        P.tt("dve", c3(Q)[:, sl3, :], pQ_[0:64, 0:G8 * 64].rearrange("p (u t) -> p u t", u=G8), c3(Q)[:, sl3, :], ALU.add,
                             [pQn, "F0"], ["F0"])
            At, Wc, X2, Uv, R2 = X8[6][0:64, 0:T], X8[7][0:64, 0:T], U8[0][0:64, 0:T], U8[1][0:64, 0:T], X8[6][0:64, 0:T]
            ATm, BPC = BK[:, 0:T], BK[:, T:2 * T]
            g8v = lambda pb_: pb_[0:64, 0:G8 * 64].rearrange("p (u t) -> p u t", u=G8)
            for g8 in range(NG8):
                sl3 = slice(g8 * G8, (g8 + 1) * G8)
                pa, pan = bank()
                for u in range(G8):
                    c = g8 * G8 + u
                    P.tr(pa[0:64, u * 64:(u + 1) * 64], AR4[:, c, 0, :], idf, AR_T + ["ident_f"], [pan])
                P.copy("act", c3(At)[:, sl3, :], g8v(pa), [pan], ["X8_6"])
                pw, pwn = bank()
                for u in range(G8):
                    c = g8 * G8 + u
                    P.mm(pw[0:64, u * 64:(u + 1) * 64], c3(Q)[:, c, :], c3(At)[:, c, :], True, True, ["F0", "X8_6"], [pwn])
                P.copy("dve", c3(Wc)[:, sl3, :], g8v(pw), [pwn], ["X8_7"])
                px, pxn = bank()
                for u in range(G8):
                    c = g8 * G8 + u
                    P.mm(px[0:64, u * 64:(u + 1) * 64], G3k[:, c, 0:64], c3(Vt)[:, c, :], True, True, GKM_T + ["X8_2"], [pxn])
                P.copy("act", c3(X2)[:, sl3, :], g8v(px), [pxn], ["U8_0"])
                pu, pun = bank()
                for u in range(G8):
                    c = g8 * G8 + u
                    P.mm(pu[0:64, u * 64:(u + 1) * 64], c3(Q)[:, c, :], c3(X2)[:, c, :], True, True, ["F0", "U8_0"], [pun])
                P.copy("dve", c3(Uv)[:, sl3, :], g8v(pu), [pun], ["U8_1"])
            for g8 in range(NG8):
                sl3 = slice(g8 * G8, (g8 + 1) * G8)
                pa, pan = bank()
                pbb, pbbn = bank()
                pr, prn = bank()
                for u in range(G8):
                    c = g8 * G8 + u
                    P.mm(pa[0:64, u * 64:(u + 1) * 64], c3(Wc)[:, c, :], c3(Btok)[:, c, :], True, True, ["X8_7", "X8_4"], [pan])
                    P.mm(pbb[0:64, u * 64:(u + 1) * 64], c3(Btok)[:, c, :], c3(Uv)[:, c, :], True, False, ["X8_4", "U8_1"], [pbbn])
                    P.mm(pbb[0:64, u * 64:(u + 1) * 64], c3(Ktok)[:, c, :], c3(Vt)[:, c, :], False, True, ["X8_5", "X8_2"], [pbbn])
                    P.mm(pr[0:64, u * 64:(u + 1) * 64], c3(Wc)[:, c, :], G3b[:, c, 64:128], True, True, ["X8_7"] + GBM_T, [prn])
                P.tt("dve", c3(ATm)[:, sl3, :], g8v(pa), idf.unsqueeze(1).to_broadcast([64, G8, 64]), ALU.add,
                     [pan, "ident_f"], ["W16_0"])
                P.tt("dve", c3(BPC)[:, sl3, :], g8v(pbb), PCt[:, sl3].unsqueeze(2).to_broadcast([64, G8, 64]), ALU.mult,
                     [pbbn, "PCt"], ["W16_0"])
                P.tt("dve", c3(R2)[:, sl3, :], g8v(pr), AR4[:, sl3, 1, :], ALU.add, [prn] + AR_T, ["X8_6"])
            P.add("dve", lambda: nc.vector.memset(Sb[0], 0.0), [], ["S0"])
            Sn = ["S0", "S1"]
            pY = None
            for c in range(NCH):
                cur, nxt = Sb[c % 2], Sb[(c + 1) % 2]
                cn, nn = Sn[c % 2], Sn[(c + 1) % 2]
                if sample:
                    cur, cn = SinT[:, c, :], "SinT"
                pS, pSn = bank(0, 6)
                P.mm(pS[0:64, 0:64], c3(ATm)[:, c, :], cur, True, True, ["W16_0", cn], [pSn])
                P.stt("dve", nxt, pS[0:64, 0:64], PCt[:, c:c + 1], c3(BPC)[:, c, :], ALU.mult, ALU.add,
                      [pSn, "PCt", "W16_0"], [nn])
                if c % 8 == 0:
                    pY, pYn = bank(6, 8)
                u = c % 8
                yo = pY[0:64, u * 64:(u + 1) * 64]
                P.mm(yo, c3(Uv)[:, c, :], G3b[:, c, 64:128], True, False, GBM_T + ["U8_1"], [pYn])
                P.mm(yo, c3(Vt)[:, c, :], G3k[:, c, 64:128], False, False, GKM_T + ["X8_2"], [pYn])
                P.mm(yo, cur, c3(R2)[:, c, :], False, True, ["X8_6", cn], [pYn])
                if sample:
                    pbs, pbsn = bank(0, 6)
                    P.tr(pbs[0:64, 0:64], nxt, idf, [nn, "ident_f"], [pbsn])
                    P.copy("act", Xs, pbs[0:64, 0:64], [pbsn], ["Xs"])
                    P.dma("pool", cSs[c, h], Xs, ["Xs"], [], "Xs")
                if u == 7 or c == NCH - 1:
                    P.copy("act", yF[:, (c - u) * 64:(c + 1) * 64], pY[0:64, 0:(u + 1) * 64], [pYn], ["F2"])
            if not sample:
                pb, pbn = bank(0, 6)
                P.tr(pb[0:64, 0:64], Sb[NCH % 2], idf, [Sn[NCH % 2], "ident_f"], [pbn])
                P.copy("act", Xs, pb[0:64, 0:64], [pbn], ["Xs"])
                P.dma("pool", cSp[h], Xs, ["Xs"], [], "Xs")
            ysq, mean, e2m, tmp = X8[4][0:64, 0:T], X8[5][0:64, 0:T], X8[6][0:64, 0:T], X8[7][0:64, 0:T]
            P.act(ysq, yF, AF.Square, ["F2"], ["X8_4"])
            for tb in range(NTB):
                sl = slice(tb * TB, (tb + 1) * TB)
                pm, pmn = bank(0, 6)
                pe, pen = bank(0, 6)
                P.mm(pm[0:64, 0:TB], onesd, yF[:, sl], True, True, ["F2", "rsm_c"], [pmn])
                P.mm(pe[0:64, 0:TB], onesd, ysq[:, sl], True, True, ["X8_4", "rsm_c"], [pen])
                P.copy("act", mean[:, sl], pm[0:64, 0:TB], [pmn], ["X8_5"])
                P.copy("dve", e2m[:, sl], pe[0:64, 0:TB], [pen], ["X8_6"])
            P.tt("pool", tmp, mean, mean, ALU.mult, ["X8_5"], ["X8_7"])
            P.tt("pool", e2m, e2m, tmp, ALU.subtract, ["X8_6", "X8_7"], ["X8_6"])
            P.ts("dve", e2m, e2m, 64e-5, None, ALU.add, None, ["X8_6"], ["X8_6"])
            P.act(e2m, e2m, AF.Sqrt, ["X8_6"], ["X8_6"])
            P.add("dve", lambda: nc.vector.reciprocal(out=e2m, in_=e2m), ["X8_6"], ["X8_6"])
            P.tt("dve", yF, yF, mean, ALU.subtract, ["F2", "X8_5"], ["F2"])
            P.tt("dve", yF, yF, e2m, ALU.mult, ["F2", "X8_6"], ["F2"])
            P.ts("dve", yF, yF, prm("c_gn_g", h), prm("c_gn_b", h), ALU.mult, ALU.add, ["F2", "prm"], ["F2"])
            P.tt("pool", tmp, bs, vF, ALU.mult, ["F1", "X8_3"], ["X8_7"])
            P.tt("dve", yF, yF, tmp, ALU.add, ["F2", "X8_7"], ["F2"])
            P.tt("dve", ygT, yF, gT, ALU.mult, ["F2", "B1"], ["B2"])
            P.dma("pool", SRC["yT"][0][h // 2, (h % 2) * 64:(h % 2) * 64 + 64, :], ygT, ["B2"],
                  [f"{SRC['yT'][1]}{h // 2}"], "B2")

    def load_yT():
        for t in range(NT):
            P.dma("sp", xnT[:, :, t * 128:(t + 1) * 128],
                  yT_scr[:, :, t * 128:(t + 1) * 128].rearrange("k p c -> p k c"),
                  [f"yT_scr{k}" for k in range(KC)], [XNT[t]], f"yTl{t % 4}")

    def final_norm(src_scr, src_tok):
        P.dma("sp", Fb[2][:], final_g[0:1, :].partition_broadcast(128), [], ["F2"], "F2")
        for t in range(NT):
            s_ = t % 2
            P.dma("sp", Fb[s_][:], src_scr[t * 128:(t + 1) * 128, :], [src_tok], [f"F{s_}"], f"F{s_}")
            P.act(Bb[2][:], Fb[s_][:], AF.Square, [f"F{s_}"], ["B2", "ssq"], accum_out=stat[:, 0:1])
            P.ts("dve", stat[:, 1:2], stat[:, 0:1], 1.0 / D, 1e-6, ALU.mult, ALU.add, ["ssq"], ["rstd0"])
            P.act(stat[:, 3:4], stat[:, 1:2], AF.Sqrt, ["rstd0"], ["rstd1"])
            P.add("dve", lambda: nc.vector.reciprocal(out=stat[:, 2:3], in_=stat[:, 3:4]), ["rstd1"], ["rstd"])
            P.stt("dve", Fb[s_][:], Fb[s_][:], stat[:, 2:3], Fb[2][:], ALU.mult, ALU.mult, [f"F{s_}", "rstd", "F2"], [f"F{s_}"])
            P.dma("pool", yp_out[t * 128:(t + 1) * 128, :], Fb[s_][:], [f"F{s_}"], [], f"fo{s_}")


    def sample_path():
        barrier()
        xflat = xnT[:, :, :].rearrange("p k t -> p (k t)")

        def SV(off, n):
            return xflat[:, 2 * off:2 * (off + n)].bitcast(F32)[0:NS, :]

        zs = SV(0, ABC)
        ysl = SV(7168, D)
        hA = SV(9216, D)
        hB = SV(11264, D)
        tmpv = SV(13312, D)
        miscA = Fb[0][0:NS, :]
        miscB = Fb[1][0:NS, :]
        gainb = Fb[2][0:NS, :]
        xsb = Bb[0][0:NS, :]
        xT6 = [Bb[1][:, m * 64:(m + 1) * 64].rearrange("p (k b) -> p k b", b=NS) for m in range(6)]
        hTs = Bb[2][0:96, 0:NS]
        pTs = Bb[2][:, 64:72].rearrange("p (k b) -> p k b", b=NS)
        UF = U[:, :].bitcast(F32)
        Kg, Vg, prod, Wt = UF[:, 0:1024], UF[:, 1024:2048], UF[:, 2048:3072], UF[:, 3072:4096]
        sc = UF[:, 4096:4104]
        biasm = UF[:, 4104:4128].rearrange("p (a h) -> p a h", a=3)
        Pm = UF[:, 4128:4136]
        onesel = UF[:, 4136:4152].rearrange("p (b m) -> p b m", b=4)
        selb = UF[0:NS, 4160:4672].rearrange("p (b m) -> p b m", b=4)
        small = UF[0:NS, 4672:4800]
        zeroT = UF[:, 4800:8192]

        def s_norm(src, gain_row, dst, eps=1e-6):
            P.dma("sp", gainb, gain_row.partition_broadcast(NS), [], ["F2"], "F2")
            P.act(tmpv, src, AF.Square, ["sx"], ["stmp", "sssq"], accum_out=stat[0:NS, 0:1])
            P.ts("dve", stat[0:NS, 1:2], stat[0:NS, 0:1], 1.0 / D, eps, ALU.mult, ALU.add, ["sssq"], ["srs0"])
            P.act(stat[0:NS, 3:4], stat[0:NS, 1:2], AF.Sqrt, ["srs0"], ["srs1"])
            P.add("dve", lambda: nc.vector.reciprocal(out=stat[0:NS, 2:3], in_=stat[0:NS, 3:4]), ["srs1"], ["srs"])
            P.stt("dve", dst, src, stat[0:NS, 2:3], gainb, ALU.mult, ALU.mult, ["sx", "srs", "F2"], ["sx"])

        def s_T(src, dstT, nk=KC, tok="sx"):
            P.copy("dve", xsb[:, 0:nk * 128], src, [tok], ["B0"])
            pb, pbn = bank()
            pbv = pb[:].bitcast(BF16)
            for kc in range(nk):
                P.tr(pbv[:, kc * NS:(kc + 1) * NS], xsb[:, kc * 128:(kc + 1) * 128], ident_b[0:NS, 0:NS],
                     ["B0", "ident_b"], [pbn])
            P.copy("act", dstT, pbv[:, 0:nk * NS].rearrange("p (k b) -> p k b", b=NS), [pbn], ["sT"])

        def s_proj(wsrc, nblocks, xT, dst, nk=KC, col0=0):
            for cb in range(nblocks):
                W, wn = load_wblock(wsrc, [(col0 + cb * 512, 512)], nk)
                pb, pbn = bank()
                for kc in range(nk):
                    P.mm(pb[0:NS, :], xT[:, kc, :], W[:, kc, :], kc == 0, kc == nk - 1, ["sT", wn], [pbn])
                P.copy("act", dst[:, cb * 512:(cb + 1) * 512], pb[0:NS, :], [pbn], ["sx"])

        P.dma("sp", onesel, c_onesel[:, :].rearrange("p (b m) -> p b m", b=4), [], ["sconst"], "sc0")
        P.dma("sp", selb, c_selb[:, :].rearrange("p (b m) -> p b m", b=4), [], ["sconst"], "sc1")
        P.add("dve", lambda: nc.vector.memset(zeroT, 0.0), [], ["zeroT"])
        for p in range(3):
            src = bass.AP(bias_scr.tensor, bias_scr[p, 0].offset + 128, [[511, 128], [65536, 8], [1, 1]])
            P.dma("sp", biasm[:, p, :].unsqueeze(2), src, ["bias_scr"], ["sconst"], f"sb{p}", allow_slow_non_contiguous=True)

        xs_t = hA
        P.dma("sp", xs_t, xs_in[:, :], [], ["sx"], "sx0")
        s_norm(xs_t, norm_g[0:1, :], miscA)
        s_T(miscA, xT6[0])
        s_proj(w_in, 14, xT6[0], zs)
        P.dma("pool", aks[:, :], zs[:, 1024:2048], ["sx"], [], "so0")
        P.dma("pool", avs[:, :], zs[:, 2048:3072], ["sx"], [], "so1")
        pnumA, pnumAn = bank(5, 6)
        pnumB, pnumBn = bank(6, 7)
        pden, pdenn = bank(7, 8)
        first = True
        for b in range(NS):
            pq0, pq0n = bank(0, 5)
            pq1, pq1n = bank(0, 5)
            P.mm(pq0[:, :], selb[:, b, :], zs[:, 0:512], True, True, ["sx", "sconst"], [pq0n])
            P.mm(pq1[:, :], selb[:, b, :], zs[:, 512:1024], True, True, ["sx", "sconst"], [pq1n])
            for p, dil in enumerate((1, 4, 16)):
                r0 = 2048 - 128 * dil
                P.dma("sp", Kg, cache_k[b, r0:2048:dil, :], [], ["Kg"], "Kg")
                P.dma("sp", Vg, cache_v[b, r0:2048:dil, :], [], ["Vg"], "Vg")
                P.tt("dve", prod[:, 0:512], Kg[:, 0:512], pq0[:, :], ALU.mult, ["Kg", pq0n], ["prod"])
                P.tt("dve", prod[:, 512:1024], Kg[:, 512:1024], pq1[:, :], ALU.mult, ["Kg", pq1n], ["prod"])
                P.add("dve", lambda: nc.vector.tensor_reduce(out=sc, in_=prod.rearrange("p (h e) -> p h e", h=8),
                                                             axis=AX.X, op=ALU.add), ["prod"], ["sc"])
                P.stt("dve", sc, sc, SCALE, biasm[:, p, :], ALU.mult, ALU.add, ["sc", "sconst"], ["sc"])
                P.act(Pm, sc, AF.Exp, ["sc"], ["Pm"])
                P.tt("dve", Wt.rearrange("p (h e) -> p h e", h=8), Vg.rearrange("p (h e) -> p h e", h=8),
                     Pm.unsqueeze(2).to_broadcast([128, 8, 128]), ALU.mult, ["Vg", "Pm"], ["Wt"])
                last = (b == NS - 1 and p == 2)
                P.mm(pnumA[0:NS, :], onesel[:, b, :], Wt[:, 0:512], first, last, ["Wt", "sconst"], [pnumAn])
                P.mm(pnumB[0:NS, :], onesel[:, b, :], Wt[:, 512:1024], first, last, ["Wt", "sconst"], [pnumBn])
                P.mm(pden[0:NS, 0:8], onesel[:, b, :], Pm, first, last, ["Pm", "sconst"], [pdenn])
                first = False
        num = miscA[:, 0:1024]
        den = small[:, 0:8]
        e0 = small[:, 8:16]
        rb0 = small[:, 16:24]
        P.copy("act", num[:, 0:512], pnumA[0:NS, :], [pnumAn], ["sx"])
        P.copy("act", num[:, 512:1024], pnumB[0:NS, :], [pnumBn], ["sx"])
        P.copy("act", den, pden[0:NS, 0:8], [pdenn], ["ssm"])
        P.dma("sp", rb0, rel_bias[0:1, :].partition_broadcast(NS), [], ["ssm"], "sx1")
        qk = miscB[:, 0:1024]
        P.tt("dve", qk, zs[:, 0:1024], zs[:, 1024:2048], ALU.mult, ["sx"], ["sx"])
        P.add("dve", lambda: nc.vector.tensor_reduce(out=e0, in_=qk.rearrange("p (h e) -> p h e", h=8), axis=AX.X,
                                                     op=ALU.add), ["sx", "ssm"], ["ssm"])
        P.stt("dve", e0, e0, SCALE, rb0, ALU.mult, ALU.add, ["ssm"], ["ssm"])
        P.act(e0, e0, AF.Exp, ["ssm"], ["ssm"])
        P.ts("dve", e0, e0, 3.0, None, ALU.mult, None, ["ssm"], ["ssm"])
        P.tt("dve", den, den, e0, ALU.add, ["ssm"], ["ssm"])
        P.add("dve", lambda: nc.vector.reciprocal(out=den, in_=den), ["ssm"], ["ssm"])
        v3 = lambda x: x.rearrange("p (h e) -> p h e", h=8)
        P.tt("dve", v3(qk), v3(zs[:, 2048:3072]), e0.unsqueeze(2).to_broadcast([NS, 8, 128]), ALU.mult, ["sx", "ssm"], ["sx"])
        P.tt("dve", num, num, qk, ALU.add, ["sx"], ["sx"])
        P.tt("dve", v3(num), v3(num), den.unsqueeze(2).to_broadcast([NS, 8, 128]), ALU.mult, ["sx", "ssm"], ["sx"])
        P.act(qk, zs[:, 3072:4096], AF.Silu, ["sx"], ["sx"])
        P.tt("dve", ysl[:, 0:1024], num, qk, ALU.mult, ["sx"], ["sx"])
        gvs = miscA[:, 0:1024]
        lgs, lbs = miscB[:, 0:1024], miscB[:, 1024:2048]
        w00, b00 = small[:, 24:32], small[:, 32:40]
        P.dma("sp", lgs, b_ln_g[0:1, :].partition_broadcast(NS), [], ["sx"], "sx2")
        P.dma("sp", lbs, b_ln_b[0:1, :].partition_broadcast(NS), [], ["sx"], "sx3")
        P.dma("sp", w00, bass.AP(b_w_s.tensor, 0, [[0, NS], [16384, 8]]), [], ["ssm"], "sx4", allow_slow_non_contiguous=True)
        P.dma("sp", b00, bass.AP(b_b_s.tensor, 0, [[0, NS], [128, 8]]), [], ["ssm"], "sx5", allow_slow_non_contiguous=True)
        P.act(gvs, zs[:, 5120:6144], AF.Gelu, ["sx"], ["sx", "slA"], accum_out=stat[0:NS, 4:5])
        P.act(tmpv[:, 0:1024], gvs, AF.Square, ["sx"], ["stmp", "slq"], accum_out=stat[0:NS, 6:7])
        P.ts("dve", stat[0:NS, 7:8], stat[0:NS, 4:5], 1.0 / 1024, None, ALU.mult, None, ["slA"], ["slm"])
        P.tt("dve", stat[0:NS, 8:9], stat[0:NS, 7:8], stat[0:NS, 7:8], ALU.mult, ["slm"], ["slm2"])
        P.stt("dve", stat[0:NS, 9:10], stat[0:NS, 6:7], 1.0 / 1024, stat[0:NS, 8:9], ALU.mult, ALU.subtract,
              ["slq", "slm2"], ["slv"])
        P.ts("dve", stat[0:NS, 9:10], stat[0:NS, 9:10], 1e-5, None, ALU.add, None, ["slv"], ["slv2"])
        P.act(stat[0:NS, 10:11], stat[0:NS, 9:10], AF.Sqrt, ["slv2"], ["slsd"])
        P.add("dve", lambda: nc.vector.reciprocal(out=stat[0:NS, 11:12], in_=stat[0:NS, 10:11]), ["slsd"], ["slrs"])
        P.ts("dve", gvs, gvs, stat[0:NS, 7:8], stat[0:NS, 11:12], ALU.subtract, ALU.mult, ["sx", "slm", "slrs"], ["sx"])
        P.tt("dve", gvs, gvs, lgs, ALU.mult, ["sx"], ["sx"])
        P.tt("dve", gvs, gvs, lbs, ALU.add, ["sx"], ["sx"])
        P.dma("pool", bvs[:, :], gvs, ["sx"], [], "so2")
        P.tt("dve", v3(gvs), v3(gvs), w00.unsqueeze(2).to_broadcast([NS, 8, 128]), ALU.mult, ["sx", "ssm"], ["sx"])
        P.tt("dve", v3(gvs), v3(gvs), b00.unsqueeze(2).to_broadcast([NS, 8, 128]), ALU.add, ["sx", "ssm"], ["sx"])
        P.act(lgs, zs[:, 4096:5120], AF.Gelu, ["sx"], ["sx"])
        P.act(lbs, zs[:, 6144:7168], AF.Silu, ["sx"], ["sx"])
        P.tt("dve", gvs, gvs, lgs, ALU.mult, ["sx"], ["sx"])
        P.tt("dve", ysl[:, 1024:2048], gvs, lbs, ALU.mult, ["sx"], ["sx"])

        def s_res_proj(wsrc, y, hin, hout):
            s_T(y, xT6[0])
            s_proj(wsrc, 4, xT6[0], tmpv)
            P.tt("dve", hout, hin, tmpv, ALU.add, ["sx"], ["sx"])

        def s_ple(layer, hin, hout):
            s_T(hin, xT6[0])
            s_proj(ple_wg[layer], 4, xT6[0], tmpv)
            P.act(tmpv, tmpv, AF.Sigmoid, ["sx"], ["sx"])
            P.dma("sp", miscB[:, 0:256], ps_in[layer], [], ["sx"], "sx6")
            s_T(miscB[:, 0:256], pTs, 2)
            s_proj(ple_wp[layer], 4, pTs, miscA, 2)
            P.tt("dve", tmpv, tmpv, miscA, ALU.mult, ["sx"], ["sx"])
            P.tt("dve", hout, hin, tmpv, ALU.add, ["sx"], ["sx"])

        s_res_proj(w_out, ysl, hA, hB)
        s_ple(0, hB, hA)
        s_norm(hA, norm_g[1:2, :], ysl)
        P.dma("pool", cxs[:, :], ysl, ["sx"], [], "so3")
        P.dma("sp", miscB, shift0[:, :], [], ["sx"], "sx7")
        P.tt("dve", miscB, miscB, ysl, ALU.subtract, ["sx"], ["sx"])
        for m in range(6):
            P.dma("sp", gainb, c_mu[m:m + 1, :].partition_broadcast(NS), [], ["F2"], "F2")
            P.tt("dve", miscA, miscB, gainb, ALU.mult, ["sx", "F2"], ["sx"])
            P.tt("dve", miscA, miscA, ysl, ALU.add, ["sx"], ["sx"])
            s_T(miscA, xT6[m], tok="sx")
        for (dst, tok) in ((rs_scr, "rs_scr"), (ks_scr, "ks_scr"), (vs_scr, "vs_scr"), (as_scr, "as_scr")):
            d3 = dst.rearrange("(k p) t -> p k t", p=128)
            for hf in range(2):
                P.dma("sp", d3[:, hf * 8:(hf + 1) * 8, :], zeroT[:, 0:2048].rearrange("p (k t) -> p k t", k=8), ["zeroT"], [tok],
                      "sz_" + tok)
        P.dma("sp", es_scr.rearrange("(k p) f -> p k f", p=128), zeroT[:, 0:2048].unsqueeze(1).to_broadcast([128, 2, 2048]),
              ["zeroT"], ["es_scr"], "sz_es")
        P.dma("sp", gs_scr.rearrange("(k p) t -> p k t", p=128),
              zeroT[:, 0:2048].bitcast(BF16).rearrange("p (k t) -> p k t", k=16), ["zeroT"], ["gs_scr"], "sz_gs")

        def s_proj_fm(wsrc, xT, dst_scr, tok, func=None, bf=False):
            for cb in range(4):
                W, wn = load_wblock(wsrc, [(cb * 512, 512)])
                pb, pbn = bank()
                for fb in range(4):
                    for kc in range(KC):
                        P.mm(pb[:, fb * NS:(fb + 1) * NS], W[:, kc, fb * 128:(fb + 1) * 128], xT[:, kc, :], kc == 0, kc == KC - 1,
                             ["sT", wn], [pbn])
                stg_ = (Fb[2][:, 1024:1040].bitcast(BF16)[:, 0:16] if bf else Fb[2][:, 1024:1040])
                if func is None:
                    P.copy("act", stg_, pb[:, 0:16], [pbn], ["sstg"])
                else:
                    P.act(stg_, pb[:, 0:16], func, [pbn], ["sstg"])
                for fb in range(4):
                    f0 = cb * 512 + fb * 128
                    P.dma("pool", dst_scr[f0:f0 + 128, 0:TS:64], stg_[:, fb * NS:(fb + 1) * NS], ["sstg"], [tok], "sstg",
                          allow_slow_non_contiguous=True)

        def s_lora_hidden(w1src, xT, func):
            W, wn = load_wblock(w1src, [(0, 96)])
            pb, pbn = bank()
            for kc in range(KC):
                P.mm(pb[0:96, 0:NS], W[:, kc, 0:96], xT[:, kc, :], kc == 0, kc == KC - 1, ["sT", wn], [pbn])
            if func is None:
                P.copy("act", hTs, pb[0:96, 0:NS], [pbn], ["shT"])
            else:
                P.act(hTs, pb[0:96, 0:NS], func, [pbn], ["shT"])

        def s_load_small(wsrc):
            i = wctr[0]
            wctr[0] += 1
            s_ = i % 2
            ss = (i * 4) % 2
            stv = wst[ss][:, :, :].rearrange("p a b -> p (a b)")
            P.dma("sp", stv[0:96, :], wsrc[:, :], [], [f"wst{ss}"], f"wst{ss}")
            wv = wbf[s_][:, 0:4, :].rearrange("p a b -> p (a b)")
            P.copy("dve", wv[0:96, :], stv[0:96, :], [f"wst{ss}"], [f"wbf{s_}"])
            return wv, f"wbf{s_}"

        s_proj_fm(c_wr, xT6[0], rs_scr, "rs_scr")
        s_proj_fm(c_wk, xT6[2], ks_scr, "ks_scr")
        s_proj_fm(c_wv, xT6[3], vs_scr, "vs_scr")
        s_proj_fm(c_wg, xT6[5], gs_scr, "gs_scr", AF.Silu, True)
        s_lora_hidden(c_a1, xT6[4], None)
        a2b, a2n = s_load_small(c_a2)
        pb, pbn = bank()
        for fb in range(KC):
            P.mm(pb[:, fb * NS:(fb + 1) * NS], a2b[0:96, fb * 128:(fb + 1) * 128], hTs, True, True, ["shT", a2n], [pbn])
        stg_a = Fb[2][:, 1040:1104]
        P.copy("act", stg_a, pb[:, 0:64], [pbn], ["sstg2"])
        for fb in range(KC):
            P.dma("pool", as_scr[fb * 128:(fb + 1) * 128, 0:TS:64], stg_a[:, fb * NS:(fb + 1) * NS], ["sstg2"], ["as_scr"],
                  "sstg2", allow_slow_non_contiguous=True)
        s_lora_hidden(c_w1, xT6[1], AF.Tanh)
        w2b, w2n = s_load_small(c_w2)
        P.dma("sp", miscB, c_w0[0:1, :].partition_broadcast(NS), [], ["sx"], "sx8")
        for cb in range(4):
            pb, pbn = bank()
            P.mm(pb[0:NS, :], hTs, w2b[0:96, cb * 512:(cb + 1) * 512], True, True, ["shT", w2n], [pbn])
            P.tt("dve", miscA[:, cb * 512:(cb + 1) * 512], pb[0:NS, :], miscB[:, cb * 512:(cb + 1) * 512], ALU.add, [pbn, "sx"],
                 ["sx"])
        P.act(miscA, miscA, AF.Sigmoid, ["sx"], ["sx"])
        P.dma("pool", es_scr[0:TS:64, :], miscA, ["sx"], ["es_scr"], "so4")
        P.copy("dve", lng[0:NS, :], hA[:, 0:1024], ["sx"], ["hsave"])
        P.copy("dve", lnb[0:NS, :], hA[:, 1024:2048], ["sx"], ["hsave"])
        rwkv_scan(range(32), 4, dict(r=(rs_scr, "rs_scr"), k=(ks_scr, "ks_scr"), v=(vs_scr, "vs_scr"), a=(as_scr, "as_scr"),
                                     e=(es_scr, "es_scr"), g=(gs_scr, "gs_scr"), yT=(yTs_scr, "yTs_scr")), True)
        barrier()
        ysT = xT6[0]
        for kc in range(KC):
            P.dma("sp", ysT[:, kc, :], yTs_scr[kc, :, 0:TS:64], [f"yTs_scr{k}" for k in range(KC)], ["sT"],
                  "sx9", allow_slow_non_contiguous=True)
        P.copy("dve", hA[:, 0:1024], lng[0:NS, :], ["hsave"], ["sx"])
        P.copy("dve", hA[:, 1024:2048], lnb[0:NS, :], ["hsave"], ["sx"])
        s_proj(c_wo, 4, ysT, tmpv)
        P.tt("dve", hB, hA, tmpv, ALU.add, ["sx"], ["sx"])
        s_ple(1, hB, hA)
        s_norm(hA, final_g[0:1, :], ysl)
        P.dma("pool", ys_out[:, :], ysl, ["sx"], [], "so5")

    if stage >= 6:
        rwkv_consts()
        phase_T(h_scr[1], "h_scr1", 1, True)
        rwkv_proj()
    if stage >= 7:
        nh1 = int(os.environ.get("NH1", "32"))
        rwkv_scan(range(nh1))
    if stage >= 8:
        barrier()
        load_yT()
        phase_proj_res(c_wo, h_scr[1], "h_scr1", h_scr[2], "h_scr2", None)
    if stage >= 9:
        phase_ple(1, h_scr[2], "h_scr2", h_scr[3], "h_scr3")
        final_norm(h_scr[3], "h_scr3")
    if stage == -1:
        rwkv_consts()
    if stage >= 10 or stage == -1:
        sample_path()

    if dbg:
        src, tok = {"h1": (h_scr[0], "h_scr0"), "h2": (h_scr[1], "h_scr1"), "h3": (h_scr[2], "h_scr2"),
                    "h4": (h_scr[3], "h_scr3"), "r": (r_scr, "r_scr"), "k": (k_scr, "k_scr"), "v": (v_scr, "v_scr"),
                    "a": (a_scr, "a_scr"), "e": (e_scr, "e_scr")}[dbg]
        for t in range(NT):
            s = t % 2
            P.dma("sp", Fb[s][:], src[t * 128:(t + 1) * 128, :], [tok], [f"F{s}"], f"F{s}")
            P.dma("sp", dbg_out[t * 128:(t + 1) * 128, :], Fb[s][:], [f"F{s}"], [], f"F{s}")

    P.sbuf_left = nc.sbuf_bytes_remaining
    P.finish()
    st.close()
    return nc, P


_CACHE = {}
OUT_NAMES = ["y_prompt","y_sample","a_k_prompt","a_v_prompt","a_k_sample","a_v_sample","b_v_sample","c_wkv_prompt","c_shift_prompt","c_wkv_sample","c_shift_sample"]


def make_in_maps(inputs):
    consts = host_consts()
    f = lambda a: np.ascontiguousarray(a, dtype=np.float32)
    x_prompt = f(inputs["x_prompt"])
    p_prompt = f(inputs["p_prompt"])
    shared = {
        "norm_g": f(inputs["norm_g"]),
        "rel_bias": f(inputs["rel_bias"]),
        "ab_w_in": f(inputs["ab_w_in"][0]),
        "ab_w_out": f(inputs["ab_w_out"][0]),
        "b_w_s": f(inputs["b_w_s"][0]),
        "b_b_s": f(inputs["b_b_s"][0]).reshape(1, 1024),
        "b_ln_g": f(inputs["b_ln_g"]).reshape(1, 1024),
        "b_ln_b": f(inputs["b_ln_b"]).reshape(1, 1024),
        "ple_w_proj": f(inputs["ple_w_proj"]),
        "ple_w_gate": f(inputs["ple_w_gate"]),
        "c_maskg": consts["maskg"], "c_maskn": consts["maskn"], "c_uneg": consts["uneg"],
        "c_selb": consts["selb"], "c_onesel": consts["onesel"],
        "c_mu": f(inputs["c_mu"][0]),
        "c_w_r": f(inputs["c_w_r"][0]), "c_w_k": f(inputs["c_w_k"][0]), "c_w_v": f(inputs["c_w_v"][0]),
        "c_w_g": f(inputs["c_w_g"][0]), "c_w_o": f(inputs["c_w_o"][0]),
        "c_w0": f(inputs["c_w0"]).reshape(1, D), "c_w1": f(inputs["c_w1"][0]), "c_w2": f(inputs["c_w2"][0]),
        "c_a0": f(inputs["c_a0"]).reshape(1, D), "c_a1": f(inputs["c_a1"][0]), "c_a2": f(inputs["c_a2"][0]),
        "c_k_k": f(inputs["c_k_k"]).reshape(1, D), "c_k_a": f(inputs["c_k_a"]).reshape(1, D),
        "c_r_k": f(inputs["c_r_k"]).reshape(1, D), "c_gn_g": f(inputs["c_gn_g"]).reshape(1, D),
        "c_gn_b": f(inputs["c_gn_b"]).reshape(1, D), "final_norm_g": f(inputs["final_norm_g"]).reshape(1, D),
        "c_ident": consts["ident"],
        "c_onehot": consts["onehot"],
        "c_trimask": consts["trimask"],
    }
    in_maps = []
    for c in range(NCORES):
        m = dict(shared)
        m["xp"] = x_prompt[c]
        sl = slice(c * NS, (c + 1) * NS)
        m["xs"] = f(inputs["x_sample"][sl, 0])
        m["cache_k"] = f(inputs["cache_a_k"][0, sl]).reshape(NS, 2048, 1024)
        m["cache_v"] = f(inputs["cache_a_v"][0, sl]).reshape(NS, 2048, 1024)
        m["wkv0"] = f(inputs["state_c_wkv"][0, sl])
        m["shift0"] = f(inputs["state_c_shift"][0, sl])
        m["ps"] = f(inputs["p_sample"][:, sl, 0])
        m["pp"] = np.ascontiguousarray(p_prompt[:, c])
        in_maps.append(m)
    return in_maps


def run_raw(inputs, stage=99, dbg=None):
    key = (stage, dbg)
    if key not in _CACHE:
        _CACHE[key] = build(stage, dbg)
    nc, P = _CACHE[key]
    res = run_bass_kernel_spmd(nc, make_in_maps(inputs), core_ids=list(range(NCORES)))
    return res.results


def kernel(**inputs):
    r = run_raw(inputs)
    st = lambda k, shp: np.stack([r[c][k].reshape(shp) for c in range(NCORES)])
    cat = lambda k, shp: np.concatenate([r[c][k].reshape(shp) for c in range(NCORES)], axis=0)
    y_prompt = st("yp", (S, D))
    y_sample = cat("ys", (NS, 1, D))
    akp = st("akp", (S, 8, 128))[None]
    avp = st("avp", (S, 8, 128))[None]
    aks = cat("aks", (NS, 1, 8, 128))[None]
    avs = cat("avs", (NS, 1, 8, 128))[None]
    bvs = cat("bvs", (NS, 1, 1024))[None]
    cSp = st("cSp", (32, 64, 64))[None]
    cxp = st("cxp", (D,))[None]
    cSs = cat("cSs", (NS, 32, 64, 64))[None]
    cxs = cat("cxs", (NS, D))[None]
    return (y_prompt, y_sample, akp, avp, aks, avs, bvs, cSp, cxp, cSs, cxs)
```
